# Optimizing a Trainium2 kernel written in Bass

```python
import numpy as np
import jax, jax.numpy as jnp
from jax import lax

D_MODEL = 1024
BATCH = 32
SEQ = 2048
DEPTH = 2

N_A_LAYERS = max(1, DEPTH // 2)
N_B_LAYERS = DEPTH - N_A_LAYERS

D_RNN = 1536
N_RNN_BLOCKS = 16
RNN_BLOCK = D_RNN // N_RNN_BLOCKS
CONV_WIDTH = 4
RGLRU_C = 8.0

D_FF = -(-8 * D_MODEL // (3 * 256)) * 256

N_HEADS = 16
N_KV_GROUPS = 4
HEADS_PER_GROUP = N_HEADS // N_KV_GROUPS
D_QK = 96
D_V = 64
ROPE_DIMS = D_QK // 4
ROPE_THETA = 500000.0
CMP_BLOCK = 32
CMP_STRIDE = 16
CMP_HIDDEN = 256
SEL_BLOCK = 64
SEL_TOPK = 8
WINDOW = 512
Q_BLOCK = 128
N_BRANCH = 3
EPS = 1e-6

kernel_name = "hybrid_rglru_nsa_yoco"


def rmsnorm(x, g):
    xf = x.astype(jnp.float32)
    y = xf * lax.rsqrt(jnp.mean(xf * xf, axis=-1, keepdims=True) + EPS)
    return (y * g.astype(jnp.float32)).astype(x.dtype)


def apply_partial_rope(x):
    S = x.shape[1]
    half = ROPE_DIMS // 2
    inv = ROPE_THETA ** (-jnp.arange(half, dtype=jnp.float32) * 2.0 / ROPE_DIMS)
    ang = jnp.arange(S, dtype=jnp.float32)[:, None] * inv[None, :]
    bshape = (1, S) + (1,) * (x.ndim - 3) + (half,)
    cos = jnp.cos(ang).reshape(bshape)
    sin = jnp.sin(ang).reshape(bshape)
    xr = x[..., :ROPE_DIMS].astype(jnp.float32)
    x1, x2 = xr[..., :half], xr[..., half:]
    rot = jnp.concatenate([x1 * cos - x2 * sin, x2 * cos + x1 * sin], axis=-1)
    return jnp.concatenate([rot.astype(x.dtype), x[..., ROPE_DIMS:]], axis=-1)


def masked_softmax(s, mask):
    s = jnp.where(mask, s, -jnp.inf)
    m = jnp.max(s, axis=-1, keepdims=True)
    m = jnp.where(jnp.isfinite(m), m, 0.0)
    e = jnp.where(mask, jnp.exp(s - m), 0.0)
    d = jnp.sum(e, axis=-1, keepdims=True)
    return e / jnp.where(d > 0, d, 1.0)


def swiglu(h, w_gu, w_down):
    u = h @ w_gu
    g, v = u[..., :D_FF], u[..., D_FF:]
    return (jax.nn.silu(g) * v) @ w_down


def rglru_mixer(h, w_in, conv_w, conv_b, w_ra, b_ra, w_ix, b_ix, lam, w_out):
    B, S, _ = h.shape
    u = h @ w_in
    xb, yb = u[..., :D_RNN], u[..., D_RNN:]
    y = jax.nn.gelu(yb)
    xc = lax.conv_general_dilated(
        xb, conv_w[:, None, :], window_strides=(1,), padding=[(CONV_WIDTH - 1, 0)],
        dimension_numbers=('NWC', 'WIO', 'NWC'), feature_group_count=D_RNN) + conv_b
    xg = xc.reshape(B, S, N_RNN_BLOCKS, RNN_BLOCK)
    r = jax.nn.sigmoid(jnp.einsum('bsni,nij->bsnj', xg, w_ra) + b_ra).reshape(B, S, D_RNN)
    i = jax.nn.sigmoid(jnp.einsum('bsni,nij->bsnj', xg, w_ix) + b_ix).reshape(B, S, D_RNN)
    log_a = -RGLRU_C * r.astype(jnp.float32) * jax.nn.softplus(-lam.astype(jnp.float32))
    a = jnp.exp(log_a)
    mult = jnp.sqrt(-jnp.expm1(2.0 * log_a))
    bx = mult * (i * xc).astype(jnp.float32)

    def step(hprev, inp):
        a_t, b_t = inp
        hn = a_t * hprev + b_t
        return hn, hn

    _, hs = lax.scan(step, jnp.zeros((B, D_RNN), jnp.float32),
                     (a.transpose(1, 0, 2), bx.transpose(1, 0, 2)))
    rec = hs.transpose(1, 0, 2).astype(h.dtype)
    return (rec * y) @ w_out


def compress_blocks(raw, pos, w1, w2):
    B, S, G, Dh = raw.shape
    nc = (S - CMP_BLOCK) // CMP_STRIDE + 1
    idx = np.arange(nc)[:, None] * CMP_STRIDE + np.arange(CMP_BLOCK)[None, :]
    blk = raw[:, idx] + pos[None, None, :, None, :]
    blk = blk.transpose(0, 1, 3, 2, 4).reshape(B, nc, G, CMP_BLOCK * Dh)
    return jax.nn.gelu(blk @ w1) @ w2


def shared_kv(h, kv_norm_g, kv_w, cmp_pos_k, cmp_w1_k, cmp_w2_k, cmp_pos_v, cmp_w1_v, cmp_w2_v):
    B, S, _ = h.shape
    hn = rmsnorm(h, kv_norm_g)
    kv = hn @ kv_w
    gk, gv = N_KV_GROUPS * D_QK, N_KV_GROUPS * D_V
    sizes = [gk, gv, gk, gv, gk, gv]
    offs = np.concatenate([[0], np.cumsum(sizes)])
    parts = [kv[..., offs[j]:offs[j + 1]] for j in range(6)]
    k_cmp, v_cmp, k_sel, v_sel, k_win, v_win = [
        p.reshape(B, S, N_KV_GROUPS, -1) for p in parts]
    kc = compress_blocks(k_cmp, cmp_pos_k, cmp_w1_k, cmp_w2_k)
    vc = compress_blocks(v_cmp, cmp_pos_v, cmp_w1_v, cmp_w2_v)
    ks = apply_partial_rope(k_sel)
    kw = apply_partial_rope(k_win)
    pad = ((0, 0), (WINDOW, 0), (0, 0), (0, 0))
    kw_pad = jnp.pad(kw, pad)
    vw_pad = jnp.pad(v_win, pad)
    return (kc, vc, ks, v_sel, kw_pad, vw_pad)


def nsa_mixer(h, w_q, gate_bias, w_o, kc, vc, ks, vs, kw_pad, vw_pad):
    B, S, _ = h.shape
    G, HPG = N_KV_GROUPS, HEADS_PER_GROUP
    nc = kc.shape[1]
    ns = S // SEL_BLOCK
    nq = S // Q_BLOCK
    k_top = min(SEL_TOPK, ns)
    cs = np.arange(nc) * CMP_STRIDE
    ss = np.arange(ns) * SEL_BLOCK
    overlap = jnp.asarray(((cs[:, None] < ss[None, :] + SEL_BLOCK) &
                           (cs[:, None] + CMP_BLOCK > ss[None, :])).astype(np.float32))
    c_end = jnp.arange(nc) * CMP_STRIDE + CMP_BLOCK - 1
    blk_ids = jnp.arange(ns)

    u = h @ w_q
    q = (u[..., :N_HEADS * D_QK] * (D_QK ** -0.5)).reshape(B, S, G, HPG, D_QK)
    gates = jax.nn.sigmoid(u[..., N_HEADS * D_QK:] + gate_bias).reshape(B, S, G, HPG, N_BRANCH)
    q_rope = apply_partial_rope(q)

    def per_seq(args):
        qn, qr, gt, kc_b, vc_b, ks_s, vs_s, kw_b, vw_b = args
        ks_blk = ks_s.reshape(ns, SEL_BLOCK, G, D_QK).transpose(2, 0, 1, 3)
        vs_blk = vs_s.reshape(ns, SEL_BLOCK, G, D_V).transpose(2, 0, 1, 3)
        gidx = jnp.arange(G)[None, :, None]

        def per_block(bargs):
            qn_t, qr_t, g_t, qi = bargs
            s0 = qi * Q_BLOCK
            t = s0 + jnp.arange(Q_BLOCK)
            s_c = jnp.einsum('tghd,cgd->tghc', qn_t, kc_b).astype(jnp.float32)
            mask_c = (c_end[None, :] <= t[:, None])[:, None, None, :]
            p_c = masked_softmax(s_c, mask_c)
            o_c = jnp.einsum('tghc,cgv->tghv', p_c.astype(vc_b.dtype), vc_b)
            imp = jnp.einsum('tghc,cn->tgn', p_c, overlap)
            cur = (t // SEL_BLOCK)[:, None, None]
            visible = (blk_ids * SEL_BLOCK)[None, None, :] <= t[:, None, None]
            forced = (blk_ids == 0) | (blk_ids == cur) | (blk_ids == cur - 1)
            score = jnp.where(forced, jnp.inf, jnp.where(visible, imp, -jnp.inf))
            _, idx = lax.top_k(score, k_top)
            kg = ks_blk[gidx, idx]
            vg = vs_blk[gidx, idx]
            s_s = jnp.einsum('tghd,tgjld->tghjl', qr_t, kg).astype(jnp.float32)
            pos = idx[..., None] * SEL_BLOCK + jnp.arange(SEL_BLOCK)
            mask_s = (pos <= t[:, None, None, None]).reshape(Q_BLOCK, G, 1, k_top * SEL_BLOCK)
            p_s = masked_softmax(s_s.reshape(Q_BLOCK, G, HPG, k_top * SEL_BLOCK), mask_s)
            p_s = p_s.reshape(Q_BLOCK, G, HPG, k_top, SEL_BLOCK)
            o_s = jnp.einsum('tghjl,tgjlv->tghv', p_s.astype(vg.dtype), vg)
            kwin = lax.dynamic_slice_in_dim(kw_b, s0, WINDOW + Q_BLOCK, axis=0)
            vwin = lax.dynamic_slice_in_dim(vw_b, s0, WINDOW + Q_BLOCK, axis=0)
            p_pos = s0 - WINDOW + jnp.arange(WINDOW + Q_BLOCK)
            mask_w = ((p_pos[None, :] >= 0) & (p_pos[None, :] <= t[:, None]) &
                      (p_pos[None, :] > t[:, None] - WINDOW))[:, None, None, :]
            s_w = jnp.einsum('tghd,sgd->tghs', qr_t, kwin).astype(jnp.float32)
            p_w = masked_softmax(s_w, mask_w)
            o_w = jnp.einsum('tghs,sgv->tghv', p_w.astype(vwin.dtype), vwin)
            return g_t[..., 0:1] * o_c + g_t[..., 1:2] * o_s + g_t[..., 2:3] * o_w

        out = lax.map(per_block, (
            qn.reshape(nq, Q_BLOCK, G, HPG, D_QK),
            qr.reshape(nq, Q_BLOCK, G, HPG, D_QK),
            gt.reshape(nq, Q_BLOCK, G, HPG, N_BRANCH),
            jnp.arange(nq)))
        return out.reshape(S, N_HEADS * D_V)

    o = lax.map(per_seq, (q, q_rope, gates, kc, vc, ks, vs, kw_pad, vw_pad))
    return o @ w_o


def setup_inputs(seed: int = 0) -> dict:
    key = jax.random.key(seed)
    ks = jax.random.split(key, 32)
    f32 = jnp.float32
    res = (2.0 * DEPTH) ** -0.5

    def nrm(k, shape, scale):
        return jax.random.normal(k, shape, f32) * scale

    kv_cols = N_BRANCH * N_KV_GROUPS * (D_QK + D_V)
    q_cols = N_HEADS * D_QK + N_HEADS * N_BRANCH
    u = jax.random.uniform(ks[9], (N_A_LAYERS, D_RNN), f32, 0.9, 0.999)
    s = u ** (1.0 / RGLRU_C)
    a_lambda = jnp.log(s) - jnp.log1p(-s)
    return {
        'x': nrm(ks[0], (BATCH, SEQ, D_MODEL), 1.0),
        'norm_g': 1.0 + nrm(ks[1], (DEPTH, 2, D_MODEL), 0.02),
        'ffn_w_gu': nrm(ks[2], (DEPTH, D_MODEL, 2 * D_FF), D_MODEL ** -0.5),
        'ffn_w_down': nrm(ks[3], (DEPTH, D_FF, D_MODEL), res * D_FF ** -0.5),
        'a_w_in': nrm(ks[4], (N_A_LAYERS, D_MODEL, 2 * D_RNN), D_MODEL ** -0.5),
        'a_conv_w': nrm(ks[5], (N_A_LAYERS, CONV_WIDTH, D_RNN), CONV_WIDTH ** -0.5),
        'a_conv_b': nrm(ks[6], (N_A_LAYERS, D_RNN), 0.01),
        'a_w_ra': nrm(ks[7], (N_A_LAYERS, N_RNN_BLOCKS, RNN_BLOCK, RNN_BLOCK), RNN_BLOCK ** -0.5),
        'a_b_ra': nrm(ks[8], (N_A_LAYERS, N_RNN_BLOCKS, RNN_BLOCK), 0.01),
        'a_w_ix': nrm(ks[10], (N_A_LAYERS, N_RNN_BLOCKS, RNN_BLOCK, RNN_BLOCK), RNN_BLOCK ** -0.5),
        'a_b_ix': nrm(ks[11], (N_A_LAYERS, N_RNN_BLOCKS, RNN_BLOCK), 0.01),
        'a_lambda': a_lambda,
        'a_w_out': nrm(ks[12], (N_A_LAYERS, D_RNN, D_MODEL), res * D_RNN ** -0.5),
        'kv_norm_g': 1.0 + nrm(ks[13], (D_MODEL,), 0.02),
        'kv_w': nrm(ks[14], (D_MODEL, kv_cols), D_MODEL ** -0.5),
        'cmp_pos_k': nrm(ks[15], (CMP_BLOCK, D_QK), 0.1),
        'cmp_w1_k': nrm(ks[16], (CMP_BLOCK * D_QK, CMP_HIDDEN), (CMP_BLOCK * D_QK) ** -0.5),
        'cmp_w2_k': nrm(ks[17], (CMP_HIDDEN, D_QK), CMP_HIDDEN ** -0.5),
        'cmp_pos_v': nrm(ks[18], (CMP_BLOCK, D_V), 0.1),
        'cmp_w1_v': nrm(ks[19], (CMP_BLOCK * D_V, CMP_HIDDEN), (CMP_BLOCK * D_V) ** -0.5),
        'cmp_w2_v': nrm(ks[20], (CMP_HIDDEN, D_V), CMP_HIDDEN ** -0.5),
        'b_w_q': nrm(ks[21], (N_B_LAYERS, D_MODEL, q_cols), D_MODEL ** -0.5),
        'b_gate_bias': nrm(ks[22], (N_B_LAYERS, N_HEADS * N_BRANCH), 0.01),
        'b_w_o': nrm(ks[23], (N_B_LAYERS, N_HEADS * D_V, D_MODEL), res * (N_HEADS * D_V) ** -0.5),
        'final_g': 1.0 + nrm(ks[24], (D_MODEL,), 0.02),
    }


def reference(x, norm_g, ffn_w_gu, ffn_w_down, a_w_in, a_conv_w, a_conv_b, a_w_ra, a_b_ra,
              a_w_ix, a_b_ix, a_lambda, a_w_out, kv_norm_g, kv_w, cmp_pos_k, cmp_w1_k, cmp_w2_k,
              cmp_pos_v, cmp_w1_v, cmp_w2_v, b_w_q, b_gate_bias, b_w_o, final_g):
    h = x
    shared = None
    for layer in range(DEPTH):
        hn = rmsnorm(h, norm_g[layer, 0])
        if layer < N_A_LAYERS:
            h = h + rglru_mixer(hn, a_w_in[layer], a_conv_w[layer], a_conv_b[layer],
                                a_w_ra[layer], a_b_ra[layer], a_w_ix[layer], a_b_ix[layer],
                                a_lambda[layer], a_w_out[layer])
        else:
            j = layer - N_A_LAYERS
            kc, vc, ks_, vs_, kw_pad, vw_pad = shared
            h = h + nsa_mixer(hn, b_w_q[j], b_gate_bias[j], b_w_o[j],
                              kc, vc, ks_, vs_, kw_pad, vw_pad)
        h = h + swiglu(rmsnorm(h, norm_g[layer, 1]), ffn_w_gu[layer], ffn_w_down[layer])
        if layer == N_A_LAYERS - 1:
            shared = shared_kv(h, kv_norm_g, kv_w, cmp_pos_k, cmp_w1_k, cmp_w2_k,
                               cmp_pos_v, cmp_w1_v, cmp_w2_v)
    return rmsnorm(h, final_g)
```

```python
import contextlib
import os
import numpy as np
import concourse.bass as bass
import concourse.mybir as mybir
from concourse.bass_utils import run_bass_kernel_spmd

F32 = mybir.dt.float32
BF16 = mybir.dt.bfloat16
AF = mybir.ActivationFunctionType
ALU = mybir.AluOpType
AX = mybir.AxisListType

D = 1024
S = 2048
T = 512
DRNN = 1536
NBLK = 16
RB = 96
DFF = 2816
NFC = 22
NH = 16
NG = 4
DQK = 96
DV = 64
NCMP = 127
EPS = 1e-6
NEG = -30000.0
GK = 0.7978845608028654
GC = 0.044715


class Res:
    __slots__ = ("name", "last_w", "readers")

    def __init__(self, name):
        self.name = name
        self.last_w = None
        self.readers = []


class Op:
    __slots__ = ("eng", "fn", "waits", "signal", "is_dma", "key", "pos", "sigidx", "is_nop")

    def __init__(self, eng, fn, is_dma=False, key=None):
        self.eng = eng
        self.fn = fn
        self.waits = []
        self.signal = False
        self.is_dma = is_dma
        self.key = key
        self.pos = -1
        self.sigidx = -1
        self.is_nop = False


ENGS = ("pe", "act", "dve", "pool", "sp")


class Prog:
    def __init__(self, nc, stack):
        self.nc = nc
        self.stack = stack
        self.streams = {e: [] for e in ENGS}
        self.emitted = {e: 0 for e in ENGS}
        self.nsig = {e: 0 for e in ENGS}
        self.npos = {e: 0 for e in ENGS}
        self.known = {e: {} for e in ENGS}
        self.dma_cnt = {}
        self.esem = {e: stack.enter_context(nc.semaphore("s_" + e)) for e in ENGS}
        self.ksem = {}

    def res(self, name="r"):
        return Res(name)

    def _dep(self, op, d, raw):
        if d is None or d is op:
            return
        E = op.eng
        k = self.known[E]
        if d.is_dma:
            src = ("dma", d.key)
            cnt = self.dma_cnt[d.key]
            if k.get(src, -1) >= cnt:
                return
            k[src] = cnt
            op.waits.append(("dma", d.key, cnt))
            return
        if d.eng == E and not op.is_dma:
            if E == "pe":
                return
        if k.get(d.eng, -1) >= d.pos:
            return
        k[d.eng] = d.pos
        d.signal = True
        op.waits.append(("eng", d))

    def op(self, eng, fn, reads=(), writes=(), is_dma=False, key=None):
        o = Op(eng, fn, is_dma, key)
        if is_dma:
            self.dma_cnt.setdefault(key, 0)
        else:
            o.pos = self.npos[eng]
            self.npos[eng] += 1
        for r in reads:
            self._dep(o, r.last_w, True)
        for w in writes:
            self._dep(o, w.last_w, False)
            for rd in w.readers:
                self._dep(o, rd, False)
        if is_dma:
            self.dma_cnt[key] += 1
            o.pos = self.dma_cnt[key]
        for r in reads:
            r.readers.append(o)
        for w in writes:
            w.last_w = o
            w.readers = []
        self.streams[eng].append(o)
        return o

    def dma(self, q, out, in_, reads=(), writes=(), key=None):
        key = id(key)
        return self.op(q, lambda e: e.dma_start(out=out, in_=in_), reads, writes, is_dma=True, key=key)

    def barrier(self):
        lasts = []
        for e in ENGS:
            comp = [o for o in self.streams[e] if not o.is_dma and not o.is_nop]
            if comp:
                lasts.append(comp[-1])
        for e in ENGS:
            o = Op(e, lambda eng: eng.nop())
            o.is_nop = True
            o.pos = self.npos[e]
            self.npos[e] += 1
            k = self.known[e]
            for d in lasts:
                if k.get(d.eng, -1) >= d.pos:
                    continue
                k[d.eng] = d.pos
                d.signal = True
                o.waits.append(("eng", d))
            for key, cnt in self.dma_cnt.items():
                src = ("dma", key)
                if k.get(src, -1) >= cnt:
                    continue
                k[src] = cnt
                o.waits.append(("dma", key, cnt))
            self.streams[e].append(o)

    def flush(self):
        self.barrier()
        nc = self.nc
        for e in ENGS:
            n = self.nsig[e]
            for o in self.streams[e]:
                if not o.is_dma and o.signal:
                    n += 1
                    o.sigidx = n
            self.nsig[e] = n
        for key in self.dma_cnt:
            if key not in self.ksem:
                self.ksem[key] = self.stack.enter_context(nc.semaphore("d_%d" % len(self.ksem)))
        esem, ksem = self.esem, self.ksem
        streams = self.streams

        def run(en, eng):
            for o in streams[en]:
                for w in o.waits:
                    if w[0] == "dma":
                        eng.wait_ge(ksem[w[1]], 16 * w[2])
                    else:
                        eng.wait_ge(esem[w[1].eng], w[1].sigidx)
                ins = o.fn(eng)
                if o.is_dma:
                    ins.then_inc(ksem[o.key], 16)
                elif o.signal:
                    ins.then_inc(esem[en], 1)

        with nc.Block() as block:
            @block.tensor
            def _(e):
                run("pe", e)

            @block.scalar
            def _(e):
                run("act", e)

            @block.vector
            def _(e):
                run("dve", e)

            @block.gpsimd
            def _(e):
                run("pool", e)

            @block.sync
            def _(e):
                run("sp", e)
        self.streams = {e: [] for e in ENGS}


class Tl:
    __slots__ = ("t", "r")

    def __init__(self, t, r):
        self.t = t
        self.r = r

    def __getitem__(self, k):
        return self.t[k]


def _rs(xs):
    out = []
    for x in xs:
        if x is None:
            continue
        out.append(x.r if isinstance(x, Tl) else x)
    return out


class K:
    def __init__(self, nc, P):
        self.nc = nc
        self.P = P
        self.n = 0
        self.rr = 0

    def sb(self, st, shape, dt=F32, name=None):
        self.n += 1
        nm = "%s_%d" % (name or "t", self.n)
        return Tl(st.enter_context(self.nc.sbuf_tensor(nm, list(shape), dt)), Res(nm))

    def ps(self, st, shape, dt=F32, name=None):
        self.n += 1
        nm = "%s_%d" % (name or "p", self.n)
        return Tl(st.enter_context(self.nc.psum_tensor(nm, list(shape), dt)), Res(nm))

    def mm(self, out, lhsT, rhs, start, stop, rd, wr, skip=False):
        if skip:
            f = lambda e: e.matmul(out, lhsT=lhsT, rhs=rhs, start=start, stop=stop, skip_group_check=True)
        else:
            f = lambda e: e.matmul(out, lhsT=lhsT, rhs=rhs, start=start, stop=stop)
        self.P.op("pe", f, _rs(rd), _rs(wr))

    def tr(self, out, in_, ident, rd, wr):
        self.P.op("pe", lambda e: e.transpose(out, in_, ident), _rs(rd), _rs(wr))

    def act(self, out, in_, func, rd, wr, scale=1.0, bias=None, accum=None):
        kw = {}
        if bias is not None:
            kw["bias"] = bias
        if accum is not None:
            kw["accum_out"] = accum
        self.P.op("act", lambda e: e.activation(out=out, in_=in_, func=func, scale=scale, **kw), _rs(rd), _rs(wr))

    def cp(self, eng, out, in_, rd, wr):
        self.P.op(eng, lambda e: e.tensor_copy(out, in_), _rs(rd), _rs(wr))

    def ts(self, eng, out, in0, s1, s2, op0, op1, rd, wr):
        self.P.op(eng, lambda e: e.tensor_scalar(out, in0, s1, s2, op0, op1), _rs(rd), _rs(wr))

    def tt(self, eng, out, in0, in1, op, rd, wr):
        self.P.op(eng, lambda e: e.tensor_tensor(out, in0, in1, op), _rs(rd), _rs(wr))

    def stt(self, out, in0, scalar, in1, op0, op1, rd, wr):
        self.P.op("dve", lambda e: e.scalar_tensor_tensor(out=out, in0=in0, scalar=scalar, in1=in1, op0=op0, op1=op1),
                  _rs(rd), _rs(wr))

    def memset(self, eng, out, val, wr):
        self.P.op(eng, lambda e: e.memset(out, val), (), _rs(wr))

    def dma(self, out, in_, rd, wr, key, q="sp"):
        self.P.dma(q, out, in_, _rs(rd), _rs(wr), key=(key.r if isinstance(key, Tl) else key))

    def ceng(self):
        self.rr += 1
        return ("dve", "pool")[self.rr % 2]

    def wload(self, stages, dst_fn, src_fn, ncol, dst, piece=2048, gain=None, npart=128):
        c0 = 0
        while c0 < ncol:
            c1 = min(ncol, c0 + piece)
            stg = stages[self.rr % len(stages)]
            self.dma(stg[0:npart, 0:c1 - c0], src_fn(c0, c1), (), (stg,), stg)
            eng = self.ceng()
            if gain is None:
                self.cp(eng, dst_fn(c0, c1), stg[0:npart, 0:c1 - c0], (stg,), (dst,))
            else:
                self.ts(eng, dst_fn(c0, c1), stg[0:npart, 0:c1 - c0], gain, 0.0, ALU.mult, ALU.add, (stg,), (dst,))
            c0 = c1


def norm_transpose(k, xin, hnbs, hnT, ss, rstd, mhalf, ident, tbanks):
    for j in range(4):
        hnb = hnbs[j % len(hnbs)]
        tb = tbanks[j % len(tbanks)]
        k.act(hnb[:, :], xin[:, j, :], AF.Square, (xin,), (hnb, ss), accum=ss[:, j:j + 1])
        k.ts("dve", rstd[:, j:j + 1], ss[:, j:j + 1], 1.0 / D, EPS, ALU.mult, ALU.add, (ss,), (rstd,))
        k.tt("pool", rstd[:, j:j + 1], rstd[:, j:j + 1], mhalf[:, 0:1], ALU.pow, (rstd, mhalf), (rstd,))
        if j % 2 == 0:
            k.ts("dve", hnb[:, :], xin[:, j, :], rstd[:, j:j + 1], 0.0, ALU.mult, ALU.add, (xin, rstd), (hnb,))
        else:
            k.act(hnb[:, :], xin[:, j, :], AF.Identity, (xin, rstd), (hnb,), scale=rstd[:, j:j + 1])
        for kc in range(8):
            k.tr(tb[:, kc * 128:(kc + 1) * 128], hnb[:, kc * 128:(kc + 1) * 128], ident[:, :], (hnb, ident), (tb,))
        src = tb[:, :].rearrange("p (a b) -> p a b", a=8)
        if j % 2 == 0:
            k.act(hnT[:, :, j * 128:(j + 1) * 128], src, AF.Copy, (tb,), (hnT,))
        else:
            k.cp("dve", hnT[:, :, j * 128:(j + 1) * 128], src, (tb,), (hnT,))


def ffn_chunk(k, hnT, actT, wgu, wdn, banks, bi, tg, tA):
    for fc in range(NFC):
        pg = banks[bi % len(banks)]
        pv = banks[(bi + 1) % len(banks)]
        bi += 2
        for kc in range(8):
            k.mm(pg[:, :], wgu[:, kc, fc * 128:(fc + 1) * 128], hnT[:, kc, :], kc == 0, kc == 7, (wgu, hnT), (pg,))
        for kc in range(8):
            k.mm(pv[:, :], wgu[:, kc, DFF + fc * 128:DFF + (fc + 1) * 128], hnT[:, kc, :], kc == 0, kc == 7, (wgu, hnT), (pv,))
        t1 = tg[fc % len(tg)]
        t2 = tA[fc % len(tA)]
        k.act(t1[:, :], pg[:, :], AF.Tanh, (pg,), (t1,), scale=0.5)
        k.stt(t2[:, :], t1[:, :], 1.0, pg[:, :], ALU.add, ALU.mult, (t1, pg), (t2,))
        k.stt(actT[:, fc, :], t2[:, :], 0.5, pv[:, :], ALU.mult, ALU.mult, (t2, pv), (actT,))
    return bi


def ffn_down(k, actT, wdn, xin, banks, bi):
    for j in range(4):
        for hf in range(2):
            pb = banks[bi % len(banks)]
            bi += 1
            for fc in range(NFC):
                k.mm(pb[:, :], actT[:, fc, j * 128:(j + 1) * 128], wdn[:, fc, hf * 512:(hf + 1) * 512], fc == 0, fc == NFC - 1,
                     (actT, wdn), (pb,))
            k.tt("dve", xin[:, j, hf * 512:(hf + 1) * 512], pb[:, :], xin[:, j, hf * 512:(hf + 1) * 512], ALU.add, (pb, xin), (xin,))
    return bi


def tok_view(ap2d, t0):
    return ap2d[t0:t0 + T, :].rearrange("(j p) d -> p j d", p=128)


def build_program(nseq, phases="ABCDE", debug=False):
    NT = nseq * S
    nc = bass.Bass("TRN2", target_bir_lowering=False)
    dt_in = {}

    def din(name, shape):
        dt_in[name] = shape
        return nc.dram_tensor(name, list(shape), F32, kind="ExternalInput").ap()

    def dscr(name, shape, dt):
        kind = "ExternalOutput" if debug else "Internal"
        return nc.dram_tensor(name, list(shape), dt, kind=kind).ap()

    x_d = din("x", (NT, D))
    identf_d = din("c_ident", (128, 128))
    tri_d = din("c_tri", (128, 256))
    maskc_d = din("c_maskc", (128, S))
    ropec_d = din("c_ropec", (128, S))
    ropes_d = din("c_ropes", (128, S))
    pm_d = din("c_pm", (128, 64))
    esel_d = din("c_esel", (32, S))
    ovl_d = din("c_ovl", (128, 33))
    tkadd_d = din("c_tkadd", (S, 32))
    tkmul_d = din("c_tkmul", (S, 32))
    g_d = din("g_all", (128, 5, 8))
    gfin_d = din("g_fin", (1, D))
    win_d = din("w_in", (128, 8 * 3072))
    cw_d = din("conv_w", (RB, NBLK * 4))
    sm_d = din("a_small", (RB, 4, NBLK))
    wra_d = din("w_ra", (RB, NBLK * RB))
    wix_d = din("w_ix", (RB, NBLK * RB))
    wout_d = din("w_out", (RB, NBLK * D))
    wgu_d = [din("w_gu%d" % l, (128, 8 * 2 * DFF)) for l in range(2)]
    wdn_d = [din("w_dn%d" % l, (128, NFC * D)) for l in range(2)]
    wkc_d = din("w_kc", (128, 8 * NG * 96))
    wvc_d = din("w_vc", (128, 8 * NG * 64))
    wks_d = din("w_ks", (128, 8 * NG * 128))
    wkw_d = din("w_kw", (128, 8 * NG * 128))
    wv_d = din("w_v", (128, 8 * 512))
    wq_d = din("w_q", (128, 8 * NH * 128))
    wqg_d = din("w_qg", (128, 8 * 48))
    gb_d = din("gate_b", (1, 48))
    wo_d = din("w_o", (128, 8 * D))
    w1k_d = din("w1k", (96, 32 * 256))
    w1v_d = din("w1v", (64, 32 * 256))
    w2k_d = din("w2k", (128, 2 * 128))
    w2v_d = din("w2v", (128, 2 * 64))
    posk_d = din("posk", (96, 32))
    posv_d = din("posv", (64, 32))

    y_d = nc.dram_tensor("y", [NT, D], F32, kind="ExternalOutput").ap()
    hmid_d = dscr("s_hmid", (NT, D), F32)
    h1_d = dscr("s_h1", (NT, D), F32)
    kcmp_d = dscr("s_kcmp", (nseq, 96, NG, S), BF16)
    vcmp_d = dscr("s_vcmp", (nseq, 64, NG, S), BF16)
    ksel_d = dscr("s_ksel", (nseq, 96, NG, S), BF16)
    kwin_d = dscr("s_kwin", (nseq, 96, NG, S), BF16)
    v_d = dscr("s_v", (NT, 520), BF16)
    qn_d = dscr("s_qn", (nseq, 96, NH, S), BF16)
    qr_d = dscr("s_qr", (nseq, 32, NH, S), BF16)
    gt_d = dscr("s_gt", (NT, 48), F32)
    o_d = dscr("s_o", (NT, D), BF16)

    with contextlib.ExitStack() as top:
        P = Prog(nc, top)
        k = K(nc, P)

        ident = k.sb(top, [128, 128], BF16, "ident")
        mhalf = k.sb(top, [128, 4], F32, "mhalf")
        gall = k.sb(top, [128, 5, 8], F32, "gall")
        with contextlib.ExitStack() as st:
            tmp = k.sb(st, [128, 128], F32)
            k.dma(tmp[:, :], identf_d, (), (tmp,), tmp)
            k.cp("dve", ident[:, :], tmp[:, :], (tmp,), (ident,))
            k.memset("dve", mhalf[:, :], -0.5, (mhalf,))
            k.dma(gall[:, :, :], g_d, (), (gall,), gall)
            P.flush()

        if "A" in phases:
            with contextlib.ExitStack() as st:
                win = k.sb(st, [128, 8, 3072], BF16, "win")
                wout = k.sb(st, [RB, NBLK, D], BF16, "wout")
                wra = k.sb(st, [RB, NBLK, RB], BF16, "wra")
                wix = k.sb(st, [RB, NBLK, RB], BF16, "wix")
                dg = k.sb(st, [RB, NBLK * 4, RB], BF16, "dg")
                cw = k.sb(st, [RB, NBLK * 4], F32, "cw")
                sm = k.sb(st, [RB, 4, NBLK], F32, "sm")
                hb = k.sb(st, [RB, 2, NBLK], F32, "hb")
                cc = k.sb(st, [RB, 2, NBLK], F32, "cc")
                cx = k.sb(st, [RB, NBLK, 4], BF16, "cx")
                hcar = k.sb(st, [RB, NBLK], F32, "hcar")
                one = k.sb(st, [RB, 1], F32, "one")
                k.memset("dve", one[:, :], 1.0, (one,))
                with contextlib.ExitStack() as st2:
                    stages = [k.sb(st2, [128, 3072], F32, "stg") for _ in range(3)]
                    for kc in range(8):
                        k.wload(stages, lambda a, b, kc=kc: win[:, kc, a:b], lambda a, b, kc=kc: win_d[:, kc * 3072 + a:kc * 3072 + b],
                                3072, win, piece=3072, gain=gall[:, 0, kc:kc + 1])
                    for n in range(NBLK):
                        k.wload(stages, lambda a, b, n=n: wout[:, n, a:b], lambda a, b, n=n: wout_d[:, n * D + a:n * D + b], D, wout,
                                piece=D, npart=RB)
                    k.wload(stages, lambda a, b: wra[:, :, :].rearrange("p n j -> p (n j)")[:, a:b], lambda a, b: wra_d[:, a:b],
                            NBLK * RB, wra, piece=NBLK * RB, npart=RB)
                    k.wload(stages, lambda a, b: wix[:, :, :].rearrange("p n j -> p (n j)")[:, a:b], lambda a, b: wix_d[:, a:b],
                            NBLK * RB, wix, piece=NBLK * RB, npart=RB)
                    k.dma(cw[:, :], cw_d, (), (cw,), cw)
                    k.dma(sm[:, :, :], sm_d, (), (sm,), sm)
                    for i in range(NBLK * 4):
                        k.ts(k.ceng(), dg[:, i, :], ident[0:RB, 0:RB], cw[:, i:i + 1], 0.0, ALU.mult, ALU.add, (ident, cw), (dg,))
                    k.ts("dve", hb[:, :, :], sm[:, 1:3, :], -1.0, 0.0, ALU.mult, ALU.add, (sm,), (hb,))
                    sg = k.sb(st2, [RB, NBLK], F32)
                    k.act(sg[:, :], sm[:, 3, :], AF.Exp, (sm,), (sg,), scale=-1.0)
                    k.ts("dve", sg[:, :], sg[:, :], 1.0, 1.0, ALU.mult, ALU.add, (sg,), (sg,))
                    k.act(sg[:, :], sg[:, :], AF.Ln, (sg,), (sg,))
                    k.ts("dve", cc[:, 0, :], sg[:, :], -8.0, 0.0, ALU.mult, ALU.add, (sg,), (cc,))
                    k.ts("dve", cc[:, 1, :], sg[:, :], -16.0, 0.0, ALU.mult, ALU.add, (sg,), (cc,))
                    P.flush()
                if os.environ.get("KSTOP") == "A0":
                    return nc

                xins = [k.sb(st, [128, 4, D], F32, "xin") for _ in range(2)]
                hnb = [k.sb(st, [128, D], BF16, "hnb") for _ in range(2)]
                hnTs = [k.sb(st, [128, 8, T], BF16, "hnT") for _ in range(2)]
                recy = k.sb(st, [RB, NBLK, T], BF16, "recy")
                ss = k.sb(st, [128, 4], F32)
                rstd = k.sb(st, [128, 4], F32)
                NSET = 2
                xbT = [k.sb(st, [RB, T + 4], BF16) for _ in range(NSET)]
                xcT = [k.sb(st, [RB, T], BF16) for _ in range(NSET)]
                trr = [k.sb(st, [RB, T], F32) for _ in range(NSET)]
                tii = [k.sb(st, [RB, T], F32) for _ in range(NSET)]
                aa = [k.sb(st, [RB, T], F32) for _ in range(NSET)]
                a2 = [k.sb(st, [RB, T], F32) for _ in range(NSET)]
                hs = [k.sb(st, [RB, T], F32) for _ in range(NSET)]
                sq = [k.sb(st, [RB, T], F32) for _ in range(NSET)]
                th = [k.sb(st, [RB, T], F32) for _ in range(NSET)]
                banks = [k.ps(st, [128, 512], F32, "bk") for _ in range(4)]
                ybanks = [k.ps(st, [128, 512], F32, "yb") for _ in range(2)]
                tbanks = [k.ps(st, [128, 1024], BF16, "tb") for _ in range(2)]

                def rcp(out, in_, rd, wr):
                    P.op("dve", lambda e: e.reciprocal(out, in_), _rs(rd), _rs(wr))
                bi = 0
                ci = 0
                KNCH = int(os.environ.get("KNCH", "4"))
                KNBLK = int(os.environ.get("KNBLK", "16"))
                KSTOP = os.environ.get("KSTOP", "")
                nchA = min(KNCH, S // T)
                chunksA = [(b, c) for b in range(nseq) for c in range(nchA)]
                k.dma(xins[0][:, :, :], tok_view(x_d, chunksA[0][0] * S + chunksA[0][1] * T), (), (xins[0],), xins[0])
                for b in range(nseq):
                    k.memset("pool", cx[:, :, :], 0.0, (cx,))
                    k.memset("pool", hcar[:, :], 0.0, (hcar,))
                    for c in range(nchA):
                        t0 = b * S + c * T
                        xin = xins[ci % 2]
                        hnT = hnTs[ci % 2]
                        ci += 1
                        if ci < len(chunksA):
                            nb_, nc_ = chunksA[ci]
                            k.dma(xins[ci % 2][:, :, :], tok_view(x_d, nb_ * S + nc_ * T), (), (xins[ci % 2],), xins[ci % 2])
                        norm_transpose(k, xin, hnb, hnT, ss, rstd, mhalf, ident, tbanks)
                        def blk(n, xin=None, hnT=hnT):
                            s_ = n % NSET
                            bA = banks[(2 * n) % 4]
                            bB = banks[(2 * n + 1) % 4]
                            pxb, pr, pcv, pi = bA, bA, bB, bB
                            pyb = ybanks[n % 2]
                            er, ei, ey = trr[s_], tii[s_], th[s_]
                            for kc in range(8):
                                k.mm(pxb[0:RB, :], win[:, kc, n * RB:(n + 1) * RB], hnT[:, kc, :], kc == 0, kc == 7, (win, hnT), (pxb,))
                            for kc in range(8):
                                k.mm(pyb[0:RB, :], win[:, kc, DRNN + n * RB:DRNN + (n + 1) * RB], hnT[:, kc, :], kc == 0, kc == 7,
                                     (win, hnT), (pyb,))
                            yield
                            k.cp("pool", xbT[s_][:, 0:4], cx[:, n, :], (cx,), (xbT[s_],))
                            k.cp("dve", xbT[s_][:, 4:4 + T], pxb[0:RB, :], (pxb,), (xbT[s_],))
                            k.cp("pool", cx[:, n, :], xbT[s_][:, T:T + 4], (xbT[s_],), (cx,))
                            k.act(sq[s_][:, :], pyb[0:RB, :], AF.Square, (pyb,), (sq[s_],))
                            yield
                            for t in range(4):
                                k.mm(pcv[0:RB, :], dg[:, n * 4 + t, :], xbT[s_][:, 1 + t:1 + t + T], t == 0, t == 3, (dg, xbT[s_]), (pcv,))
                            k.ts("pool", sq[s_][:, :], sq[s_][:, :], GC, 1.0, ALU.mult, ALU.add, (sq[s_],), (sq[s_],))
                            yield
                            k.ts("dve", xcT[s_][:, :], pcv[0:RB, :], sm[:, 0, n:n + 1], 0.0, ALU.add, ALU.add, (pcv, sm), (xcT[s_],))
                            k.tt("dve", sq[s_][:, :], sq[s_][:, :], pyb[0:RB, :], ALU.mult, (sq[s_], pyb), (sq[s_],))
                            yield
                            k.mm(pr[0:RB, :], wra[:, n, :], xcT[s_][:, :], True, True, (wra, xcT[s_]), (pr,))
                            k.mm(pi[0:RB, :], wix[:, n, :], xcT[s_][:, :], True, True, (wix, xcT[s_]), (pi,))
                            k.act(ey[:, :], sq[s_][:, :], AF.Exp, (sq[s_],), (ey,), scale=-2.0 * GK)
                            yield
                            k.act(er[:, :], pr[0:RB, :], AF.Exp, (pr, hb), (er,), scale=-1.0, bias=hb[:, 0, n:n + 1])
                            k.act(ei[:, :], pi[0:RB, :], AF.Exp, (pi, hb), (ei,), scale=-1.0, bias=hb[:, 1, n:n + 1])
                            k.act(ey[:, :], ey[:, :], AF.Ln, (ey, one), (ey,), bias=one[:, 0:1])
                            yield
                            k.act(er[:, :], er[:, :], AF.Ln, (er, one), (er,), bias=one[:, 0:1])
                            k.act(ei[:, :], ei[:, :], AF.Ln, (ei, one), (ei,), bias=one[:, 0:1])
                            k.act(ey[:, :], ey[:, :], AF.Exp, (ey,), (ey,), scale=-1.0)
                            yield
                            k.act(er[:, :], er[:, :], AF.Exp, (er,), (er,), scale=-1.0)
                            k.tt("dve", ey[:, :], ey[:, :], pyb[0:RB, :], ALU.mult, (ey, pyb), (ey,))
                            yield
                            k.act(aa[s_][:, :], er[:, :], AF.Exp, (er, cc), (aa[s_],), scale=cc[:, 0, n:n + 1])
                            k.act(a2[s_][:, :], er[:, :], AF.Exp, (er, cc), (a2[s_],), scale=cc[:, 1, n:n + 1])
                            yield
                            k.ts("pool", a2[s_][:, :], a2[s_][:, :], -1.0, 1.0, ALU.mult, ALU.add, (a2[s_],), (a2[s_],))
                            k.ts("pool", a2[s_][:, :], a2[s_][:, :], 1.0, 1e-12, ALU.min, ALU.max, (a2[s_],), (a2[s_],))
                            yield
                            k.act(a2[s_][:, :], a2[s_][:, :], AF.Ln, (a2[s_],), (a2[s_],))
                            yield
                            k.stt(ei[:, :], a2[s_][:, :], 0.5, ei[:, :], ALU.mult, ALU.subtract, (a2[s_], ei), (ei,))
                            yield
                            k.act(ei[:, :], ei[:, :], AF.Exp, (ei,), (ei,))
                            yield
                            k.tt("dve", ei[:, :], ei[:, :], xcT[s_][:, :], ALU.mult, (ei, xcT[s_]), (ei,))
                            k.P.op("dve", (lambda e, o=hs[s_][:, :], d0=aa[s_][:, :], d1=ei[:, :], i0=hcar[:, n:n + 1]:
                                           e.tensor_tensor_scan(o, d0, d1, i0, ALU.mult, ALU.add)),
                                   _rs((aa[s_], ei, hcar)), _rs((hs[s_],)))
                            k.cp("pool", hcar[:, n:n + 1], hs[s_][:, T - 1:T], (hs[s_],), (hcar,))
                            k.tt("dve", recy[:, n, :], ey[:, :], hs[s_][:, :], ALU.mult, (ey, hs[s_]), (recy,))
                            yield

                        active = []
                        nxt_n = 0
                        nblk = min(KNBLK, NBLK)
                        while active or nxt_n < nblk:
                            if nxt_n < nblk and len(active) < NSET and (not active or active[-1][1] >= int(os.environ.get('KSKEW', '8'))):
                                active.append([blk(nxt_n), 0])
                                nxt_n += 1
                            for ent in list(active):
                                try:
                                    next(ent[0])
                                    ent[1] += 1
                                except StopIteration:
                                    active.remove(ent)
                        for j in range(4):
                            for hf in range(2):
                                pb = banks[bi % 4]
                                bi += 1
                                for n in range(NBLK):
                                    k.mm(pb[:, :], recy[:, n, j * 128:(j + 1) * 128], wout[:, n, hf * 512:(hf + 1) * 512], n == 0, n == NBLK - 1,
                                         (recy, wout), (pb,))
                                k.tt("dve", xin[:, j, hf * 512:(hf + 1) * 512], pb[:, :], xin[:, j, hf * 512:(hf + 1) * 512], ALU.add, (pb, xin), (xin,))
                        k.dma(tok_view(hmid_d, t0), xin[:, :, :], (xin,), (), xin)
                P.flush()

        def phase_ffn(layer, src_d, dst_d, with_o, final):
            with contextlib.ExitStack() as st:
                if os.environ.get("KDBG"):
                    print("ffn sbuf remaining at start", nc.sbuf_bytes_remaining)
                wgu = k.sb(st, [128, 8, 2 * DFF], BF16, "wgu")
                wdn = k.sb(st, [128, NFC, D], BF16, "wdn")
                wo = k.sb(st, [128, 8, D], BF16, "wo") if with_o else None
                gfin = k.sb(st, [128, D], F32, "gfin") if final else None
                with contextlib.ExitStack() as st2:
                    stages = [k.sb(st2, [128, 2816], F32, "stg") for _ in range(3)]
                    gi = 1 if layer == 0 else 4
                    for kc in range(8):
                        k.wload(stages, lambda a, b, kc=kc: wgu[:, kc, a:b], lambda a, b, kc=kc: wgu_d[layer][:, kc * 2 * DFF + a:kc * 2 * DFF + b],
                                2 * DFF, wgu, piece=2816, gain=gall[:, gi, kc:kc + 1])
                    for fc in range(0, NFC, 2):
                        k.wload(stages, lambda a, b, fc=fc: wdn[:, fc:fc + 2, :].rearrange("p f d -> p (f d)")[:, a:b],
                                lambda a, b, fc=fc: wdn_d[layer][:, fc * D + a:fc * D + b], 2 * D, wdn, piece=2 * D)
                    if with_o:
                        for kc in range(0, 8, 2):
                            k.wload(stages, lambda a, b, kc=kc: wo[:, kc:kc + 2, :].rearrange("p f d -> p (f d)")[:, a:b],
                                    lambda a, b, kc=kc: wo_d[:, kc * D + a:kc * D + b], 2 * D, wo, piece=2 * D)
                    if final:
                        g1 = k.sb(st2, [1, D], F32)
                        on = k.sb(st2, [1, 128], F32)
                        pbk = k.ps(st2, [128, 512], F32)
                        k.dma(g1[:, :], gfin_d, (), (g1,), g1)
                        k.memset("dve", on[:, :], 1.0, (on,))
                        for hf in range(2):
                            k.mm(pbk[:, :], on[:, :], g1[:, hf * 512:(hf + 1) * 512], True, True, (on, g1), (pbk,))
                            k.cp("dve", gfin[:, hf * 512:(hf + 1) * 512], pbk[:, :], (pbk,), (gfin,))
                    P.flush()
                if os.environ.get("KDBG"):
                    print("ffn sbuf remaining before acts", nc.sbuf_bytes_remaining)
                xins = [k.sb(st, [128, 4, D], F32, "xin") for _ in range(1 if with_o else 2)] * 2
                hnb = [k.sb(st, [128, D], BF16, "hnb") for _ in range(1 if with_o else 2)] * 2
                hnT = k.sb(st, [128, 8, T], BF16, "hnT")
                if os.environ.get("KDBG"):
                    print("ffn sbuf remaining before actT", nc.sbuf_bytes_remaining)
                actT = k.sb(st, [128, NFC, T], BF16, "actT")
                ss = k.sb(st, [128, 4], F32)
                rstd = k.sb(st, [128, 4], F32)
                tg = [k.sb(st, [128, T], BF16) for _ in range(1 if with_o else 2)]
                tA = [k.sb(st, [128, T], F32) for _ in range(1)]
                banks = [k.ps(st, [128, 512], F32, "bk") for _ in range(6)]
                tbanks = [k.ps(st, [128, 1024], BF16, "tb") for _ in range(2)]
                bi = 0
                if not with_o:
                    k.dma(xins[0][:, :, :], tok_view(src_d, 0), (), (xins[0],), xins[0])
                for ci in range(NT // T):
                    t0 = ci * T
                    xin = xins[ci % 2]
                    if with_o:
                        k.dma(xin[:, :, :], tok_view(src_d, t0), (), (xin,), xin)
                    elif ci + 1 < NT // T:
                        k.dma(xins[(ci + 1) % 2][:, :, :], tok_view(src_d, t0 + T), (), (xins[(ci + 1) % 2],), xins[(ci + 1) % 2])
                    if with_o:
                        oi = actT[:, 0:8, :].rearrange("p (j a) t -> p j (a t)", j=4)
                        k.dma(oi, tok_view(o_d, t0), (), (actT,), actT)
                        for j in range(4):
                            tb = tbanks[j % 2]
                            for kc in range(8):
                                k.tr(tb[:, kc * 128:(kc + 1) * 128], oi[:, j, kc * 128:(kc + 1) * 128], ident[:, :], (actT, ident), (tb,))
                            src = tb[:, :].rearrange("p (a b) -> p a b", a=8)
                            if j % 2 == 0:
                                k.act(hnT[:, :, j * 128:(j + 1) * 128], src, AF.Copy, (tb,), (hnT,))
                            else:
                                k.cp("dve", hnT[:, :, j * 128:(j + 1) * 128], src, (tb,), (hnT,))
                        for j in range(4):
                            for hf in range(2):
                                pb = banks[bi % 6]
                                bi += 1
                                for kc in range(8):
                                    k.mm(pb[:, :], hnT[:, kc, j * 128:(j + 1) * 128], wo[:, kc, hf * 512:(hf + 1) * 512], kc == 0, kc == 7,
                                         (hnT, wo), (pb,))
                                k.tt("dve", xin[:, j, hf * 512:(hf + 1) * 512], pb[:, :], xin[:, j, hf * 512:(hf + 1) * 512], ALU.add, (pb, xin), (xin,))
                    norm_transpose(k, xin, hnb, hnT, ss, rstd, mhalf, ident, tbanks)
                    bi = ffn_chunk(k, hnT, actT, wgu, wdn, banks, bi, tg, tA)
                    bi = ffn_down(k, actT, wdn, xin, banks, bi)
                    if final:
                        for j in range(4):
                            hb_ = hnb[j % 2]
                            k.act(hb_[:, :], xin[:, j, :], AF.Square, (xin,), (hb_, ss), accum=ss[:, j:j + 1])
                        k.ts("dve", rstd[:, 0:4], ss[:, 0:4], 1.0 / D, EPS, ALU.mult, ALU.add, (ss,), (rstd,))
                        k.tt("pool", rstd[:, 0:4], rstd[:, 0:4], mhalf[:, 0:4], ALU.pow, (rstd, mhalf), (rstd,))
                        for j in range(4):
                            k.stt(xin[:, j, :], xin[:, j, :], rstd[:, j:j + 1], gfin[:, :], ALU.mult, ALU.mult, (xin, rstd, gfin), (xin,))
                    k.dma(tok_view(dst_d, t0), xin[:, :, :], (xin,), (), xin)
                P.flush()

        if "B" in phases:
            phase_ffn(0, hmid_d, h1_d, False, False)

        if "C" in phases:
            with contextlib.ExitStack() as st:
                wkc = k.sb(st, [128, 8, NG * 96], BF16, "wkc")
                wvc = k.sb(st, [128, 8, NG * 64], BF16, "wvc")
                wks = k.sb(st, [128, 8, NG * 128], BF16, "wks")
                wkw = k.sb(st, [128, 8, NG * 128], BF16, "wkw")
                wv = k.sb(st, [128, 8, 512], BF16, "wv")
                wq = k.sb(st, [128, 8, NH * 128], BF16, "wq")
                wqg = k.sb(st, [128, 8, 48], BF16, "wqg")
                gbb = k.sb(st, [1, 48], BF16, "gbb")
                ones1 = k.sb(st, [1, 128], BF16, "ones1")
                pm = k.sb(st, [128, 64], BF16, "pm")
                rc = k.sb(st, [128, S], F32, "rc")
                rsn = k.sb(st, [128, S], F32, "rsn")
                with contextlib.ExitStack() as st2:
                    stages = [k.sb(st2, [128, 2048], F32, "stg") for _ in range(3)]
                    for kc in range(8):
                        for (wt, wd, nco, gi) in ((wkc, wkc_d, NG * 96, 2), (wvc, wvc_d, NG * 64, 2), (wks, wks_d, NG * 128, 2),
                                                  (wkw, wkw_d, NG * 128, 2), (wv, wv_d, 512, 2), (wq, wq_d, NH * 128, 3), (wqg, wqg_d, 48, 3)):
                            k.wload(stages, lambda a, b, wt=wt, kc=kc: wt[:, kc, a:b], lambda a, b, wd=wd, kc=kc, nco=nco: wd[:, kc * nco + a:kc * nco + b],
                                    nco, wt, piece=2048, gain=gall[:, gi, kc:kc + 1])
                    k.wload(stages, lambda a, b: pm[:, a:b], lambda a, b: pm_d[:, a:b], 64, pm)
                    k.wload(stages, lambda a, b: gbb[:, a:b], lambda a, b: gb_d[:, a:b], 48, gbb, npart=1)
                    k.memset("dve", ones1[:, :], 1.0, (ones1,))
                    k.dma(rc[:, :], ropec_d, (), (rc,), rc)
                    k.dma(rsn[:, :], ropes_d, (), (rsn,), rsn)
                    P.flush()
                xins = [k.sb(st, [128, 4, D], F32, "xin") for _ in range(2)]
                hnb = [k.sb(st, [128, D], BF16, "hnb") for _ in range(2)]
                hnT = k.sb(st, [128, 8, T], BF16, "hnT")
                ss = k.sb(st, [128, 4], F32)
                rstd = k.sb(st, [128, 4], F32)
                o_kc = [k.sb(st, [96, NG, T], BF16) for _ in range(1)] * 2
                o_vc = [k.sb(st, [64, NG, T], BF16) for _ in range(1)] * 2
                o_ks = [k.sb(st, [128, NG, T], BF16) for _ in range(1)] * 2
                o_kw = [k.sb(st, [128, NG, T], BF16) for _ in range(1)] * 2
                o_v = [k.sb(st, [128, 4, 520], BF16) for _ in range(2)]
                for ov_ in o_v:
                    k.memset("pool", ov_[:, :, :], 1.0, (ov_,))
                o_qn = [k.sb(st, [128, NH, T], BF16) for _ in range(1)] * 2
                o_qr = [k.sb(st, [64, NH, T], BF16) for _ in range(1)] * 2
                o_gt = [k.sb(st, [128, 4, 48], F32) for _ in range(2)]
                r1 = [k.sb(st, [64, T], F32) for _ in range(2)]
                r2 = [k.sb(st, [64, T], F32) for _ in range(2)]
                banks = [k.ps(st, [128, 512], F32, "bk") for _ in range(6)]
                tbanks = [k.ps(st, [128, 1024], BF16, "tb") for _ in range(2)]
                bi = 0
                ri = 0
                k.dma(xins[0][:, :, :], tok_view(h1_d, 0), (), (xins[0],), xins[0])
                for ci in range(NT // T):
                    t0 = ci * T
                    b = t0 // S
                    p0 = t0 % S
                    xin = xins[ci % 2]
                    s_ = ci % 2
                    if ci + 1 < NT // T:
                        k.dma(xins[(ci + 1) % 2][:, :, :], tok_view(h1_d, t0 + T), (), (xins[(ci + 1) % 2],), xins[(ci + 1) % 2])
                    norm_transpose(k, xin, hnb, hnT, ss, rstd, mhalf, ident, tbanks)

                    def proj(wt, c0, m):
                        nonlocal bi
                        pb = banks[bi % 6]
                        bi += 1
                        for kc in range(8):
                            k.mm(pb[0:m, :], wt[:, kc, c0:c0 + m], hnT[:, kc, :], kc == 0, kc == 7, (wt, hnT), (pb,))
                        return pb

                    def rope(dst, dstr, g, scale):
                        nonlocal bi, ri
                        pb = banks[bi % 6]
                        bi += 1
                        k.mm(pb[0:64, :], pm[:, :], dst[:, g, :], True, True, (pm, dst), (pb,))
                        a = r1[ri % 2]
                        bb = r2[ri % 2]
                        ri += 1
                        k.tt("dve", a[32:64, :], pb[32:64, :], rsn[32:64, p0:p0 + T], ALU.mult, (pb, rsn), (a,))
                        k.tt("pool", bb[32:64, :], dst[32:64, g, :], rc[32:64, p0:p0 + T], ALU.mult, (dst, rc), (bb,))
                        return a, bb

                    for g in range(NG):
                        pb = proj(wkc, g * 96, 96)
                        k.act(o_kc[s_][:, g, :], pb[0:96, :], AF.Copy, (pb,), (o_kc[s_],))
                        pb = proj(wvc, g * 64, 64)
                        k.cp("dve", o_vc[s_][:, g, :], pb[0:64, :], (pb,), (o_vc[s_],))
                    for (wt, ot) in ((wks, o_ks[s_]), (wkw, o_kw[s_])):
                        for g in range(NG):
                            pb = proj(wt, g * 128, 128)
                            k.act(ot[:, g, :], pb[:, :], AF.Copy, (pb,), (ot,))
                            a, bb = rope(ot, ot, g, 1.0)
                            k.tt("dve", ot[32:64, g, :], a[32:64, :], bb[32:64, :], ALU.add, (a, bb), (ot,))
                    for j in range(4):
                        pb = banks[bi % 6]
                        bi += 1
                        for kc in range(8):
                            k.mm(pb[:, :], hnT[:, kc, j * 128:(j + 1) * 128], wv[:, kc, :], kc == 0, kc == 7, (wv, hnT), (pb,))
                        k.cp("dve", o_v[s_][:, j, :].rearrange("p (a c) -> p a c", c=65)[:, :, 0:64], pb[:, :].rearrange("p (a c) -> p a c", c=64),
                             (pb,), (o_v[s_],))
                        pb = banks[bi % 6]
                        bi += 1
                        for kc in range(8):
                            k.mm(pb[:, 0:48], hnT[:, kc, j * 128:(j + 1) * 128], wqg[:, kc, :], kc == 0, False, (wqg, hnT), (pb,))
                        k.mm(pb[:, 0:48], ones1[:, :], gbb[:, :], False, True, (ones1, gbb), (pb,))
                        k.act(o_gt[s_][:, j, :], pb[:, 0:48], AF.Tanh, (pb,), (o_gt[s_],), scale=0.5)
                        k.ts("dve", o_gt[s_][:, j, :], o_gt[s_][:, j, :], 0.5, 0.5, ALU.mult, ALU.add, (o_gt[s_],), (o_gt[s_],))
                    for h in range(NH):
                        pb = proj(wq, h * 128, 128)
                        k.act(o_qn[s_][:, h, :], pb[:, :], AF.Copy, (pb,), (o_qn[s_],), scale=float(DQK ** -0.5))
                        a, bb = rope(o_qn[s_], o_qr[s_], h, 1.0)
                        k.tt("dve", o_qr[s_][32:64, h, :], a[32:64, :], bb[32:64, :], ALU.add, (a, bb), (o_qr[s_],))
                    k.dma(kcmp_d[b, :, :, p0:p0 + T], o_kc[s_][:, :, :], (o_kc[s_],), (), o_kc[s_])
                    k.dma(vcmp_d[b, :, :, p0:p0 + T], o_vc[s_][:, :, :], (o_vc[s_],), (), o_vc[s_])
                    k.dma(ksel_d[b, :, :, p0:p0 + T], o_ks[s_][32:128, :, :], (o_ks[s_],), (), o_ks[s_])
                    k.dma(kwin_d[b, :, :, p0:p0 + T], o_kw[s_][32:128, :, :], (o_kw[s_],), (), o_kw[s_])
                    k.dma(v_d[t0:t0 + T, :].rearrange("(j p) d -> p j d", p=128), o_v[s_][:, :, :], (o_v[s_],), (), o_v[s_])
                    k.dma(qn_d[b, :, :, p0:p0 + T], o_qn[s_][32:128, :, :], (o_qn[s_],), (), o_qn[s_])
                    k.dma(qr_d[b, :, :, p0:p0 + T], o_qr[s_][32:64, :, :], (o_qr[s_],), (), o_qr[s_])
                    k.dma(gt_d[t0:t0 + T, :].rearrange("(j p) d -> p j d", p=128), o_gt[s_][:, :, :], (o_gt[s_],), (), o_gt[s_])
                P.flush()

        if "D" in phases:
            with contextlib.ExitStack() as st:
                w1k = k.sb(st, [96, 32, 256], BF16, "w1k")
                w1v = k.sb(st, [64, 32, 256], BF16, "w1v")
                w2k = k.sb(st, [128, 2, 128], BF16, "w2k")
                w2v = k.sb(st, [128, 2, 64], BF16, "w2v")
                posk = k.sb(st, [96, 32], BF16, "posk")
                posv = k.sb(st, [64, 32], BF16, "posv")
                pbias = k.sb(st, [128, 4], F32, "pbias")
                tri = k.sb(st, [128, 256], BF16, "tri")
                maskc = k.sb(st, [128, S], BF16, "maskc")
                tkadd = k.sb(st, [128, 16, 32], F32, "tkadd")
                tkmul = k.sb(st, [128, 16, 32], F32, "tkmul")
                vcaug = k.sb(st, [128, NG, 97], BF16, "vcaug")
                kcT = k.sb(st, [128, NG, 128], BF16, "kcT")
                ksT = k.sb(st, [128, NG, S], BF16, "ksT")
                kwT = k.sb(st, [128, NG, S], BF16, "kwT")
                with contextlib.ExitStack() as st2:
                    stages = [k.sb(st2, [128, 2048], F32, "stg") for _ in range(3)]
                    for l0 in range(0, 32, 8):
                        k.wload(stages, lambda a, b, l0=l0: w1k[:, l0:l0 + 8, :].rearrange("p l h -> p (l h)")[:, a:b],
                                lambda a, b, l0=l0: w1k_d[:, l0 * 256 + a:l0 * 256 + b], 2048, w1k, npart=96)
                        k.wload(stages, lambda a, b, l0=l0: w1v[:, l0:l0 + 8, :].rearrange("p l h -> p (l h)")[:, a:b],
                                lambda a, b, l0=l0: w1v_d[:, l0 * 256 + a:l0 * 256 + b], 2048, w1v, npart=64)
                    k.wload(stages, lambda a, b: w2k[:, :, :].rearrange("p l h -> p (l h)")[:, a:b], lambda a, b: w2k_d[:, a:b], 256, w2k)
                    k.wload(stages, lambda a, b: w2v[:, :, :].rearrange("p l h -> p (l h)")[:, a:b], lambda a, b: w2v_d[:, a:b], 128, w2v)
                    k.wload(stages, lambda a, b: posk[:, a:b], lambda a, b: posk_d[:, a:b], 32, posk, npart=96)
                    k.wload(stages, lambda a, b: posv[:, a:b], lambda a, b: posv_d[:, a:b], 32, posv, npart=64)
                    k.wload(stages, lambda a, b: tri[:, a:b], lambda a, b: tri_d[:, a:b], 256, tri)
                    k.wload(stages, lambda a, b: maskc[:, a:b], lambda a, b: maskc_d[:, a:b], S, maskc)
                    k.dma(tkadd[:, :, :], tkadd_d.rearrange("(j p) n -> p j n", p=128), (), (tkadd,), tkadd)
                    k.dma(tkmul[:, :, :], tkmul_d.rearrange("(j p) n -> p j n", p=128), (), (tkmul,), tkmul)
                    k.memset("pool", vcaug[:, :, 0:64], 0.0, (vcaug,))
                    for g in range(NG):
                        k.wload(stages, lambda a, b, g=g: vcaug[:, g, 64 + a:64 + b], lambda a, b: ovl_d[:, a:b], 33, vcaug)
                        k.wload(stages, lambda a, b, g=g: ksT[0:32, g, a:b], lambda a, b: esel_d[:, a:b], S, ksT, npart=32)
                    k.memset("pool", kwT[0:32, :, :], 0.0, (kwT,))
                    k.memset("dve", kcT[:, :, :], 0.0, (kcT,))
                    pbk = k.ps(st2, [128, 512], F32)
                    for (w1, pos, col) in ((w1k, posk, 0), (w1v, posv, 2)):
                        for hh in range(2):
                            for l in range(32):
                                k.mm(pbk[:, col + hh:col + hh + 1], w1[:, l, hh * 128:(hh + 1) * 128], pos[:, l:l + 1], l == 0, l == 31,
                                     (w1, pos), (pbk,), skip=True)
                    k.cp("dve", pbias[:, :], pbk[:, 0:4], (pbk,), (pbias,))
                    P.flush()

                kcmp = k.sb(st, [96, NG, S], BF16, "kcmp")
                vcmp = k.sb(st, [64, NG, S], BF16, "vcmp")
                vaug = k.sb(st, [128, 16, 2, NG, 65], BF16, "vaug")
                gts = k.sb(st, [128, 16, 48], F32, "gts")
                qnt = k.sb(st, [128, 4, S], BF16, "qn")
                qat = k.sb(st, [128, 4, S], BF16, "qa")
                oacc = k.sb(st, [128, 16, 256], F32, "oacc")
                obf = k.sb(st, [128, 16, 256], BF16, "obf")
                hid = [k.sb(st, [128, 2, 512], BF16) for _ in range(2)]
                gx = k.sb(st, [128, 512], F32)
                gs = k.sb(st, [128, 512], F32)
                gt_ = k.sb(st, [128, 512], BF16)
                em = [k.sb(st, [128, 512], BF16) for _ in range(2)]
                pmk = [k.sb(st, [128, 512], BF16) for _ in range(4)]
                rden = [k.sb(st, [128, 4], F32) for _ in range(3)]
                fac = [k.sb(st, [128, 4], F32) for _ in range(3)]
                tmp = [k.sb(st, [128, 4, 64], F32) for _ in range(3)]
                impn = k.sb(st, [128, 4, 32], F32)
                imp = k.sb(st, [128, 32], F32)
                top8 = k.sb(st, [128, 8], F32)
                selb = k.sb(st, [128, 128], BF16)
                sbanks = [k.ps(st, [128, 512], F32, "sb") for _ in range(3)]
                abanks = [k.ps(st, [128, 512], F32, "ab") for _ in range(4)]
                tbank = k.ps(st, [128, 1024], BF16, "tb")
                k.memset("dve", qnt[0:32, :, :], 0.0, (qnt,))
                k.memset("dve", selb[:, :], 0.0, (selb,))
                cnt = {"s": 0, "a": 0, "p": 0, "e": 0, "n": 0}

                def nxt(lst, key):
                    cnt[key] += 1
                    return lst[cnt[key] % len(lst)]

                def rcp(out, in_, rd, wr):
                    P.op("dve", lambda e: e.reciprocal(out, in_), _rs(rd), _rs(wr))

                for b in range(nseq):
                    k.dma(kcmp[:, :, :], kcmp_d[b], (), (kcmp,), kcmp)
                    k.dma(vcmp[:, :, :], vcmp_d[b], (), (vcmp,), vcmp)
                    k.dma(ksT[32:128, :, :], ksel_d[b], (), (ksT,), ksT)
                    k.dma(kwT[32:128, :, :], kwin_d[b], (), (kwT,), kwT)
                    k.dma(vaug[:, :, :, :, :].rearrange("p j b g c -> p j (b g c)"), v_d[b * S:(b + 1) * S, :].rearrange("(j p) c -> p j c", p=128),
                          (), (vaug,), vaug)
                    k.dma(gts[:, :, :], gt_d[b * S:(b + 1) * S, :].rearrange("(j p) n -> p j n", p=128), (), (gts,), gts)
                    for (which, raw, w1) in ((0, kcmp, w1k), (1, vcmp, w1v)):
                        for hh in range(2):
                            pb = nxt(sbanks, "s")
                            for l in range(32):
                                k.mm(pb[:, 0:NG * NCMP].rearrange("p (g c) -> p g c", g=NG), w1[:, l, hh * 128:(hh + 1) * 128],
                                     raw[:, :, l:l + 16 * (NCMP - 1) + 1:16], l == 0, l == 31, (w1, raw), (pb,))
                            bcol = pbias[:, which * 2 + hh:which * 2 + hh + 1]
                            k.act(gx[:, 0:508], pb[:, 0:508], AF.Identity, (pb, pbias), (gx,), bias=bcol)
                            k.act(gs[:, 0:508], pb[:, 0:508], AF.Square, (pb, pbias), (gs,), bias=bcol)
                            k.ts("dve", gs[:, 0:508], gs[:, 0:508], GC, 1.0, ALU.mult, ALU.add, (gs,), (gs,))
                            k.tt("dve", gs[:, 0:508], gs[:, 0:508], gx[:, 0:508], ALU.mult, (gs, gx), (gs,))
                            k.act(gt_[:, 0:508], gs[:, 0:508], AF.Tanh, (gs,), (gt_,), scale=GK)
                            k.stt(gs[:, 0:508], gt_[:, 0:508], 1.0, gx[:, 0:508], ALU.add, ALU.mult, (gt_, gx), (gs,))
                            k.ts("dve", hid[which][:, hh, 0:508], gs[:, 0:508], 0.5, 0.0, ALU.mult, ALU.add, (gs,), (hid[which],))
                    for g in range(NG):
                        pb = nxt(abanks, "a")
                        for hh in range(2):
                            k.mm(pb[:, 0:NCMP], w2k[:, hh, :], hid[0][:, hh, g * NCMP:(g + 1) * NCMP], hh == 0, hh == 1, (w2k, hid[0]), (pb,))
                        k.cp("dve", kcT[:, g, 0:NCMP], pb[:, 0:NCMP], (pb,), (kcT,))
                        pb = nxt(abanks, "a")
                        for hh in range(2):
                            k.mm(pb[0:NCMP, 0:64], hid[1][:, hh, g * NCMP:(g + 1) * NCMP], w2v[:, hh, :], hh == 0, hh == 1, (w2v, hid[1]), (pb,))
                        k.cp("dve", vcaug[0:NCMP, g, 0:64], pb[0:NCMP, 0:64], (pb,), (vcaug,))
                    for g in range(NG):
                        k.dma(qnt[32:128, :, :], qn_d[b, :, 4 * g:4 * g + 4, :], (), (qnt,), qnt)
                        k.dma(qat[64:128, :, :], qn_d[b, 32:96, 4 * g:4 * g + 4, :], (), (qat,), qat)
                        k.dma(qat[32:64, :, :], qr_d[b, :, 4 * g:4 * g + 4, :], (), (qat,), qat)
                        for qc in range(4):
                            e2s = []
                            for h in range(4):
                                pb = nxt(sbanks, "s")
                                k.mm(pb[0:NCMP, :], kcT[:, g, 0:NCMP], qnt[:, h, qc * T:(qc + 1) * T], True, True, (kcT, qnt), (pb,))
                                e1 = nxt(em, "e")
                                e2 = pmk[h]
                                k.act(e1[0:NCMP, :], pb[0:NCMP, :], AF.Exp, (pb,), (e1,))
                                k.tt("pool", e2[0:NCMP, :], e1[0:NCMP, :], maskc[0:NCMP, qc * T:(qc + 1) * T], ALU.mult, (e1, maskc), (e2,))
                                e2s.append(e2)
                            for j in range(4):
                                qt = qc * 4 + j
                                pa = nxt(abanks, "a")
                                for h in range(4):
                                    k.mm(pa[:, h * 97:(h + 1) * 97], e2s[h][0:NCMP, j * 128:(j + 1) * 128], vcaug[0:NCMP, g, 0:97], h == 0, h == 3,
                                         (e2s[h], vcaug), (pa,), skip=True)
                                p3 = pa[:, 0:388].rearrange("p (h c) -> p h c", h=4)
                                rd_ = nxt(rden, "n")
                                fc_ = fac[cnt["n"] % 3]
                                k.ts("dve", rd_[:, :], p3[:, :, 96], 1e-30, 0.0, ALU.max, ALU.add, (pa,), (rd_,))
                                rcp(rd_[:, :], rd_[:, :], (rd_,), (rd_,))
                                k.tt("dve", impn[:, :, :], p3[:, :, 64:96], rd_[:, :].unsqueeze(2).to_broadcast([128, 4, 32]), ALU.mult, (pa, rd_), (impn,))
                                P.op("dve", lambda e, o=imp[:, :], i=impn[:, :, :].rearrange("p h n -> p n h"): e.tensor_reduce(o, i, axis=AX.X, op=ALU.add),
                                     _rs((impn,)), _rs((imp,)))
                                k.tt("dve", imp[:, :], imp[:, :], tkmul[:, qt, :], ALU.mult, (imp, tkmul), (imp,))
                                k.tt("dve", imp[:, :], imp[:, :], tkadd[:, qt, :], ALU.add, (imp, tkadd), (imp,))
                                P.op("dve", lambda e, o=top8[:, :], i=imp[:, :]: e.max(o, i), _rs((imp,)), _rs((top8,)))
                                k.ts("dve", imp[:, :], imp[:, :], top8[:, 7:8], 1.0, ALU.is_ge, ALU.subtract, (imp, top8), (imp,))
                                k.ts("dve", selb[:, 0:32], imp[:, :], -NEG, 0.0, ALU.mult, ALU.add, (imp,), (selb,))
                                k.tr(tbank[:, 0:128], selb[:, :], ident[:, :], (selb, ident), (tbank,))
                                k.cp("dve", qat[0:32, :, qt * 128:(qt + 1) * 128], tbank[0:32, 0:128].unsqueeze(1).to_broadcast([32, 4, 128]), (tbank,), (qat,))
                                k.tt("dve", fc_[:, :], rd_[:, :], gts[:, qt, 12 * g:12 * g + 12:3], ALU.mult, (rd_, gts), (fc_,))
                                k.tt("dve", oacc[:, qt, :].rearrange("p (h v) -> p h v", h=4), p3[:, :, 0:64],
                                     fc_[:, :].unsqueeze(2).to_broadcast([128, 4, 64]), ALU.mult, (pa, fc_), (oacc,))
                        tiles = []
                        for h in range(4):
                            for qc in range(4):
                                nkt = 4 * qc + 4
                                lst = []
                                for kt in range(nkt):
                                    q0 = max(qc * T, kt * 128)
                                    ncol = (qc + 1) * T - q0
                                    masks = [(0, 0)] if kt * 128 >= qc * T else []
                                    pv = [(qt - 4 * qc, qt * 128 - q0) for qt in range(q0 // 128, 4 * qc + 4)]
                                    lst.append(dict(kT=ksT, br=0, kt=kt, q0=q0, ncol=ncol, masks=masks, pv=pv))
                                tiles.append(dict(h=h, qc=qc, br=1, lst=lst))
                                lst = []
                                for kt in range(max(0, 4 * qc - 4), 4 * qc + 4):
                                    qlo = max(kt, 4 * qc)
                                    qhi = min(kt + 4, 4 * qc + 3)
                                    q0 = qlo * 128
                                    ncol = (qhi - qlo + 1) * 128
                                    masks = []
                                    if qlo == kt:
                                        masks.append((0, 0))
                                    if qhi == kt + 4:
                                        masks.append((ncol - 128, 128))
                                    pv = [(qt - 4 * qc, (qt - qlo) * 128) for qt in range(qlo, qhi + 1)]
                                    lst.append(dict(kT=kwT, br=1, kt=kt, q0=q0, ncol=ncol, masks=masks, pv=pv))
                                tiles.append(dict(h=h, qc=qc, br=2, lst=lst))
                        flat = []
                        for tg_ in tiles:
                            po = None
                            for i_, t_ in enumerate(tg_["lst"]):
                                flat.append((tg_, t_, i_ == 0, i_ == len(tg_["lst"]) - 1))
                        LOOK = 2
                        pend = []
                        for idx in range(len(flat) + LOOK):
                            if idx < len(flat):
                                tg_, t_, isf, isl = flat[idx]
                                h = tg_["h"]
                                if isf:
                                    tg_["po"] = nxt(abanks, "a")
                                ps_ = nxt(sbanks, "s")
                                ncol = t_["ncol"]
                                k.mm(ps_[:, 0:ncol], t_["kT"][:, g, t_["kt"] * 128:(t_["kt"] + 1) * 128], qat[:, h, t_["q0"]:t_["q0"] + ncol], True, True,
                                     (t_["kT"], qat), (ps_,))
                                pb_ = nxt(pmk, "p")
                                k.act(pb_[:, 0:ncol], ps_[:, 0:ncol], AF.Exp, (ps_,), (pb_,))
                                for (c0, m0) in t_["masks"]:
                                    k.tt("pool", pb_[:, c0:c0 + 128], pb_[:, c0:c0 + 128], tri[:, m0:m0 + 128], ALU.mult, (pb_, tri), (pb_,))
                                t_["pb"] = pb_
                            if idx >= LOOK:
                                tg_, t_, isf, isl = flat[idx - LOOK]
                                h, qc, po = tg_["h"], tg_["qc"], tg_["po"]
                                pb_ = t_["pb"]
                                for pi_, (j, off) in enumerate(t_["pv"]):
                                    k.mm(po[:, j * 65:(j + 1) * 65], pb_[:, off:off + 128], vaug[:, t_["kt"], t_["br"], g, :], isf and pi_ == 0, isl,
                                         (pb_, vaug), (po,), skip=True)
                                if isl:
                                    br = tg_["br"]
                                    p3 = po[:, 0:260].rearrange("p (j c) -> p j c", j=4)
                                    rd_ = nxt(rden, "n")
                                    fc_ = fac[cnt["n"] % 3]
                                    tm_ = tmp[cnt["n"] % 3]
                                    rcp(rd_[:, :], p3[:, :, 64], (po,), (rd_,))
                                    gcol = (4 * g + h) * 3 + br
                                    k.tt("dve", fc_[:, :], rd_[:, :], gts[:, 4 * qc:4 * qc + 4, gcol], ALU.mult, (rd_, gts), (fc_,))
                                    k.tt("dve", tm_[:, :, :], p3[:, :, 0:64], fc_[:, :].unsqueeze(2).to_broadcast([128, 4, 64]), ALU.mult, (po, fc_), (tm_,))
                                    k.tt("pool", oacc[:, 4 * qc:4 * qc + 4, h * 64:(h + 1) * 64], oacc[:, 4 * qc:4 * qc + 4, h * 64:(h + 1) * 64],
                                         tm_[:, :, :], ALU.add, (oacc, tm_), (oacc,))
                        k.act(obf[:, :, :], oacc[:, :, :], AF.Copy, (oacc,), (obf,))
                        k.dma(o_d[b * S:(b + 1) * S, g * 256:(g + 1) * 256].rearrange("(j p) v -> p j v", p=128), obf[:, :, :], (obf,), (), obf)
                P.flush()
        if "E" in phases:
            phase_ffn(1, h1_d, y_d, True, True)
    return nc


def _consts():
    c = {}
    c["c_ident"] = np.eye(128, dtype=np.float32)
    kk = np.arange(128)[:, None]
    qq = np.arange(128)[None, :]
    c["c_tri"] = np.concatenate([(kk <= qq), (kk > qq)], axis=1).astype(np.float32)
    cc = np.arange(128)[:, None]
    tt = np.arange(S)[None, :]
    c["c_maskc"] = ((16 * cc + 31 <= tt) & (cc < NCMP)).astype(np.float32)
    half = 12
    inv = (np.float32(500000.0) ** (-np.arange(half, dtype=np.float32) * np.float32(2.0) / np.float32(24))).astype(np.float32)
    ang = (np.arange(S, dtype=np.float32)[None, :] * inv[:, None]).astype(np.float32)
    cs = np.cos(ang.astype(np.float64)).astype(np.float32)
    sn = np.sin(ang.astype(np.float64)).astype(np.float32)
    rc = np.ones((128, S), np.float32)
    rs = np.zeros((128, S), np.float32)
    rc[32:44] = cs
    rc[44:56] = cs
    rs[32:44] = -sn
    rs[44:56] = sn
    c["c_ropec"] = rc
    c["c_ropes"] = rs
    pm = np.zeros((128, 64), np.float32)
    for d in range(12):
        pm[32 + d + 12, 32 + d] = 1.0
        pm[32 + d, 32 + d + 12] = 1.0
    c["c_pm"] = pm
    c["c_esel"] = (np.arange(S)[None, :] // 64 == np.arange(32)[:, None]).astype(np.float32)
    ncmp = NCMP
    cs_ = np.arange(ncmp) * 16
    ss_ = np.arange(32) * 64
    ovl = ((cs_[:, None] < ss_[None, :] + 64) & (cs_[:, None] + 32 > ss_[None, :])).astype(np.float32)
    o = np.zeros((128, 33), np.float32)
    o[:ncmp, :32] = ovl
    o[:ncmp, 32] = 1.0
    c["c_ovl"] = o
    t = np.arange(S)[:, None]
    n = np.arange(32)[None, :]
    cur = t // 64
    forced = (n == 0) | (n == cur) | (n == cur - 1)
    vis = (n * 64 <= t)
    c["c_tkadd"] = np.where(forced, 1e9, np.where(vis, 0.0, -1e9)).astype(np.float32)
    c["c_tkmul"] = (vis & ~forced).astype(np.float32)
    return c


def _kc(w):
    K_, M = w.shape
    return np.ascontiguousarray(w.reshape(K_ // 128, 128, M).transpose(1, 0, 2).reshape(128, -1))


def _prep_weights(inp):
    f = lambda a: np.ascontiguousarray(np.asarray(a, dtype=np.float32))
    w = {}
    ng = inp["norm_g"]
    gs = np.stack([ng[0, 0], ng[0, 1], inp["kv_norm_g"], ng[1, 0], ng[1, 1]], 0)
    w["g_all"] = f(gs.reshape(5, 8, 128).transpose(2, 0, 1))
    w["g_fin"] = f(inp["final_g"].reshape(1, D))
    w["w_in"] = _kc(f(inp["a_w_in"][0]))
    w["conv_w"] = f(inp["a_conv_w"][0].reshape(4, NBLK, RB).transpose(2, 1, 0).reshape(RB, NBLK * 4))
    sm = np.stack([inp["a_conv_b"][0].reshape(NBLK, RB).T, inp["a_b_ra"][0].T, inp["a_b_ix"][0].T,
                   inp["a_lambda"][0].reshape(NBLK, RB).T], 1)
    w["a_small"] = f(sm)
    w["w_ra"] = f(inp["a_w_ra"][0].transpose(1, 0, 2).reshape(RB, NBLK * RB))
    w["w_ix"] = f(inp["a_w_ix"][0].transpose(1, 0, 2).reshape(RB, NBLK * RB))
    w["w_out"] = f(inp["a_w_out"][0].reshape(NBLK, RB, D).transpose(1, 0, 2).reshape(RB, NBLK * D))
    for l in range(2):
        w["w_gu%d" % l] = _kc(f(inp["ffn_w_gu"][l]))
        w["w_dn%d" % l] = f(inp["ffn_w_down"][l].reshape(NFC, 128, D).transpose(1, 0, 2).reshape(128, NFC * D))
    kv = f(inp["kv_w"])
    kcmp_, vcmp_, ksel_, vsel_, kwin_, vwin_ = kv[:, 0:384], kv[:, 384:640], kv[:, 640:1024], kv[:, 1024:1280], kv[:, 1280:1664], kv[:, 1664:1920]
    w["w_kc"] = _kc(kcmp_)
    w["w_vc"] = _kc(vcmp_)

    def aug(m, n):
        z = np.zeros((m.shape[0], n, 128), np.float32)
        z[:, :, 32:] = m.reshape(m.shape[0], n, 96)
        return z.reshape(m.shape[0], n * 128)
    w["w_ks"] = _kc(aug(ksel_, NG))
    w["w_kw"] = _kc(aug(kwin_, NG))
    w["w_v"] = _kc(np.concatenate([vsel_, vwin_], 1))
    wq = f(inp["b_w_q"][0])
    w["w_q"] = _kc(aug(wq[:, :NH * DQK], NH))
    w["w_qg"] = _kc(wq[:, NH * DQK:])
    w["gate_b"] = f(inp["b_gate_bias"][0].reshape(1, 48))
    w["w_o"] = _kc(f(inp["b_w_o"][0]))
    w["w1k"] = f(inp["cmp_w1_k"].reshape(32, 96, 256).transpose(1, 0, 2).reshape(96, 32 * 256))
    w["w1v"] = f(inp["cmp_w1_v"].reshape(32, 64, 256).transpose(1, 0, 2).reshape(64, 32 * 256))
    w2k = np.zeros((256, 128), np.float32)
    w2k[:, 32:] = inp["cmp_w2_k"]
    w["w2k"] = f(w2k.reshape(2, 128, 128).transpose(1, 0, 2).reshape(128, 256))
    w["w2v"] = f(np.asarray(inp["cmp_w2_v"], np.float32).reshape(2, 128, 64).transpose(1, 0, 2).reshape(128, 128))
    w["posk"] = f(np.asarray(inp["cmp_pos_k"]).T)
    w["posv"] = f(np.asarray(inp["cmp_pos_v"]).T)
    return w


_CACHE = {}


def kernel(**inputs):
    inp = {k_: np.asarray(v) for k_, v in inputs.items()}
    ncores = 8
    nseq = inp["x"].shape[0] // ncores
    if "nc" not in _CACHE:
        _CACHE["nc"] = build_program(nseq)
    nc = _CACHE["nc"]
    shared = _consts()
    shared.update(_prep_weights(inp))
    x = np.ascontiguousarray(inp["x"], dtype=np.float32).reshape(ncores, nseq * S, D)
    in_maps = []
    for c in range(ncores):
        m = dict(shared)
        m["x"] = x[c]
        in_maps.append(m)
    res = run_bass_kernel_spmd(nc, in_maps, core_ids=list(range(ncores)))
    out = np.stack([np.asarray(r["y"]) for r in res.results], 0)
    return out.reshape(inp["x"].shape).astype(np.float32)
```

```python
import contextlib
import os
import numpy as np
import concourse.bass as bass
import concourse.mybir as mybir
from concourse.bass_utils import run_bass_kernel_spmd

F32 = mybir.dt.float32
BF16 = mybir.dt.bfloat16
AF = mybir.ActivationFunctionType
ALU = mybir.AluOpType
AX = mybir.AxisListType

D = 1024
S = 2048
T = 512
DRNN = 1536
NBLK = 16
RB = 96
DFF = 2816
NFC = 22
NH = 16
NG = 4
DQK = 96
DV = 64
NCMP = 127
EPS = 1e-6
NEG = -30000.0
GK = 0.7978845608028654
GC = 0.044715


class Res:
    __slots__ = ("name", "last_w", "readers")

    def __init__(self, name):
        self.name = name
        self.last_w = None
        self.readers = []


class Op:
    __slots__ = ("eng", "fn", "waits", "signal", "is_dma", "key", "pos", "sigidx", "is_nop")

    def __init__(self, eng, fn, is_dma=False, key=None):
        self.eng = eng
        self.fn = fn
        self.waits = []
        self.signal = False
        self.is_dma = is_dma
        self.key = key
        self.pos = -1
        self.sigidx = -1
        self.is_nop = False


ENGS = ("pe", "act", "dve", "pool", "sp")


class Prog:
    def __init__(self, nc, stack):
        self.nc = nc
        self.stack = stack
        self.streams = {e: [] for e in ENGS}
        self.emitted = {e: 0 for e in ENGS}
        self.nsig = {e: 0 for e in ENGS}
        self.npos = {e: 0 for e in ENGS}
        self.known = {e: {} for e in ENGS}
        self.dma_cnt = {}
        self.esem = {e: stack.enter_context(nc.semaphore("s_" + e)) for e in ENGS}
        self.ksem = {}

    def res(self, name="r"):
        return Res(name)

    def _dep(self, op, d, raw):
        if d is None or d is op:
            return
        E = op.eng
        k = self.known[E]
        if d.is_dma:
            src = ("dma", d.key)
            cnt = self.dma_cnt[d.key]
            if k.get(src, -1) >= cnt:
                return
            k[src] = cnt
            op.waits.append(("dma", d.key, cnt))
            return
        if d.eng == E and not op.is_dma:
            if E == "pe":
                return
        if k.get(d.eng, -1) >= d.pos:
            return
        k[d.eng] = d.pos
        d.signal = True
        op.waits.append(("eng", d))

    def op(self, eng, fn, reads=(), writes=(), is_dma=False, key=None):
        o = Op(eng, fn, is_dma, key)
        if is_dma:
            self.dma_cnt.setdefault(key, 0)
        else:
            o.pos = self.npos[eng]
            self.npos[eng] += 1
        for r in reads:
            self._dep(o, r.last_w, True)
        for w in writes:
            self._dep(o, w.last_w, False)
            for rd in w.readers:
                self._dep(o, rd, False)
        if is_dma:
            self.dma_cnt[key] += 1
            o.pos = self.dma_cnt[key]
        for r in reads:
            r.readers.append(o)
        for w in writes:
            w.last_w = o
            w.readers = []
        self.streams[eng].append(o)
        return o

    def dma(self, q, out, in_, reads=(), writes=(), key=None):
        key = id(key)
        return self.op(q, lambda e: e.dma_start(out=out, in_=in_), reads, writes, is_dma=True, key=key)

    def barrier(self):
        lasts = []
        for e in ENGS:
            comp = [o for o in self.streams[e] if not o.is_dma and not o.is_nop]
            if comp:
                lasts.append(comp[-1])
        for e in ENGS:
            o = Op(e, lambda eng: eng.nop())
            o.is_nop = True
            o.pos = self.npos[e]
            self.npos[e] += 1
            k = self.known[e]
            for d in lasts:
                if k.get(d.eng, -1) >= d.pos:
                    continue
                k[d.eng] = d.pos
                d.signal = True
                o.waits.append(("eng", d))
            for key, cnt in self.dma_cnt.items():
                src = ("dma", key)
                if k.get(src, -1) >= cnt:
                    continue
                k[src] = cnt
                o.waits.append(("dma", key, cnt))
            self.streams[e].append(o)

    def flush(self):
        self.barrier()
        nc = self.nc
        for e in ENGS:
            n = self.nsig[e]
            for o in self.streams[e]:
                if not o.is_dma and o.signal:
                    n += 1
                    o.sigidx = n
            self.nsig[e] = n
        for key in self.dma_cnt:
            if key not in self.ksem:
                self.ksem[key] = self.stack.enter_context(nc.semaphore("d_%d" % len(self.ksem)))
        esem, ksem = self.esem, self.ksem
        streams = self.streams

        def run(en, eng):
            for o in streams[en]:
                for w in o.waits:
                    if w[0] == "dma":
                        eng.wait_ge(ksem[w[1]], 16 * w[2])
                    else:
                        eng.wait_ge(esem[w[1].eng], w[1].sigidx)
                ins = o.fn(eng)
                if o.is_dma:
                    ins.then_inc(ksem[o.key], 16)
                elif o.signal:
                    ins.then_inc(esem[en], 1)

        with nc.Block() as block:
            @block.tensor
            def _(e):
                run("pe", e)

            @block.scalar
            def _(e):
                run("act", e)

            @block.vector
            def _(e):
                run("dve", e)

            @block.gpsimd
            def _(e):
                run("pool", e)

            @block.sync
            def _(e):
                run("sp", e)
        self.streams = {e: [] for e in ENGS}


class Tl:
    __slots__ = ("t", "r")

    def __init__(self, t, r):
        self.t = t
        self.r = r

    def __getitem__(self, k):
        return self.t[k]


def _rs(xs):
    out = []
    for x in xs:
        if x is None:
            continue
        out.append(x.r if isinstance(x, Tl) else x)
    return out


class K:
    def __init__(self, nc, P):
        self.nc = nc
        self.P = P
        self.n = 0
        self.rr = 0

    def sb(self, st, shape, dt=F32, name=None):
        self.n += 1
        nm = "%s_%d" % (name or "t", self.n)
        return Tl(st.enter_context(self.nc.sbuf_tensor(nm, list(shape), dt)), Res(nm))

    def ps(self, st, shape, dt=F32, name=None):
        self.n += 1
        nm = "%s_%d" % (name or "p", self.n)
        return Tl(st.enter_context(self.nc.psum_tensor(nm, list(shape), dt)), Res(nm))

    def mm(self, out, lhsT, rhs, start, stop, rd, wr, skip=False):
        if skip:
            f = lambda e: e.matmul(out, lhsT=lhsT, rhs=rhs, start=start, stop=stop, skip_group_check=True)
        else:
            f = lambda e: e.matmul(out, lhsT=lhsT, rhs=rhs, start=start, stop=stop)
        self.P.op("pe", f, _rs(rd), _rs(wr))

    def tr(self, out, in_, ident, rd, wr):
        self.P.op("pe", lambda e: e.transpose(out, in_, ident), _rs(rd), _rs(wr))

    def act(self, out, in_, func, rd, wr, scale=1.0, bias=None, accum=None):
        kw = {}
        if bias is not None:
            kw["bias"] = bias
        if accum is not None:
            kw["accum_out"] = accum
        self.P.op("act", lambda e: e.activation(out=out, in_=in_, func=func, scale=scale, **kw), _rs(rd), _rs(wr))

    def cp(self, eng, out, in_, rd, wr):
        self.P.op(eng, lambda e: e.tensor_copy(out, in_), _rs(rd), _rs(wr))

    def ts(self, eng, out, in0, s1, s2, op0, op1, rd, wr):
        self.P.op(eng, lambda e: e.tensor_scalar(out, in0, s1, s2, op0, op1), _rs(rd), _rs(wr))

    def tt(self, eng, out, in0, in1, op, rd, wr):
        self.P.op(eng, lambda e: e.tensor_tensor(out, in0, in1, op), _rs(rd), _rs(wr))

    def stt(self, out, in0, scalar, in1, op0, op1, rd, wr):
        self.P.op("dve", lambda e: e.scalar_tensor_tensor(out=out, in0=in0, scalar=scalar, in1=in1, op0=op0, op1=op1),
                  _rs(rd), _rs(wr))

    def memset(self, eng, out, val, wr):
        self.P.op(eng, lambda e: e.memset(out, val), (), _rs(wr))

    def dma(self, out, in_, rd, wr, key, q="sp"):
        self.P.dma(q, out, in_, _rs(rd), _rs(wr), key=(key.r if isinstance(key, Tl) else key))

    def ceng(self):
        self.rr += 1
        return ("dve", "pool")[self.rr % 2]

    def wload(self, stages, dst_fn, src_fn, ncol, dst, piece=2048, gain=None, npart=128):
        c0 = 0
        while c0 < ncol:
            c1 = min(ncol, c0 + piece)
            stg = stages[self.rr % len(stages)]
            self.dma(stg[0:npart, 0:c1 - c0], src_fn(c0, c1), (), (stg,), stg)
            eng = self.ceng()
            if gain is None:
                self.cp(eng, dst_fn(c0, c1), stg[0:npart, 0:c1 - c0], (stg,), (dst,))
            else:
                self.ts(eng, dst_fn(c0, c1), stg[0:npart, 0:c1 - c0], gain, 0.0, ALU.mult, ALU.add, (stg,), (dst,))
            c0 = c1


def norm_transpose(k, xin, hnbs, hnT, ss, rstd, mhalf, ident, tbanks):
    for j in range(4):
        hnb = hnbs[j % len(hnbs)]
        tb = tbanks[j % len(tbanks)]
        k.act(hnb[:, :], xin[:, j, :], AF.Square, (xin,), (hnb, ss), accum=ss[:, j:j + 1])
        k.ts("dve", rstd[:, j:j + 1], ss[:, j:j + 1], 1.0 / D, EPS, ALU.mult, ALU.add, (ss,), (rstd,))
        k.tt("pool", rstd[:, j:j + 1], rstd[:, j:j + 1], mhalf[:, 0:1], ALU.pow, (rstd, mhalf), (rstd,))
        if j % 2 == 0:
            k.ts("dve", hnb[:, :], xin[:, j, :], rstd[:, j:j + 1], 0.0, ALU.mult, ALU.add, (xin, rstd), (hnb,))
        else:
            k.act(hnb[:, :], xin[:, j, :], AF.Identity, (xin, rstd), (hnb,), scale=rstd[:, j:j + 1])
        for kc in range(8):
            k.tr(tb[:, kc * 128:(kc + 1) * 128], hnb[:, kc * 128:(kc + 1) * 128], ident[:, :], (hnb, ident), (tb,))
        src = tb[:, :].rearrange("p (a b) -> p a b", a=8)
        if j % 2 == 0:
            k.act(hnT[:, :, j * 128:(j + 1) * 128], src, AF.Copy, (tb,), (hnT,))
        else:
            k.cp("dve", hnT[:, :, j * 128:(j + 1) * 128], src, (tb,), (hnT,))


def ffn_chunk(k, hnT, actT, wgu, wdn, banks, bi, tg, tA):
    for fc in range(NFC):
        pg = banks[bi % len(banks)]
        pv = banks[(bi + 1) % len(banks)]
        bi += 2
        for kc in range(8):
            k.mm(pg[:, :], wgu[:, kc, fc * 128:(fc + 1) * 128], hnT[:, kc, :], kc == 0, kc == 7, (wgu, hnT), (pg,))
        for kc in range(8):
            k.mm(pv[:, :], wgu[:, kc, DFF + fc * 128:DFF + (fc + 1) * 128], hnT[:, kc, :], kc == 0, kc == 7, (wgu, hnT), (pv,))
        t1 = tg[fc % len(tg)]
        t2 = tA[fc % len(tA)]
        k.act(t1[:, :], pg[:, :], AF.Tanh, (pg,), (t1,), scale=0.5)
        k.stt(t2[:, :], t1[:, :], 1.0, pg[:, :], ALU.add, ALU.mult, (t1, pg), (t2,))
        k.stt(actT[:, fc, :], t2[:, :], 0.5, pv[:, :], ALU.mult, ALU.mult, (t2, pv), (actT,))
    return bi


def ffn_down(k, actT, wdn, xin, banks, bi):
    for j in range(4):
        for hf in range(2):
            pb = banks[bi % len(banks)]
            bi += 1
            for fc in range(NFC):
                k.mm(pb[:, :], actT[:, fc, j * 128:(j + 1) * 128], wdn[:, fc, hf * 512:(hf + 1) * 512], fc == 0, fc == NFC - 1,
                     (actT, wdn), (pb,))
            k.tt("dve", xin[:, j, hf * 512:(hf + 1) * 512], pb[:, :], xin[:, j, hf * 512:(hf + 1) * 512], ALU.add, (pb, xin), (xin,))
    return bi


def tok_view(ap2d, t0):
    return ap2d[t0:t0 + T, :].rearrange("(j p) d -> p j d", p=128)


def build_program(nseq, phases="ABCDE", debug=False):
    NT = nseq * S
    nc = bass.Bass("TRN2", target_bir_lowering=False)
    dt_in = {}

    def din(name, shape):
        dt_in[name] = shape
        return nc.dram_tensor(name, list(shape), F32, kind="ExternalInput").ap()

    def dscr(name, shape, dt):
        kind = "ExternalOutput" if debug else "Internal"
        return nc.dram_tensor(name, list(shape), dt, kind=kind).ap()

    x_d = din("x", (NT, D))
    identf_d = din("c_ident", (128, 128))
    tri_d = din("c_tri", (128, 256))
    maskc_d = din("c_maskc", (128, S))
    ropec_d = din("c_ropec", (128, S))
    ropes_d = din("c_ropes", (128, S))
    pm_d = din("c_pm", (128, 64))
    esel_d = din("c_esel", (32, S))
    ovl_d = din("c_ovl", (128, 33))
    tkadd_d = din("c_tkadd", (S, 32))
    tkmul_d = din("c_tkmul", (S, 32))
    g_d = din("g_all", (128, 5, 8))
    gfin_d = din("g_fin", (1, D))
    win_d = din("w_in", (128, 8 * 3072))
    cw_d = din("conv_w", (RB, NBLK * 4))
    sm_d = din("a_small", (RB, 4, NBLK))
    wra_d = din("w_ra", (RB, NBLK * RB))
    wix_d = din("w_ix", (RB, NBLK * RB))
    wout_d = din("w_out", (RB, NBLK * D))
    wgu_d = [din("w_gu%d" % l, (128, 8 * 2 * DFF)) for l in range(2)]
    wdn_d = [din("w_dn%d" % l, (128, NFC * D)) for l in range(2)]
    wkc_d = din("w_kc", (128, 8 * NG * 96))
    wvc_d = din("w_vc", (128, 8 * NG * 64))
    wks_d = din("w_ks", (128, 8 * NG * 128))
    wkw_d = din("w_kw", (128, 8 * NG * 128))
    wv_d = din("w_v", (128, 8 * 512))
    wq_d = din("w_q", (128, 8 * NH * 128))
    wqg_d = din("w_qg", (128, 8 * 48))
    gb_d = din("gate_b", (1, 48))
    wo_d = din("w_o", (128, 8 * D))
    w1k_d = din("w1k", (96, 32 * 256))
    w1v_d = din("w1v", (64, 32 * 256))
    w2k_d = din("w2k", (128, 2 * 128))
    w2v_d = din("w2v", (128, 2 * 64))
    posk_d = din("posk", (96, 32))
    posv_d = din("posv", (64, 32))

    y_d = nc.dram_tensor("y", [NT, D], F32, kind="ExternalOutput").ap()
    hmid_d = dscr("s_hmid", (NT, D), F32)
    h1_d = dscr("s_h1", (NT, D), F32)
    kcmp_d = dscr("s_kcmp", (nseq, 96, NG, S), BF16)
    vcmp_d = dscr("s_vcmp", (nseq, 64, NG, S), BF16)
    ksel_d = dscr("s_ksel", (nseq, 96, NG, S), BF16)
    kwin_d = dscr("s_kwin", (nseq, 96, NG, S), BF16)
    v_d = dscr("s_v", (NT, 520), BF16)
    qn_d = dscr("s_qn", (nseq, 96, NH, S), BF16)
    qr_d = dscr("s_qr", (nseq, 32, NH, S), BF16)
    gt_d = dscr("s_gt", (NT, 48), F32)
    o_d = dscr("s_o", (NT, D), BF16)

    with contextlib.ExitStack() as top:
        P = Prog(nc, top)
        k = K(nc, P)

        ident = k.sb(top, [128, 128], BF16, "ident")
        mhalf = k.sb(top, [128, 4], F32, "mhalf")
        gall = k.sb(top, [128, 5, 8], F32, "gall")
        with contextlib.ExitStack() as st:
            tmp = k.sb(st, [128, 128], F32)
            k.dma(tmp[:, :], identf_d, (), (tmp,), tmp)
            k.cp("dve", ident[:, :], tmp[:, :], (tmp,), (ident,))
            k.memset("dve", mhalf[:, :], -0.5, (mhalf,))
            k.dma(gall[:, :, :], g_d, (), (gall,), gall)
            P.flush()

        if "A" in phases:
            with contextlib.ExitStack() as st:
                win = k.sb(st, [128, 8, 3072], BF16, "win")
                wout = k.sb(st, [RB, NBLK, D], BF16, "wout")
                wra = k.sb(st, [RB, NBLK, RB], BF16, "wra")
                wix = k.sb(st, [RB, NBLK, RB], BF16, "wix")
                dg = k.sb(st, [RB, NBLK * 4, RB], BF16, "dg")
                cw = k.sb(st, [RB, NBLK * 4], F32, "cw")
                sm = k.sb(st, [RB, 4, NBLK], F32, "sm")
                hb = k.sb(st, [RB, 2, NBLK], F32, "hb")
                cc = k.sb(st, [RB, 2, NBLK], F32, "cc")
                cx = k.sb(st, [RB, NBLK, 4], BF16, "cx")
                hcar = k.sb(st, [RB, NBLK], F32, "hcar")
                one = k.sb(st, [RB, 1], F32, "one")
                k.memset("dve", one[:, :], 1.0, (one,))
                with contextlib.ExitStack() as st2:
                    stages = [k.sb(st2, [128, 3072], F32, "stg") for _ in range(3)]
                    for kc in range(8):
                        k.wload(stages, lambda a, b, kc=kc: win[:, kc, a:b], lambda a, b, kc=kc: win_d[:, kc * 3072 + a:kc * 3072 + b],
                                3072, win, piece=3072, gain=gall[:, 0, kc:kc + 1])
                    for n in range(NBLK):
                        k.wload(stages, lambda a, b, n=n: wout[:, n, a:b], lambda a, b, n=n: wout_d[:, n * D + a:n * D + b], D, wout,
                                piece=D, npart=RB)
                    k.wload(stages, lambda a, b: wra[:, :, :].rearrange("p n j -> p (n j)")[:, a:b], lambda a, b: wra_d[:, a:b],
                            NBLK * RB, wra, piece=NBLK * RB, npart=RB)
                    k.wload(stages, lambda a, b: wix[:, :, :].rearrange("p n j -> p (n j)")[:, a:b], lambda a, b: wix_d[:, a:b],
                            NBLK * RB, wix, piece=NBLK * RB, npart=RB)
                    k.dma(cw[:, :], cw_d, (), (cw,), cw)
                    k.dma(sm[:, :, :], sm_d, (), (sm,), sm)
                    for i in range(NBLK * 4):
                        k.ts(k.ceng(), dg[:, i, :], ident[0:RB, 0:RB], cw[:, i:i + 1], 0.0, ALU.mult, ALU.add, (ident, cw), (dg,))
                    k.ts("dve", hb[:, :, :], sm[:, 1:3, :], -1.0, 0.0, ALU.mult, ALU.add, (sm,), (hb,))
                    sg = k.sb(st2, [RB, NBLK], F32)
                    k.act(sg[:, :], sm[:, 3, :], AF.Exp, (sm,), (sg,), scale=-1.0)
                    k.ts("dve", sg[:, :], sg[:, :], 1.0, 1.0, ALU.mult, ALU.add, (sg,), (sg,))
                    k.act(sg[:, :], sg[:, :], AF.Ln, (sg,), (sg,))
                    k.ts("dve", cc[:, 0, :], sg[:, :], -8.0, 0.0, ALU.mult, ALU.add, (sg,), (cc,))
                    k.ts("dve", cc[:, 1, :], sg[:, :], -16.0, 0.0, ALU.mult, ALU.add, (sg,), (cc,))
                    P.flush()
                if os.environ.get("KSTOP") == "A0":
                    return nc

                xins = [k.sb(st, [128, 4, D], F32, "xin") for _ in range(2)]
                hnb = [k.sb(st, [128, D], BF16, "hnb") for _ in range(2)]
                hnTs = [k.sb(st, [128, 8, T], BF16, "hnT") for _ in range(2)]
                recy = k.sb(st, [RB, NBLK, T], BF16, "recy")
                ss = k.sb(st, [128, 4], F32)
                rstd = k.sb(st, [128, 4], F32)
                NSET = 2
                xbT = [k.sb(st, [RB, T + 4], BF16) for _ in range(NSET)]
                xcT = [k.sb(st, [RB, T], BF16) for _ in range(NSET)]
                trr = [k.sb(st, [RB, T], F32) for _ in range(NSET)]
                tii = [k.sb(st, [RB, T], F32) for _ in range(NSET)]
                aa = [k.sb(st, [RB, T], F32) for _ in range(NSET)]
                a2 = [k.sb(st, [RB, T], F32) for _ in range(NSET)]
                hs = [k.sb(st, [RB, T], F32) for _ in range(NSET)]
                sq = [k.sb(st, [RB, T], F32) for _ in range(NSET)]
                th = [k.sb(st, [RB, T], F32) for _ in range(NSET)]
                banks = [k.ps(st, [128, 512], F32, "bk") for _ in range(4)]
                ybanks = [k.ps(st, [128, 512], F32, "yb") for _ in range(2)]
                tbanks = [k.ps(st, [128, 1024], BF16, "tb") for _ in range(2)]

                def rcp(out, in_, rd, wr):
                    P.op("dve", lambda e: e.reciprocal(out, in_), _rs(rd), _rs(wr))
                bi = 0
                ci = 0
                KNCH = int(os.environ.get("KNCH", "4"))
                KNBLK = int(os.environ.get("KNBLK", "16"))
                KSTOP = os.environ.get("KSTOP", "")
                nchA = min(KNCH, S // T)
                chunksA = [(b, c) for b in range(nseq) for c in range(nchA)]
                k.dma(xins[0][:, :, :], tok_view(x_d, chunksA[0][0] * S + chunksA[0][1] * T), (), (xins[0],), xins[0])
                for b in range(nseq):
                    k.memset("pool", cx[:, :, :], 0.0, (cx,))
                    k.memset("pool", hcar[:, :], 0.0, (hcar,))
                    for c in range(nchA):
                        t0 = b * S + c * T
                        xin = xins[ci % 2]
                        hnT = hnTs[ci % 2]
                        ci += 1
                        if ci < len(chunksA):
                            nb_, nc_ = chunksA[ci]
                            k.dma(xins[ci % 2][:, :, :], tok_view(x_d, nb_ * S + nc_ * T), (), (xins[ci % 2],), xins[ci % 2])
                        norm_transpose(k, xin, hnb, hnT, ss, rstd, mhalf, ident, tbanks)
                        def blk(n, xin=None, hnT=hnT):
                            s_ = n % NSET
                            bA = banks[(2 * n) % 4]
                            bB = banks[(2 * n + 1) % 4]
                            pxb, pr, pcv, pi = bA, bA, bB, bB
                            pyb = ybanks[n % 2]
                            er, ei, ey = trr[s_], tii[s_], th[s_]
                            for kc in range(8):
                                k.mm(pxb[0:RB, :], win[:, kc, n * RB:(n + 1) * RB], hnT[:, kc, :], kc == 0, kc == 7, (win, hnT), (pxb,))
                            for kc in range(8):
                                k.mm(pyb[0:RB, :], win[:, kc, DRNN + n * RB:DRNN + (n + 1) * RB], hnT[:, kc, :], kc == 0, kc == 7,
                                     (win, hnT), (pyb,))
                            yield
                            k.cp("pool", xbT[s_][:, 0:4], cx[:, n, :], (cx,), (xbT[s_],))
                            k.cp("dve", xbT[s_][:, 4:4 + T], pxb[0:RB, :], (pxb,), (xbT[s_],))
                            k.cp("pool", cx[:, n, :], xbT[s_][:, T:T + 4], (xbT[s_],), (cx,))
                            k.act(sq[s_][:, :], pyb[0:RB, :], AF.Square, (pyb,), (sq[s_],))
                            yield
                            for t in range(4):
                                k.mm(pcv[0:RB, :], dg[:, n * 4 + t, :], xbT[s_][:, 1 + t:1 + t + T], t == 0, t == 3, (dg, xbT[s_]), (pcv,))
                            k.ts("pool", sq[s_][:, :], sq[s_][:, :], GC, 1.0, ALU.mult, ALU.add, (sq[s_],), (sq[s_],))
                            yield
                            k.ts("dve", xcT[s_][:, :], pcv[0:RB, :], sm[:, 0, n:n + 1], 0.0, ALU.add, ALU.add, (pcv, sm), (xcT[s_],))
                            k.tt("dve", sq[s_][:, :], sq[s_][:, :], pyb[0:RB, :], ALU.mult, (sq[s_], pyb), (sq[s_],))
                            yield
                            k.mm(pr[0:RB, :], wra[:, n, :], xcT[s_][:, :], True, True, (wra, xcT[s_]), (pr,))
                            k.mm(pi[0:RB, :], wix[:, n, :], xcT[s_][:, :], True, True, (wix, xcT[s_]), (pi,))
                            k.act(ey[:, :], sq[s_][:, :], AF.Exp, (sq[s_],), (ey,), scale=-2.0 * GK)
                            yield
                            k.act(er[:, :], pr[0:RB, :], AF.Exp, (pr, hb), (er,), scale=-1.0, bias=hb[:, 0, n:n + 1])
                            k.act(ei[:, :], pi[0:RB, :], AF.Exp, (pi, hb), (ei,), scale=-1.0, bias=hb[:, 1, n:n + 1])
                            k.act(ey[:, :], ey[:, :], AF.Ln, (ey, one), (ey,), bias=one[:, 0:1])
                            yield
                            k.act(er[:, :], er[:, :], AF.Ln, (er, one), (er,), bias=one[:, 0:1])
                            k.act(ei[:, :], ei[:, :], AF.Ln, (ei, one), (ei,), bias=one[:, 0:1])
                            k.act(ey[:, :], ey[:, :], AF.Exp, (ey,), (ey,), scale=-1.0)
                            yield
                            k.act(er[:, :], er[:, :], AF.Exp, (er,), (er,), scale=-1.0)
                            k.tt("dve", ey[:, :], ey[:, :], pyb[0:RB, :], ALU.mult, (ey, pyb), (ey,))
                            yield
                            k.act(aa[s_][:, :], er[:, :], AF.Exp, (er, cc), (aa[s_],), scale=cc[:, 0, n:n + 1])
                            k.act(a2[s_][:, :], er[:, :], AF.Exp, (er, cc), (a2[s_],), scale=cc[:, 1, n:n + 1])
                            yield
                            k.ts("pool", a2[s_][:, :], a2[s_][:, :], -1.0, 1.0, ALU.mult, ALU.add, (a2[s_],), (a2[s_],))
                            k.ts("pool", a2[s_][:, :], a2[s_][:, :], 1.0, 1e-12, ALU.min, ALU.max, (a2[s_],), (a2[s_],))
                            yield
                            k.act(a2[s_][:, :], a2[s_][:, :], AF.Ln, (a2[s_],), (a2[s_],))
                            yield
                            k.stt(ei[:, :], a2[s_][:, :], 0.5, ei[:, :], ALU.mult, ALU.subtract, (a2[s_], ei), (ei,))
                            yield
                            k.act(ei[:, :], ei[:, :], AF.Exp, (ei,), (ei,))
                            yield
                            k.tt("dve", ei[:, :], ei[:, :], xcT[s_][:, :], ALU.mult, (ei, xcT[s_]), (ei,))
                            k.P.op("dve", (lambda e, o=hs[s_][:, :], d0=aa[s_][:, :], d1=ei[:, :], i0=hcar[:, n:n + 1]:
                                           e.tensor_tensor_scan(o, d0, d1, i0, ALU.mult, ALU.add)),
                                   _rs((aa[s_], ei, hcar)), _rs((hs[s_],)))
                            k.cp("pool", hcar[:, n:n + 1], hs[s_][:, T - 1:T], (hs[s_],), (hcar,))
                            k.tt("dve", recy[:, n, :], ey[:, :], hs[s_][:, :], ALU.mult, (ey, hs[s_]), (recy,))
                            yield

                        active = []
                        nxt_n = 0
                        nblk = min(KNBLK, NBLK)
                        while active or nxt_n < nblk:
                            if nxt_n < nblk and len(active) < NSET and (not active or active[-1][1] >= int(os.environ.get('KSKEW', '8'))):
                                active.append([blk(nxt_n), 0])
                                nxt_n += 1
                            for ent in list(active):
                                try:
                                    next(ent[0])
                                    ent[1] += 1
                                except StopIteration:
                                    active.remove(ent)
                        for j in range(4):
                            for hf in range(2):
                                pb = banks[bi % 4]
                                bi += 1
                                for n in range(NBLK):
                                    k.mm(pb[:, :], recy[:, n, j * 128:(j + 1) * 128], wout[:, n, hf * 512:(hf + 1) * 512], n == 0, n == NBLK - 1,
                                         (recy, wout), (pb,))
                                k.tt("dve", xin[:, j, hf * 512:(hf + 1) * 512], pb[:, :], xin[:, j, hf * 512:(hf + 1) * 512], ALU.add, (pb, xin), (xin,))
                        k.dma(tok_view(hmid_d, t0), xin[:, :, :], (xin,), (), xin)
                P.flush()

        def phase_ffn(layer, src_d, dst_d, with_o, final):
            with contextlib.ExitStack() as st:
                if os.environ.get("KDBG"):
                    print("ffn sbuf remaining at start", nc.sbuf_bytes_remaining)
                wgu = k.sb(st, [128, 8, 2 * DFF], BF16, "wgu")
                wdn = k.sb(st, [128, NFC, D], BF16, "wdn")
                wo = k.sb(st, [128, 8, D], BF16, "wo") if with_o else None
                gfin = k.sb(st, [128, D], F32, "gfin") if final else None
                with contextlib.ExitStack() as st2:
                    stages = [k.sb(st2, [128, 2816], F32, "stg") for _ in range(3)]
                    gi = 1 if layer == 0 else 4
                    for kc in range(8):
                        k.wload(stages, lambda a, b, kc=kc: wgu[:, kc, a:b], lambda a, b, kc=kc: wgu_d[layer][:, kc * 2 * DFF + a:kc * 2 * DFF + b],
                                2 * DFF, wgu, piece=2816, gain=gall[:, gi, kc:kc + 1])
                    for fc in range(0, NFC, 2):
                        k.wload(stages, lambda a, b, fc=fc: wdn[:, fc:fc + 2, :].rearrange("p f d -> p (f d)")[:, a:b],
                                lambda a, b, fc=fc: wdn_d[layer][:, fc * D + a:fc * D + b], 2 * D, wdn, piece=2 * D)
                    if with_o:
                        for kc in range(0, 8, 2):
                            k.wload(stages, lambda a, b, kc=kc: wo[:, kc:kc + 2, :].rearrange("p f d -> p (f d)")[:, a:b],
                                    lambda a, b, kc=kc: wo_d[:, kc * D + a:kc * D + b], 2 * D, wo, piece=2 * D)
                    if final:
                        g1 = k.sb(st2, [1, D], F32)
                        on = k.sb(st2, [1, 128], F32)
                        pbk = k.ps(st2, [128, 512], F32)
                        k.dma(g1[:, :], gfin_d, (), (g1,), g1)
                        k.memset("dve", on[:, :], 1.0, (on,))
                        for hf in range(2):
                            k.mm(pbk[:, :], on[:, :], g1[:, hf * 512:(hf + 1) * 512], True, True, (on, g1), (pbk,))
                            k.cp("dve", gfin[:, hf * 512:(hf + 1) * 512], pbk[:, :], (pbk,), (gfin,))
                    P.flush()
                if os.environ.get("KDBG"):
                    print("ffn sbuf remaining before acts", nc.sbuf_bytes_remaining)
                xins = [k.sb(st, [128, 4, D], F32, "xin") for _ in range(1 if with_o else 2)] * 2
                hnb = [k.sb(st, [128, D], BF16, "hnb") for _ in range(1 if with_o else 2)] * 2
                hnT = k.sb(st, [128, 8, T], BF16, "hnT")
                if os.environ.get("KDBG"):
                    print("ffn sbuf remaining before actT", nc.sbuf_bytes_remaining)
                actT = k.sb(st, [128, NFC, T], BF16, "actT")
                ss = k.sb(st, [128, 4], F32)
                rstd = k.sb(st, [128, 4], F32)
                tg = [k.sb(st, [128, T], BF16) for _ in range(1 if with_o else 2)]
                tA = [k.sb(st, [128, T], F32) for _ in range(1)]
                banks = [k.ps(st, [128, 512], F32, "bk") for _ in range(6)]
                tbanks = [k.ps(st, [128, 1024], BF16, "tb") for _ in range(2)]
                bi = 0
                if not with_o:
                    k.dma(xins[0][:, :, :], tok_view(src_d, 0), (), (xins[0],), xins[0])
                for ci in range(NT // T):
                    t0 = ci * T
                    xin = xins[ci % 2]
                    if with_o:
                        k.dma(xin[:, :, :], tok_view(src_d, t0), (), (xin,), xin)
                    elif ci + 1 < NT // T:
                        k.dma(xins[(ci + 1) % 2][:, :, :], tok_view(src_d, t0 + T), (), (xins[(ci + 1) % 2],), xins[(ci + 1) % 2])
                    if with_o:
                        oi = actT[:, 0:8, :].rearrange("p (j a) t -> p j (a t)", j=4)
                        k.dma(oi, tok_view(o_d, t0), (), (actT,), actT)
                        for j in range(4):
                            tb = tbanks[j % 2]
                            for kc in range(8):
                                k.tr(tb[:, kc * 128:(kc + 1) * 128], oi[:, j, kc * 128:(kc + 1) * 128], ident[:, :], (actT, ident), (tb,))
                            src = tb[:, :].rearrange("p (a b) -> p a b", a=8)
                            if j % 2 == 0:
                                k.act(hnT[:, :, j * 128:(j + 1) * 128], src, AF.Copy, (tb,), (hnT,))
                            else:
                                k.cp("dve", hnT[:, :, j * 128:(j + 1) * 128], src, (tb,), (hnT,))
                        for j in range(4):
                            for hf in range(2):
                                pb = banks[bi % 6]
                                bi += 1
                                for kc in range(8):
                                    k.mm(pb[:, :], hnT[:, kc, j * 128:(j + 1) * 128], wo[:, kc, hf * 512:(hf + 1) * 512], kc == 0, kc == 7,
                                         (hnT, wo), (pb,))
                                k.tt("dve", xin[:, j, hf * 512:(hf + 1) * 512], pb[:, :], xin[:, j, hf * 512:(hf + 1) * 512], ALU.add, (pb, xin), (xin,))
                    norm_transpose(k, xin, hnb, hnT, ss, rstd, mhalf, ident, tbanks)
                    bi = ffn_chunk(k, hnT, actT, wgu, wdn, banks, bi, tg, tA)
                    bi = ffn_down(k, actT, wdn, xin, banks, bi)
                    if final:
                        for j in range(4):
                            hb_ = hnb[j % 2]
                            k.act(hb_[:, :], xin[:, j, :], AF.Square, (xin,), (hb_, ss), accum=ss[:, j:j + 1])
                        k.ts("dve", rstd[:, 0:4], ss[:, 0:4], 1.0 / D, EPS, ALU.mult, ALU.add, (ss,), (rstd,))
                        k.tt("pool", rstd[:, 0:4], rstd[:, 0:4], mhalf[:, 0:4], ALU.pow, (rstd, mhalf), (rstd,))
                        for j in range(4):
                            k.stt(xin[:, j, :], xin[:, j, :], rstd[:, j:j + 1], gfin[:, :], ALU.mult, ALU.mult, (xin, rstd, gfin), (xin,))
                    k.dma(tok_view(dst_d, t0), xin[:, :, :], (xin,), (), xin)
                P.flush()

        if "B" in phases:
            phase_ffn(0, hmid_d, h1_d, False, False)

        if "C" in phases:
            with contextlib.ExitStack() as st:
                wkc = k.sb(st, [128, 8, NG * 96], BF16, "wkc")
                wvc = k.sb(st, [128, 8, NG * 64], BF16, "wvc")
                wks = k.sb(st, [128, 8, NG * 128], BF16, "wks")
                wkw = k.sb(st, [128, 8, NG * 128], BF16, "wkw")
                wv = k.sb(st, [128, 8, 512], BF16, "wv")
                wq = k.sb(st, [128, 8, NH * 128], BF16, "wq")
                wqg = k.sb(st, [128, 8, 48], BF16, "wqg")
                gbb = k.sb(st, [1, 48], BF16, "gbb")
                ones1 = k.sb(st, [1, 128], BF16, "ones1")
                pm = k.sb(st, [128, 64], BF16, "pm")
                rc = k.sb(st, [128, S], F32, "rc")
                rsn = k.sb(st, [128, S], F32, "rsn")
                with contextlib.ExitStack() as st2:
                    stages = [k.sb(st2, [128, 2048], F32, "stg") for _ in range(3)]
                    for kc in range(8):
                        for (wt, wd, nco, gi) in ((wkc, wkc_d, NG * 96, 2), (wvc, wvc_d, NG * 64, 2), (wks, wks_d, NG * 128, 2),
                                                  (wkw, wkw_d, NG * 128, 2), (wv, wv_d, 512, 2), (wq, wq_d, NH * 128, 3), (wqg, wqg_d, 48, 3)):
                            k.wload(stages, lambda a, b, wt=wt, kc=kc: wt[:, kc, a:b], lambda a, b, wd=wd, kc=kc, nco=nco: wd[:, kc * nco + a:kc * nco + b],
                                    nco, wt, piece=2048, gain=gall[:, gi, kc:kc + 1])
                    k.wload(stages, lambda a, b: pm[:, a:b], lambda a, b: pm_d[:, a:b], 64, pm)
                    k.wload(stages, lambda a, b: gbb[:, a:b], lambda a, b: gb_d[:, a:b], 48, gbb, npart=1)
                    k.memset("dve", ones1[:, :], 1.0, (ones1,))
                    k.dma(rc[:, :], ropec_d, (), (rc,), rc)
                    k.dma(rsn[:, :], ropes_d, (), (rsn,), rsn)
                    P.flush()
                xins = [k.sb(st, [128, 4, D], F32, "xin") for _ in range(2)]
                hnb = [k.sb(st, [128, D], BF16, "hnb") for _ in range(2)]
                hnT = k.sb(st, [128, 8, T], BF16, "hnT")
                ss = k.sb(st, [128, 4], F32)
                rstd = k.sb(st, [128, 4], F32)
                o_kc = [k.sb(st, [96, NG, T], BF16) for _ in range(1)] * 2
                o_vc = [k.sb(st, [64, NG, T], BF16) for _ in range(1)] * 2
                o_ks = [k.sb(st, [128, NG, T], BF16) for _ in range(1)] * 2
                o_kw = [k.sb(st, [128, NG, T], BF16) for _ in range(1)] * 2
                o_v = [k.sb(st, [128, 4, 520], BF16) for _ in range(2)]
                for ov_ in o_v:
                    k.memset("pool", ov_[:, :, :], 1.0, (ov_,))
                o_qn = [k.sb(st, [128, NH, T], BF16) for _ in range(1)] * 2
                o_qr = [k.sb(st, [64, NH, T], BF16) for _ in range(1)] * 2
                o_gt = [k.sb(st, [128, 4, 48], F32) for _ in range(2)]
                r1 = [k.sb(st, [64, T], F32) for _ in range(2)]
                r2 = [k.sb(st, [64, T], F32) for _ in range(2)]
                banks = [k.ps(st, [128, 512], F32, "bk") for _ in range(6)]
                tbanks = [k.ps(st, [128, 1024], BF16, "tb") for _ in range(2)]
                bi = 0
                ri = 0
                k.dma(xins[0][:, :, :], tok_view(h1_d, 0), (), (xins[0],), xins[0])
                for ci in range(NT // T):
                    t0 = ci * T
                    b = t0 // S
                    p0 = t0 % S
                    xin = xins[ci % 2]
                    s_ = ci % 2
                    if ci + 1 < NT // T:
                        k.dma(xins[(ci + 1) % 2][:, :, :], tok_view(h1_d, t0 + T), (), (xins[(ci + 1) % 2],), xins[(ci + 1) % 2])
                    norm_transpose(k, xin, hnb, hnT, ss, rstd, mhalf, ident, tbanks)

                    def proj(wt, c0, m):
                        nonlocal bi
                        pb = banks[bi % 6]
                        bi += 1
                        for kc in range(8):
                            k.mm(pb[0:m, :], wt[:, kc, c0:c0 + m], hnT[:, kc, :], kc == 0, kc == 7, (wt, hnT), (pb,))
                        return pb

                    def rope(dst, dstr, g, scale):
                        nonlocal bi, ri
                        pb = banks[bi % 6]
                        bi += 1
                        k.mm(pb[0:64, :], pm[:, :], dst[:, g, :], True, True, (pm, dst), (pb,))
                        a = r1[ri % 2]
                        bb = r2[ri % 2]
                        ri += 1
                        k.tt("dve", a[32:64, :], pb[32:64, :], rsn[32:64, p0:p0 + T], ALU.mult, (pb, rsn), (a,))
                        k.tt("pool", bb[32:64, :], dst[32:64, g, :], rc[32:64, p0:p0 + T], ALU.mult, (dst, rc), (bb,))
                        return a, bb

                    for g in range(NG):
                        pb = proj(wkc, g * 96, 96)
                        k.act(o_kc[s_][:, g, :], pb[0:96, :], AF.Copy, (pb,), (o_kc[s_],))
                        pb = proj(wvc, g * 64, 64)
                        k.cp("dve", o_vc[s_][:, g, :], pb[0:64, :], (pb,), (o_vc[s_],))
                    for (wt, ot) in ((wks, o_ks[s_]), (wkw, o_kw[s_])):
                        for g in range(NG):
                            pb = proj(wt, g * 128, 128)
                            k.act(ot[:, g, :], pb[:, :], AF.Copy, (pb,), (ot,))
                            a, bb = rope(ot, ot, g, 1.0)
                            k.tt("dve", ot[32:64, g, :], a[32:64, :], bb[32:64, :], ALU.add, (a, bb), (ot,))
                    for j in range(4):
                        pb = banks[bi % 6]
                        bi += 1
                        for kc in range(8):
                            k.mm(pb[:, :], hnT[:, kc, j * 128:(j + 1) * 128], wv[:, kc, :], kc == 0, kc == 7, (wv, hnT), (pb,))
                        k.cp("dve", o_v[s_][:, j, :].rearrange("p (a c) -> p a c", c=65)[:, :, 0:64], pb[:, :].rearrange("p (a c) -> p a c", c=64),
                             (pb,), (o_v[s_],))
                        pb = banks[bi % 6]
                        bi += 1
                        for kc in range(8):
                            k.mm(pb[:, 0:48], hnT[:, kc, j * 128:(j + 1) * 128], wqg[:, kc, :], kc == 0, False, (wqg, hnT), (pb,))
                        k.mm(pb[:, 0:48], ones1[:, :], gbb[:, :], False, True, (ones1, gbb), (pb,))
                        k.act(o_gt[s_][:, j, :], pb[:, 0:48], AF.Tanh, (pb,), (o_gt[s_],), scale=0.5)
                        k.ts("dve", o_gt[s_][:, j, :], o_gt[s_][:, j, :], 0.5, 0.5, ALU.mult, ALU.add, (o_gt[s_],), (o_gt[s_],))
                    for h in range(NH):
                        pb = proj(wq, h * 128, 128)
                        k.act(o_qn[s_][:, h, :], pb[:, :], AF.Copy, (pb,), (o_qn[s_],), scale=float(DQK ** -0.5))
                        a, bb = rope(o_qn[s_], o_qr[s_], h, 1.0)
                        k.tt("dve", o_qr[s_][32:64, h, :], a[32:64, :], bb[32:64, :], ALU.add, (a, bb), (o_qr[s_],))
                    k.dma(kcmp_d[b, :, :, p0:p0 + T], o_kc[s_][:, :, :], (o_kc[s_],), (), o_kc[s_])
                    k.dma(vcmp_d[b, :, :, p0:p0 + T], o_vc[s_][:, :, :], (o_vc[s_],), (), o_vc[s_])
                    k.dma(ksel_d[b, :, :, p0:p0 + T], o_ks[s_][32:128, :, :], (o_ks[s_],), (), o_ks[s_])
                    k.dma(kwin_d[b, :, :, p0:p0 + T], o_kw[s_][32:128, :, :], (o_kw[s_],), (), o_kw[s_])
                    k.dma(v_d[t0:t0 + T, :].rearrange("(j p) d -> p j d", p=128), o_v[s_][:, :, :], (o_v[s_],), (), o_v[s_])
                    k.dma(qn_d[b, :, :, p0:p0 + T], o_qn[s_][32:128, :, :], (o_qn[s_],), (), o_qn[s_])
                    k.dma(qr_d[b, :, :, p0:p0 + T], o_qr[s_][32:64, :, :], (o_qr[s_],), (), o_qr[s_])
                    k.dma(gt_d[t0:t0 + T, :].rearrange("(j p) d -> p j d", p=128), o_gt[s_][:, :, :], (o_gt[s_],), (), o_gt[s_])
                P.flush()

        if "D" in phases:
            with contextlib.ExitStack() as st:
                w1k = k.sb(st, [96, 32, 256], BF16, "w1k")
                w1v = k.sb(st, [64, 32, 256], BF16, "w1v")
                w2k = k.sb(st, [128, 2, 128], BF16, "w2k")
                w2v = k.sb(st, [128, 2, 64], BF16, "w2v")
                posk = k.sb(st, [96, 32], BF16, "posk")
                posv = k.sb(st, [64, 32], BF16, "posv")
                pbias = k.sb(st, [128, 4], F32, "pbias")
                tri = k.sb(st, [128, 256], BF16, "tri")
                maskc = k.sb(st, [128, S], BF16, "maskc")
                tkadd = k.sb(st, [128, 16, 32], F32, "tkadd")
                tkmul = k.sb(st, [128, 16, 32], F32, "tkmul")
                vcaug = k.sb(st, [128, NG, 97], BF16, "vcaug")
                kcT = k.sb(st, [128, NG, 128], BF16, "kcT")
                ksT = k.sb(st, [128, NG, S], BF16, "ksT")
                kwT = k.sb(st, [128, NG, S], BF16, "kwT")
                with contextlib.ExitStack() as st2:
                    stages = [k.sb(st2, [128, 2048], F32, "stg") for _ in range(3)]
                    for l0 in range(0, 32, 8):
                        k.wload(stages, lambda a, b, l0=l0: w1k[:, l0:l0 + 8, :].rearrange("p l h -> p (l h)")[:, a:b],
                                lambda a, b, l0=l0: w1k_d[:, l0 * 256 + a:l0 * 256 + b], 2048, w1k, npart=96)
                        k.wload(stages, lambda a, b, l0=l0: w1v[:, l0:l0 + 8, :].rearrange("p l h -> p (l h)")[:, a:b],
                                lambda a, b, l0=l0: w1v_d[:, l0 * 256 + a:l0 * 256 + b], 2048, w1v, npart=64)
                    k.wload(stages, lambda a, b: w2k[:, :, :].rearrange("p l h -> p (l h)")[:, a:b], lambda a, b: w2k_d[:, a:b], 256, w2k)
                    k.wload(stages, lambda a, b: w2v[:, :, :].rearrange("p l h -> p (l h)")[:, a:b], lambda a, b: w2v_d[:, a:b], 128, w2v)
                    k.wload(stages, lambda a, b: posk[:, a:b], lambda a, b: posk_d[:, a:b], 32, posk, npart=96)
                    k.wload(stages, lambda a, b: posv[:, a:b], lambda a, b: posv_d[:, a:b], 32, posv, npart=64)
                    k.wload(stages, lambda a, b: tri[:, a:b], lambda a, b: tri_d[:, a:b], 256, tri)
                    k.wload(stages, lambda a, b: maskc[:, a:b], lambda a, b: maskc_d[:, a:b], S, maskc)
                    k.dma(tkadd[:, :, :], tkadd_d.rearrange("(j p) n -> p j n", p=128), (), (tkadd,), tkadd)
                    k.dma(tkmul[:, :, :], tkmul_d.rearrange("(j p) n -> p j n", p=128), (), (tkmul,), tkmul)
                    k.memset("pool", vcaug[:, :, 0:64], 0.0, (vcaug,))
                    for g in range(NG):
                        k.wload(stages, lambda a, b, g=g: vcaug[:, g, 64 + a:64 + b], lambda a, b: ovl_d[:, a:b], 33, vcaug)
                        k.wload(stages, lambda a, b, g=g: ksT[0:32, g, a:b], lambda a, b: esel_d[:, a:b], S, ksT, npart=32)
                    k.memset("pool", kwT[0:32, :, :], 0.0, (kwT,))
                    k.memset("dve", kcT[:, :, :], 0.0, (kcT,))
                    pbk = k.ps(st2, [128, 512], F32)
                    for (w1, pos, col) in ((w1k, posk, 0), (w1v, posv, 2)):
                        for hh in range(2):
                            for l in range(32):
                                k.mm(pbk[:, col + hh:col + hh + 1], w1[:, l, hh * 128:(hh + 1) * 128], pos[:, l:l + 1], l == 0, l == 31,
                                     (w1, pos), (pbk,), skip=True)
                    k.cp("dve", pbias[:, :], pbk[:, 0:4], (pbk,), (pbias,))
                    P.flush()

                kcmp = k.sb(st, [96, NG, S], BF16, "kcmp")
                vcmp = k.sb(st, [64, NG, S], BF16, "vcmp")
                vaug = k.sb(st, [128, 16, 2, NG, 65], BF16, "vaug")
                gts = k.sb(st, [128, 16, 48], F32, "gts")
                qnt = k.sb(st, [128, 4, S], BF16, "qn")
                qat = k.sb(st, [128, 4, S], BF16, "qa")
                oacc = k.sb(st, [128, 16, 256], F32, "oacc")
                obf = k.sb(st, [128, 16, 256], BF16, "obf")
                hid = [k.sb(st, [128, 2, 512], BF16) for _ in range(2)]
                gx = k.sb(st, [128, 512], F32)
                gs = k.sb(st, [128, 512], F32)
                gt_ = k.sb(st, [128, 512], BF16)
                em = [k.sb(st, [128, 512], BF16) for _ in range(2)]
                pmk = [k.sb(st, [128, 512], BF16) for _ in range(4)]
                e2b = [k.sb(st, [128, 512], BF16) for _ in range(4)]
                qa_r = [Res("qa%d" % i) for i in range(4)]
                oa_r = [Res("oa%d" % i) for i in range(4)]
                rden = [k.sb(st, [128, 4], F32) for _ in range(3)]
                fac = [k.sb(st, [128, 4], F32) for _ in range(3)]
                tmp = [k.sb(st, [128, 4, 64], F32) for _ in range(3)]
                impn = k.sb(st, [128, 4, 32], F32)
                imp = k.sb(st, [128, 32], F32)
                top8 = k.sb(st, [128, 8], F32)
                selb = k.sb(st, [128, 128], BF16)
                sbanks = [k.ps(st, [128, 512], F32, "sb") for _ in range(3)]
                abanks = [k.ps(st, [128, 512], F32, "ab") for _ in range(4)]
                tbank = k.ps(st, [128, 1024], BF16, "tb")
                k.memset("dve", qnt[0:32, :, :], 0.0, (qnt,))
                k.memset("dve", selb[:, :], 0.0, (selb,))
                cnt = {"s": 0, "a": 0, "p": 0, "e": 0, "n": 0}

                def nxt(lst, key):
                    cnt[key] += 1
                    return lst[cnt[key] % len(lst)]

                def rcp(out, in_, rd, wr):
                    P.op("dve", lambda e: e.reciprocal(out, in_), _rs(rd), _rs(wr))

                for b in range(nseq):
                    k.dma(kcmp[:, :, :], kcmp_d[b], (), (kcmp,), kcmp)
                    k.dma(vcmp[:, :, :], vcmp_d[b], (), (vcmp,), vcmp)
                    k.dma(ksT[32:128, :, :], ksel_d[b], (), (ksT,), ksT)
                    k.dma(kwT[32:128, :, :], kwin_d[b], (), (kwT,), kwT)
                    k.dma(vaug[:, :, :, :, :].rearrange("p j b g c -> p j (b g c)"), v_d[b * S:(b + 1) * S, :].rearrange("(j p) c -> p j c", p=128),
                          (), (vaug,), vaug)
                    k.dma(gts[:, :, :], gt_d[b * S:(b + 1) * S, :].rearrange("(j p) n -> p j n", p=128), (), (gts,), gts)
                    for (which, raw, w1) in ((0, kcmp, w1k), (1, vcmp, w1v)):
                        for hh in range(2):
                            pb = nxt(sbanks, "s")
                            for l in range(32):
                                k.mm(pb[:, 0:NG * NCMP].rearrange("p (g c) -> p g c", g=NG), w1[:, l, hh * 128:(hh + 1) * 128],
                                     raw[:, :, l:l + 16 * (NCMP - 1) + 1:16], l == 0, l == 31, (w1, raw), (pb,))
                            bcol = pbias[:, which * 2 + hh:which * 2 + hh + 1]
                            k.act(gx[:, 0:508], pb[:, 0:508], AF.Identity, (pb, pbias), (gx,), bias=bcol)
                            k.act(gs[:, 0:508], pb[:, 0:508], AF.Square, (pb, pbias), (gs,), bias=bcol)
                            k.ts("dve", gs[:, 0:508], gs[:, 0:508], GC, 1.0, ALU.mult, ALU.add, (gs,), (gs,))
                            k.tt("dve", gs[:, 0:508], gs[:, 0:508], gx[:, 0:508], ALU.mult, (gs, gx), (gs,))
                            k.act(gt_[:, 0:508], gs[:, 0:508], AF.Tanh, (gs,), (gt_,), scale=GK)
                            k.stt(gs[:, 0:508], gt_[:, 0:508], 1.0, gx[:, 0:508], ALU.add, ALU.mult, (gt_, gx), (gs,))
                            k.ts("dve", hid[which][:, hh, 0:508], gs[:, 0:508], 0.5, 0.0, ALU.mult, ALU.add, (gs,), (hid[which],))
                    for g in range(NG):
                        pb = nxt(abanks, "a")
                        for hh in range(2):
                            k.mm(pb[:, 0:NCMP], w2k[:, hh, :], hid[0][:, hh, g * NCMP:(g + 1) * NCMP], hh == 0, hh == 1, (w2k, hid[0]), (pb,))
                        k.cp("dve", kcT[:, g, 0:NCMP], pb[:, 0:NCMP], (pb,), (kcT,))
                        pb = nxt(abanks, "a")
                        for hh in range(2):
                            k.mm(pb[0:NCMP, 0:64], hid[1][:, hh, g * NCMP:(g + 1) * NCMP], w2v[:, hh, :], hh == 0, hh == 1, (w2v, hid[1]), (pb,))
                        k.cp("dve", vcaug[0:NCMP, g, 0:64], pb[0:NCMP, 0:64], (pb,), (vcaug,))
                    for g in range(NG):
                        k.dma(qnt[32:128, :, :], qn_d[b, :, 4 * g:4 * g + 4, :], (), (qnt,), qnt)
                        k.dma(qat[64:128, :, :], qn_d[b, 32:96, 4 * g:4 * g + 4, :], (), tuple(qa_r), qat)
                        k.dma(qat[32:64, :, :], qr_d[b, :, 4 * g:4 * g + 4, :], (), tuple(qa_r), qat)
                        def d2_setup(qc, g=g):
                            for h in range(4):
                                pb = nxt(sbanks, "s")
                                k.mm(pb[0:NCMP, :], kcT[:, g, 0:NCMP], qnt[:, h, qc * T:(qc + 1) * T], True, True, (kcT, qnt), (pb,))
                                e1 = nxt(em, "e")
                                e2 = e2b[h]
                                k.act(e1[0:NCMP, :], pb[0:NCMP, :], AF.Exp, (pb,), (e1,))
                                k.tt("pool", e2[0:NCMP, :], e1[0:NCMP, :], maskc[0:NCMP, qc * T:(qc + 1) * T], ALU.mult, (e1, maskc), (e2,))

                        def d2_qtile(qc, j, g=g):
                            qt = qc * 4 + j
                            pa = nxt(abanks, "a")
                            for h in range(4):
                                k.mm(pa[:, h * 97:(h + 1) * 97], e2b[h][0:NCMP, j * 128:(j + 1) * 128], vcaug[0:NCMP, g, 0:97], h == 0, h == 3,
                                     (e2b[h], vcaug), (pa,), skip=True)
                            p3 = pa[:, 0:388].rearrange("p (h c) -> p h c", h=4)
                            rd_ = nxt(rden, "n")
                            fc_ = fac[cnt["n"] % 3]
                            k.ts("dve", rd_[:, :], p3[:, :, 96], 1e-30, 0.0, ALU.max, ALU.add, (pa,), (rd_,))
                            rcp(rd_[:, :], rd_[:, :], (rd_,), (rd_,))
                            k.tt("dve", impn[:, :, :], p3[:, :, 64:96], rd_[:, :].unsqueeze(2).to_broadcast([128, 4, 32]), ALU.mult, (pa, rd_), (impn,))
                            P.op("dve", lambda e, o=imp[:, :], i=impn[:, :, :].rearrange("p h n -> p n h"): e.tensor_reduce(o, i, axis=AX.X, op=ALU.add),
                                 _rs((impn,)), _rs((imp,)))
                            k.tt("dve", imp[:, :], imp[:, :], tkmul[:, qt, :], ALU.mult, (imp, tkmul), (imp,))
                            k.tt("dve", imp[:, :], imp[:, :], tkadd[:, qt, :], ALU.add, (imp, tkadd), (imp,))
                            P.op("dve", lambda e, o=top8[:, :], i=imp[:, :]: e.max(o, i), _rs((imp,)), _rs((top8,)))
                            k.ts("dve", imp[:, :], imp[:, :], top8[:, 7:8], 1.0, ALU.is_ge, ALU.subtract, (imp, top8), (imp,))
                            k.ts("dve", selb[:, 0:32], imp[:, :], -NEG, 0.0, ALU.mult, ALU.add, (imp,), (selb,))
                            k.tr(tbank[:, 0:128], selb[:, :], ident[:, :], (selb, ident), (tbank,))
                            k.cp("dve", qat[0:32, :, qt * 128:(qt + 1) * 128], tbank[0:32, 0:128].unsqueeze(1).to_broadcast([32, 4, 128]), (tbank,), (qa_r[qc],))
                            k.tt("dve", fc_[:, :], rd_[:, :], gts[:, qt, 12 * g:12 * g + 12:3], ALU.mult, (rd_, gts), (fc_,))
                            k.tt("dve", oacc[:, qt, :].rearrange("p (h v) -> p h v", h=4), p3[:, :, 0:64],
                                 fc_[:, :].unsqueeze(2).to_broadcast([128, 4, 64]), ALU.mult, (pa, fc_), (oa_r[qc],))

                        d2_setup(0)
                        for j in range(4):
                            d2_qtile(0, j)
                        tiles = []
                        for qc in range(4):
                            for h in range(4):
                                pre = []
                                if qc < 3:
                                    if h == 0:
                                        pre.append(lambda qc=qc: d2_setup(qc + 1))
                                    pre.append(lambda qc=qc, h=h: d2_qtile(qc + 1, h))
                                nkt = 4 * qc + 4
                                lst = []
                                for kt in range(nkt):
                                    q0 = max(qc * T, kt * 128)
                                    ncol = (qc + 1) * T - q0
                                    masks = [(0, 0)] if kt * 128 >= qc * T else []
                                    pv = [(qt - 4 * qc, qt * 128 - q0) for qt in range(q0 // 128, 4 * qc + 4)]
                                    lst.append(dict(kT=ksT, br=0, kt=kt, q0=q0, ncol=ncol, masks=masks, pv=pv))
                                tiles.append(dict(h=h, qc=qc, br=1, lst=lst, pre=pre))
                                lst = []
                                for kt in range(max(0, 4 * qc - 4), 4 * qc + 4):
                                    qlo = max(kt, 4 * qc)
                                    qhi = min(kt + 4, 4 * qc + 3)
                                    q0 = qlo * 128
                                    ncol = (qhi - qlo + 1) * 128
                                    masks = []
                                    if qlo == kt:
                                        masks.append((0, 0))
                                    if qhi == kt + 4:
                                        masks.append((ncol - 128, 128))
                                    pv = [(qt - 4 * qc, (qt - qlo) * 128) for qt in range(qlo, qhi + 1)]
                                    lst.append(dict(kT=kwT, br=1, kt=kt, q0=q0, ncol=ncol, masks=masks, pv=pv))
                                tiles.append(dict(h=h, qc=qc, br=2, lst=lst, pre=[]))
                        flat = []
                        for tg_ in tiles:
                            po = None
                            for i_, t_ in enumerate(tg_["lst"]):
                                flat.append((tg_, t_, i_ == 0, i_ == len(tg_["lst"]) - 1))
                        LOOK = 2
                        pend = []
                        for idx in range(len(flat) + LOOK):
                            if idx < len(flat):
                                tg_, t_, isf, isl = flat[idx]
                                h = tg_["h"]
                                if isf:
                                    for th_ in tg_["pre"]:
                                        th_()
                                    tg_["po"] = nxt(abanks, "a")
                                ps_ = nxt(sbanks, "s")
                                ncol = t_["ncol"]
                                k.mm(ps_[:, 0:ncol], t_["kT"][:, g, t_["kt"] * 128:(t_["kt"] + 1) * 128], qat[:, h, t_["q0"]:t_["q0"] + ncol], True, True,
                                     (t_["kT"], qa_r[tg_["qc"]]), (ps_,))
                                pb_ = nxt(pmk, "p")
                                k.act(pb_[:, 0:ncol], ps_[:, 0:ncol], AF.Exp, (ps_,), (pb_,))
                                for (c0, m0) in t_["masks"]:
                                    k.tt("pool", pb_[:, c0:c0 + 128], pb_[:, c0:c0 + 128], tri[:, m0:m0 + 128], ALU.mult, (pb_, tri), (pb_,))
                                t_["pb"] = pb_
                            if idx >= LOOK:
                                tg_, t_, isf, isl = flat[idx - LOOK]
                                h, qc, po = tg_["h"], tg_["qc"], tg_["po"]
                                pb_ = t_["pb"]
                                for pi_, (j, off) in enumerate(t_["pv"]):
                                    k.mm(po[:, j * 65:(j + 1) * 65], pb_[:, off:off + 128], vaug[:, t_["kt"], t_["br"], g, :], isf and pi_ == 0, isl,
                                         (pb_, vaug), (po,), skip=True)
                                if isl:
                                    br = tg_["br"]
                                    p3 = po[:, 0:260].rearrange("p (j c) -> p j c", j=4)
                                    rd_ = nxt(rden, "n")
                                    fc_ = fac[cnt["n"] % 3]
                                    tm_ = tmp[cnt["n"] % 3]
                                    rcp(rd_[:, :], p3[:, :, 64], (po,), (rd_,))
                                    gcol = (4 * g + h) * 3 + br
                                    k.tt("dve", fc_[:, :], rd_[:, :], gts[:, 4 * qc:4 * qc + 4, gcol], ALU.mult, (rd_, gts), (fc_,))
                                    k.tt("dve", tm_[:, :, :], p3[:, :, 0:64], fc_[:, :].unsqueeze(2).to_broadcast([128, 4, 64]), ALU.mult, (po, fc_), (tm_,))
                                    k.tt("pool", oacc[:, 4 * qc:4 * qc + 4, h * 64:(h + 1) * 64], oacc[:, 4 * qc:4 * qc + 4, h * 64:(h + 1) * 64],
                                         tm_[:, :, :], ALU.add, (oa_r[qc], tm_), (oa_r[qc],))
                        k.act(obf[:, :, :], oacc[:, :, :], AF.Copy, tuple(oa_r), (obf,))
                        k.dma(o_d[b * S:(b + 1) * S, g * 256:(g + 1) * 256].rearrange("(j p) v -> p j v", p=128), obf[:, :, :], (obf,), (), obf)
                P.flush()
        if "E" in phases:
            phase_ffn(1, h1_d, y_d, True, True)
    return nc


def _consts():
    c = {}
    c["c_ident"] = np.eye(128, dtype=np.float32)
    kk = np.arange(128)[:, None]
    qq = np.arange(128)[None, :]
    c["c_tri"] = np.concatenate([(kk <= qq), (kk > qq)], axis=1).astype(np.float32)
    cc = np.arange(128)[:, None]
    tt = np.arange(S)[None, :]
    c["c_maskc"] = ((16 * cc + 31 <= tt) & (cc < NCMP)).astype(np.float32)
    half = 12
    inv = (np.float32(500000.0) ** (-np.arange(half, dtype=np.float32) * np.float32(2.0) / np.float32(24))).astype(np.float32)
    ang = (np.arange(S, dtype=np.float32)[None, :] * inv[:, None]).astype(np.float32)
    cs = np.cos(ang.astype(np.float64)).astype(np.float32)
    sn = np.sin(ang.astype(np.float64)).astype(np.float32)
    rc = np.ones((128, S), np.float32)
    rs = np.zeros((128, S), np.float32)
    rc[32:44] = cs
    rc[44:56] = cs
    rs[32:44] = -sn
    rs[44:56] = sn
    c["c_ropec"] = rc
    c["c_ropes"] = rs
    pm = np.zeros((128, 64), np.float32)
    for d in range(12):
        pm[32 + d + 12, 32 + d] = 1.0
        pm[32 + d, 32 + d + 12] = 1.0
    c["c_pm"] = pm
    c["c_esel"] = (np.arange(S)[None, :] // 64 == np.arange(32)[:, None]).astype(np.float32)
    ncmp = NCMP
    cs_ = np.arange(ncmp) * 16
    ss_ = np.arange(32) * 64
    ovl = ((cs_[:, None] < ss_[None, :] + 64) & (cs_[:, None] + 32 > ss_[None, :])).astype(np.float32)
    o = np.zeros((128, 33), np.float32)
    o[:ncmp, :32] = ovl
    o[:ncmp, 32] = 1.0
    c["c_ovl"] = o
    t = np.arange(S)[:, None]
    n = np.arange(32)[None, :]
    cur = t // 64
    forced = (n == 0) | (n == cur) | (n == cur - 1)
    vis = (n * 64 <= t)
    c["c_tkadd"] = np.where(forced, 1e9, np.where(vis, 0.0, -1e9)).astype(np.float32)
    c["c_tkmul"] = (vis & ~forced).astype(np.float32)
    return c


def _kc(w):
    K_, M = w.shape
    return np.ascontiguousarray(w.reshape(K_ // 128, 128, M).transpose(1, 0, 2).reshape(128, -1))


def _prep_weights(inp):
    f = lambda a: np.ascontiguousarray(np.asarray(a, dtype=np.float32))
    w = {}
    ng = inp["norm_g"]
    gs = np.stack([ng[0, 0], ng[0, 1], inp["kv_norm_g"], ng[1, 0], ng[1, 1]], 0)
    w["g_all"] = f(gs.reshape(5, 8, 128).transpose(2, 0, 1))
    w["g_fin"] = f(inp["final_g"].reshape(1, D))
    w["w_in"] = _kc(f(inp["a_w_in"][0]))
    w["conv_w"] = f(inp["a_conv_w"][0].reshape(4, NBLK, RB).transpose(2, 1, 0).reshape(RB, NBLK * 4))
    sm = np.stack([inp["a_conv_b"][0].reshape(NBLK, RB).T, inp["a_b_ra"][0].T, inp["a_b_ix"][0].T,
                   inp["a_lambda"][0].reshape(NBLK, RB).T], 1)
    w["a_small"] = f(sm)
    w["w_ra"] = f(inp["a_w_ra"][0].transpose(1, 0, 2).reshape(RB, NBLK * RB))
    w["w_ix"] = f(inp["a_w_ix"][0].transpose(1, 0, 2).reshape(RB, NBLK * RB))
    w["w_out"] = f(inp["a_w_out"][0].reshape(NBLK, RB, D).transpose(1, 0, 2).reshape(RB, NBLK * D))
    for l in range(2):
        w["w_gu%d" % l] = _kc(f(inp["ffn_w_gu"][l]))
        w["w_dn%d" % l] = f(inp["ffn_w_down"][l].reshape(NFC, 128, D).transpose(1, 0, 2).reshape(128, NFC * D))
    kv = f(inp["kv_w"])
    kcmp_, vcmp_, ksel_, vsel_, kwin_, vwin_ = kv[:, 0:384], kv[:, 384:640], kv[:, 640:1024], kv[:, 1024:1280], kv[:, 1280:1664], kv[:, 1664:1920]
    w["w_kc"] = _kc(kcmp_)
    w["w_vc"] = _kc(vcmp_)

    def aug(m, n):
        z = np.zeros((m.shape[0], n, 128), np.float32)
        z[:, :, 32:] = m.reshape(m.shape[0], n, 96)
        return z.reshape(m.shape[0], n * 128)
    w["w_ks"] = _kc(aug(ksel_, NG))
    w["w_kw"] = _kc(aug(kwin_, NG))
    w["w_v"] = _kc(np.concatenate([vsel_, vwin_], 1))
    wq = f(inp["b_w_q"][0])
    w["w_q"] = _kc(aug(wq[:, :NH * DQK], NH))
    w["w_qg"] = _kc(wq[:, NH * DQK:])
    w["gate_b"] = f(inp["b_gate_bias"][0].reshape(1, 48))
    w["w_o"] = _kc(f(inp["b_w_o"][0]))
    w["w1k"] = f(inp["cmp_w1_k"].reshape(32, 96, 256).transpose(1, 0, 2).reshape(96, 32 * 256))
    w["w1v"] = f(inp["cmp_w1_v"].reshape(32, 64, 256).transpose(1, 0, 2).reshape(64, 32 * 256))
    w2k = np.zeros((256, 128), np.float32)
    w2k[:, 32:] = inp["cmp_w2_k"]
    w["w2k"] = f(w2k.reshape(2, 128, 128).transpose(1, 0, 2).reshape(128, 256))
    w["w2v"] = f(np.asarray(inp["cmp_w2_v"], np.float32).reshape(2, 128, 64).transpose(1, 0, 2).reshape(128, 128))
    w["posk"] = f(np.asarray(inp["cmp_pos_k"]).T)
    w["posv"] = f(np.asarray(inp["cmp_pos_v"]).T)
    return w


_CACHE = {}


def kernel(**inputs):
    inp = {k_: np.asarray(v) for k_, v in inputs.items()}
    ncores = 8
    nseq = inp["x"].shape[0] // ncores
    if "nc" not in _CACHE:
        _CACHE["nc"] = build_program(nseq)
    nc = _CACHE["nc"]
    shared = _consts()
    shared.update(_prep_weights(inp))
    x = np.ascontiguousarray(inp["x"], dtype=np.float32).reshape(ncores, nseq * S, D)
    in_maps = []
    for c in range(ncores):
        m = dict(shared)
        m["x"] = x[c]
        in_maps.append(m)
    res = run_bass_kernel_spmd(nc, in_maps, core_ids=list(range(ncores)))
    out = np.stack([np.asarray(r["y"]) for r in res.results], 0)
    return out.reshape(inp["x"].shape).astype(np.float32)
```

```python
import contextlib
import os
import numpy as np
import concourse.bass as bass
import concourse.mybir as mybir
from concourse.bass_utils import run_bass_kernel_spmd

F32 = mybir.dt.float32
BF16 = mybir.dt.bfloat16
AF = mybir.ActivationFunctionType
ALU = mybir.AluOpType
AX = mybir.AxisListType

D = 1024
S = 2048
T = 512
DRNN = 1536
NBLK = 16
RB = 96
DFF = 2816
NFC = 22
NH = 16
NG = 4
DQK = 96
DV = 64
NCMP = 127
EPS = 1e-6
NEG = -30000.0
GK = 0.7978845608028654
GC = 0.044715


class Res:
    __slots__ = ("name", "last_w", "readers")

    def __init__(self, name):
        self.name = name
        self.last_w = None
        self.readers = []


class Op:
    __slots__ = ("eng", "fn", "waits", "signal", "is_dma", "key", "pos", "sigidx", "is_nop")

    def __init__(self, eng, fn, is_dma=False, key=None):
        self.eng = eng
        self.fn = fn
        self.waits = []
        self.signal = False
        self.is_dma = is_dma
        self.key = key
        self.pos = -1
        self.sigidx = -1
        self.is_nop = False


ENGS = ("pe", "act", "dve", "pool", "sp")


class Prog:
    def __init__(self, nc, stack):
        self.nc = nc
        self.stack = stack
        self.streams = {e: [] for e in ENGS}
        self.emitted = {e: 0 for e in ENGS}
        self.nsig = {e: 0 for e in ENGS}
        self.npos = {e: 0 for e in ENGS}
        self.known = {e: {} for e in ENGS}
        self.dma_cnt = {}
        self.esem = {e: stack.enter_context(nc.semaphore("s_" + e)) for e in ENGS}
        self.ksem = {}

    def res(self, name="r"):
        return Res(name)

    def _dep(self, op, d, raw):
        if d is None or d is op:
            return
        E = op.eng
        k = self.known[E]
        if d.is_dma:
            src = ("dma", d.key)
            cnt = self.dma_cnt[d.key]
            if k.get(src, -1) >= cnt:
                return
            k[src] = cnt
            op.waits.append(("dma", d.key, cnt))
            return
        if d.eng == E and not op.is_dma:
            if E == "pe":
                return
        if k.get(d.eng, -1) >= d.pos:
            return
        k[d.eng] = d.pos
        d.signal = True
        op.waits.append(("eng", d))

    def op(self, eng, fn, reads=(), writes=(), is_dma=False, key=None):
        o = Op(eng, fn, is_dma, key)
        if is_dma:
            self.dma_cnt.setdefault(key, 0)
        else:
            o.pos = self.npos[eng]
            self.npos[eng] += 1
        for r in reads:
            self._dep(o, r.last_w, True)
        for w in writes:
            self._dep(o, w.last_w, False)
            for rd in w.readers:
                self._dep(o, rd, False)
        if is_dma:
            self.dma_cnt[key] += 1
            o.pos = self.dma_cnt[key]
        for r in reads:
            r.readers.append(o)
        for w in writes:
            w.last_w = o
            w.readers = []
        self.streams[eng].append(o)
        return o

    def dma(self, q, out, in_, reads=(), writes=(), key=None):
        key = id(key)
        return self.op(q, lambda e: e.dma_start(out=out, in_=in_), reads, writes, is_dma=True, key=key)

    def barrier(self):
        lasts = []
        for e in ENGS:
            comp = [o for o in self.streams[e] if not o.is_dma and not o.is_nop]
            if comp:
                lasts.append(comp[-1])
        for e in ENGS:
            o = Op(e, lambda eng: eng.nop())
            o.is_nop = True
            o.pos = self.npos[e]
            self.npos[e] += 1
            k = self.known[e]
            for d in lasts:
                if k.get(d.eng, -1) >= d.pos:
                    continue
                k[d.eng] = d.pos
                d.signal = True
                o.waits.append(("eng", d))
            for key, cnt in self.dma_cnt.items():
                src = ("dma", key)
                if k.get(src, -1) >= cnt:
                    continue
                k[src] = cnt
                o.waits.append(("dma", key, cnt))
            self.streams[e].append(o)

    def flush(self):
        self.barrier()
        nc = self.nc
        for e in ENGS:
            n = self.nsig[e]
            for o in self.streams[e]:
                if not o.is_dma and o.signal:
                    n += 1
                    o.sigidx = n
            self.nsig[e] = n
        for key in self.dma_cnt:
            if key not in self.ksem:
                self.ksem[key] = self.stack.enter_context(nc.semaphore("d_%d" % len(self.ksem)))
        esem, ksem = self.esem, self.ksem
        streams = self.streams

        def run(en, eng):
            for o in streams[en]:
                for w in o.waits:
                    if w[0] == "dma":
                        eng.wait_ge(ksem[w[1]], 16 * w[2])
                    else:
                        eng.wait_ge(esem[w[1].eng], w[1].sigidx)
                ins = o.fn(eng)
                if o.is_dma:
                    ins.then_inc(ksem[o.key], 16)
                elif o.signal:
                    ins.then_inc(esem[en], 1)

        with nc.Block() as block:
            @block.tensor
            def _(e):
                run("pe", e)

            @block.scalar
            def _(e):
                run("act", e)

            @block.vector
            def _(e):
                run("dve", e)

            @block.gpsimd
            def _(e):
                run("pool", e)

            @block.sync
            def _(e):
                run("sp", e)
        self.streams = {e: [] for e in ENGS}


class Tl:
    __slots__ = ("t", "r")

    def __init__(self, t, r):
        self.t = t
        self.r = r

    def __getitem__(self, k):
        return self.t[k]


def _rs(xs):
    out = []
    for x in xs:
        if x is None:
            continue
        out.append(x.r if isinstance(x, Tl) else x)
    return out


class K:
    def __init__(self, nc, P):
        self.nc = nc
        self.P = P
        self.n = 0
        self.rr = 0

    def sb(self, st, shape, dt=F32, name=None):
        self.n += 1
        nm = "%s_%d" % (name or "t", self.n)
        return Tl(st.enter_context(self.nc.sbuf_tensor(nm, list(shape), dt)), Res(nm))

    def ps(self, st, shape, dt=F32, name=None):
        self.n += 1
        nm = "%s_%d" % (name or "p", self.n)
        return Tl(st.enter_context(self.nc.psum_tensor(nm, list(shape), dt)), Res(nm))

    def mm(self, out, lhsT, rhs, start, stop, rd, wr, skip=False):
        if skip:
            f = lambda e: e.matmul(out, lhsT=lhsT, rhs=rhs, start=start, stop=stop, skip_group_check=True)
        else:
            f = lambda e: e.matmul(out, lhsT=lhsT, rhs=rhs, start=start, stop=stop)
        self.P.op("pe", f, _rs(rd), _rs(wr))

    def tr(self, out, in_, ident, rd, wr):
        self.P.op("pe", lambda e: e.transpose(out, in_, ident), _rs(rd), _rs(wr))

    def act(self, out, in_, func, rd, wr, scale=1.0, bias=None, accum=None):
        kw = {}
        if bias is not None:
            kw["bias"] = bias
        if accum is not None:
            kw["accum_out"] = accum
        self.P.op("act", lambda e: e.activation(out=out, in_=in_, func=func, scale=scale, **kw), _rs(rd), _rs(wr))

    def cp(self, eng, out, in_, rd, wr):
        self.P.op(eng, lambda e: e.tensor_copy(out, in_), _rs(rd), _rs(wr))

    def ts(self, eng, out, in0, s1, s2, op0, op1, rd, wr):
        self.P.op(eng, lambda e: e.tensor_scalar(out, in0, s1, s2, op0, op1), _rs(rd), _rs(wr))

    def tt(self, eng, out, in0, in1, op, rd, wr):
        self.P.op(eng, lambda e: e.tensor_tensor(out, in0, in1, op), _rs(rd), _rs(wr))

    def stt(self, out, in0, scalar, in1, op0, op1, rd, wr):
        self.P.op("dve", lambda e: e.scalar_tensor_tensor(out=out, in0=in0, scalar=scalar, in1=in1, op0=op0, op1=op1),
                  _rs(rd), _rs(wr))

    def memset(self, eng, out, val, wr):
        self.P.op(eng, lambda e: e.memset(out, val), (), _rs(wr))

    def dma(self, out, in_, rd, wr, key, q="sp"):
        self.P.dma(q, out, in_, _rs(rd), _rs(wr), key=(key.r if isinstance(key, Tl) else key))

    def ceng(self):
        self.rr += 1
        return ("dve", "pool")[self.rr % 2]

    def wload(self, stages, dst_fn, src_fn, ncol, dst, piece=2048, gain=None, npart=128):
        c0 = 0
        while c0 < ncol:
            c1 = min(ncol, c0 + piece)
            stg = stages[self.rr % len(stages)]
            self.dma(stg[0:npart, 0:c1 - c0], src_fn(c0, c1), (), (stg,), stg)
            eng = self.ceng()
            if gain is None:
                self.cp(eng, dst_fn(c0, c1), stg[0:npart, 0:c1 - c0], (stg,), (dst,))
            else:
                self.ts(eng, dst_fn(c0, c1), stg[0:npart, 0:c1 - c0], gain, 0.0, ALU.mult, ALU.add, (stg,), (dst,))
            c0 = c1


def norm_transpose(k, xin, hnbs, hnT, ss, rstd, mhalf, ident, tbanks):
    for j in range(4):
        hnb = hnbs[j % len(hnbs)]
        tb = tbanks[j % len(tbanks)]
        k.act(hnb[:, :], xin[:, j, :], AF.Square, (xin,), (hnb, ss), accum=ss[:, j:j + 1])
        k.ts("dve", rstd[:, j:j + 1], ss[:, j:j + 1], 1.0 / D, EPS, ALU.mult, ALU.add, (ss,), (rstd,))
        k.tt("pool", rstd[:, j:j + 1], rstd[:, j:j + 1], mhalf[:, 0:1], ALU.pow, (rstd, mhalf), (rstd,))
        if j % 2 == 0:
            k.ts("dve", hnb[:, :], xin[:, j, :], rstd[:, j:j + 1], 0.0, ALU.mult, ALU.add, (xin, rstd), (hnb,))
        else:
            k.act(hnb[:, :], xin[:, j, :], AF.Identity, (xin, rstd), (hnb,), scale=rstd[:, j:j + 1])
        for kc in range(8):
            k.tr(tb[:, kc * 128:(kc + 1) * 128], hnb[:, kc * 128:(kc + 1) * 128], ident[:, :], (hnb, ident), (tb,))
        src = tb[:, :].rearrange("p (a b) -> p a b", a=8)
        if j % 2 == 0:
            k.act(hnT[:, :, j * 128:(j + 1) * 128], src, AF.Copy, (tb,), (hnT,))
        else:
            k.cp("dve", hnT[:, :, j * 128:(j + 1) * 128], src, (tb,), (hnT,))


def ffn_chunk(k, hnT, actT, wgu, wdn, banks, bi, tg, tA):
    for fc in range(NFC):
        pg = banks[bi % len(banks)]
        pv = banks[(bi + 1) % len(banks)]
        bi += 2
        for kc in range(8):
            k.mm(pg[:, :], wgu[:, kc, fc * 128:(fc + 1) * 128], hnT[:, kc, :], kc == 0, kc == 7, (wgu, hnT), (pg,))
        for kc in range(8):
            k.mm(pv[:, :], wgu[:, kc, DFF + fc * 128:DFF + (fc + 1) * 128], hnT[:, kc, :], kc == 0, kc == 7, (wgu, hnT), (pv,))
        t1 = tg[fc % len(tg)]
        t2 = tA[fc % len(tA)]
        k.act(t1[:, :], pg[:, :], AF.Tanh, (pg,), (t1,), scale=0.5)
        k.stt(t2[:, :], t1[:, :], 1.0, pg[:, :], ALU.add, ALU.mult, (t1, pg), (t2,))
        k.stt(actT[:, fc, :], t2[:, :], 0.5, pv[:, :], ALU.mult, ALU.mult, (t2, pv), (actT,))
    return bi


def ffn_down(k, actT, wdn, xin, banks, bi):
    for j in range(4):
        for hf in range(2):
            pb = banks[bi % len(banks)]
            bi += 1
            for fc in range(NFC):
                k.mm(pb[:, :], actT[:, fc, j * 128:(j + 1) * 128], wdn[:, fc, hf * 512:(hf + 1) * 512], fc == 0, fc == NFC - 1,
                     (actT, wdn), (pb,))
            k.tt("dve", xin[:, j, hf * 512:(hf + 1) * 512], pb[:, :], xin[:, j, hf * 512:(hf + 1) * 512], ALU.add, (pb, xin), (xin,))
    return bi


def tok_view(ap2d, t0):
    return ap2d[t0:t0 + T, :].rearrange("(j p) d -> p j d", p=128)


def build_program(nseq, phases="ABCDE", debug=False):
    NT = nseq * S
    nc = bass.Bass("TRN2", target_bir_lowering=False)
    dt_in = {}

    def din(name, shape):
        dt_in[name] = shape
        return nc.dram_tensor(name, list(shape), F32, kind="ExternalInput").ap()

    def dscr(name, shape, dt):
        kind = "ExternalOutput" if debug else "Internal"
        return nc.dram_tensor(name, list(shape), dt, kind=kind).ap()

    x_d = din("x", (NT, D))
    identf_d = din("c_ident", (128, 128))
    tri_d = din("c_tri", (128, 256))
    maskc_d = din("c_maskc", (128, S))
    ropec_d = din("c_ropec", (128, S))
    ropes_d = din("c_ropes", (128, S))
    pm_d = din("c_pm", (128, 64))
    esel_d = din("c_esel", (32, S))
    ovl_d = din("c_ovl", (128, 33))
    tkadd_d = din("c_tkadd", (S, 32))
    tkmul_d = din("c_tkmul", (S, 32))
    g_d = din("g_all", (128, 5, 8))
    gfin_d = din("g_fin", (1, D))
    win_d = din("w_in", (128, 8 * 3072))
    cw_d = din("conv_w", (RB, NBLK * 4))
    sm_d = din("a_small", (RB, 4, NBLK))
    wra_d = din("w_ra", (RB, NBLK * RB))
    wix_d = din("w_ix", (RB, NBLK * RB))
    wout_d = din("w_out", (RB, NBLK * D))
    wgu_d = [din("w_gu%d" % l, (128, 8 * 2 * DFF)) for l in range(2)]
    wdn_d = [din("w_dn%d" % l, (128, NFC * D)) for l in range(2)]
    wkc_d = din("w_kc", (128, 8 * NG * 96))
    wvc_d = din("w_vc", (128, 8 * NG * 64))
    wks_d = din("w_ks", (128, 8 * NG * 128))
    wkw_d = din("w_kw", (128, 8 * NG * 128))
    wv_d = din("w_v", (128, 8 * 512))
    wq_d = din("w_q", (128, 8 * NH * 128))
    wqg_d = din("w_qg", (128, 8 * 48))
    gb_d = din("gate_b", (1, 48))
    wo_d = din("w_o", (128, 8 * D))
    w1k_d = din("w1k", (96, 32 * 256))
    w1v_d = din("w1v", (64, 32 * 256))
    w2k_d = din("w2k", (128, 2 * 128))
    w2v_d = din("w2v", (128, 2 * 64))
    posk_d = din("posk", (96, 32))
    posv_d = din("posv", (64, 32))

    y_d = nc.dram_tensor("y", [NT, D], F32, kind="ExternalOutput").ap()
    hmid_d = dscr("s_hmid", (NT, D), F32)
    h1_d = dscr("s_h1", (NT, D), F32)
    kcmp_d = dscr("s_kcmp", (nseq, 96, NG, S), BF16)
    vcmp_d = dscr("s_vcmp", (nseq, 64, NG, S), BF16)
    ksel_d = dscr("s_ksel", (nseq, 96, NG, S), BF16)
    kwin_d = dscr("s_kwin", (nseq, 96, NG, S), BF16)
    v_d = dscr("s_v", (NT, 520), BF16)
    qn_d = dscr("s_qn", (nseq, 96, NH, S), BF16)
    qr_d = dscr("s_qr", (nseq, 32, NH, S), BF16)
    gt_d = dscr("s_gt", (NT, 48), F32)
    o_d = dscr("s_o", (NT, D), BF16)

    with contextlib.ExitStack() as top:
        P = Prog(nc, top)
        k = K(nc, P)

        ident = k.sb(top, [128, 128], BF16, "ident")
        mhalf = k.sb(top, [128, 4], F32, "mhalf")
        gall = k.sb(top, [128, 5, 8], F32, "gall")
        with contextlib.ExitStack() as st:
            tmp = k.sb(st, [128, 128], F32)
            k.dma(tmp[:, :], identf_d, (), (tmp,), tmp)
            k.cp("dve", ident[:, :], tmp[:, :], (tmp,), (ident,))
            k.memset("dve", mhalf[:, :], -0.5, (mhalf,))
            k.dma(gall[:, :, :], g_d, (), (gall,), gall)
            P.flush()

        if "A" in phases:
            with contextlib.ExitStack() as st:
                win = k.sb(st, [128, 8, 3072], BF16, "win")
                wout = k.sb(st, [RB, NBLK, D], BF16, "wout")
                wra = k.sb(st, [RB, NBLK, RB], BF16, "wra")
                wix = k.sb(st, [RB, NBLK, RB], BF16, "wix")
                dg = k.sb(st, [RB, NBLK * 4, RB], BF16, "dg")
                cw = k.sb(st, [RB, NBLK * 4], F32, "cw")
                sm = k.sb(st, [RB, 4, NBLK], F32, "sm")
                hb = k.sb(st, [RB, 2, NBLK], F32, "hb")
                cc = k.sb(st, [RB, 2, NBLK], F32, "cc")
                cx = k.sb(st, [RB, NBLK, 4], BF16, "cx")
                hcar = k.sb(st, [RB, NBLK], F32, "hcar")
                one = k.sb(st, [RB, 1], F32, "one")
                k.memset("dve", one[:, :], 1.0, (one,))
                with contextlib.ExitStack() as st2:
                    stages = [k.sb(st2, [128, 3072], F32, "stg") for _ in range(3)]
                    for kc in range(8):
                        k.wload(stages, lambda a, b, kc=kc: win[:, kc, a:b], lambda a, b, kc=kc: win_d[:, kc * 3072 + a:kc * 3072 + b],
                                3072, win, piece=3072, gain=gall[:, 0, kc:kc + 1])
                    for n in range(NBLK):
                        k.wload(stages, lambda a, b, n=n: wout[:, n, a:b], lambda a, b, n=n: wout_d[:, n * D + a:n * D + b], D, wout,
                                piece=D, npart=RB)
                    k.wload(stages, lambda a, b: wra[:, :, :].rearrange("p n j -> p (n j)")[:, a:b], lambda a, b: wra_d[:, a:b],
                            NBLK * RB, wra, piece=NBLK * RB, npart=RB)
                    k.wload(stages, lambda a, b: wix[:, :, :].rearrange("p n j -> p (n j)")[:, a:b], lambda a, b: wix_d[:, a:b],
                            NBLK * RB, wix, piece=NBLK * RB, npart=RB)
                    k.dma(cw[:, :], cw_d, (), (cw,), cw)
                    k.dma(sm[:, :, :], sm_d, (), (sm,), sm)
                    for i in range(NBLK * 4):
                        k.ts(k.ceng(), dg[:, i, :], ident[0:RB, 0:RB], cw[:, i:i + 1], 0.0, ALU.mult, ALU.add, (ident, cw), (dg,))
                    k.ts("dve", hb[:, :, :], sm[:, 1:3, :], -1.0, 0.0, ALU.mult, ALU.add, (sm,), (hb,))
                    sg = k.sb(st2, [RB, NBLK], F32)
                    k.act(sg[:, :], sm[:, 3, :], AF.Exp, (sm,), (sg,), scale=-1.0)
                    k.ts("dve", sg[:, :], sg[:, :], 1.0, 1.0, ALU.mult, ALU.add, (sg,), (sg,))
                    k.act(sg[:, :], sg[:, :], AF.Ln, (sg,), (sg,))
                    k.ts("dve", cc[:, 0, :], sg[:, :], -8.0, 0.0, ALU.mult, ALU.add, (sg,), (cc,))
                    k.ts("dve", cc[:, 1, :], sg[:, :], -16.0, 0.0, ALU.mult, ALU.add, (sg,), (cc,))
                    P.flush()
                if os.environ.get("KSTOP") == "A0":
                    return nc

                xins = [k.sb(st, [128, 4, D], F32, "xin") for _ in range(2)]
                hnb = [k.sb(st, [128, D], BF16, "hnb") for _ in range(2)]
                hnTs = [k.sb(st, [128, 8, T], BF16, "hnT") for _ in range(2)]
                recy = k.sb(st, [RB, NBLK, T], BF16, "recy")
                ss = k.sb(st, [128, 4], F32)
                rstd = k.sb(st, [128, 4], F32)
                NSET = 2
                xbT = [k.sb(st, [RB, T + 4], BF16) for _ in range(NSET)]
                xcT = [k.sb(st, [RB, T], BF16) for _ in range(NSET)]
                trr = [k.sb(st, [RB, T], F32) for _ in range(NSET)]
                tii = [k.sb(st, [RB, T], F32) for _ in range(NSET)]
                aa = [k.sb(st, [RB, T], F32) for _ in range(NSET)]
                a2 = [k.sb(st, [RB, T], F32) for _ in range(NSET)]
                hs = [k.sb(st, [RB, T], F32) for _ in range(NSET)]
                sq = [k.sb(st, [RB, T], F32) for _ in range(NSET)]
                th = [k.sb(st, [RB, T], F32) for _ in range(NSET)]
                banks = [k.ps(st, [128, 512], F32, "bk") for _ in range(4)]
                ybanks = [k.ps(st, [128, 512], F32, "yb") for _ in range(2)]
                tbanks = [k.ps(st, [128, 1024], BF16, "tb") for _ in range(2)]

                def rcp(out, in_, rd, wr):
                    P.op("dve", lambda e: e.reciprocal(out, in_), _rs(rd), _rs(wr))
                bi = 0
                ci = 0
                KNCH = int(os.environ.get("KNCH", "4"))
                KNBLK = int(os.environ.get("KNBLK", "16"))
                KSTOP = os.environ.get("KSTOP", "")
                nchA = min(KNCH, S // T)
                chunksA = [(b, c) for b in range(nseq) for c in range(nchA)]
                k.dma(xins[0][:, :, :], tok_view(x_d, chunksA[0][0] * S + chunksA[0][1] * T), (), (xins[0],), xins[0])
                for b in range(nseq):
                    k.memset("pool", cx[:, :, :], 0.0, (cx,))
                    k.memset("pool", hcar[:, :], 0.0, (hcar,))
                    for c in range(nchA):
                        t0 = b * S + c * T
                        xin = xins[ci % 2]
                        hnT = hnTs[ci % 2]
                        ci += 1
                        if ci < len(chunksA):
                            nb_, nc_ = chunksA[ci]
                            k.dma(xins[ci % 2][:, :, :], tok_view(x_d, nb_ * S + nc_ * T), (), (xins[ci % 2],), xins[ci % 2])
                        norm_transpose(k, xin, hnb, hnT, ss, rstd, mhalf, ident, tbanks)
                        def blk(n, xin=None, hnT=hnT):
                            s_ = n % NSET
                            bA = banks[(2 * n) % 4]
                            bB = banks[(2 * n + 1) % 4]
                            pxb, pr, pcv, pi = bA, bA, bB, bB
                            pyb = ybanks[n % 2]
                            er, ei, ey = trr[s_], tii[s_], th[s_]
                            for kc in range(8):
                                k.mm(pxb[0:RB, :], win[:, kc, n * RB:(n + 1) * RB], hnT[:, kc, :], kc == 0, kc == 7, (win, hnT), (pxb,))
                            for kc in range(8):
                                k.mm(pyb[0:RB, :], win[:, kc, DRNN + n * RB:DRNN + (n + 1) * RB], hnT[:, kc, :], kc == 0, kc == 7,
                                     (win, hnT), (pyb,))
                            yield
                            k.cp("pool", xbT[s_][:, 0:4], cx[:, n, :], (cx,), (xbT[s_],))
                            k.cp("dve", xbT[s_][:, 4:4 + T], pxb[0:RB, :], (pxb,), (xbT[s_],))
                            k.cp("pool", cx[:, n, :], xbT[s_][:, T:T + 4], (xbT[s_],), (cx,))
                            k.act(sq[s_][:, :], pyb[0:RB, :], AF.Square, (pyb,), (sq[s_],))
                            yield
                            for t in range(4):
                                k.mm(pcv[0:RB, :], dg[:, n * 4 + t, :], xbT[s_][:, 1 + t:1 + t + T], t == 0, t == 3, (dg, xbT[s_]), (pcv,))
                            k.ts("pool", sq[s_][:, :], sq[s_][:, :], GC, 1.0, ALU.mult, ALU.add, (sq[s_],), (sq[s_],))
                            yield
                            k.ts("dve", xcT[s_][:, :], pcv[0:RB, :], sm[:, 0, n:n + 1], 0.0, ALU.add, ALU.add, (pcv, sm), (xcT[s_],))
                            k.tt("dve", sq[s_][:, :], sq[s_][:, :], pyb[0:RB, :], ALU.mult, (sq[s_], pyb), (sq[s_],))
                            yield
                            k.mm(pr[0:RB, :], wra[:, n, :], xcT[s_][:, :], True, True, (wra, xcT[s_]), (pr,))
                            k.mm(pi[0:RB, :], wix[:, n, :], xcT[s_][:, :], True, True, (wix, xcT[s_]), (pi,))
                            k.act(ey[:, :], sq[s_][:, :], AF.Exp, (sq[s_],), (ey,), scale=-2.0 * GK)
                            yield
                            k.act(er[:, :], pr[0:RB, :], AF.Exp, (pr, hb), (er,), scale=-1.0, bias=hb[:, 0, n:n + 1])
                            k.act(ei[:, :], pi[0:RB, :], AF.Exp, (pi, hb), (ei,), scale=-1.0, bias=hb[:, 1, n:n + 1])
                            k.act(ey[:, :], ey[:, :], AF.Ln, (ey, one), (ey,), bias=one[:, 0:1])
                            yield
                            k.act(er[:, :], er[:, :], AF.Ln, (er, one), (er,), bias=one[:, 0:1])
                            k.act(ei[:, :], ei[:, :], AF.Ln, (ei, one), (ei,), bias=one[:, 0:1])
                            k.act(ey[:, :], ey[:, :], AF.Exp, (ey,), (ey,), scale=-1.0)
                            yield
                            k.act(er[:, :], er[:, :], AF.Exp, (er,), (er,), scale=-1.0)
                            k.tt("dve", ey[:, :], ey[:, :], pyb[0:RB, :], ALU.mult, (ey, pyb), (ey,))
                            yield
                            k.act(aa[s_][:, :], er[:, :], AF.Exp, (er, cc), (aa[s_],), scale=cc[:, 0, n:n + 1])
                            k.act(a2[s_][:, :], er[:, :], AF.Exp, (er, cc), (a2[s_],), scale=cc[:, 1, n:n + 1])
                            yield
                            k.ts("pool", a2[s_][:, :], a2[s_][:, :], -1.0, 1.0, ALU.mult, ALU.add, (a2[s_],), (a2[s_],))
                            k.ts("pool", a2[s_][:, :], a2[s_][:, :], 1.0, 1e-12, ALU.min, ALU.max, (a2[s_],), (a2[s_],))
                            yield
                            k.act(a2[s_][:, :], a2[s_][:, :], AF.Ln, (a2[s_],), (a2[s_],))
                            yield
                            k.stt(ei[:, :], a2[s_][:, :], 0.5, ei[:, :], ALU.mult, ALU.subtract, (a2[s_], ei), (ei,))
                            yield
                            k.act(ei[:, :], ei[:, :], AF.Exp, (ei,), (ei,))
                            yield
                            k.tt("dve", ei[:, :], ei[:, :], xcT[s_][:, :], ALU.mult, (ei, xcT[s_]), (ei,))
                            k.P.op("dve", (lambda e, o=hs[s_][:, :], d0=aa[s_][:, :], d1=ei[:, :], i0=hcar[:, n:n + 1]:
                                           e.tensor_tensor_scan(o, d0, d1, i0, ALU.mult, ALU.add)),
                                   _rs((aa[s_], ei, hcar)), _rs((hs[s_],)))
                            k.cp("pool", hcar[:, n:n + 1], hs[s_][:, T - 1:T], (hs[s_],), (hcar,))
                            k.tt("dve", recy[:, n, :], ey[:, :], hs[s_][:, :], ALU.mult, (ey, hs[s_]), (recy,))
                            yield

                        active = []
                        nxt_n = 0
                        nblk = min(KNBLK, NBLK)
                        while active or nxt_n < nblk:
                            if nxt_n < nblk and len(active) < NSET and (not active or active[-1][1] >= int(os.environ.get('KSKEW', '8'))):
                                active.append([blk(nxt_n), 0])
                                nxt_n += 1
                            for ent in list(active):
                                try:
                                    next(ent[0])
                                    ent[1] += 1
                                except StopIteration:
                                    active.remove(ent)
                        for j in range(4):
                            for hf in range(2):
                                pb = banks[bi % 4]
                                bi += 1
                                for n in range(NBLK):
                                    k.mm(pb[:, :], recy[:, n, j * 128:(j + 1) * 128], wout[:, n, hf * 512:(hf + 1) * 512], n == 0, n == NBLK - 1,
                                         (recy, wout), (pb,))
                                k.tt("dve", xin[:, j, hf * 512:(hf + 1) * 512], pb[:, :], xin[:, j, hf * 512:(hf + 1) * 512], ALU.add, (pb, xin), (xin,))
                        k.dma(tok_view(hmid_d, t0), xin[:, :, :], (xin,), (), xin)
                P.flush()

        def phase_ffn(layer, src_d, dst_d, with_o, final):
            with contextlib.ExitStack() as st:
                if os.environ.get("KDBG"):
                    print("ffn sbuf remaining at start", nc.sbuf_bytes_remaining)
                wgu = k.sb(st, [128, 8, 2 * DFF], BF16, "wgu")
                wdn = k.sb(st, [128, NFC, D], BF16, "wdn")
                wo = k.sb(st, [128, 8, D], BF16, "wo") if with_o else None
                gfin = k.sb(st, [128, D], F32, "gfin") if final else None
                with contextlib.ExitStack() as st2:
                    stages = [k.sb(st2, [128, 2816], F32, "stg") for _ in range(3)]
                    gi = 1 if layer == 0 else 4
                    for kc in range(8):
                        k.wload(stages, lambda a, b, kc=kc: wgu[:, kc, a:b], lambda a, b, kc=kc: wgu_d[layer][:, kc * 2 * DFF + a:kc * 2 * DFF + b],
                                2 * DFF, wgu, piece=2816, gain=gall[:, gi, kc:kc + 1])
                    for fc in range(0, NFC, 2):
                        k.wload(stages, lambda a, b, fc=fc: wdn[:, fc:fc + 2, :].rearrange("p f d -> p (f d)")[:, a:b],
                                lambda a, b, fc=fc: wdn_d[layer][:, fc * D + a:fc * D + b], 2 * D, wdn, piece=2 * D)
                    if with_o:
                        for kc in range(0, 8, 2):
                            k.wload(stages, lambda a, b, kc=kc: wo[:, kc:kc + 2, :].rearrange("p f d -> p (f d)")[:, a:b],
                                    lambda a, b, kc=kc: wo_d[:, kc * D + a:kc * D + b], 2 * D, wo, piece=2 * D)
                    if final:
                        g1 = k.sb(st2, [1, D], F32)
                        on = k.sb(st2, [1, 128], F32)
                        pbk = k.ps(st2, [128, 512], F32)
                        k.dma(g1[:, :], gfin_d, (), (g1,), g1)
                        k.memset("dve", on[:, :], 1.0, (on,))
                        for hf in range(2):
                            k.mm(pbk[:, :], on[:, :], g1[:, hf * 512:(hf + 1) * 512], True, True, (on, g1), (pbk,))
                            k.cp("dve", gfin[:, hf * 512:(hf + 1) * 512], pbk[:, :], (pbk,), (gfin,))
                    P.flush()
                if os.environ.get("KDBG"):
                    print("ffn sbuf remaining before acts", nc.sbuf_bytes_remaining)
                xins = [k.sb(st, [128, 4, D], F32, "xin") for _ in range(1 if with_o else 2)] * 2
                hnb = [k.sb(st, [128, D], BF16, "hnb") for _ in range(1 if with_o else 2)] * 2
                hnT = k.sb(st, [128, 8, T], BF16, "hnT")
                if os.environ.get("KDBG"):
                    print("ffn sbuf remaining before actT", nc.sbuf_bytes_remaining)
                actT = k.sb(st, [128, NFC, T], BF16, "actT")
                ss = k.sb(st, [128, 4], F32)
                rstd = k.sb(st, [128, 4], F32)
                tg = [k.sb(st, [128, T], BF16) for _ in range(1 if with_o else 2)]
                tA = [k.sb(st, [128, T], F32) for _ in range(1)]
                banks = [k.ps(st, [128, 512], F32, "bk") for _ in range(6)]
                tbanks = [k.ps(st, [128, 1024], BF16, "tb") for _ in range(2)]
                bi = 0
                if not with_o:
                    k.dma(xins[0][:, :, :], tok_view(src_d, 0), (), (xins[0],), xins[0])
                for ci in range(NT // T):
                    t0 = ci * T
                    xin = xins[ci % 2]
                    if with_o:
                        k.dma(xin[:, :, :], tok_view(src_d, t0), (), (xin,), xin)
                    elif ci + 1 < NT // T:
                        k.dma(xins[(ci + 1) % 2][:, :, :], tok_view(src_d, t0 + T), (), (xins[(ci + 1) % 2],), xins[(ci + 1) % 2])
                    if with_o:
                        oi = actT[:, 0:8, :].rearrange("p (j a) t -> p j (a t)", j=4)
                        k.dma(oi, tok_view(o_d, t0), (), (actT,), actT)
                        for j in range(4):
                            tb = tbanks[j % 2]
                            for kc in range(8):
                                k.tr(tb[:, kc * 128:(kc + 1) * 128], oi[:, j, kc * 128:(kc + 1) * 128], ident[:, :], (actT, ident), (tb,))
                            src = tb[:, :].rearrange("p (a b) -> p a b", a=8)
                            if j % 2 == 0:
                                k.act(hnT[:, :, j * 128:(j + 1) * 128], src, AF.Copy, (tb,), (hnT,))
                            else:
                                k.cp("dve", hnT[:, :, j * 128:(j + 1) * 128], src, (tb,), (hnT,))
                        for j in range(4):
                            for hf in range(2):
                                pb = banks[bi % 6]
                                bi += 1
                                for kc in range(8):
                                    k.mm(pb[:, :], hnT[:, kc, j * 128:(j + 1) * 128], wo[:, kc, hf * 512:(hf + 1) * 512], kc == 0, kc == 7,
                                         (hnT, wo), (pb,))
                                k.tt("dve", xin[:, j, hf * 512:(hf + 1) * 512], pb[:, :], xin[:, j, hf * 512:(hf + 1) * 512], ALU.add, (pb, xin), (xin,))
                    norm_transpose(k, xin, hnb, hnT, ss, rstd, mhalf, ident, tbanks)
                    bi = ffn_chunk(k, hnT, actT, wgu, wdn, banks, bi, tg, tA)
                    bi = ffn_down(k, actT, wdn, xin, banks, bi)
                    if final:
                        for j in range(4):
                            hb_ = hnb[j % 2]
                            k.act(hb_[:, :], xin[:, j, :], AF.Square, (xin,), (hb_, ss), accum=ss[:, j:j + 1])
                        k.ts("dve", rstd[:, 0:4], ss[:, 0:4], 1.0 / D, EPS, ALU.mult, ALU.add, (ss,), (rstd,))
                        k.tt("pool", rstd[:, 0:4], rstd[:, 0:4], mhalf[:, 0:4], ALU.pow, (rstd, mhalf), (rstd,))
                        for j in range(4):
                            k.stt(xin[:, j, :], xin[:, j, :], rstd[:, j:j + 1], gfin[:, :], ALU.mult, ALU.mult, (xin, rstd, gfin), (xin,))
                    k.dma(tok_view(dst_d, t0), xin[:, :, :], (xin,), (), xin)
                P.flush()

        if "B" in phases:
            phase_ffn(0, hmid_d, h1_d, False, False)

        if "C" in phases:
            with contextlib.ExitStack() as st:
                wkc = k.sb(st, [128, 8, NG * 96], BF16, "wkc")
                wvc = k.sb(st, [128, 8, NG * 64], BF16, "wvc")
                wks = k.sb(st, [128, 8, NG * 128], BF16, "wks")
                wkw = k.sb(st, [128, 8, NG * 128], BF16, "wkw")
                wv = k.sb(st, [128, 8, 512], BF16, "wv")
                wq = k.sb(st, [128, 8, NH * 128], BF16, "wq")
                wqg = k.sb(st, [128, 8, 48], BF16, "wqg")
                gbb = k.sb(st, [1, 48], BF16, "gbb")
                ones1 = k.sb(st, [1, 128], BF16, "ones1")
                pm = k.sb(st, [128, 64], BF16, "pm")
                rc = k.sb(st, [128, S], F32, "rc")
                rsn = k.sb(st, [128, S], F32, "rsn")
                with contextlib.ExitStack() as st2:
                    stages = [k.sb(st2, [128, 2048], F32, "stg") for _ in range(3)]
                    for kc in range(8):
                        for (wt, wd, nco, gi) in ((wkc, wkc_d, NG * 96, 2), (wvc, wvc_d, NG * 64, 2), (wks, wks_d, NG * 128, 2),
                                                  (wkw, wkw_d, NG * 128, 2), (wv, wv_d, 512, 2), (wq, wq_d, NH * 128, 3), (wqg, wqg_d, 48, 3)):
                            k.wload(stages, lambda a, b, wt=wt, kc=kc: wt[:, kc, a:b], lambda a, b, wd=wd, kc=kc, nco=nco: wd[:, kc * nco + a:kc * nco + b],
                                    nco, wt, piece=2048, gain=gall[:, gi, kc:kc + 1])
                    k.wload(stages, lambda a, b: pm[:, a:b], lambda a, b: pm_d[:, a:b], 64, pm)
                    k.wload(stages, lambda a, b: gbb[:, a:b], lambda a, b: gb_d[:, a:b], 48, gbb, npart=1)
                    k.memset("dve", ones1[:, :], 1.0, (ones1,))
                    k.dma(rc[:, :], ropec_d, (), (rc,), rc)
                    k.dma(rsn[:, :], ropes_d, (), (rsn,), rsn)
                    P.flush()
                xins = [k.sb(st, [128, 4, D], F32, "xin") for _ in range(2)]
                hnb = [k.sb(st, [128, D], BF16, "hnb") for _ in range(2)]
                hnT = k.sb(st, [128, 8, T], BF16, "hnT")
                ss = k.sb(st, [128, 4], F32)
                rstd = k.sb(st, [128, 4], F32)
                o_kc = [k.sb(st, [96, NG, T], BF16) for _ in range(1)] * 2
                o_vc = [k.sb(st, [64, NG, T], BF16) for _ in range(1)] * 2
                o_ks = [k.sb(st, [128, NG, T], BF16) for _ in range(1)] * 2
                o_kw = [k.sb(st, [128, NG, T], BF16) for _ in range(1)] * 2
                o_v = [k.sb(st, [128, 4, 520], BF16) for _ in range(2)]
                for ov_ in o_v:
                    k.memset("pool", ov_[:, :, :], 1.0, (ov_,))
                o_qn = [k.sb(st, [128, NH, T], BF16) for _ in range(1)] * 2
                o_qr = [k.sb(st, [64, NH, T], BF16) for _ in range(1)] * 2
                o_gt = [k.sb(st, [128, 4, 48], F32) for _ in range(2)]
                r1 = [k.sb(st, [64, T], F32) for _ in range(2)]
                r2 = [k.sb(st, [64, T], F32) for _ in range(2)]
                ks_r = [Res("ks%d" % i) for i in range(NG)]
                kw_r = [Res("kw%d" % i) for i in range(NG)]
                qn_r = [Res("qn%d" % i) for i in range(NH)]
                qr_r = [Res("qr%d" % i) for i in range(NH)]
                banks = [k.ps(st, [128, 512], F32, "bk") for _ in range(6)]
                tbanks = [k.ps(st, [128, 1024], BF16, "tb") for _ in range(2)]
                bi = 0
                ri = 0
                k.dma(xins[0][:, :, :], tok_view(h1_d, 0), (), (xins[0],), xins[0])
                for ci in range(NT // T):
                    t0 = ci * T
                    b = t0 // S
                    p0 = t0 % S
                    xin = xins[ci % 2]
                    s_ = ci % 2
                    if ci + 1 < NT // T:
                        k.dma(xins[(ci + 1) % 2][:, :, :], tok_view(h1_d, t0 + T), (), (xins[(ci + 1) % 2],), xins[(ci + 1) % 2])
                    norm_transpose(k, xin, hnb, hnT, ss, rstd, mhalf, ident, tbanks)

                    def proj(wt, c0, m):
                        nonlocal bi
                        pb = banks[bi % 6]
                        bi += 1
                        for kc in range(8):
                            k.mm(pb[0:m, :], wt[:, kc, c0:c0 + m], hnT[:, kc, :], kc == 0, kc == 7, (wt, hnT), (pb,))
                        return pb

                    for g in range(NG):
                        pb = proj(wkc, g * 96, 96)
                        k.act(o_kc[s_][:, g, :], pb[0:96, :], AF.Copy, (pb,), (o_kc[s_],))
                        pb = proj(wvc, g * 64, 64)
                        k.cp("dve", o_vc[s_][:, g, :], pb[0:64, :], (pb,), (o_vc[s_],))
                    units = []
                    for (wt, ot, rl) in ((wks, o_ks[s_], ks_r), (wkw, o_kw[s_], kw_r)):
                        for g in range(NG):
                            units.append((wt, g * 128, ot, g, rl[g], ot, rl[g], 1.0))
                    for h in range(NH):
                        units.append((wq, h * 128, o_qn[s_], h, qn_r[h], o_qr[s_], qr_r[h], float(DQK ** -0.5)))
                    prev = None
                    for u in units + [None]:
                        if u is not None:
                            (wt, c0, dst, slot, dres, rdst, rres, scale) = u
                            pb = proj(wt, c0, 128)
                            k.act(dst[:, slot, :], pb[:, :], AF.Copy, (pb,), (dres,), scale=scale)
                        if prev is not None:
                            (wt, c0, dst, slot, dres, rdst, rres, scale) = prev
                            pb = banks[bi % 6]
                            bi += 1
                            k.mm(pb[0:64, :], pm[:, :], dst[:, slot, :], True, True, (pm, dres), (pb,))
                            a = r1[ri % 2]
                            bb = r2[ri % 2]
                            ri += 1
                            k.tt("dve", a[32:64, :], pb[32:64, :], rsn[32:64, p0:p0 + T], ALU.mult, (pb, rsn), (a,))
                            k.tt("pool", bb[32:64, :], dst[32:64, slot, :], rc[32:64, p0:p0 + T], ALU.mult, (dres, rc), (bb,))
                            k.tt("dve", rdst[32:64, slot, :], a[32:64, :], bb[32:64, :], ALU.add, (a, bb), (rres,))
                        prev = u
                    for j in range(4):
                        pb = banks[bi % 6]
                        bi += 1
                        for kc in range(8):
                            k.mm(pb[:, :], hnT[:, kc, j * 128:(j + 1) * 128], wv[:, kc, :], kc == 0, kc == 7, (wv, hnT), (pb,))
                        k.cp("dve", o_v[s_][:, j, :].rearrange("p (a c) -> p a c", c=65)[:, :, 0:64], pb[:, :].rearrange("p (a c) -> p a c", c=64),
                             (pb,), (o_v[s_],))
                        pb = banks[bi % 6]
                        bi += 1
                        for kc in range(8):
                            k.mm(pb[:, 0:48], hnT[:, kc, j * 128:(j + 1) * 128], wqg[:, kc, :], kc == 0, False, (wqg, hnT), (pb,))
                        k.mm(pb[:, 0:48], ones1[:, :], gbb[:, :], False, True, (ones1, gbb), (pb,))
                        k.act(o_gt[s_][:, j, :], pb[:, 0:48], AF.Tanh, (pb,), (o_gt[s_],), scale=0.5)
                        k.ts("dve", o_gt[s_][:, j, :], o_gt[s_][:, j, :], 0.5, 0.5, ALU.mult, ALU.add, (o_gt[s_],), (o_gt[s_],))
                    k.dma(kcmp_d[b, :, :, p0:p0 + T], o_kc[s_][:, :, :], (o_kc[s_],), (), o_kc[s_])
                    k.dma(vcmp_d[b, :, :, p0:p0 + T], o_vc[s_][:, :, :], (o_vc[s_],), (), o_vc[s_])
                    k.dma(ksel_d[b, :, :, p0:p0 + T], o_ks[s_][32:128, :, :], tuple(ks_r), (), o_ks[s_])
                    k.dma(kwin_d[b, :, :, p0:p0 + T], o_kw[s_][32:128, :, :], tuple(kw_r), (), o_kw[s_])
                    k.dma(v_d[t0:t0 + T, :].rearrange("(j p) d -> p j d", p=128), o_v[s_][:, :, :], (o_v[s_],), (), o_v[s_])
                    k.dma(qn_d[b, :, :, p0:p0 + T], o_qn[s_][32:128, :, :], tuple(qn_r), (), o_qn[s_])
                    k.dma(qr_d[b, :, :, p0:p0 + T], o_qr[s_][32:64, :, :], tuple(qr_r) + tuple(qn_r), (), o_qr[s_])
                    k.dma(gt_d[t0:t0 + T, :].rearrange("(j p) d -> p j d", p=128), o_gt[s_][:, :, :], (o_gt[s_],), (), o_gt[s_])
                P.flush()

        if "D" in phases:
            with contextlib.ExitStack() as st:
                w1k = k.sb(st, [96, 32, 256], BF16, "w1k")
                w1v = k.sb(st, [64, 32, 256], BF16, "w1v")
                w2k = k.sb(st, [128, 2, 128], BF16, "w2k")
                w2v = k.sb(st, [128, 2, 64], BF16, "w2v")
                posk = k.sb(st, [96, 32], BF16, "posk")
                posv = k.sb(st, [64, 32], BF16, "posv")
                pbias = k.sb(st, [128, 4], F32, "pbias")
                tri = k.sb(st, [128, 256], BF16, "tri")
                maskc = k.sb(st, [128, S], BF16, "maskc")
                tkadd = k.sb(st, [128, 16, 32], F32, "tkadd")
                tkmul = k.sb(st, [128, 16, 32], F32, "tkmul")
                vcaug = k.sb(st, [128, NG, 97], BF16, "vcaug")
                kcT = k.sb(st, [128, NG, 128], BF16, "kcT")
                ksT = k.sb(st, [128, NG, S], BF16, "ksT")
                kwT = k.sb(st, [128, NG, S], BF16, "kwT")
                with contextlib.ExitStack() as st2:
                    stages = [k.sb(st2, [128, 2048], F32, "stg") for _ in range(3)]
                    for l0 in range(0, 32, 8):
                        k.wload(stages, lambda a, b, l0=l0: w1k[:, l0:l0 + 8, :].rearrange("p l h -> p (l h)")[:, a:b],
                                lambda a, b, l0=l0: w1k_d[:, l0 * 256 + a:l0 * 256 + b], 2048, w1k, npart=96)
                        k.wload(stages, lambda a, b, l0=l0: w1v[:, l0:l0 + 8, :].rearrange("p l h -> p (l h)")[:, a:b],
                                lambda a, b, l0=l0: w1v_d[:, l0 * 256 + a:l0 * 256 + b], 2048, w1v, npart=64)
                    k.wload(stages, lambda a, b: w2k[:, :, :].rearrange("p l h -> p (l h)")[:, a:b], lambda a, b: w2k_d[:, a:b], 256, w2k)
                    k.wload(stages, lambda a, b: w2v[:, :, :].rearrange("p l h -> p (l h)")[:, a:b], lambda a, b: w2v_d[:, a:b], 128, w2v)
                    k.wload(stages, lambda a, b: posk[:, a:b], lambda a, b: posk_d[:, a:b], 32, posk, npart=96)
                    k.wload(stages, lambda a, b: posv[:, a:b], lambda a, b: posv_d[:, a:b], 32, posv, npart=64)
                    k.wload(stages, lambda a, b: tri[:, a:b], lambda a, b: tri_d[:, a:b], 256, tri)
                    k.wload(stages, lambda a, b: maskc[:, a:b], lambda a, b: maskc_d[:, a:b], S, maskc)
                    k.dma(tkadd[:, :, :], tkadd_d.rearrange("(j p) n -> p j n", p=128), (), (tkadd,), tkadd)
                    k.dma(tkmul[:, :, :], tkmul_d.rearrange("(j p) n -> p j n", p=128), (), (tkmul,), tkmul)
                    k.memset("pool", vcaug[:, :, 0:64], 0.0, (vcaug,))
                    for g in range(NG):
                        k.wload(stages, lambda a, b, g=g: vcaug[:, g, 64 + a:64 + b], lambda a, b: ovl_d[:, a:b], 33, vcaug)
                        k.wload(stages, lambda a, b, g=g: ksT[0:32, g, a:b], lambda a, b: esel_d[:, a:b], S, ksT, npart=32)
                    k.memset("pool", kwT[0:32, :, :], 0.0, (kwT,))
                    k.memset("dve", kcT[:, :, :], 0.0, (kcT,))
                    pbk = k.ps(st2, [128, 512], F32)
                    for (w1, pos, col) in ((w1k, posk, 0), (w1v, posv, 2)):
                        for hh in range(2):
                            for l in range(32):
                                k.mm(pbk[:, col + hh:col + hh + 1], w1[:, l, hh * 128:(hh + 1) * 128], pos[:, l:l + 1], l == 0, l == 31,
                                     (w1, pos), (pbk,), skip=True)
                    k.cp("dve", pbias[:, :], pbk[:, 0:4], (pbk,), (pbias,))
                    P.flush()

                kcmp = k.sb(st, [96, NG, S], BF16, "kcmp")
                vcmp = k.sb(st, [64, NG, S], BF16, "vcmp")
                vaug = k.sb(st, [128, 16, 2, NG, 65], BF16, "vaug")
                gts = k.sb(st, [128, 16, 48], F32, "gts")
                qnt = k.sb(st, [128, 4, S], BF16, "qn")
                qat = k.sb(st, [128, 4, S], BF16, "qa")
                oacc = k.sb(st, [128, 16, 256], F32, "oacc")
                obf = k.sb(st, [128, 16, 256], BF16, "obf")
                hid = [k.sb(st, [128, 2, 512], BF16) for _ in range(2)]
                gx = k.sb(st, [128, 512], F32)
                gs = k.sb(st, [128, 512], F32)
                gt_ = k.sb(st, [128, 512], BF16)
                em = [k.sb(st, [128, 512], BF16) for _ in range(2)]
                pmk = [k.sb(st, [128, 512], BF16) for _ in range(4)]
                e2b = [k.sb(st, [128, 512], BF16) for _ in range(4)]
                qa_r = [Res("qa%d" % i) for i in range(4)]
                oa_r = [Res("oa%d" % i) for i in range(4)]
                rden = [k.sb(st, [128, 4], F32) for _ in range(3)]
                fac = [k.sb(st, [128, 4], F32) for _ in range(3)]
                tmp = [k.sb(st, [128, 4, 64], F32) for _ in range(3)]
                impn = k.sb(st, [128, 4, 32], F32)
                imp = k.sb(st, [128, 32], F32)
                top8 = k.sb(st, [128, 8], F32)
                selb = k.sb(st, [128, 128], BF16)
                sbanks = [k.ps(st, [128, 512], F32, "sb") for _ in range(3)]
                abanks = [k.ps(st, [128, 512], F32, "ab") for _ in range(4)]
                tbank = k.ps(st, [128, 1024], BF16, "tb")
                k.memset("dve", qnt[0:32, :, :], 0.0, (qnt,))
                k.memset("dve", selb[:, :], 0.0, (selb,))
                cnt = {"s": 0, "a": 0, "p": 0, "e": 0, "n": 0}

                def nxt(lst, key):
                    cnt[key] += 1
                    return lst[cnt[key] % len(lst)]

                def rcp(out, in_, rd, wr):
                    P.op("dve", lambda e: e.reciprocal(out, in_), _rs(rd), _rs(wr))

                for b in range(nseq):
                    k.dma(kcmp[:, :, :], kcmp_d[b], (), (kcmp,), kcmp)
                    k.dma(vcmp[:, :, :], vcmp_d[b], (), (vcmp,), vcmp)
                    k.dma(ksT[32:128, :, :], ksel_d[b], (), (ksT,), ksT)
                    k.dma(kwT[32:128, :, :], kwin_d[b], (), (kwT,), kwT)
                    k.dma(vaug[:, :, :, :, :].rearrange("p j b g c -> p j (b g c)"), v_d[b * S:(b + 1) * S, :].rearrange("(j p) c -> p j c", p=128),
                          (), (vaug,), vaug)
                    k.dma(gts[:, :, :], gt_d[b * S:(b + 1) * S, :].rearrange("(j p) n -> p j n", p=128), (), (gts,), gts)
                    for (which, raw, w1) in ((0, kcmp, w1k), (1, vcmp, w1v)):
                        for hh in range(2):
                            pb = nxt(sbanks, "s")
                            for l in range(32):
                                k.mm(pb[:, 0:NG * NCMP].rearrange("p (g c) -> p g c", g=NG), w1[:, l, hh * 128:(hh + 1) * 128],
                                     raw[:, :, l:l + 16 * (NCMP - 1) + 1:16], l == 0, l == 31, (w1, raw), (pb,))
                            bcol = pbias[:, which * 2 + hh:which * 2 + hh + 1]
                            k.act(gx[:, 0:508], pb[:, 0:508], AF.Identity, (pb, pbias), (gx,), bias=bcol)
                            k.act(gs[:, 0:508], pb[:, 0:508], AF.Square, (pb, pbias), (gs,), bias=bcol)
                            k.ts("dve", gs[:, 0:508], gs[:, 0:508], GC, 1.0, ALU.mult, ALU.add, (gs,), (gs,))
                            k.tt("dve", gs[:, 0:508], gs[:, 0:508], gx[:, 0:508], ALU.mult, (gs, gx), (gs,))
                            k.act(gt_[:, 0:508], gs[:, 0:508], AF.Tanh, (gs,), (gt_,), scale=GK)
                            k.stt(gs[:, 0:508], gt_[:, 0:508], 1.0, gx[:, 0:508], ALU.add, ALU.mult, (gt_, gx), (gs,))
                            k.ts("dve", hid[which][:, hh, 0:508], gs[:, 0:508], 0.5, 0.0, ALU.mult, ALU.add, (gs,), (hid[which],))
                    for g in range(NG):
                        pb = nxt(abanks, "a")
                        for hh in range(2):
                            k.mm(pb[:, 0:NCMP], w2k[:, hh, :], hid[0][:, hh, g * NCMP:(g + 1) * NCMP], hh == 0, hh == 1, (w2k, hid[0]), (pb,))
                        k.cp("dve", kcT[:, g, 0:NCMP], pb[:, 0:NCMP], (pb,), (kcT,))
                        pb = nxt(abanks, "a")
                        for hh in range(2):
                            k.mm(pb[0:NCMP, 0:64], hid[1][:, hh, g * NCMP:(g + 1) * NCMP], w2v[:, hh, :], hh == 0, hh == 1, (w2v, hid[1]), (pb,))
                        k.cp("dve", vcaug[0:NCMP, g, 0:64], pb[0:NCMP, 0:64], (pb,), (vcaug,))
                    for g in range(NG):
                        k.dma(qnt[32:128, :, :], qn_d[b, :, 4 * g:4 * g + 4, :], (), (qnt,), qnt)
                        k.dma(qat[64:128, :, :], qn_d[b, 32:96, 4 * g:4 * g + 4, :], (), tuple(qa_r), qat)
                        k.dma(qat[32:64, :, :], qr_d[b, :, 4 * g:4 * g + 4, :], (), tuple(qa_r), qat)
                        def d2_setup(qc, g=g):
                            for h in range(4):
                                pb = nxt(sbanks, "s")
                                k.mm(pb[0:NCMP, :], kcT[:, g, 0:NCMP], qnt[:, h, qc * T:(qc + 1) * T], True, True, (kcT, qnt), (pb,))
                                e1 = nxt(em, "e")
                                e2 = e2b[h]
                                k.act(e1[0:NCMP, :], pb[0:NCMP, :], AF.Exp, (pb,), (e1,))
                                k.tt("pool", e2[0:NCMP, :], e1[0:NCMP, :], maskc[0:NCMP, qc * T:(qc + 1) * T], ALU.mult, (e1, maskc), (e2,))

                        def d2_qtile(qc, j, g=g):
                            qt = qc * 4 + j
                            pa = nxt(abanks, "a")
                            for h in range(4):
                                k.mm(pa[:, h * 97:(h + 1) * 97], e2b[h][0:NCMP, j * 128:(j + 1) * 128], vcaug[0:NCMP, g, 0:97], h == 0, h == 3,
                                     (e2b[h], vcaug), (pa,), skip=True)
                            p3 = pa[:, 0:388].rearrange("p (h c) -> p h c", h=4)
                            rd_ = nxt(rden, "n")
                            fc_ = fac[cnt["n"] % 3]
                            k.ts("dve", rd_[:, :], p3[:, :, 96], 1e-30, 0.0, ALU.max, ALU.add, (pa,), (rd_,))
                            rcp(rd_[:, :], rd_[:, :], (rd_,), (rd_,))
                            k.tt("dve", impn[:, :, :], p3[:, :, 64:96], rd_[:, :].unsqueeze(2).to_broadcast([128, 4, 32]), ALU.mult, (pa, rd_), (impn,))
                            P.op("dve", lambda e, o=imp[:, :], i=impn[:, :, :].rearrange("p h n -> p n h"): e.tensor_reduce(o, i, axis=AX.X, op=ALU.add),
                                 _rs((impn,)), _rs((imp,)))
                            k.tt("dve", imp[:, :], imp[:, :], tkmul[:, qt, :], ALU.mult, (imp, tkmul), (imp,))
                            k.tt("dve", imp[:, :], imp[:, :], tkadd[:, qt, :], ALU.add, (imp, tkadd), (imp,))
                            P.op("dve", lambda e, o=top8[:, :], i=imp[:, :]: e.max(o, i), _rs((imp,)), _rs((top8,)))
                            k.ts("dve", imp[:, :], imp[:, :], top8[:, 7:8], 1.0, ALU.is_ge, ALU.subtract, (imp, top8), (imp,))
                            k.ts("dve", selb[:, 0:32], imp[:, :], -NEG, 0.0, ALU.mult, ALU.add, (imp,), (selb,))
                            k.tr(tbank[:, 0:128], selb[:, :], ident[:, :], (selb, ident), (tbank,))
                            k.cp("dve", qat[0:32, :, qt * 128:(qt + 1) * 128], tbank[0:32, 0:128].unsqueeze(1).to_broadcast([32, 4, 128]), (tbank,), (qa_r[qc],))
                            k.tt("dve", fc_[:, :], rd_[:, :], gts[:, qt, 12 * g:12 * g + 12:3], ALU.mult, (rd_, gts), (fc_,))
                            k.tt("dve", oacc[:, qt, :].rearrange("p (h v) -> p h v", h=4), p3[:, :, 0:64],
                                 fc_[:, :].unsqueeze(2).to_broadcast([128, 4, 64]), ALU.mult, (pa, fc_), (oa_r[qc],))

                        d2_setup(0)
                        for j in range(4):
                            d2_qtile(0, j)
                        tiles = []
                        for qc in range(4):
                            for h in range(4):
                                pre = []
                                if qc < 3:
                                    if h == 0:
                                        pre.append(lambda qc=qc: d2_setup(qc + 1))
                                    pre.append(lambda qc=qc, h=h: d2_qtile(qc + 1, h))
                                nkt = 4 * qc + 4
                                lst = []
                                for kt in range(nkt):
                                    q0 = max(qc * T, kt * 128)
                                    ncol = (qc + 1) * T - q0
                                    masks = [(0, 0)] if kt * 128 >= qc * T else []
                                    pv = [(qt - 4 * qc, qt * 128 - q0) for qt in range(q0 // 128, 4 * qc + 4)]
                                    lst.append(dict(kT=ksT, br=0, kt=kt, q0=q0, ncol=ncol, masks=masks, pv=pv))
                                tiles.append(dict(h=h, qc=qc, br=1, lst=lst, pre=pre))
                                lst = []
                                for kt in range(max(0, 4 * qc - 4), 4 * qc + 4):
                                    qlo = max(kt, 4 * qc)
                                    qhi = min(kt + 4, 4 * qc + 3)
                                    q0 = qlo * 128
                                    ncol = (qhi - qlo + 1) * 128
                                    masks = []
                                    if qlo == kt:
                                        masks.append((0, 0))
                                    if qhi == kt + 4:
                                        masks.append((ncol - 128, 128))
                                    pv = [(qt - 4 * qc, (qt - qlo) * 128) for qt in range(qlo, qhi + 1)]
                                    lst.append(dict(kT=kwT, br=1, kt=kt, q0=q0, ncol=ncol, masks=masks, pv=pv))
                                tiles.append(dict(h=h, qc=qc, br=2, lst=lst, pre=[]))
                        flat = []
                        for tg_ in tiles:
                            po = None
                            for i_, t_ in enumerate(tg_["lst"]):
                                flat.append((tg_, t_, i_ == 0, i_ == len(tg_["lst"]) - 1))
                        LOOK = 2
                        pend = []
                        for idx in range(len(flat) + LOOK):
                            if idx < len(flat):
                                tg_, t_, isf, isl = flat[idx]
                                h = tg_["h"]
                                if isf:
                                    for th_ in tg_["pre"]:
                                        th_()
                                    tg_["po"] = nxt(abanks, "a")
                                ps_ = nxt(sbanks, "s")
                                ncol = t_["ncol"]
                                k.mm(ps_[:, 0:ncol], t_["kT"][:, g, t_["kt"] * 128:(t_["kt"] + 1) * 128], qat[:, h, t_["q0"]:t_["q0"] + ncol], True, True,
                                     (t_["kT"], qa_r[tg_["qc"]]), (ps_,))
                                pb_ = nxt(pmk, "p")
                                k.act(pb_[:, 0:ncol], ps_[:, 0:ncol], AF.Exp, (ps_,), (pb_,))
                                for (c0, m0) in t_["masks"]:
                                    k.tt("pool", pb_[:, c0:c0 + 128], pb_[:, c0:c0 + 128], tri[:, m0:m0 + 128], ALU.mult, (pb_, tri), (pb_,))
                                t_["pb"] = pb_
                            if idx >= LOOK:
                                tg_, t_, isf, isl = flat[idx - LOOK]
                                h, qc, po = tg_["h"], tg_["qc"], tg_["po"]
                                pb_ = t_["pb"]
                                for pi_, (j, off) in enumerate(t_["pv"]):
                                    k.mm(po[:, j * 65:(j + 1) * 65], pb_[:, off:off + 128], vaug[:, t_["kt"], t_["br"], g, :], isf and pi_ == 0, isl,
                                         (pb_, vaug), (po,), skip=True)
                                if isl:
                                    br = tg_["br"]
                                    p3 = po[:, 0:260].rearrange("p (j c) -> p j c", j=4)
                                    rd_ = nxt(rden, "n")
                                    fc_ = fac[cnt["n"] % 3]
                                    tm_ = tmp[cnt["n"] % 3]
                                    rcp(rd_[:, :], p3[:, :, 64], (po,), (rd_,))
                                    gcol = (4 * g + h) * 3 + br
                                    k.tt("dve", fc_[:, :], rd_[:, :], gts[:, 4 * qc:4 * qc + 4, gcol], ALU.mult, (rd_, gts), (fc_,))
                                    k.tt("dve", tm_[:, :, :], p3[:, :, 0:64], fc_[:, :].unsqueeze(2).to_broadcast([128, 4, 64]), ALU.mult, (po, fc_), (tm_,))
                                    k.tt("pool", oacc[:, 4 * qc:4 * qc + 4, h * 64:(h + 1) * 64], oacc[:, 4 * qc:4 * qc + 4, h * 64:(h + 1) * 64],
                                         tm_[:, :, :], ALU.add, (oa_r[qc], tm_), (oa_r[qc],))
                        k.act(obf[:, :, :], oacc[:, :, :], AF.Copy, tuple(oa_r), (obf,))
                        k.dma(o_d[b * S:(b + 1) * S, g * 256:(g + 1) * 256].rearrange("(j p) v -> p j v", p=128), obf[:, :, :], (obf,), (), obf)
                P.flush()
        if "E" in phases:
            phase_ffn(1, h1_d, y_d, True, True)
    return nc


def _consts():
    c = {}
    c["c_ident"] = np.eye(128, dtype=np.float32)
    kk = np.arange(128)[:, None]
    qq = np.arange(128)[None, :]
    c["c_tri"] = np.concatenate([(kk <= qq), (kk > qq)], axis=1).astype(np.float32)
    cc = np.arange(128)[:, None]
    tt = np.arange(S)[None, :]
    c["c_maskc"] = ((16 * cc + 31 <= tt) & (cc < NCMP)).astype(np.float32)
    half = 12
    inv = (np.float32(500000.0) ** (-np.arange(half, dtype=np.float32) * np.float32(2.0) / np.float32(24))).astype(np.float32)
    ang = (np.arange(S, dtype=np.float32)[None, :] * inv[:, None]).astype(np.float32)
    cs = np.cos(ang.astype(np.float64)).astype(np.float32)
    sn = np.sin(ang.astype(np.float64)).astype(np.float32)
    rc = np.ones((128, S), np.float32)
    rs = np.zeros((128, S), np.float32)
    rc[32:44] = cs
    rc[44:56] = cs
    rs[32:44] = -sn
    rs[44:56] = sn
    c["c_ropec"] = rc
    c["c_ropes"] = rs
    pm = np.zeros((128, 64), np.float32)
    for d in range(12):
        pm[32 + d + 12, 32 + d] = 1.0
        pm[32 + d, 32 + d + 12] = 1.0
    c["c_pm"] = pm
    c["c_esel"] = (np.arange(S)[None, :] // 64 == np.arange(32)[:, None]).astype(np.float32)
    ncmp = NCMP
    cs_ = np.arange(ncmp) * 16
    ss_ = np.arange(32) * 64
    ovl = ((cs_[:, None] < ss_[None, :] + 64) & (cs_[:, None] + 32 > ss_[None, :])).astype(np.float32)
    o = np.zeros((128, 33), np.float32)
    o[:ncmp, :32] = ovl
    o[:ncmp, 32] = 1.0
    c["c_ovl"] = o
    t = np.arange(S)[:, None]
    n = np.arange(32)[None, :]
    cur = t // 64
    forced = (n == 0) | (n == cur) | (n == cur - 1)
    vis = (n * 64 <= t)
    c["c_tkadd"] = np.where(forced, 1e9, np.where(vis, 0.0, -1e9)).astype(np.float32)
    c["c_tkmul"] = (vis & ~forced).astype(np.float32)
    return c


def _kc(w):
    K_, M = w.shape
    return np.ascontiguousarray(w.reshape(K_ // 128, 128, M).transpose(1, 0, 2).reshape(128, -1))


def _prep_weights(inp):
    f = lambda a: np.ascontiguousarray(np.asarray(a, dtype=np.float32))
    w = {}
    ng = inp["norm_g"]
    gs = np.stack([ng[0, 0], ng[0, 1], inp["kv_norm_g"], ng[1, 0], ng[1, 1]], 0)
    w["g_all"] = f(gs.reshape(5, 8, 128).transpose(2, 0, 1))
    w["g_fin"] = f(inp["final_g"].reshape(1, D))
    w["w_in"] = _kc(f(inp["a_w_in"][0]))
    w["conv_w"] = f(inp["a_conv_w"][0].reshape(4, NBLK, RB).transpose(2, 1, 0).reshape(RB, NBLK * 4))
    sm = np.stack([inp["a_conv_b"][0].reshape(NBLK, RB).T, inp["a_b_ra"][0].T, inp["a_b_ix"][0].T,
                   inp["a_lambda"][0].reshape(NBLK, RB).T], 1)
    w["a_small"] = f(sm)
    w["w_ra"] = f(inp["a_w_ra"][0].transpose(1, 0, 2).reshape(RB, NBLK * RB))
    w["w_ix"] = f(inp["a_w_ix"][0].transpose(1, 0, 2).reshape(RB, NBLK * RB))
    w["w_out"] = f(inp["a_w_out"][0].reshape(NBLK, RB, D).transpose(1, 0, 2).reshape(RB, NBLK * D))
    for l in range(2):
        w["w_gu%d" % l] = _kc(f(inp["ffn_w_gu"][l]))
        w["w_dn%d" % l] = f(inp["ffn_w_down"][l].reshape(NFC, 128, D).transpose(1, 0, 2).reshape(128, NFC * D))
    kv = f(inp["kv_w"])
    kcmp_, vcmp_, ksel_, vsel_, kwin_, vwin_ = kv[:, 0:384], kv[:, 384:640], kv[:, 640:1024], kv[:, 1024:1280], kv[:, 1280:1664], kv[:, 1664:1920]
    w["w_kc"] = _kc(kcmp_)
    w["w_vc"] = _kc(vcmp_)

    def aug(m, n):
        z = np.zeros((m.shape[0], n, 128), np.float32)
        z[:, :, 32:] = m.reshape(m.shape[0], n, 96)
        return z.reshape(m.shape[0], n * 128)
    w["w_ks"] = _kc(aug(ksel_, NG))
    w["w_kw"] = _kc(aug(kwin_, NG))
    w["w_v"] = _kc(np.concatenate([vsel_, vwin_], 1))
    wq = f(inp["b_w_q"][0])
    w["w_q"] = _kc(aug(wq[:, :NH * DQK], NH))
    w["w_qg"] = _kc(wq[:, NH * DQK:])
    w["gate_b"] = f(inp["b_gate_bias"][0].reshape(1, 48))
    w["w_o"] = _kc(f(inp["b_w_o"][0]))
    w["w1k"] = f(inp["cmp_w1_k"].reshape(32, 96, 256).transpose(1, 0, 2).reshape(96, 32 * 256))
    w["w1v"] = f(inp["cmp_w1_v"].reshape(32, 64, 256).transpose(1, 0, 2).reshape(64, 32 * 256))
    w2k = np.zeros((256, 128), np.float32)
    w2k[:, 32:] = inp["cmp_w2_k"]
    w["w2k"] = f(w2k.reshape(2, 128, 128).transpose(1, 0, 2).reshape(128, 256))
    w["w2v"] = f(np.asarray(inp["cmp_w2_v"], np.float32).reshape(2, 128, 64).transpose(1, 0, 2).reshape(128, 128))
    w["posk"] = f(np.asarray(inp["cmp_pos_k"]).T)
    w["posv"] = f(np.asarray(inp["cmp_pos_v"]).T)
    return w


_CACHE = {}


def kernel(**inputs):
    inp = {k_: np.asarray(v) for k_, v in inputs.items()}
    ncores = 8
    nseq = inp["x"].shape[0] // ncores
    if "nc" not in _CACHE:
        _CACHE["nc"] = build_program(nseq)
    nc = _CACHE["nc"]
    shared = _consts()
    shared.update(_prep_weights(inp))
    x = np.ascontiguousarray(inp["x"], dtype=np.float32).reshape(ncores, nseq * S, D)
    in_maps = []
    for c in range(ncores):
        m = dict(shared)
        m["x"] = x[c]
        in_maps.append(m)
    res = run_bass_kernel_spmd(nc, in_maps, core_ids=list(range(ncores)))
    out = np.stack([np.asarray(r["y"]) for r in res.results], 0)
    return out.reshape(inp["x"].shape).astype(np.float32)
```

```python
import contextlib
import os
import numpy as np
import concourse.bass as bass
import concourse.mybir as mybir
from concourse.bass_utils import run_bass_kernel_spmd

F32 = mybir.dt.float32
BF16 = mybir.dt.bfloat16
AF = mybir.ActivationFunctionType
ALU = mybir.AluOpType
AX = mybir.AxisListType

D = 1024
S = 2048
T = 512
DRNN = 1536
NBLK = 16
RB = 96
DFF = 2816
NFC = 22
NH = 16
NG = 4
DQK = 96
DV = 64
NCMP = 127
EPS = 1e-6
NEG = -30000.0
GK = 0.7978845608028654
GC = 0.044715


class Res:
    __slots__ = ("name", "last_w", "readers")

    def __init__(self, name):
        self.name = name
        self.last_w = None
        self.readers = []


class Op:
    __slots__ = ("eng", "fn", "waits", "signal", "is_dma", "key", "pos", "sigidx", "is_nop")

    def __init__(self, eng, fn, is_dma=False, key=None):
        self.eng = eng
        self.fn = fn
        self.waits = []
        self.signal = False
        self.is_dma = is_dma
        self.key = key
        self.pos = -1
        self.sigidx = -1
        self.is_nop = False


ENGS = ("pe", "act", "dve", "pool", "sp")


class Prog:
    def __init__(self, nc, stack):
        self.nc = nc
        self.stack = stack
        self.streams = {e: [] for e in ENGS}
        self.emitted = {e: 0 for e in ENGS}
        self.nsig = {e: 0 for e in ENGS}
        self.npos = {e: 0 for e in ENGS}
        self.known = {e: {} for e in ENGS}
        self.dma_cnt = {}
        self.esem = {e: stack.enter_context(nc.semaphore("s_" + e)) for e in ENGS}
        self.ksem = {}

    def res(self, name="r"):
        return Res(name)

    def _dep(self, op, d, raw):
        if d is None or d is op:
            return
        E = op.eng
        k = self.known[E]
        if d.is_dma:
            src = ("dma", d.key)
            cnt = self.dma_cnt[d.key]
            if k.get(src, -1) >= cnt:
                return
            k[src] = cnt
            op.waits.append(("dma", d.key, cnt))
            return
        if d.eng == E and not op.is_dma:
            if E == "pe":
                return
        if k.get(d.eng, -1) >= d.pos:
            return
        k[d.eng] = d.pos
        d.signal = True
        op.waits.append(("eng", d))

    def op(self, eng, fn, reads=(), writes=(), is_dma=False, key=None):
        o = Op(eng, fn, is_dma, key)
        if is_dma:
            self.dma_cnt.setdefault(key, 0)
        else:
            o.pos = self.npos[eng]
            self.npos[eng] += 1
        for r in reads:
            self._dep(o, r.last_w, True)
        for w in writes:
            self._dep(o, w.last_w, False)
            for rd in w.readers:
                self._dep(o, rd, False)
        if is_dma:
            self.dma_cnt[key] += 1
            o.pos = self.dma_cnt[key]
        for r in reads:
            r.readers.append(o)
        for w in writes:
            w.last_w = o
            w.readers = []
        self.streams[eng].append(o)
        return o

    def dma(self, q, out, in_, reads=(), writes=(), key=None):
        key = id(key)
        return self.op(q, lambda e: e.dma_start(out=out, in_=in_), reads, writes, is_dma=True, key=key)

    def barrier(self):
        lasts = []
        for e in ENGS:
            comp = [o for o in self.streams[e] if not o.is_dma and not o.is_nop]
            if comp:
                lasts.append(comp[-1])
        for e in ENGS:
            o = Op(e, lambda eng: eng.nop())
            o.is_nop = True
            o.pos = self.npos[e]
            self.npos[e] += 1
            k = self.known[e]
            for d in lasts:
                if k.get(d.eng, -1) >= d.pos:
                    continue
                k[d.eng] = d.pos
                d.signal = True
                o.waits.append(("eng", d))
            for key, cnt in self.dma_cnt.items():
                src = ("dma", key)
                if k.get(src, -1) >= cnt:
                    continue
                k[src] = cnt
                o.waits.append(("dma", key, cnt))
            self.streams[e].append(o)

    def flush(self):
        self.barrier()
        nc = self.nc
        for e in ENGS:
            n = self.nsig[e]
            for o in self.streams[e]:
                if not o.is_dma and o.signal:
                    n += 1
                    o.sigidx = n
            self.nsig[e] = n
        for key in self.dma_cnt:
            if key not in self.ksem:
                self.ksem[key] = self.stack.enter_context(nc.semaphore("d_%d" % len(self.ksem)))
        esem, ksem = self.esem, self.ksem
        streams = self.streams

        def run(en, eng):
            for o in streams[en]:
                for w in o.waits:
                    if w[0] == "dma":
                        eng.wait_ge(ksem[w[1]], 16 * w[2])
                    else:
                        eng.wait_ge(esem[w[1].eng], w[1].sigidx)
                ins = o.fn(eng)
                if o.is_dma:
                    ins.then_inc(ksem[o.key], 16)
                elif o.signal:
                    ins.then_inc(esem[en], 1)

        with nc.Block() as block:
            @block.tensor
            def _(e):
                run("pe", e)

            @block.scalar
            def _(e):
                run("act", e)

            @block.vector
            def _(e):
                run("dve", e)

            @block.gpsimd
            def _(e):
                run("pool", e)

            @block.sync
            def _(e):
                run("sp", e)
        self.streams = {e: [] for e in ENGS}


class Tl:
    __slots__ = ("t", "r")

    def __init__(self, t, r):
        self.t = t
        self.r = r

    def __getitem__(self, k):
        return self.t[k]


def _rs(xs):
    out = []
    for x in xs:
        if x is None:
            continue
        out.append(x.r if isinstance(x, Tl) else x)
    return out


class K:
    def __init__(self, nc, P):
        self.nc = nc
        self.P = P
        self.n = 0
        self.rr = 0

    def sb(self, st, shape, dt=F32, name=None):
        self.n += 1
        nm = "%s_%d" % (name or "t", self.n)
        return Tl(st.enter_context(self.nc.sbuf_tensor(nm, list(shape), dt)), Res(nm))

    def ps(self, st, shape, dt=F32, name=None):
        self.n += 1
        nm = "%s_%d" % (name or "p", self.n)
        return Tl(st.enter_context(self.nc.psum_tensor(nm, list(shape), dt)), Res(nm))

    def mm(self, out, lhsT, rhs, start, stop, rd, wr, skip=False):
        if skip:
            f = lambda e: e.matmul(out, lhsT=lhsT, rhs=rhs, start=start, stop=stop, skip_group_check=True)
        else:
            f = lambda e: e.matmul(out, lhsT=lhsT, rhs=rhs, start=start, stop=stop)
        self.P.op("pe", f, _rs(rd), _rs(wr))

    def tr(self, out, in_, ident, rd, wr):
        self.P.op("pe", lambda e: e.transpose(out, in_, ident), _rs(rd), _rs(wr))

    def act(self, out, in_, func, rd, wr, scale=1.0, bias=None, accum=None):
        kw = {}
        if bias is not None:
            kw["bias"] = bias
        if accum is not None:
            kw["accum_out"] = accum
        self.P.op("act", lambda e: e.activation(out=out, in_=in_, func=func, scale=scale, **kw), _rs(rd), _rs(wr))

    def cp(self, eng, out, in_, rd, wr):
        self.P.op(eng, lambda e: e.tensor_copy(out, in_), _rs(rd), _rs(wr))

    def ts(self, eng, out, in0, s1, s2, op0, op1, rd, wr):
        self.P.op(eng, lambda e: e.tensor_scalar(out, in0, s1, s2, op0, op1), _rs(rd), _rs(wr))

    def tt(self, eng, out, in0, in1, op, rd, wr):
        self.P.op(eng, lambda e: e.tensor_tensor(out, in0, in1, op), _rs(rd), _rs(wr))

    def stt(self, out, in0, scalar, in1, op0, op1, rd, wr):
        self.P.op("dve", lambda e: e.scalar_tensor_tensor(out=out, in0=in0, scalar=scalar, in1=in1, op0=op0, op1=op1),
                  _rs(rd), _rs(wr))

    def memset(self, eng, out, val, wr):
        self.P.op(eng, lambda e: e.memset(out, val), (), _rs(wr))

    def dma(self, out, in_, rd, wr, key, q="sp"):
        self.P.dma(q, out, in_, _rs(rd), _rs(wr), key=(key.r if isinstance(key, Tl) else key))

    def ceng(self):
        self.rr += 1
        return ("dve", "pool")[self.rr % 2]

    def wload(self, stages, dst_fn, src_fn, ncol, dst, piece=2048, gain=None, npart=128):
        c0 = 0
        while c0 < ncol:
            c1 = min(ncol, c0 + piece)
            stg = stages[self.rr % len(stages)]
            self.dma(stg[0:npart, 0:c1 - c0], src_fn(c0, c1), (), (stg,), stg)
            eng = self.ceng()
            if gain is None:
                self.cp(eng, dst_fn(c0, c1), stg[0:npart, 0:c1 - c0], (stg,), (dst,))
            else:
                self.ts(eng, dst_fn(c0, c1), stg[0:npart, 0:c1 - c0], gain, 0.0, ALU.mult, ALU.add, (stg,), (dst,))
            c0 = c1


def norm_transpose(k, xin, hnbs, hnT, ss, rstd, mhalf, ident, tbanks):
    for j in range(4):
        hnb = hnbs[j % len(hnbs)]
        tb = tbanks[j % len(tbanks)]
        k.act(hnb[:, :], xin[:, j, :], AF.Square, (xin,), (hnb, ss), accum=ss[:, j:j + 1])
        k.ts("dve", rstd[:, j:j + 1], ss[:, j:j + 1], 1.0 / D, EPS, ALU.mult, ALU.add, (ss,), (rstd,))
        k.tt("pool", rstd[:, j:j + 1], rstd[:, j:j + 1], mhalf[:, 0:1], ALU.pow, (rstd, mhalf), (rstd,))
        if j % 2 == 0:
            k.ts("dve", hnb[:, :], xin[:, j, :], rstd[:, j:j + 1], 0.0, ALU.mult, ALU.add, (xin, rstd), (hnb,))
        else:
            k.act(hnb[:, :], xin[:, j, :], AF.Identity, (xin, rstd), (hnb,), scale=rstd[:, j:j + 1])
        for kc in range(8):
            k.tr(tb[:, kc * 128:(kc + 1) * 128], hnb[:, kc * 128:(kc + 1) * 128], ident[:, :], (hnb, ident), (tb,))
        src = tb[:, :].rearrange("p (a b) -> p a b", a=8)
        if j % 2 == 0:
            k.act(hnT[:, :, j * 128:(j + 1) * 128], src, AF.Copy, (tb,), (hnT,))
        else:
            k.cp("dve", hnT[:, :, j * 128:(j + 1) * 128], src, (tb,), (hnT,))


def ffn_chunk(k, hnT, actT, wgu, wdn, banks, bi, tg, tA):
    for fc in range(NFC):
        pg = banks[bi % len(banks)]
        pv = banks[(bi + 1) % len(banks)]
        bi += 2
        for kc in range(8):
            k.mm(pg[:, :], wgu[:, kc, fc * 128:(fc + 1) * 128], hnT[:, kc, :], kc == 0, kc == 7, (wgu, hnT), (pg,))
        for kc in range(8):
            k.mm(pv[:, :], wgu[:, kc, DFF + fc * 128:DFF + (fc + 1) * 128], hnT[:, kc, :], kc == 0, kc == 7, (wgu, hnT), (pv,))
        t1 = tg[fc % len(tg)]
        t2 = tA[fc % len(tA)]
        k.act(t1[:, :], pg[:, :], AF.Tanh, (pg,), (t1,), scale=0.5)
        k.stt(t2[:, :], t1[:, :], 1.0, pg[:, :], ALU.add, ALU.mult, (t1, pg), (t2,))
        k.stt(actT[:, fc, :], t2[:, :], 0.5, pv[:, :], ALU.mult, ALU.mult, (t2, pv), (actT,))
    return bi


def ffn_down(k, actT, wdn, xin, banks, bi):
    for j in range(4):
        for hf in range(2):
            pb = banks[bi % len(banks)]
            bi += 1
            for fc in range(NFC):
                k.mm(pb[:, :], actT[:, fc, j * 128:(j + 1) * 128], wdn[:, fc, hf * 512:(hf + 1) * 512], fc == 0, fc == NFC - 1,
                     (actT, wdn), (pb,))
            k.tt("dve", xin[:, j, hf * 512:(hf + 1) * 512], pb[:, :], xin[:, j, hf * 512:(hf + 1) * 512], ALU.add, (pb, xin), (xin,))
    return bi


def tok_view(ap2d, t0):
    return ap2d[t0:t0 + T, :].rearrange("(j p) d -> p j d", p=128)


def build_program(nseq, phases="ABCDE", debug=False):
    NT = nseq * S
    nc = bass.Bass("TRN2", target_bir_lowering=False)
    dt_in = {}

    def din(name, shape):
        dt_in[name] = shape
        return nc.dram_tensor(name, list(shape), F32, kind="ExternalInput").ap()

    def dscr(name, shape, dt):
        kind = "ExternalOutput" if debug else "Internal"
        return nc.dram_tensor(name, list(shape), dt, kind=kind).ap()

    x_d = din("x", (NT, D))
    identf_d = din("c_ident", (128, 128))
    tri_d = din("c_tri", (128, 256))
    maskc_d = din("c_maskc", (128, S))
    ropec_d = din("c_ropec", (128, S))
    ropes_d = din("c_ropes", (128, S))
    pm_d = din("c_pm", (128, 64))
    esel_d = din("c_esel", (32, S))
    ovl_d = din("c_ovl", (128, 33))
    tkadd_d = din("c_tkadd", (S, 32))
    tkmul_d = din("c_tkmul", (S, 32))
    g_d = din("g_all", (128, 5, 8))
    gfin_d = din("g_fin", (1, D))
    win_d = din("w_in", (128, 8 * 3072))
    cw_d = din("conv_w", (RB, NBLK * 4))
    sm_d = din("a_small", (RB, 4, NBLK))
    wra_d = din("w_ra", (RB, NBLK * RB))
    wix_d = din("w_ix", (RB, NBLK * RB))
    wout_d = din("w_out", (RB, NBLK * D))
    wgu_d = [din("w_gu%d" % l, (128, 8 * 2 * DFF)) for l in range(2)]
    wdn_d = [din("w_dn%d" % l, (128, NFC * D)) for l in range(2)]
    wkc_d = din("w_kc", (128, 8 * NG * 96))
    wvc_d = din("w_vc", (128, 8 * NG * 64))
    wks_d = din("w_ks", (128, 8 * NG * 128))
    wkw_d = din("w_kw", (128, 8 * NG * 128))
    wv_d = din("w_v", (128, 8 * 512))
    wq_d = din("w_q", (128, 8 * NH * 128))
    wqg_d = din("w_qg", (128, 8 * 48))
    gb_d = din("gate_b", (1, 48))
    wo_d = din("w_o", (128, 8 * D))
    w1k_d = din("w1k", (96, 32 * 256))
    w1v_d = din("w1v", (64, 32 * 256))
    w2k_d = din("w2k", (128, 2 * 128))
    w2v_d = din("w2v", (128, 2 * 64))
    posk_d = din("posk", (96, 32))
    posv_d = din("posv", (64, 32))

    y_d = nc.dram_tensor("y", [NT, D], F32, kind="ExternalOutput").ap()
    hmid_d = dscr("s_hmid", (NT, D), F32)
    h1_d = dscr("s_h1", (NT, D), F32)
    kcmp_d = dscr("s_kcmp", (nseq, 96, NG, S), BF16)
    vcmp_d = dscr("s_vcmp", (nseq, 64, NG, S), BF16)
    ksel_d = dscr("s_ksel", (nseq, 96, NG, S), BF16)
    kwin_d = dscr("s_kwin", (nseq, 96, NG, S), BF16)
    v_d = dscr("s_v", (NT, 520), BF16)
    qn_d = dscr("s_qn", (nseq, 96, NH, S), BF16)
    qr_d = dscr("s_qr", (nseq, 32, NH, S), BF16)
    gt_d = dscr("s_gt", (NT, 48), F32)
    o_d = dscr("s_o", (NT, D), BF16)

    with contextlib.ExitStack() as top:
        P = Prog(nc, top)
        k = K(nc, P)

        ident = k.sb(top, [128, 128], BF16, "ident")
        mhalf = k.sb(top, [128, 4], F32, "mhalf")
        gall = k.sb(top, [128, 5, 8], F32, "gall")
        with contextlib.ExitStack() as st:
            tmp = k.sb(st, [128, 128], F32)
            k.dma(tmp[:, :], identf_d, (), (tmp,), tmp)
            k.cp("dve", ident[:, :], tmp[:, :], (tmp,), (ident,))
            k.memset("dve", mhalf[:, :], -0.5, (mhalf,))
            k.dma(gall[:, :, :], g_d, (), (gall,), gall)
            P.flush()

        if "A" in phases:
            with contextlib.ExitStack() as st:
                win = k.sb(st, [128, 8, 3072], BF16, "win")
                wout = k.sb(st, [RB, NBLK, D], BF16, "wout")
                wra = k.sb(st, [RB, NBLK, RB], BF16, "wra")
                wix = k.sb(st, [RB, NBLK, RB], BF16, "wix")
                dg = k.sb(st, [RB, NBLK * 4, RB], BF16, "dg")
                cw = k.sb(st, [RB, NBLK * 4], F32, "cw")
                sm = k.sb(st, [RB, 4, NBLK], F32, "sm")
                hb = k.sb(st, [RB, 2, NBLK], F32, "hb")
                cc = k.sb(st, [RB, 2, NBLK], F32, "cc")
                cx = k.sb(st, [RB, NBLK, 4], BF16, "cx")
                hcar = k.sb(st, [RB, NBLK], F32, "hcar")
                one = k.sb(st, [RB, 1], F32, "one")
                k.memset("dve", one[:, :], 1.0, (one,))
                with contextlib.ExitStack() as st2:
                    stages = [k.sb(st2, [128, 3072], F32, "stg") for _ in range(3)]
                    for kc in range(8):
                        k.wload(stages, lambda a, b, kc=kc: win[:, kc, a:b], lambda a, b, kc=kc: win_d[:, kc * 3072 + a:kc * 3072 + b],
                                3072, win, piece=3072, gain=gall[:, 0, kc:kc + 1])
                    for n in range(NBLK):
                        k.wload(stages, lambda a, b, n=n: wout[:, n, a:b], lambda a, b, n=n: wout_d[:, n * D + a:n * D + b], D, wout,
                                piece=D, npart=RB)
                    k.wload(stages, lambda a, b: wra[:, :, :].rearrange("p n j -> p (n j)")[:, a:b], lambda a, b: wra_d[:, a:b],
                            NBLK * RB, wra, piece=NBLK * RB, npart=RB)
                    k.wload(stages, lambda a, b: wix[:, :, :].rearrange("p n j -> p (n j)")[:, a:b], lambda a, b: wix_d[:, a:b],
                            NBLK * RB, wix, piece=NBLK * RB, npart=RB)
                    k.dma(cw[:, :], cw_d, (), (cw,), cw)
                    k.dma(sm[:, :, :], sm_d, (), (sm,), sm)
                    for i in range(NBLK * 4):
                        k.ts(k.ceng(), dg[:, i, :], ident[0:RB, 0:RB], cw[:, i:i + 1], 0.0, ALU.mult, ALU.add, (ident, cw), (dg,))
                    k.ts("dve", hb[:, :, :], sm[:, 1:3, :], -1.0, 0.0, ALU.mult, ALU.add, (sm,), (hb,))
                    sg = k.sb(st2, [RB, NBLK], F32)
                    k.act(sg[:, :], sm[:, 3, :], AF.Exp, (sm,), (sg,), scale=-1.0)
                    k.ts("dve", sg[:, :], sg[:, :], 1.0, 1.0, ALU.mult, ALU.add, (sg,), (sg,))
                    k.act(sg[:, :], sg[:, :], AF.Ln, (sg,), (sg,))
                    k.ts("dve", cc[:, 0, :], sg[:, :], -8.0, 0.0, ALU.mult, ALU.add, (sg,), (cc,))
                    k.ts("dve", cc[:, 1, :], sg[:, :], -16.0, 0.0, ALU.mult, ALU.add, (sg,), (cc,))
                    P.flush()
                if os.environ.get("KSTOP") == "A0":
                    return nc

                xins = [k.sb(st, [128, 4, D], F32, "xin") for _ in range(2)]
                hnb = [k.sb(st, [128, D], BF16, "hnb") for _ in range(2)]
                hnTs = [k.sb(st, [128, 8, T], BF16, "hnT") for _ in range(2)]
                recy = k.sb(st, [RB, NBLK, T], BF16, "recy")
                ss = k.sb(st, [128, 4], F32)
                rstd = k.sb(st, [128, 4], F32)
                NSET = 2
                xbT = [k.sb(st, [RB, T + 4], BF16) for _ in range(NSET)]
                xcT = [k.sb(st, [RB, T], BF16) for _ in range(NSET)]
                trr = [k.sb(st, [RB, T], F32) for _ in range(NSET)]
                tii = [k.sb(st, [RB, T], F32) for _ in range(NSET)]
                aa = [k.sb(st, [RB, T], F32) for _ in range(NSET)]
                a2 = [k.sb(st, [RB, T], F32) for _ in range(NSET)]
                hs = [k.sb(st, [RB, T], F32) for _ in range(NSET)]
                sq = [k.sb(st, [RB, T], F32) for _ in range(NSET)]
                th = [k.sb(st, [RB, T], F32) for _ in range(NSET)]
                banks = [k.ps(st, [128, 512], F32, "bk") for _ in range(4)]
                ybanks = [k.ps(st, [128, 512], F32, "yb") for _ in range(2)]
                tbanks = [k.ps(st, [128, 1024], BF16, "tb") for _ in range(2)]

                def rcp(out, in_, rd, wr):
                    P.op("dve", lambda e: e.reciprocal(out, in_), _rs(rd), _rs(wr))
                bi = 0
                ci = 0
                KNCH = int(os.environ.get("KNCH", "4"))
                KNBLK = int(os.environ.get("KNBLK", "16"))
                KSTOP = os.environ.get("KSTOP", "")
                nchA = min(KNCH, S // T)
                chunksA = [(b, c) for b in range(nseq) for c in range(nchA)]
                k.dma(xins[0][:, :, :], tok_view(x_d, chunksA[0][0] * S + chunksA[0][1] * T), (), (xins[0],), xins[0])
                for b in range(nseq):
                    k.memset("pool", cx[:, :, :], 0.0, (cx,))
                    k.memset("pool", hcar[:, :], 0.0, (hcar,))
                    for c in range(nchA):
                        t0 = b * S + c * T
                        xin = xins[ci % 2]
                        hnT = hnTs[ci % 2]
                        ci += 1
                        if ci < len(chunksA):
                            nb_, nc_ = chunksA[ci]
                            k.dma(xins[ci % 2][:, :, :], tok_view(x_d, nb_ * S + nc_ * T), (), (xins[ci % 2],), xins[ci % 2])
                        norm_transpose(k, xin, hnb, hnT, ss, rstd, mhalf, ident, tbanks)
                        def blk(n, xin=None, hnT=hnT):
                            s_ = n % NSET
                            bA = banks[(2 * n) % 4]
                            bB = banks[(2 * n + 1) % 4]
                            pxb, pr, pcv, pi = bA, bA, bB, bB
                            pyb = ybanks[n % 2]
                            er, ei, ey = trr[s_], tii[s_], th[s_]
                            for kc in range(8):
                                k.mm(pxb[0:RB, :], win[:, kc, n * RB:(n + 1) * RB], hnT[:, kc, :], kc == 0, kc == 7, (win, hnT), (pxb,))
                            for kc in range(8):
                                k.mm(pyb[0:RB, :], win[:, kc, DRNN + n * RB:DRNN + (n + 1) * RB], hnT[:, kc, :], kc == 0, kc == 7,
                                     (win, hnT), (pyb,))
                            yield
                            k.cp("pool", xbT[s_][:, 0:4], cx[:, n, :], (cx,), (xbT[s_],))
                            k.cp("dve", xbT[s_][:, 4:4 + T], pxb[0:RB, :], (pxb,), (xbT[s_],))
                            k.cp("pool", cx[:, n, :], xbT[s_][:, T:T + 4], (xbT[s_],), (cx,))
                            k.act(sq[s_][:, :], pyb[0:RB, :], AF.Square, (pyb,), (sq[s_],))
                            yield
                            for t in range(4):
                                k.mm(pcv[0:RB, :], dg[:, n * 4 + t, :], xbT[s_][:, 1 + t:1 + t + T], t == 0, t == 3, (dg, xbT[s_]), (pcv,))
                            k.ts("pool", sq[s_][:, :], sq[s_][:, :], GC, 1.0, ALU.mult, ALU.add, (sq[s_],), (sq[s_],))
                            yield
                            k.ts("dve", xcT[s_][:, :], pcv[0:RB, :], sm[:, 0, n:n + 1], 0.0, ALU.add, ALU.add, (pcv, sm), (xcT[s_],))
                            k.tt("dve", sq[s_][:, :], sq[s_][:, :], pyb[0:RB, :], ALU.mult, (sq[s_], pyb), (sq[s_],))
                            yield
                            k.mm(pr[0:RB, :], wra[:, n, :], xcT[s_][:, :], True, True, (wra, xcT[s_]), (pr,))
                            k.mm(pi[0:RB, :], wix[:, n, :], xcT[s_][:, :], True, True, (wix, xcT[s_]), (pi,))
                            k.act(ey[:, :], sq[s_][:, :], AF.Exp, (sq[s_],), (ey,), scale=-2.0 * GK)
                            yield
                            k.act(er[:, :], pr[0:RB, :], AF.Exp, (pr, hb), (er,), scale=-1.0, bias=hb[:, 0, n:n + 1])
                            k.act(ei[:, :], pi[0:RB, :], AF.Exp, (pi, hb), (ei,), scale=-1.0, bias=hb[:, 1, n:n + 1])
                            k.act(ey[:, :], ey[:, :], AF.Ln, (ey, one), (ey,), bias=one[:, 0:1])
                            yield
                            k.act(er[:, :], er[:, :], AF.Ln, (er, one), (er,), bias=one[:, 0:1])
                            k.act(ei[:, :], ei[:, :], AF.Ln, (ei, one), (ei,), bias=one[:, 0:1])
                            k.act(ey[:, :], ey[:, :], AF.Exp, (ey,), (ey,), scale=-1.0)
                            yield
                            k.act(er[:, :], er[:, :], AF.Exp, (er,), (er,), scale=-1.0)
                            k.tt("dve", ey[:, :], ey[:, :], pyb[0:RB, :], ALU.mult, (ey, pyb), (ey,))
                            yield
                            k.act(aa[s_][:, :], er[:, :], AF.Exp, (er, cc), (aa[s_],), scale=cc[:, 0, n:n + 1])
                            k.act(a2[s_][:, :], er[:, :], AF.Exp, (er, cc), (a2[s_],), scale=cc[:, 1, n:n + 1])
                            yield
                            k.ts("pool", a2[s_][:, :], a2[s_][:, :], -1.0, 1.0, ALU.mult, ALU.add, (a2[s_],), (a2[s_],))
                            k.ts("pool", a2[s_][:, :], a2[s_][:, :], 1.0, 1e-12, ALU.min, ALU.max, (a2[s_],), (a2[s_],))
                            yield
                            k.act(a2[s_][:, :], a2[s_][:, :], AF.Ln, (a2[s_],), (a2[s_],))
                            yield
                            k.stt(ei[:, :], a2[s_][:, :], 0.5, ei[:, :], ALU.mult, ALU.subtract, (a2[s_], ei), (ei,))
                            yield
                            k.act(ei[:, :], ei[:, :], AF.Exp, (ei,), (ei,))
                            yield
                            k.tt("dve", ei[:, :], ei[:, :], xcT[s_][:, :], ALU.mult, (ei, xcT[s_]), (ei,))
                            k.P.op("dve", (lambda e, o=hs[s_][:, :], d0=aa[s_][:, :], d1=ei[:, :], i0=hcar[:, n:n + 1]:
                                           e.tensor_tensor_scan(o, d0, d1, i0, ALU.mult, ALU.add)),
                                   _rs((aa[s_], ei, hcar)), _rs((hs[s_],)))
                            k.cp("pool", hcar[:, n:n + 1], hs[s_][:, T - 1:T], (hs[s_],), (hcar,))
                            k.tt("dve", recy[:, n, :], ey[:, :], hs[s_][:, :], ALU.mult, (ey, hs[s_]), (recy,))
                            yield

                        active = []
                        nxt_n = 0
                        nblk = min(KNBLK, NBLK)
                        while active or nxt_n < nblk:
                            if nxt_n < nblk and len(active) < NSET and (not active or active[-1][1] >= int(os.environ.get('KSKEW', '8'))):
                                active.append([blk(nxt_n), 0])
                                nxt_n += 1
                            for ent in list(active):
                                try:
                                    next(ent[0])
                                    ent[1] += 1
                                except StopIteration:
                                    active.remove(ent)
                        for j in range(4):
                            for hf in range(2):
                                pb = banks[bi % 4]
                                bi += 1
                                for n in range(NBLK):
                                    k.mm(pb[:, :], recy[:, n, j * 128:(j + 1) * 128], wout[:, n, hf * 512:(hf + 1) * 512], n == 0, n == NBLK - 1,
                                         (recy, wout), (pb,))
                                k.tt("dve", xin[:, j, hf * 512:(hf + 1) * 512], pb[:, :], xin[:, j, hf * 512:(hf + 1) * 512], ALU.add, (pb, xin), (xin,))
                        k.dma(tok_view(hmid_d, t0), xin[:, :, :], (xin,), (), xin)
                P.flush()

        def phase_ffn(layer, src_d, dst_d, with_o, final):
            with contextlib.ExitStack() as st:
                if os.environ.get("KDBG"):
                    print("ffn sbuf remaining at start", nc.sbuf_bytes_remaining)
                wgu = k.sb(st, [128, 8, 2 * DFF], BF16, "wgu")
                wdn = k.sb(st, [128, NFC, D], BF16, "wdn")
                wo = k.sb(st, [128, 8, D], BF16, "wo") if with_o else None
                gfin = k.sb(st, [128, D], F32, "gfin") if final else None
                with contextlib.ExitStack() as st2:
                    stages = [k.sb(st2, [128, 2816], F32, "stg") for _ in range(3)]
                    gi = 1 if layer == 0 else 4
                    for kc in range(8):
                        k.wload(stages, lambda a, b, kc=kc: wgu[:, kc, a:b], lambda a, b, kc=kc: wgu_d[layer][:, kc * 2 * DFF + a:kc * 2 * DFF + b],
                                2 * DFF, wgu, piece=2816, gain=gall[:, gi, kc:kc + 1])
                    for fc in range(0, NFC, 2):
                        k.wload(stages, lambda a, b, fc=fc: wdn[:, fc:fc + 2, :].rearrange("p f d -> p (f d)")[:, a:b],
                                lambda a, b, fc=fc: wdn_d[layer][:, fc * D + a:fc * D + b], 2 * D, wdn, piece=2 * D)
                    if with_o:
                        for kc in range(0, 8, 2):
                            k.wload(stages, lambda a, b, kc=kc: wo[:, kc:kc + 2, :].rearrange("p f d -> p (f d)")[:, a:b],
                                    lambda a, b, kc=kc: wo_d[:, kc * D + a:kc * D + b], 2 * D, wo, piece=2 * D)
                    if final:
                        g1 = k.sb(st2, [1, D], F32)
                        on = k.sb(st2, [1, 128], F32)
                        pbk = k.ps(st2, [128, 512], F32)
                        k.dma(g1[:, :], gfin_d, (), (g1,), g1)
                        k.memset("dve", on[:, :], 1.0, (on,))
                        for hf in range(2):
                            k.mm(pbk[:, :], on[:, :], g1[:, hf * 512:(hf + 1) * 512], True, True, (on, g1), (pbk,))
                            k.cp("dve", gfin[:, hf * 512:(hf + 1) * 512], pbk[:, :], (pbk,), (gfin,))
                    P.flush()
                if os.environ.get("KDBG"):
                    print("ffn sbuf remaining before acts", nc.sbuf_bytes_remaining)
                xins = [k.sb(st, [128, 4, D], F32, "xin") for _ in range(1 if with_o else 2)] * 2
                hnb = [k.sb(st, [128, D], BF16, "hnb") for _ in range(1 if with_o else 2)] * 2
                hnT = k.sb(st, [128, 8, T], BF16, "hnT")
                if os.environ.get("KDBG"):
                    print("ffn sbuf remaining before actT", nc.sbuf_bytes_remaining)
                actT = k.sb(st, [128, NFC, T], BF16, "actT")
                ss = k.sb(st, [128, 4], F32)
                rstd = k.sb(st, [128, 4], F32)
                tg = [k.sb(st, [128, T], BF16) for _ in range(1 if with_o else 2)]
                tA = [k.sb(st, [128, T], F32) for _ in range(1)]
                banks = [k.ps(st, [128, 512], F32, "bk") for _ in range(6)]
                tbanks = [k.ps(st, [128, 1024], BF16, "tb") for _ in range(2)]
                bi = 0
                if not with_o:
                    k.dma(xins[0][:, :, :], tok_view(src_d, 0), (), (xins[0],), xins[0])
                for ci in range(NT // T):
                    t0 = ci * T
                    xin = xins[ci % 2]
                    if with_o:
                        k.dma(xin[:, :, :], tok_view(src_d, t0), (), (xin,), xin)
                    elif ci + 1 < NT // T:
                        k.dma(xins[(ci + 1) % 2][:, :, :], tok_view(src_d, t0 + T), (), (xins[(ci + 1) % 2],), xins[(ci + 1) % 2])
                    if with_o:
                        oi = actT[:, 0:8, :].rearrange("p (j a) t -> p j (a t)", j=4)
                        k.dma(oi, tok_view(o_d, t0), (), (actT,), actT)
                        for j in range(4):
                            tb = tbanks[j % 2]
                            for kc in range(8):
                                k.tr(tb[:, kc * 128:(kc + 1) * 128], oi[:, j, kc * 128:(kc + 1) * 128], ident[:, :], (actT, ident), (tb,))
                            src = tb[:, :].rearrange("p (a b) -> p a b", a=8)
                            if j % 2 == 0:
                                k.act(hnT[:, :, j * 128:(j + 1) * 128], src, AF.Copy, (tb,), (hnT,))
                            else:
                                k.cp("dve", hnT[:, :, j * 128:(j + 1) * 128], src, (tb,), (hnT,))
                        for j in range(4):
                            for hf in range(2):
                                pb = banks[bi % 6]
                                bi += 1
                                for kc in range(8):
                                    k.mm(pb[:, :], hnT[:, kc, j * 128:(j + 1) * 128], wo[:, kc, hf * 512:(hf + 1) * 512], kc == 0, kc == 7,
                                         (hnT, wo), (pb,))
                                k.tt("dve", xin[:, j, hf * 512:(hf + 1) * 512], pb[:, :], xin[:, j, hf * 512:(hf + 1) * 512], ALU.add, (pb, xin), (xin,))
                    norm_transpose(k, xin, hnb, hnT, ss, rstd, mhalf, ident, tbanks)
                    bi = ffn_chunk(k, hnT, actT, wgu, wdn, banks, bi, tg, tA)
                    bi = ffn_down(k, actT, wdn, xin, banks, bi)
                    if final:
                        for j in range(4):
                            hb_ = hnb[j % 2]
                            k.act(hb_[:, :], xin[:, j, :], AF.Square, (xin,), (hb_, ss), accum=ss[:, j:j + 1])
                        k.ts("dve", rstd[:, 0:4], ss[:, 0:4], 1.0 / D, EPS, ALU.mult, ALU.add, (ss,), (rstd,))
                        k.tt("pool", rstd[:, 0:4], rstd[:, 0:4], mhalf[:, 0:4], ALU.pow, (rstd, mhalf), (rstd,))
                        for j in range(4):
                            k.stt(xin[:, j, :], xin[:, j, :], rstd[:, j:j + 1], gfin[:, :], ALU.mult, ALU.mult, (xin, rstd, gfin), (xin,))
                    k.dma(tok_view(dst_d, t0), xin[:, :, :], (xin,), (), xin)
                P.flush()

        if "B" in phases:
            phase_ffn(0, hmid_d, h1_d, False, False)

        if "C" in phases:
            with contextlib.ExitStack() as st:
                wkc = k.sb(st, [128, 8, NG * 96], BF16, "wkc")
                wvc = k.sb(st, [128, 8, NG * 64], BF16, "wvc")
                wks = k.sb(st, [128, 8, NG * 128], BF16, "wks")
                wkw = k.sb(st, [128, 8, NG * 128], BF16, "wkw")
                wv = k.sb(st, [128, 8, 512], BF16, "wv")
                wq = k.sb(st, [128, 8, NH * 128], BF16, "wq")
                wqg = k.sb(st, [128, 8, 48], BF16, "wqg")
                gbb = k.sb(st, [1, 48], BF16, "gbb")
                ones1 = k.sb(st, [1, 128], BF16, "ones1")
                pm = k.sb(st, [128, 64], BF16, "pm")
                rc = k.sb(st, [128, S], F32, "rc")
                rsn = k.sb(st, [128, S], F32, "rsn")
                with contextlib.ExitStack() as st2:
                    stages = [k.sb(st2, [128, 2048], F32, "stg") for _ in range(3)]
                    for kc in range(8):
                        for (wt, wd, nco, gi) in ((wkc, wkc_d, NG * 96, 2), (wvc, wvc_d, NG * 64, 2), (wks, wks_d, NG * 128, 2),
                                                  (wkw, wkw_d, NG * 128, 2), (wv, wv_d, 512, 2), (wq, wq_d, NH * 128, 3), (wqg, wqg_d, 48, 3)):
                            k.wload(stages, lambda a, b, wt=wt, kc=kc: wt[:, kc, a:b], lambda a, b, wd=wd, kc=kc, nco=nco: wd[:, kc * nco + a:kc * nco + b],
                                    nco, wt, piece=2048, gain=gall[:, gi, kc:kc + 1])
                    k.wload(stages, lambda a, b: pm[:, a:b], lambda a, b: pm_d[:, a:b], 64, pm)
                    k.wload(stages, lambda a, b: gbb[:, a:b], lambda a, b: gb_d[:, a:b], 48, gbb, npart=1)
                    k.memset("dve", ones1[:, :], 1.0, (ones1,))
                    k.dma(rc[:, :], ropec_d, (), (rc,), rc)
                    k.dma(rsn[:, :], ropes_d, (), (rsn,), rsn)
                    P.flush()
                xins = [k.sb(st, [128, 4, D], F32, "xin") for _ in range(2)]
                hnb = [k.sb(st, [128, D], BF16, "hnb") for _ in range(2)]
                hnT = k.sb(st, [128, 8, T], BF16, "hnT")
                ss = k.sb(st, [128, 4], F32)
                rstd = k.sb(st, [128, 4], F32)
                o_kc = [k.sb(st, [96, NG, T], BF16) for _ in range(1)] * 2
                o_vc = [k.sb(st, [64, NG, T], BF16) for _ in range(1)] * 2
                o_ks = [k.sb(st, [128, NG, T], BF16) for _ in range(1)] * 2
                o_kw = [k.sb(st, [128, NG, T], BF16) for _ in range(1)] * 2
                o_v = [k.sb(st, [128, 4, 520], BF16) for _ in range(2)]
                for ov_ in o_v:
                    k.memset("pool", ov_[:, :, :], 1.0, (ov_,))
                o_qn = [k.sb(st, [128, NH, T], BF16) for _ in range(1)] * 2
                o_qr = [k.sb(st, [64, NH, T], BF16) for _ in range(1)] * 2
                o_gt = [k.sb(st, [128, 4, 48], F32) for _ in range(2)]
                r1 = [k.sb(st, [64, T], F32) for _ in range(2)]
                r2 = [k.sb(st, [64, T], F32) for _ in range(2)]
                ks_r = [Res("ks%d" % i) for i in range(NG)]
                kw_r = [Res("kw%d" % i) for i in range(NG)]
                qn_r = [Res("qn%d" % i) for i in range(NH)]
                qr_r = [Res("qr%d" % i) for i in range(NH)]
                banks = [k.ps(st, [128, 512], F32, "bk") for _ in range(6)]
                tbanks = [k.ps(st, [128, 1024], BF16, "tb") for _ in range(2)]
                bi = 0
                ri = 0
                k.dma(xins[0][:, :, :], tok_view(h1_d, 0), (), (xins[0],), xins[0])
                for ci in range(NT // T):
                    t0 = ci * T
                    b = t0 // S
                    p0 = t0 % S
                    xin = xins[ci % 2]
                    s_ = ci % 2
                    if ci + 1 < NT // T:
                        k.dma(xins[(ci + 1) % 2][:, :, :], tok_view(h1_d, t0 + T), (), (xins[(ci + 1) % 2],), xins[(ci + 1) % 2])
                    norm_transpose(k, xin, hnb, hnT, ss, rstd, mhalf, ident, tbanks)

                    def proj(wt, c0, m):
                        nonlocal bi
                        pb = banks[bi % 6]
                        bi += 1
                        for kc in range(8):
                            k.mm(pb[0:m, :], wt[:, kc, c0:c0 + m], hnT[:, kc, :], kc == 0, kc == 7, (wt, hnT), (pb,))
                        return pb

                    for g in range(NG):
                        pb = proj(wkc, g * 96, 96)
                        k.act(o_kc[s_][:, g, :], pb[0:96, :], AF.Copy, (pb,), (o_kc[s_],))
                        pb = proj(wvc, g * 64, 64)
                        k.cp("dve", o_vc[s_][:, g, :], pb[0:64, :], (pb,), (o_vc[s_],))
                    units = []
                    for (wt, ot, rl) in ((wks, o_ks[s_], ks_r), (wkw, o_kw[s_], kw_r)):
                        for g in range(NG):
                            units.append((wt, g * 128, ot, g, rl[g], ot, rl[g], 1.0))
                    for h in range(NH):
                        units.append((wq, h * 128, o_qn[s_], h, qn_r[h], o_qr[s_], qr_r[h], float(DQK ** -0.5)))
                    prev = None
                    for u in units + [None]:
                        if u is not None:
                            (wt, c0, dst, slot, dres, rdst, rres, scale) = u
                            pb = proj(wt, c0, 128)
                            k.act(dst[:, slot, :], pb[:, :], AF.Copy, (pb,), (dres,), scale=scale)
                        if prev is not None:
                            (wt, c0, dst, slot, dres, rdst, rres, scale) = prev
                            pb = banks[bi % 6]
                            bi += 1
                            k.mm(pb[0:64, :], pm[:, :], dst[:, slot, :], True, True, (pm, dres), (pb,))
                            a = r1[ri % 2]
                            bb = r2[ri % 2]
                            ri += 1
                            k.tt("dve", a[32:64, :], pb[32:64, :], rsn[32:64, p0:p0 + T], ALU.mult, (pb, rsn), (a,))
                            k.tt("pool", bb[32:64, :], dst[32:64, slot, :], rc[32:64, p0:p0 + T], ALU.mult, (dres, rc), (bb,))
                            k.tt("dve", rdst[32:64, slot, :], a[32:64, :], bb[32:64, :], ALU.add, (a, bb), (rres,))
                        prev = u
                    for j in range(4):
                        pb = banks[bi % 6]
                        bi += 1
                        for kc in range(8):
                            k.mm(pb[:, :], hnT[:, kc, j * 128:(j + 1) * 128], wv[:, kc, :], kc == 0, kc == 7, (wv, hnT), (pb,))
                        k.cp("dve", o_v[s_][:, j, :].rearrange("p (a c) -> p a c", c=65)[:, :, 0:64], pb[:, :].rearrange("p (a c) -> p a c", c=64),
                             (pb,), (o_v[s_],))
                        pb = banks[bi % 6]
                        bi += 1
                        for kc in range(8):
                            k.mm(pb[:, 0:48], hnT[:, kc, j * 128:(j + 1) * 128], wqg[:, kc, :], kc == 0, False, (wqg, hnT), (pb,))
                        k.mm(pb[:, 0:48], ones1[:, :], gbb[:, :], False, True, (ones1, gbb), (pb,))
                        k.act(o_gt[s_][:, j, :], pb[:, 0:48], AF.Tanh, (pb,), (o_gt[s_],), scale=0.5)
                        k.ts("dve", o_gt[s_][:, j, :], o_gt[s_][:, j, :], 0.5, 0.5, ALU.mult, ALU.add, (o_gt[s_],), (o_gt[s_],))
                    k.dma(kcmp_d[b, :, :, p0:p0 + T], o_kc[s_][:, :, :], (o_kc[s_],), (), o_kc[s_])
                    k.dma(vcmp_d[b, :, :, p0:p0 + T], o_vc[s_][:, :, :], (o_vc[s_],), (), o_vc[s_])
                    k.dma(ksel_d[b, :, :, p0:p0 + T], o_ks[s_][32:128, :, :], tuple(ks_r), (), o_ks[s_])
                    k.dma(kwin_d[b, :, :, p0:p0 + T], o_kw[s_][32:128, :, :], tuple(kw_r), (), o_kw[s_])
                    k.dma(v_d[t0:t0 + T, :].rearrange("(j p) d -> p j d", p=128), o_v[s_][:, :, :], (o_v[s_],), (), o_v[s_])
                    k.dma(qn_d[b, :, :, p0:p0 + T], o_qn[s_][32:128, :, :], tuple(qn_r), (), o_qn[s_])
                    k.dma(qr_d[b, :, :, p0:p0 + T], o_qr[s_][32:64, :, :], tuple(qr_r) + tuple(qn_r), (), o_qr[s_])
                    k.dma(gt_d[t0:t0 + T, :].rearrange("(j p) d -> p j d", p=128), o_gt[s_][:, :, :], (o_gt[s_],), (), o_gt[s_])
                P.flush()

        if "D" in phases:
            with contextlib.ExitStack() as st:
                w1k = k.sb(st, [96, 32, 256], BF16, "w1k")
                w1v = k.sb(st, [64, 32, 256], BF16, "w1v")
                w2k = k.sb(st, [128, 2, 128], BF16, "w2k")
                w2v = k.sb(st, [128, 2, 64], BF16, "w2v")
                posk = k.sb(st, [96, 32], BF16, "posk")
                posv = k.sb(st, [64, 32], BF16, "posv")
                pbias = k.sb(st, [128, 4], F32, "pbias")
                tri = k.sb(st, [128, 256], BF16, "tri")
                maskc = k.sb(st, [128, S], BF16, "maskc")
                tkadd = k.sb(st, [128, 16, 32], F32, "tkadd")
                tkmul = k.sb(st, [128, 16, 32], F32, "tkmul")
                vcaug = k.sb(st, [128, NG, 97], BF16, "vcaug")
                kcT = k.sb(st, [128, NG, 128], BF16, "kcT")
                ksT = k.sb(st, [128, NG, S], BF16, "ksT")
                kwT = k.sb(st, [128, NG, S], BF16, "kwT")
                with contextlib.ExitStack() as st2:
                    stages = [k.sb(st2, [128, 2048], F32, "stg") for _ in range(3)]
                    for l0 in range(0, 32, 8):
                        k.wload(stages, lambda a, b, l0=l0: w1k[:, l0:l0 + 8, :].rearrange("p l h -> p (l h)")[:, a:b],
                                lambda a, b, l0=l0: w1k_d[:, l0 * 256 + a:l0 * 256 + b], 2048, w1k, npart=96)
                        k.wload(stages, lambda a, b, l0=l0: w1v[:, l0:l0 + 8, :].rearrange("p l h -> p (l h)")[:, a:b],
                                lambda a, b, l0=l0: w1v_d[:, l0 * 256 + a:l0 * 256 + b], 2048, w1v, npart=64)
                    k.wload(stages, lambda a, b: w2k[:, :, :].rearrange("p l h -> p (l h)")[:, a:b], lambda a, b: w2k_d[:, a:b], 256, w2k)
                    k.wload(stages, lambda a, b: w2v[:, :, :].rearrange("p l h -> p (l h)")[:, a:b], lambda a, b: w2v_d[:, a:b], 128, w2v)
                    k.wload(stages, lambda a, b: posk[:, a:b], lambda a, b: posk_d[:, a:b], 32, posk, npart=96)
                    k.wload(stages, lambda a, b: posv[:, a:b], lambda a, b: posv_d[:, a:b], 32, posv, npart=64)
                    k.wload(stages, lambda a, b: tri[:, a:b], lambda a, b: tri_d[:, a:b], 256, tri)
                    k.wload(stages, lambda a, b: maskc[:, a:b], lambda a, b: maskc_d[:, a:b], S, maskc)
                    k.dma(tkadd[:, :, :], tkadd_d.rearrange("(j p) n -> p j n", p=128), (), (tkadd,), tkadd)
                    k.dma(tkmul[:, :, :], tkmul_d.rearrange("(j p) n -> p j n", p=128), (), (tkmul,), tkmul)
                    k.memset("pool", vcaug[:, :, 0:64], 0.0, (vcaug,))
                    for g in range(NG):
                        k.wload(stages, lambda a, b, g=g: vcaug[:, g, 64 + a:64 + b], lambda a, b: ovl_d[:, a:b], 33, vcaug)
                        k.wload(stages, lambda a, b, g=g: ksT[0:32, g, a:b], lambda a, b: esel_d[:, a:b], S, ksT, npart=32)
                    k.memset("pool", kwT[0:32, :, :], 0.0, (kwT,))
                    k.memset("dve", kcT[:, :, :], 0.0, (kcT,))
                    pbk = k.ps(st2, [128, 512], F32)
                    for (w1, pos, col) in ((w1k, posk, 0), (w1v, posv, 2)):
                        for hh in range(2):
                            for l in range(32):
                                k.mm(pbk[:, col + hh:col + hh + 1], w1[:, l, hh * 128:(hh + 1) * 128], pos[:, l:l + 1], l == 0, l == 31,
                                     (w1, pos), (pbk,), skip=True)
                    k.cp("dve", pbias[:, :], pbk[:, 0:4], (pbk,), (pbias,))
                    P.flush()

                kcmp = k.sb(st, [96, NG, S], BF16, "kcmp")
                vcmp = k.sb(st, [64, NG, S], BF16, "vcmp")
                vaug = k.sb(st, [128, 16, 2, NG, 65], BF16, "vaug")
                gts = k.sb(st, [128, 16, 48], F32, "gts")
                qnt = k.sb(st, [128, 4, S], BF16, "qn")
                qat = k.sb(st, [128, 4, S], BF16, "qa")
                oacc = k.sb(st, [128, 16, 256], F32, "oacc")
                obf = k.sb(st, [128, 16, 256], BF16, "obf")
                hid = [k.sb(st, [128, 2, 512], BF16) for _ in range(2)]
                gx = k.sb(st, [128, 512], F32)
                gs = k.sb(st, [128, 512], F32)
                gt_ = k.sb(st, [128, 512], BF16)
                em = [k.sb(st, [128, 512], BF16) for _ in range(2)]
                pmk = [k.sb(st, [128, 512], BF16) for _ in range(5)]
                e2b = [k.sb(st, [128, 512], BF16) for _ in range(4)]
                qa_r = [Res("qa%d" % i) for i in range(4)]
                oa_r = [Res("oa%d" % i) for i in range(4)]
                rden = [k.sb(st, [128, 4], F32) for _ in range(3)]
                fac = [k.sb(st, [128, 4], F32) for _ in range(3)]
                tmp = [k.sb(st, [128, 4, 64], F32) for _ in range(3)]
                impn = k.sb(st, [128, 4, 32], F32)
                imp = k.sb(st, [128, 32], F32)
                top8 = k.sb(st, [128, 8], F32)
                selb = k.sb(st, [128, 128], BF16)
                sbanks = [k.ps(st, [128, 512], F32, "sb") for _ in range(4)]
                abanks = [k.ps(st, [128, 512], F32, "ab") for _ in range(3)]
                tbank = k.ps(st, [128, 1024], BF16, "tb")
                k.memset("dve", qnt[0:32, :, :], 0.0, (qnt,))
                k.memset("dve", selb[:, :], 0.0, (selb,))
                cnt = {"s": 0, "a": 0, "p": 0, "e": 0, "n": 0}

                def nxt(lst, key):
                    cnt[key] += 1
                    return lst[cnt[key] % len(lst)]

                def rcp(out, in_, rd, wr):
                    P.op("dve", lambda e: e.reciprocal(out, in_), _rs(rd), _rs(wr))

                for b in range(nseq):
                    k.dma(kcmp[:, :, :], kcmp_d[b], (), (kcmp,), kcmp)
                    k.dma(vcmp[:, :, :], vcmp_d[b], (), (vcmp,), vcmp)
                    k.dma(ksT[32:128, :, :], ksel_d[b], (), (ksT,), ksT)
                    k.dma(kwT[32:128, :, :], kwin_d[b], (), (kwT,), kwT)
                    k.dma(vaug[:, :, :, :, :].rearrange("p j b g c -> p j (b g c)"), v_d[b * S:(b + 1) * S, :].rearrange("(j p) c -> p j c", p=128),
                          (), (vaug,), vaug)
                    k.dma(gts[:, :, :], gt_d[b * S:(b + 1) * S, :].rearrange("(j p) n -> p j n", p=128), (), (gts,), gts)
                    for (which, raw, w1) in ((0, kcmp, w1k), (1, vcmp, w1v)):
                        for hh in range(2):
                            pb = nxt(sbanks, "s")
                            for l in range(32):
                                k.mm(pb[:, 0:NG * NCMP].rearrange("p (g c) -> p g c", g=NG), w1[:, l, hh * 128:(hh + 1) * 128],
                                     raw[:, :, l:l + 16 * (NCMP - 1) + 1:16], l == 0, l == 31, (w1, raw), (pb,))
                            bcol = pbias[:, which * 2 + hh:which * 2 + hh + 1]
                            k.act(gx[:, 0:508], pb[:, 0:508], AF.Identity, (pb, pbias), (gx,), bias=bcol)
                            k.act(gs[:, 0:508], pb[:, 0:508], AF.Square, (pb, pbias), (gs,), bias=bcol)
                            k.ts("dve", gs[:, 0:508], gs[:, 0:508], GC, 1.0, ALU.mult, ALU.add, (gs,), (gs,))
                            k.tt("dve", gs[:, 0:508], gs[:, 0:508], gx[:, 0:508], ALU.mult, (gs, gx), (gs,))
                            k.act(gt_[:, 0:508], gs[:, 0:508], AF.Tanh, (gs,), (gt_,), scale=GK)
                            k.stt(gs[:, 0:508], gt_[:, 0:508], 1.0, gx[:, 0:508], ALU.add, ALU.mult, (gt_, gx), (gs,))
                            k.ts("dve", hid[which][:, hh, 0:508], gs[:, 0:508], 0.5, 0.0, ALU.mult, ALU.add, (gs,), (hid[which],))
                    for g in range(NG):
                        pb = nxt(abanks, "a")
                        for hh in range(2):
                            k.mm(pb[:, 0:NCMP], w2k[:, hh, :], hid[0][:, hh, g * NCMP:(g + 1) * NCMP], hh == 0, hh == 1, (w2k, hid[0]), (pb,))
                        k.cp("dve", kcT[:, g, 0:NCMP], pb[:, 0:NCMP], (pb,), (kcT,))
                        pb = nxt(abanks, "a")
                        for hh in range(2):
                            k.mm(pb[0:NCMP, 0:64], hid[1][:, hh, g * NCMP:(g + 1) * NCMP], w2v[:, hh, :], hh == 0, hh == 1, (w2v, hid[1]), (pb,))
                        k.cp("dve", vcaug[0:NCMP, g, 0:64], pb[0:NCMP, 0:64], (pb,), (vcaug,))
                    for g in range(NG):
                        k.dma(qnt[32:128, :, :], qn_d[b, :, 4 * g:4 * g + 4, :], (), (qnt,), qnt)
                        k.dma(qat[64:128, :, :], qn_d[b, 32:96, 4 * g:4 * g + 4, :], (), tuple(qa_r), qat)
                        k.dma(qat[32:64, :, :], qr_d[b, :, 4 * g:4 * g + 4, :], (), tuple(qa_r), qat)
                        def d2_setup(qc, g=g):
                            for h in range(4):
                                pb = nxt(sbanks, "s")
                                k.mm(pb[0:NCMP, :], kcT[:, g, 0:NCMP], qnt[:, h, qc * T:(qc + 1) * T], True, True, (kcT, qnt), (pb,))
                                e1 = nxt(em, "e")
                                e2 = e2b[h]
                                k.act(e1[0:NCMP, :], pb[0:NCMP, :], AF.Exp, (pb,), (e1,))
                                k.tt("pool", e2[0:NCMP, :], e1[0:NCMP, :], maskc[0:NCMP, qc * T:(qc + 1) * T], ALU.mult, (e1, maskc), (e2,))

                        def d2_qtile(qc, j, g=g):
                            qt = qc * 4 + j
                            pa = nxt(abanks, "a")
                            for h in range(4):
                                k.mm(pa[:, h * 97:(h + 1) * 97], e2b[h][0:NCMP, j * 128:(j + 1) * 128], vcaug[0:NCMP, g, 0:97], h == 0, h == 3,
                                     (e2b[h], vcaug), (pa,), skip=True)
                            p3 = pa[:, 0:388].rearrange("p (h c) -> p h c", h=4)
                            rd_ = nxt(rden, "n")
                            fc_ = fac[cnt["n"] % 3]
                            k.ts("dve", rd_[:, :], p3[:, :, 96], 1e-30, 0.0, ALU.max, ALU.add, (pa,), (rd_,))
                            rcp(rd_[:, :], rd_[:, :], (rd_,), (rd_,))
                            k.tt("dve", impn[:, :, :], p3[:, :, 64:96], rd_[:, :].unsqueeze(2).to_broadcast([128, 4, 32]), ALU.mult, (pa, rd_), (impn,))
                            P.op("dve", lambda e, o=imp[:, :], i=impn[:, :, :].rearrange("p h n -> p n h"): e.tensor_reduce(o, i, axis=AX.X, op=ALU.add),
                                 _rs((impn,)), _rs((imp,)))
                            k.tt("dve", imp[:, :], imp[:, :], tkmul[:, qt, :], ALU.mult, (imp, tkmul), (imp,))
                            k.tt("dve", imp[:, :], imp[:, :], tkadd[:, qt, :], ALU.add, (imp, tkadd), (imp,))
                            P.op("dve", lambda e, o=top8[:, :], i=imp[:, :]: e.max(o, i), _rs((imp,)), _rs((top8,)))
                            k.ts("dve", imp[:, :], imp[:, :], top8[:, 7:8], 1.0, ALU.is_ge, ALU.subtract, (imp, top8), (imp,))
                            k.ts("dve", selb[:, 0:32], imp[:, :], -NEG, 0.0, ALU.mult, ALU.add, (imp,), (selb,))
                            k.tr(tbank[:, 0:128], selb[:, :], ident[:, :], (selb, ident), (tbank,))
                            k.cp("dve", qat[0:32, :, qt * 128:(qt + 1) * 128], tbank[0:32, 0:128].unsqueeze(1).to_broadcast([32, 4, 128]), (tbank,), (qa_r[qc],))
                            k.tt("dve", fc_[:, :], rd_[:, :], gts[:, qt, 12 * g:12 * g + 12:3], ALU.mult, (rd_, gts), (fc_,))
                            k.tt("dve", oacc[:, qt, :].rearrange("p (h v) -> p h v", h=4), p3[:, :, 0:64],
                                 fc_[:, :].unsqueeze(2).to_broadcast([128, 4, 64]), ALU.mult, (pa, fc_), (oa_r[qc],))

                        d2_setup(0)
                        for j in range(4):
                            d2_qtile(0, j)
                        tiles = []
                        for qc in range(4):
                            for h in range(4):
                                pre = []
                                if qc < 3:
                                    if h == 0:
                                        pre.append(lambda qc=qc: d2_setup(qc + 1))
                                    pre.append(lambda qc=qc, h=h: d2_qtile(qc + 1, h))
                                nkt = 4 * qc + 4
                                lst = []
                                for kt in range(nkt):
                                    q0 = max(qc * T, kt * 128)
                                    ncol = (qc + 1) * T - q0
                                    masks = [(0, 0)] if kt * 128 >= qc * T else []
                                    pv = [(qt - 4 * qc, qt * 128 - q0) for qt in range(q0 // 128, 4 * qc + 4)]
                                    lst.append(dict(kT=ksT, br=0, kt=kt, q0=q0, ncol=ncol, masks=masks, pv=pv))
                                tiles.append(dict(h=h, qc=qc, br=1, lst=lst, pre=pre))
                                lst = []
                                for kt in range(max(0, 4 * qc - 4), 4 * qc + 4):
                                    qlo = max(kt, 4 * qc)
                                    qhi = min(kt + 4, 4 * qc + 3)
                                    q0 = qlo * 128
                                    ncol = (qhi - qlo + 1) * 128
                                    masks = []
                                    if qlo == kt:
                                        masks.append((0, 0))
                                    if qhi == kt + 4:
                                        masks.append((ncol - 128, 128))
                                    pv = [(qt - 4 * qc, (qt - qlo) * 128) for qt in range(qlo, qhi + 1)]
                                    lst.append(dict(kT=kwT, br=1, kt=kt, q0=q0, ncol=ncol, masks=masks, pv=pv))
                                tiles.append(dict(h=h, qc=qc, br=2, lst=lst, pre=[]))
                        flat = []
                        for tg_ in tiles:
                            po = None
                            for i_, t_ in enumerate(tg_["lst"]):
                                flat.append((tg_, t_, i_ == 0, i_ == len(tg_["lst"]) - 1))
                        LOOK = 3
                        pend = []
                        for idx in range(len(flat) + LOOK):
                            if idx < len(flat):
                                tg_, t_, isf, isl = flat[idx]
                                h = tg_["h"]
                                if isf:
                                    for th_ in tg_["pre"]:
                                        th_()
                                    tg_["po"] = nxt(abanks, "a")
                                ps_ = nxt(sbanks, "s")
                                ncol = t_["ncol"]
                                k.mm(ps_[:, 0:ncol], t_["kT"][:, g, t_["kt"] * 128:(t_["kt"] + 1) * 128], qat[:, h, t_["q0"]:t_["q0"] + ncol], True, True,
                                     (t_["kT"], qa_r[tg_["qc"]]), (ps_,))
                                pb_ = nxt(pmk, "p")
                                k.act(pb_[:, 0:ncol], ps_[:, 0:ncol], AF.Exp, (ps_,), (pb_,))
                                for (c0, m0) in t_["masks"]:
                                    k.tt("pool", pb_[:, c0:c0 + 128], pb_[:, c0:c0 + 128], tri[:, m0:m0 + 128], ALU.mult, (pb_, tri), (pb_,))
                                t_["pb"] = pb_
                            if idx >= LOOK:
                                tg_, t_, isf, isl = flat[idx - LOOK]
                                h, qc, po = tg_["h"], tg_["qc"], tg_["po"]
                                pb_ = t_["pb"]
                                for pi_, (j, off) in enumerate(t_["pv"]):
                                    k.mm(po[:, j * 65:(j + 1) * 65], pb_[:, off:off + 128], vaug[:, t_["kt"], t_["br"], g, :], isf and pi_ == 0, isl,
                                         (pb_, vaug), (po,), skip=True)
                                if isl:
                                    br = tg_["br"]
                                    p3 = po[:, 0:260].rearrange("p (j c) -> p j c", j=4)
                                    rd_ = nxt(rden, "n")
                                    fc_ = fac[cnt["n"] % 3]
                                    tm_ = tmp[cnt["n"] % 3]
                                    rcp(rd_[:, :], p3[:, :, 64], (po,), (rd_,))
                                    gcol = (4 * g + h) * 3 + br
                                    k.tt("dve", fc_[:, :], rd_[:, :], gts[:, 4 * qc:4 * qc + 4, gcol], ALU.mult, (rd_, gts), (fc_,))
                                    k.tt("dve", tm_[:, :, :], p3[:, :, 0:64], fc_[:, :].unsqueeze(2).to_broadcast([128, 4, 64]), ALU.mult, (po, fc_), (tm_,))
                                    k.tt("pool", oacc[:, 4 * qc:4 * qc + 4, h * 64:(h + 1) * 64], oacc[:, 4 * qc:4 * qc + 4, h * 64:(h + 1) * 64],
                                         tm_[:, :, :], ALU.add, (oa_r[qc], tm_), (oa_r[qc],))
                        k.act(obf[:, :, :], oacc[:, :, :], AF.Copy, tuple(oa_r), (obf,))
                        k.dma(o_d[b * S:(b + 1) * S, g * 256:(g + 1) * 256].rearrange("(j p) v -> p j v", p=128), obf[:, :, :], (obf,), (), obf)
                P.flush()
        if "E" in phases:
            phase_ffn(1, h1_d, y_d, True, True)
    return nc


def _consts():
    c = {}
    c["c_ident"] = np.eye(128, dtype=np.float32)
    kk = np.arange(128)[:, None]
    qq = np.arange(128)[None, :]
    c["c_tri"] = np.concatenate([(kk <= qq), (kk > qq)], axis=1).astype(np.float32)
    cc = np.arange(128)[:, None]
    tt = np.arange(S)[None, :]
    c["c_maskc"] = ((16 * cc + 31 <= tt) & (cc < NCMP)).astype(np.float32)
    half = 12
    inv = (np.float32(500000.0) ** (-np.arange(half, dtype=np.float32) * np.float32(2.0) / np.float32(24))).astype(np.float32)
    ang = (np.arange(S, dtype=np.float32)[None, :] * inv[:, None]).astype(np.float32)
    cs = np.cos(ang.astype(np.float64)).astype(np.float32)
    sn = np.sin(ang.astype(np.float64)).astype(np.float32)
    rc = np.ones((128, S), np.float32)
    rs = np.zeros((128, S), np.float32)
    rc[32:44] = cs
    rc[44:56] = cs
    rs[32:44] = -sn
    rs[44:56] = sn
    c["c_ropec"] = rc
    c["c_ropes"] = rs
    pm = np.zeros((128, 64), np.float32)
    for d in range(12):
        pm[32 + d + 12, 32 + d] = 1.0
        pm[32 + d, 32 + d + 12] = 1.0
    c["c_pm"] = pm
    c["c_esel"] = (np.arange(S)[None, :] // 64 == np.arange(32)[:, None]).astype(np.float32)
    ncmp = NCMP
    cs_ = np.arange(ncmp) * 16
    ss_ = np.arange(32) * 64
    ovl = ((cs_[:, None] < ss_[None, :] + 64) & (cs_[:, None] + 32 > ss_[None, :])).astype(np.float32)
    o = np.zeros((128, 33), np.float32)
    o[:ncmp, :32] = ovl
    o[:ncmp, 32] = 1.0
    c["c_ovl"] = o
    t = np.arange(S)[:, None]
    n = np.arange(32)[None, :]
    cur = t // 64
    forced = (n == 0) | (n == cur) | (n == cur - 1)
    vis = (n * 64 <= t)
    c["c_tkadd"] = np.where(forced, 1e9, np.where(vis, 0.0, -1e9)).astype(np.float32)
    c["c_tkmul"] = (vis & ~forced).astype(np.float32)
    return c


def _kc(w):
    K_, M = w.shape
    return np.ascontiguousarray(w.reshape(K_ // 128, 128, M).transpose(1, 0, 2).reshape(128, -1))


def _prep_weights(inp):
    f = lambda a: np.ascontiguousarray(np.asarray(a, dtype=np.float32))
    w = {}
    ng = inp["norm_g"]
    gs = np.stack([ng[0, 0], ng[0, 1], inp["kv_norm_g"], ng[1, 0], ng[1, 1]], 0)
    w["g_all"] = f(gs.reshape(5, 8, 128).transpose(2, 0, 1))
    w["g_fin"] = f(inp["final_g"].reshape(1, D))
    w["w_in"] = _kc(f(inp["a_w_in"][0]))
    w["conv_w"] = f(inp["a_conv_w"][0].reshape(4, NBLK, RB).transpose(2, 1, 0).reshape(RB, NBLK * 4))
    sm = np.stack([inp["a_conv_b"][0].reshape(NBLK, RB).T, inp["a_b_ra"][0].T, inp["a_b_ix"][0].T,
                   inp["a_lambda"][0].reshape(NBLK, RB).T], 1)
    w["a_small"] = f(sm)
    w["w_ra"] = f(inp["a_w_ra"][0].transpose(1, 0, 2).reshape(RB, NBLK * RB))
    w["w_ix"] = f(inp["a_w_ix"][0].transpose(1, 0, 2).reshape(RB, NBLK * RB))
    w["w_out"] = f(inp["a_w_out"][0].reshape(NBLK, RB, D).transpose(1, 0, 2).reshape(RB, NBLK * D))
    for l in range(2):
        w["w_gu%d" % l] = _kc(f(inp["ffn_w_gu"][l]))
        w["w_dn%d" % l] = f(inp["ffn_w_down"][l].reshape(NFC, 128, D).transpose(1, 0, 2).reshape(128, NFC * D))
    kv = f(inp["kv_w"])
    kcmp_, vcmp_, ksel_, vsel_, kwin_, vwin_ = kv[:, 0:384], kv[:, 384:640], kv[:, 640:1024], kv[:, 1024:1280], kv[:, 1280:1664], kv[:, 1664:1920]
    w["w_kc"] = _kc(kcmp_)
    w["w_vc"] = _kc(vcmp_)

    def aug(m, n):
        z = np.zeros((m.shape[0], n, 128), np.float32)
        z[:, :, 32:] = m.reshape(m.shape[0], n, 96)
        return z.reshape(m.shape[0], n * 128)
    w["w_ks"] = _kc(aug(ksel_, NG))
    w["w_kw"] = _kc(aug(kwin_, NG))
    w["w_v"] = _kc(np.concatenate([vsel_, vwin_], 1))
    wq = f(inp["b_w_q"][0])
    w["w_q"] = _kc(aug(wq[:, :NH * DQK], NH))
    w["w_qg"] = _kc(wq[:, NH * DQK:])
    w["gate_b"] = f(inp["b_gate_bias"][0].reshape(1, 48))
    w["w_o"] = _kc(f(inp["b_w_o"][0]))
    w["w1k"] = f(inp["cmp_w1_k"].reshape(32, 96, 256).transpose(1, 0, 2).reshape(96, 32 * 256))
    w["w1v"] = f(inp["cmp_w1_v"].reshape(32, 64, 256).transpose(1, 0, 2).reshape(64, 32 * 256))
    w2k = np.zeros((256, 128), np.float32)
    w2k[:, 32:] = inp["cmp_w2_k"]
    w["w2k"] = f(w2k.reshape(2, 128, 128).transpose(1, 0, 2).reshape(128, 256))
    w["w2v"] = f(np.asarray(inp["cmp_w2_v"], np.float32).reshape(2, 128, 64).transpose(1, 0, 2).reshape(128, 128))
    w["posk"] = f(np.asarray(inp["cmp_pos_k"]).T)
    w["posv"] = f(np.asarray(inp["cmp_pos_v"]).T)
    return w


_CACHE = {}


def kernel(**inputs):
    inp = {k_: np.asarray(v) for k_, v in inputs.items()}
    ncores = 8
    nseq = inp["x"].shape[0] // ncores
    if "nc" not in _CACHE:
        _CACHE["nc"] = build_program(nseq)
    nc = _CACHE["nc"]
    shared = _consts()
    shared.update(_prep_weights(inp))
    x = np.ascontiguousarray(inp["x"], dtype=np.float32).reshape(ncores, nseq * S, D)
    in_maps = []
    for c in range(ncores):
        m = dict(shared)
        m["x"] = x[c]
        in_maps.append(m)
    res = run_bass_kernel_spmd(nc, in_maps, core_ids=list(range(ncores)))
    out = np.stack([np.asarray(r["y"]) for r in res.results], 0)
    return out.reshape(inp["x"].shape).astype(np.float32)
```

```python
import contextlib
import os
import numpy as np
import concourse.bass as bass
import concourse.mybir as mybir
from concourse.bass_utils import run_bass_kernel_spmd

F32 = mybir.dt.float32
BF16 = mybir.dt.bfloat16
AF = mybir.ActivationFunctionType
ALU = mybir.AluOpType
AX = mybir.AxisListType

D = 1024
S = 2048
T = 512
DRNN = 1536
NBLK = 16
RB = 96
DFF = 2816
NFC = 22
NH = 16
NG = 4
DQK = 96
DV = 64
NCMP = 127
EPS = 1e-6
NEG = -30000.0
GK = 0.7978845608028654
GC = 0.044715


class Res:
    __slots__ = ("name", "last_w", "readers")

    def __init__(self, name):
        self.name = name
        self.last_w = None
        self.readers = []


class Op:
    __slots__ = ("eng", "fn", "waits", "signal", "is_dma", "key", "pos", "sigidx", "is_nop")

    def __init__(self, eng, fn, is_dma=False, key=None):
        self.eng = eng
        self.fn = fn
        self.waits = []
        self.signal = False
        self.is_dma = is_dma
        self.key = key
        self.pos = -1
        self.sigidx = -1
        self.is_nop = False


ENGS = ("pe", "act", "dve", "pool", "sp")


class Prog:
    def __init__(self, nc, stack):
        self.nc = nc
        self.stack = stack
        self.streams = {e: [] for e in ENGS}
        self.emitted = {e: 0 for e in ENGS}
        self.nsig = {e: 0 for e in ENGS}
        self.npos = {e: 0 for e in ENGS}
        self.known = {e: {} for e in ENGS}
        self.dma_cnt = {}
        self.esem = {e: stack.enter_context(nc.semaphore("s_" + e)) for e in ENGS}
        self.ksem = {}

    def res(self, name="r"):
        return Res(name)

    def _dep(self, op, d, raw):
        if d is None or d is op:
            return
        E = op.eng
        k = self.known[E]
        if d.is_dma:
            src = ("dma", d.key)
            cnt = self.dma_cnt[d.key]
            if k.get(src, -1) >= cnt:
                return
            k[src] = cnt
            op.waits.append(("dma", d.key, cnt))
            return
        if d.eng == E and not op.is_dma:
            if E == "pe":
                return
        if k.get(d.eng, -1) >= d.pos:
            return
        k[d.eng] = d.pos
        d.signal = True
        op.waits.append(("eng", d))

    def op(self, eng, fn, reads=(), writes=(), is_dma=False, key=None):
        o = Op(eng, fn, is_dma, key)
        if is_dma:
            self.dma_cnt.setdefault(key, 0)
        else:
            o.pos = self.npos[eng]
            self.npos[eng] += 1
        for r in reads:
            self._dep(o, r.last_w, True)
        for w in writes:
            self._dep(o, w.last_w, False)
            for rd in w.readers:
                self._dep(o, rd, False)
        if is_dma:
            self.dma_cnt[key] += 1
            o.pos = self.dma_cnt[key]
        for r in reads:
            r.readers.append(o)
        for w in writes:
            w.last_w = o
            w.readers = []
        self.streams[eng].append(o)
        return o

    def dma(self, q, out, in_, reads=(), writes=(), key=None):
        key = id(key)
        return self.op(q, lambda e: e.dma_start(out=out, in_=in_), reads, writes, is_dma=True, key=key)

    def barrier(self):
        lasts = []
        for e in ENGS:
            comp = [o for o in self.streams[e] if not o.is_dma and not o.is_nop]
            if comp:
                lasts.append(comp[-1])
        for e in ENGS:
            o = Op(e, lambda eng: eng.nop())
            o.is_nop = True
            o.pos = self.npos[e]
            self.npos[e] += 1
            k = self.known[e]
            for d in lasts:
                if k.get(d.eng, -1) >= d.pos:
                    continue
                k[d.eng] = d.pos
                d.signal = True
                o.waits.append(("eng", d))
            for key, cnt in self.dma_cnt.items():
                src = ("dma", key)
                if k.get(src, -1) >= cnt:
                    continue
                k[src] = cnt
                o.waits.append(("dma", key, cnt))
            self.streams[e].append(o)

    def flush(self):
        self.barrier()
        nc = self.nc
        for e in ENGS:
            n = self.nsig[e]
            for o in self.streams[e]:
                if not o.is_dma and o.signal:
                    n += 1
                    o.sigidx = n
            self.nsig[e] = n
        for key in self.dma_cnt:
            if key not in self.ksem:
                self.ksem[key] = self.stack.enter_context(nc.semaphore("d_%d" % len(self.ksem)))
        esem, ksem = self.esem, self.ksem
        streams = self.streams

        def run(en, eng):
            for o in streams[en]:
                for w in o.waits:
                    if w[0] == "dma":
                        eng.wait_ge(ksem[w[1]], 16 * w[2])
                    else:
                        eng.wait_ge(esem[w[1].eng], w[1].sigidx)
                ins = o.fn(eng)
                if o.is_dma:
                    ins.then_inc(ksem[o.key], 16)
                elif o.signal:
                    ins.then_inc(esem[en], 1)

        with nc.Block() as block:
            @block.tensor
            def _(e):
                run("pe", e)

            @block.scalar
            def _(e):
                run("act", e)

            @block.vector
            def _(e):
                run("dve", e)

            @block.gpsimd
            def _(e):
                run("pool", e)

            @block.sync
            def _(e):
                run("sp", e)
        self.streams = {e: [] for e in ENGS}


class Tl:
    __slots__ = ("t", "r")

    def __init__(self, t, r):
        self.t = t
        self.r = r

    def __getitem__(self, k):
        return self.t[k]


def _rs(xs):
    out = []
    for x in xs:
        if x is None:
            continue
        out.append(x.r if isinstance(x, Tl) else x)
    return out


class K:
    def __init__(self, nc, P):
        self.nc = nc
        self.P = P
        self.n = 0
        self.rr = 0

    def sb(self, st, shape, dt=F32, name=None):
        self.n += 1
        nm = "%s_%d" % (name or "t", self.n)
        return Tl(st.enter_context(self.nc.sbuf_tensor(nm, list(shape), dt)), Res(nm))

    def ps(self, st, shape, dt=F32, name=None):
        self.n += 1
        nm = "%s_%d" % (name or "p", self.n)
        return Tl(st.enter_context(self.nc.psum_tensor(nm, list(shape), dt)), Res(nm))

    def mm(self, out, lhsT, rhs, start, stop, rd, wr, skip=False):
        if skip:
            f = lambda e: e.matmul(out, lhsT=lhsT, rhs=rhs, start=start, stop=stop, skip_group_check=True)
        else:
            f = lambda e: e.matmul(out, lhsT=lhsT, rhs=rhs, start=start, stop=stop)
        self.P.op("pe", f, _rs(rd), _rs(wr))

    def tr(self, out, in_, ident, rd, wr):
        self.P.op("pe", lambda e: e.transpose(out, in_, ident), _rs(rd), _rs(wr))

    def act(self, out, in_, func, rd, wr, scale=1.0, bias=None, accum=None):
        kw = {}
        if bias is not None:
            kw["bias"] = bias
        if accum is not None:
            kw["accum_out"] = accum
        self.P.op("act", lambda e: e.activation(out=out, in_=in_, func=func, scale=scale, **kw), _rs(rd), _rs(wr))

    def cp(self, eng, out, in_, rd, wr):
        self.P.op(eng, lambda e: e.tensor_copy(out, in_), _rs(rd), _rs(wr))

    def ts(self, eng, out, in0, s1, s2, op0, op1, rd, wr):
        self.P.op(eng, lambda e: e.tensor_scalar(out, in0, s1, s2, op0, op1), _rs(rd), _rs(wr))

    def tt(self, eng, out, in0, in1, op, rd, wr):
        self.P.op(eng, lambda e: e.tensor_tensor(out, in0, in1, op), _rs(rd), _rs(wr))

    def stt(self, out, in0, scalar, in1, op0, op1, rd, wr):
        self.P.op("dve", lambda e: e.scalar_tensor_tensor(out=out, in0=in0, scalar=scalar, in1=in1, op0=op0, op1=op1),
                  _rs(rd), _rs(wr))

    def memset(self, eng, out, val, wr):
        self.P.op(eng, lambda e: e.memset(out, val), (), _rs(wr))

    def dma(self, out, in_, rd, wr, key, q="sp"):
        self.P.dma(q, out, in_, _rs(rd), _rs(wr), key=(key.r if isinstance(key, Tl) else key))

    def ceng(self):
        self.rr += 1
        return ("dve", "pool")[self.rr % 2]

    def wload(self, stages, dst_fn, src_fn, ncol, dst, piece=2048, gain=None, npart=128):
        c0 = 0
        while c0 < ncol:
            c1 = min(ncol, c0 + piece)
            stg = stages[self.rr % len(stages)]
            self.dma(stg[0:npart, 0:c1 - c0], src_fn(c0, c1), (), (stg,), stg)
            eng = self.ceng()
            if gain is None:
                self.cp(eng, dst_fn(c0, c1), stg[0:npart, 0:c1 - c0], (stg,), (dst,))
            else:
                self.ts(eng, dst_fn(c0, c1), stg[0:npart, 0:c1 - c0], gain, 0.0, ALU.mult, ALU.add, (stg,), (dst,))
            c0 = c1


def norm_transpose(k, xin, hnbs, hnT, ss, rstd, mhalf, ident, tbanks):
    for j in range(4):
        hnb = hnbs[j % len(hnbs)]
        tb = tbanks[j % len(tbanks)]
        k.act(hnb[:, :], xin[:, j, :], AF.Square, (xin,), (hnb, ss), accum=ss[:, j:j + 1])
        k.ts("dve", rstd[:, j:j + 1], ss[:, j:j + 1], 1.0 / D, EPS, ALU.mult, ALU.add, (ss,), (rstd,))
        k.tt("pool", rstd[:, j:j + 1], rstd[:, j:j + 1], mhalf[:, 0:1], ALU.pow, (rstd, mhalf), (rstd,))
        if j % 2 == 0:
            k.ts("dve", hnb[:, :], xin[:, j, :], rstd[:, j:j + 1], 0.0, ALU.mult, ALU.add, (xin, rstd), (hnb,))
        else:
            k.act(hnb[:, :], xin[:, j, :], AF.Identity, (xin, rstd), (hnb,), scale=rstd[:, j:j + 1])
        for kc in range(8):
            k.tr(tb[:, kc * 128:(kc + 1) * 128], hnb[:, kc * 128:(kc + 1) * 128], ident[:, :], (hnb, ident), (tb,))
        src = tb[:, :].rearrange("p (a b) -> p a b", a=8)
        if j % 2 == 0:
            k.act(hnT[:, :, j * 128:(j + 1) * 128], src, AF.Copy, (tb,), (hnT,))
        else:
            k.cp("dve", hnT[:, :, j * 128:(j + 1) * 128], src, (tb,), (hnT,))


def ffn_chunk(k, hnT, actT, wgu, wdn, banks, bi, tg, tA):
    for fc in range(NFC):
        pg = banks[bi % len(banks)]
        pv = banks[(bi + 1) % len(banks)]
        bi += 2
        for kc in range(8):
            k.mm(pg[:, :], wgu[:, kc, fc * 128:(fc + 1) * 128], hnT[:, kc, :], kc == 0, kc == 7, (wgu, hnT), (pg,))
        for kc in range(8):
            k.mm(pv[:, :], wgu[:, kc, DFF + fc * 128:DFF + (fc + 1) * 128], hnT[:, kc, :], kc == 0, kc == 7, (wgu, hnT), (pv,))
        t1 = tg[fc % len(tg)]
        t2 = tA[fc % len(tA)]
        k.act(t1[:, :], pg[:, :], AF.Tanh, (pg,), (t1,), scale=0.5)
        k.stt(t2[:, :], t1[:, :], 1.0, pg[:, :], ALU.add, ALU.mult, (t1, pg), (t2,))
        k.stt(actT[:, fc, :], t2[:, :], 0.5, pv[:, :], ALU.mult, ALU.mult, (t2, pv), (actT,))
    return bi


def ffn_down(k, actT, wdn, xin, banks, bi):
    for j in range(4):
        for hf in range(2):
            pb = banks[bi % len(banks)]
            bi += 1
            for fc in range(NFC):
                k.mm(pb[:, :], actT[:, fc, j * 128:(j + 1) * 128], wdn[:, fc, hf * 512:(hf + 1) * 512], fc == 0, fc == NFC - 1,
                     (actT, wdn), (pb,))
            k.tt("dve", xin[:, j, hf * 512:(hf + 1) * 512], pb[:, :], xin[:, j, hf * 512:(hf + 1) * 512], ALU.add, (pb, xin), (xin,))
    return bi


def tok_view(ap2d, t0):
    return ap2d[t0:t0 + T, :].rearrange("(j p) d -> p j d", p=128)


def build_program(nseq, phases="ABCDE", debug=False):
    NT = nseq * S
    nc = bass.Bass("TRN2", target_bir_lowering=False)
    dt_in = {}

    def din(name, shape):
        dt_in[name] = shape
        return nc.dram_tensor(name, list(shape), F32, kind="ExternalInput").ap()

    def dscr(name, shape, dt):
        kind = "ExternalOutput" if debug else "Internal"
        return nc.dram_tensor(name, list(shape), dt, kind=kind).ap()

    x_d = din("x", (NT, D))
    identf_d = din("c_ident", (128, 128))
    tri_d = din("c_tri", (128, 256))
    maskc_d = din("c_maskc", (128, S))
    ropec_d = din("c_ropec", (128, S))
    ropes_d = din("c_ropes", (128, S))
    pm_d = din("c_pm", (128, 64))
    esel_d = din("c_esel", (32, S))
    ovl_d = din("c_ovl", (128, 33))
    tkadd_d = din("c_tkadd", (S, 32))
    tkmul_d = din("c_tkmul", (S, 32))
    g_d = din("g_all", (128, 5, 8))
    gfin_d = din("g_fin", (1, D))
    win_d = din("w_in", (128, 8 * 3072))
    cw_d = din("conv_w", (RB, NBLK * 4))
    sm_d = din("a_small", (RB, 4, NBLK))
    wra_d = din("w_ra", (RB, NBLK * RB))
    wix_d = din("w_ix", (RB, NBLK * RB))
    wout_d = din("w_out", (RB, NBLK * D))
    wgu_d = [din("w_gu%d" % l, (128, 8 * 2 * DFF)) for l in range(2)]
    wdn_d = [din("w_dn%d" % l, (128, NFC * D)) for l in range(2)]
    wkc_d = din("w_kc", (128, 8 * NG * 96))
    wvc_d = din("w_vc", (128, 8 * NG * 64))
    wks_d = din("w_ks", (128, 8 * NG * 128))
    wkw_d = din("w_kw", (128, 8 * NG * 128))
    wv_d = din("w_v", (128, 8 * 512))
    wq_d = din("w_q", (128, 8 * NH * 128))
    wqg_d = din("w_qg", (128, 8 * 48))
    gb_d = din("gate_b", (1, 48))
    wo_d = din("w_o", (128, 8 * D))
    w1k_d = din("w1k", (96, 32 * 256))
    w1v_d = din("w1v", (64, 32 * 256))
    w2k_d = din("w2k", (128, 2 * 128))
    w2v_d = din("w2v", (128, 2 * 64))
    posk_d = din("posk", (96, 32))
    posv_d = din("posv", (64, 32))

    y_d = nc.dram_tensor("y", [NT, D], F32, kind="ExternalOutput").ap()
    hmid_d = dscr("s_hmid", (NT, D), F32)
    h1_d = dscr("s_h1", (NT, D), F32)
    kcmp_d = dscr("s_kcmp", (nseq, 96, NG, S), BF16)
    vcmp_d = dscr("s_vcmp", (nseq, 64, NG, S), BF16)
    ksel_d = dscr("s_ksel", (nseq, 96, NG, S), BF16)
    kwin_d = dscr("s_kwin", (nseq, 96, NG, S), BF16)
    v_d = dscr("s_v", (NT, 520), BF16)
    qn_d = dscr("s_qn", (nseq, 96, NH, S), BF16)
    qr_d = dscr("s_qr", (nseq, 32, NH, S), BF16)
    gt_d = dscr("s_gt", (NT, 48), F32)
    o_d = dscr("s_o", (NT, D), BF16)

    with contextlib.ExitStack() as top:
        P = Prog(nc, top)
        k = K(nc, P)

        ident = k.sb(top, [128, 128], BF16, "ident")
        mhalf = k.sb(top, [128, 4], F32, "mhalf")
        gall = k.sb(top, [128, 5, 8], F32, "gall")
        with contextlib.ExitStack() as st:
            tmp = k.sb(st, [128, 128], F32)
            k.dma(tmp[:, :], identf_d, (), (tmp,), tmp)
            k.cp("dve", ident[:, :], tmp[:, :], (tmp,), (ident,))
            k.memset("dve", mhalf[:, :], -0.5, (mhalf,))
            k.dma(gall[:, :, :], g_d, (), (gall,), gall)
            P.flush()

        if "A" in phases:
            with contextlib.ExitStack() as st:
                win = k.sb(st, [128, 8, 3072], BF16, "win")
                wout = k.sb(st, [RB, NBLK, D], BF16, "wout")
                wra = k.sb(st, [RB, NBLK, RB], BF16, "wra")
                wix = k.sb(st, [RB, NBLK, RB], BF16, "wix")
                dg = k.sb(st, [RB, NBLK * 4, RB], BF16, "dg")
                cw = k.sb(st, [RB, NBLK * 4], F32, "cw")
                sm = k.sb(st, [RB, 4, NBLK], F32, "sm")
                hb = k.sb(st, [RB, 2, NBLK], F32, "hb")
                cc = k.sb(st, [RB, 2, NBLK], F32, "cc")
                cx = k.sb(st, [RB, NBLK, 4], BF16, "cx")
                hcar = k.sb(st, [RB, NBLK], F32, "hcar")
                one = k.sb(st, [RB, 1], F32, "one")
                k.memset("dve", one[:, :], 1.0, (one,))
                with contextlib.ExitStack() as st2:
                    stages = [k.sb(st2, [128, 3072], F32, "stg") for _ in range(3)]
                    for kc in range(8):
                        k.wload(stages, lambda a, b, kc=kc: win[:, kc, a:b], lambda a, b, kc=kc: win_d[:, kc * 3072 + a:kc * 3072 + b],
                                3072, win, piece=3072, gain=gall[:, 0, kc:kc + 1])
                    for n in range(NBLK):
                        k.wload(stages, lambda a, b, n=n: wout[:, n, a:b], lambda a, b, n=n: wout_d[:, n * D + a:n * D + b], D, wout,
                                piece=D, npart=RB)
                    k.wload(stages, lambda a, b: wra[:, :, :].rearrange("p n j -> p (n j)")[:, a:b], lambda a, b: wra_d[:, a:b],
                            NBLK * RB, wra, piece=NBLK * RB, npart=RB)
                    k.wload(stages, lambda a, b: wix[:, :, :].rearrange("p n j -> p (n j)")[:, a:b], lambda a, b: wix_d[:, a:b],
                            NBLK * RB, wix, piece=NBLK * RB, npart=RB)
                    k.dma(cw[:, :], cw_d, (), (cw,), cw)
                    k.dma(sm[:, :, :], sm_d, (), (sm,), sm)
                    for i in range(NBLK * 4):
                        k.ts(k.ceng(), dg[:, i, :], ident[0:RB, 0:RB], cw[:, i:i + 1], 0.0, ALU.mult, ALU.add, (ident, cw), (dg,))
                    k.ts("dve", hb[:, :, :], sm[:, 1:3, :], -1.0, 0.0, ALU.mult, ALU.add, (sm,), (hb,))
                    sg = k.sb(st2, [RB, NBLK], F32)
                    k.act(sg[:, :], sm[:, 3, :], AF.Exp, (sm,), (sg,), scale=-1.0)
                    k.ts("dve", sg[:, :], sg[:, :], 1.0, 1.0, ALU.mult, ALU.add, (sg,), (sg,))
                    k.act(sg[:, :], sg[:, :], AF.Ln, (sg,), (sg,))
                    k.ts("dve", cc[:, 0, :], sg[:, :], -8.0, 0.0, ALU.mult, ALU.add, (sg,), (cc,))
                    k.ts("dve", cc[:, 1, :], sg[:, :], -16.0, 0.0, ALU.mult, ALU.add, (sg,), (cc,))
                    P.flush()
                if os.environ.get("KSTOP") == "A0":
                    return nc

                xins = [k.sb(st, [128, 4, D], F32, "xin") for _ in range(2)]
                hnb = [k.sb(st, [128, D], BF16, "hnb") for _ in range(2)]
                hnTs = [k.sb(st, [128, 8, T], BF16, "hnT") for _ in range(1)] * 2
                recy = k.sb(st, [RB, NBLK, T], BF16, "recy")
                ss = k.sb(st, [128, 4], F32)
                rstd = k.sb(st, [128, 4], F32)
                NSET = 3
                xbT = [k.sb(st, [RB, T + 4], BF16) for _ in range(NSET)]
                xcT = [k.sb(st, [RB, T], BF16) for _ in range(NSET)]
                trr = [k.sb(st, [RB, T], F32) for _ in range(NSET)]
                tii = [k.sb(st, [RB, T], F32) for _ in range(NSET)]
                aa = [k.sb(st, [RB, T], F32) for _ in range(NSET)]
                a2 = [k.sb(st, [RB, T], F32) for _ in range(NSET)]
                hs = [k.sb(st, [RB, T], F32) for _ in range(NSET)]
                sq = [k.sb(st, [RB, T], F32) for _ in range(NSET)]
                th = [k.sb(st, [RB, T], F32) for _ in range(NSET)]
                banks = [k.ps(st, [128, 512], F32, "bk") for _ in range(4)]
                ybanks = [k.ps(st, [128, 512], F32, "yb") for _ in range(3)]
                tbanks = [k.ps(st, [128, 1024], BF16, "tb") for _ in range(1)]

                def rcp(out, in_, rd, wr):
                    P.op("dve", lambda e: e.reciprocal(out, in_), _rs(rd), _rs(wr))
                if os.environ.get("KDBG"):
                    print("phase A sbuf remaining", nc.sbuf_bytes_remaining)
                bi = 0
                ci = 0
                KNCH = int(os.environ.get("KNCH", "4"))
                KNBLK = int(os.environ.get("KNBLK", "16"))
                KSTOP = os.environ.get("KSTOP", "")
                nchA = min(KNCH, S // T)
                chunksA = [(b, c) for b in range(nseq) for c in range(nchA)]
                k.dma(xins[0][:, :, :], tok_view(x_d, chunksA[0][0] * S + chunksA[0][1] * T), (), (xins[0],), xins[0])
                for b in range(nseq):
                    k.memset("pool", cx[:, :, :], 0.0, (cx,))
                    k.memset("pool", hcar[:, :], 0.0, (hcar,))
                    for c in range(nchA):
                        t0 = b * S + c * T
                        xin = xins[ci % 2]
                        hnT = hnTs[ci % 2]
                        ci += 1
                        if ci < len(chunksA):
                            nb_, nc_ = chunksA[ci]
                            k.dma(xins[ci % 2][:, :, :], tok_view(x_d, nb_ * S + nc_ * T), (), (xins[ci % 2],), xins[ci % 2])
                        norm_transpose(k, xin, hnb, hnT, ss, rstd, mhalf, ident, tbanks)
                        def blk(n, xin=None, hnT=hnT):
                            s_ = n % NSET
                            bA = banks[(2 * n) % 4]
                            bB = banks[(2 * n + 1) % 4]
                            pxb, pr, pcv, pi = bA, bA, bB, bB
                            pyb = ybanks[n % 3]
                            er, ei, ey = trr[s_], tii[s_], th[s_]
                            for kc in range(8):
                                k.mm(pxb[0:RB, :], win[:, kc, n * RB:(n + 1) * RB], hnT[:, kc, :], kc == 0, kc == 7, (win, hnT), (pxb,))
                            for kc in range(8):
                                k.mm(pyb[0:RB, :], win[:, kc, DRNN + n * RB:DRNN + (n + 1) * RB], hnT[:, kc, :], kc == 0, kc == 7,
                                     (win, hnT), (pyb,))
                            yield
                            k.cp("pool", xbT[s_][:, 0:4], cx[:, n, :], (cx,), (xbT[s_],))
                            k.cp("dve", xbT[s_][:, 4:4 + T], pxb[0:RB, :], (pxb,), (xbT[s_],))
                            k.cp("pool", cx[:, n, :], xbT[s_][:, T:T + 4], (xbT[s_],), (cx,))
                            k.act(sq[s_][:, :], pyb[0:RB, :], AF.Square, (pyb,), (sq[s_],))
                            yield
                            for t in range(4):
                                k.mm(pcv[0:RB, :], dg[:, n * 4 + t, :], xbT[s_][:, 1 + t:1 + t + T], t == 0, t == 3, (dg, xbT[s_]), (pcv,))
                            k.ts("pool", sq[s_][:, :], sq[s_][:, :], GC, 1.0, ALU.mult, ALU.add, (sq[s_],), (sq[s_],))
                            yield
                            k.ts("dve", xcT[s_][:, :], pcv[0:RB, :], sm[:, 0, n:n + 1], 0.0, ALU.add, ALU.add, (pcv, sm), (xcT[s_],))
                            k.tt("dve", sq[s_][:, :], sq[s_][:, :], pyb[0:RB, :], ALU.mult, (sq[s_], pyb), (sq[s_],))
                            yield
                            k.mm(pr[0:RB, :], wra[:, n, :], xcT[s_][:, :], True, True, (wra, xcT[s_]), (pr,))
                            k.mm(pi[0:RB, :], wix[:, n, :], xcT[s_][:, :], True, True, (wix, xcT[s_]), (pi,))
                            k.act(ey[:, :], sq[s_][:, :], AF.Exp, (sq[s_],), (ey,), scale=-2.0 * GK)
                            yield
                            k.act(er[:, :], pr[0:RB, :], AF.Exp, (pr, hb), (er,), scale=-1.0, bias=hb[:, 0, n:n + 1])
                            k.act(ei[:, :], pi[0:RB, :], AF.Exp, (pi, hb), (ei,), scale=-1.0, bias=hb[:, 1, n:n + 1])
                            k.act(ey[:, :], ey[:, :], AF.Ln, (ey, one), (ey,), bias=one[:, 0:1])
                            yield
                            k.act(er[:, :], er[:, :], AF.Ln, (er, one), (er,), bias=one[:, 0:1])
                            k.act(ei[:, :], ei[:, :], AF.Ln, (ei, one), (ei,), bias=one[:, 0:1])
                            k.act(ey[:, :], ey[:, :], AF.Exp, (ey,), (ey,), scale=-1.0)
                            yield
                            k.act(er[:, :], er[:, :], AF.Exp, (er,), (er,), scale=-1.0)
                            k.tt("dve", ey[:, :], ey[:, :], pyb[0:RB, :], ALU.mult, (ey, pyb), (ey,))
                            yield
                            k.act(aa[s_][:, :], er[:, :], AF.Exp, (er, cc), (aa[s_],), scale=cc[:, 0, n:n + 1])
                            k.act(a2[s_][:, :], er[:, :], AF.Exp, (er, cc), (a2[s_],), scale=cc[:, 1, n:n + 1])
                            yield
                            k.ts("pool", a2[s_][:, :], a2[s_][:, :], -1.0, 1.0, ALU.mult, ALU.add, (a2[s_],), (a2[s_],))
                            k.ts("pool", a2[s_][:, :], a2[s_][:, :], 1.0, 1e-12, ALU.min, ALU.max, (a2[s_],), (a2[s_],))
                            yield
                            k.act(a2[s_][:, :], a2[s_][:, :], AF.Ln, (a2[s_],), (a2[s_],))
                            yield
                            k.stt(ei[:, :], a2[s_][:, :], 0.5, ei[:, :], ALU.mult, ALU.subtract, (a2[s_], ei), (ei,))
                            yield
                            k.act(ei[:, :], ei[:, :], AF.Exp, (ei,), (ei,))
                            yield
                            k.tt("dve", ei[:, :], ei[:, :], xcT[s_][:, :], ALU.mult, (ei, xcT[s_]), (ei,))
                            k.P.op("dve", (lambda e, o=hs[s_][:, :], d0=aa[s_][:, :], d1=ei[:, :], i0=hcar[:, n:n + 1]:
                                           e.tensor_tensor_scan(o, d0, d1, i0, ALU.mult, ALU.add)),
                                   _rs((aa[s_], ei, hcar)), _rs((hs[s_],)))
                            k.cp("pool", hcar[:, n:n + 1], hs[s_][:, T - 1:T], (hs[s_],), (hcar,))
                            k.tt("dve", recy[:, n, :], ey[:, :], hs[s_][:, :], ALU.mult, (ey, hs[s_]), (recy,))
                            yield

                        active = []
                        nxt_n = 0
                        nblk = min(KNBLK, NBLK)
                        while active or nxt_n < nblk:
                            if nxt_n < nblk and len(active) < NSET and (not active or active[-1][1] >= int(os.environ.get('KSKEW', '3'))):
                                active.append([blk(nxt_n), 0])
                                nxt_n += 1
                            for ent in list(active):
                                try:
                                    next(ent[0])
                                    ent[1] += 1
                                except StopIteration:
                                    active.remove(ent)
                        for j in range(4):
                            for hf in range(2):
                                pb = banks[bi % 4]
                                bi += 1
                                for n in range(NBLK):
                                    k.mm(pb[:, :], recy[:, n, j * 128:(j + 1) * 128], wout[:, n, hf * 512:(hf + 1) * 512], n == 0, n == NBLK - 1,
                                         (recy, wout), (pb,))
                                k.tt("dve", xin[:, j, hf * 512:(hf + 1) * 512], pb[:, :], xin[:, j, hf * 512:(hf + 1) * 512], ALU.add, (pb, xin), (xin,))
                        k.dma(tok_view(hmid_d, t0), xin[:, :, :], (xin,), (), xin)
                P.flush()

        def phase_ffn(layer, src_d, dst_d, with_o, final):
            with contextlib.ExitStack() as st:
                if os.environ.get("KDBG"):
                    print("ffn sbuf remaining at start", nc.sbuf_bytes_remaining)
                wgu = k.sb(st, [128, 8, 2 * DFF], BF16, "wgu")
                wdn = k.sb(st, [128, NFC, D], BF16, "wdn")
                wo = k.sb(st, [128, 8, D], BF16, "wo") if with_o else None
                gfin = k.sb(st, [128, D], F32, "gfin") if final else None
                with contextlib.ExitStack() as st2:
                    stages = [k.sb(st2, [128, 2816], F32, "stg") for _ in range(3)]
                    gi = 1 if layer == 0 else 4
                    for kc in range(8):
                        k.wload(stages, lambda a, b, kc=kc: wgu[:, kc, a:b], lambda a, b, kc=kc: wgu_d[layer][:, kc * 2 * DFF + a:kc * 2 * DFF + b],
                                2 * DFF, wgu, piece=2816, gain=gall[:, gi, kc:kc + 1])
                    for fc in range(0, NFC, 2):
                        k.wload(stages, lambda a, b, fc=fc: wdn[:, fc:fc + 2, :].rearrange("p f d -> p (f d)")[:, a:b],
                                lambda a, b, fc=fc: wdn_d[layer][:, fc * D + a:fc * D + b], 2 * D, wdn, piece=2 * D)
                    if with_o:
                        for kc in range(0, 8, 2):
                            k.wload(stages, lambda a, b, kc=kc: wo[:, kc:kc + 2, :].rearrange("p f d -> p (f d)")[:, a:b],
                                    lambda a, b, kc=kc: wo_d[:, kc * D + a:kc * D + b], 2 * D, wo, piece=2 * D)
                    if final:
                        g1 = k.sb(st2, [1, D], F32)
                        on = k.sb(st2, [1, 128], F32)
                        pbk = k.ps(st2, [128, 512], F32)
                        k.dma(g1[:, :], gfin_d, (), (g1,), g1)
                        k.memset("dve", on[:, :], 1.0, (on,))
                        for hf in range(2):
                            k.mm(pbk[:, :], on[:, :], g1[:, hf * 512:(hf + 1) * 512], True, True, (on, g1), (pbk,))
                            k.cp("dve", gfin[:, hf * 512:(hf + 1) * 512], pbk[:, :], (pbk,), (gfin,))
                    P.flush()
                if os.environ.get("KDBG"):
                    print("ffn sbuf remaining before acts", nc.sbuf_bytes_remaining)
                xins = [k.sb(st, [128, 4, D], F32, "xin") for _ in range(1 if with_o else 2)] * 2
                hnb = [k.sb(st, [128, D], BF16, "hnb") for _ in range(1 if with_o else 2)] * 2
                hnT = k.sb(st, [128, 8, T], BF16, "hnT")
                if os.environ.get("KDBG"):
                    print("ffn sbuf remaining before actT", nc.sbuf_bytes_remaining)
                actT = k.sb(st, [128, NFC, T], BF16, "actT")
                ss = k.sb(st, [128, 4], F32)
                rstd = k.sb(st, [128, 4], F32)
                tg = [k.sb(st, [128, T], BF16) for _ in range(1 if with_o else 2)]
                tA = [k.sb(st, [128, T], F32) for _ in range(1)]
                banks = [k.ps(st, [128, 512], F32, "bk") for _ in range(6)]
                tbanks = [k.ps(st, [128, 1024], BF16, "tb") for _ in range(2)]
                bi = 0
                if not with_o:
                    k.dma(xins[0][:, :, :], tok_view(src_d, 0), (), (xins[0],), xins[0])
                for ci in range(NT // T):
                    t0 = ci * T
                    xin = xins[ci % 2]
                    if with_o:
                        k.dma(xin[:, :, :], tok_view(src_d, t0), (), (xin,), xin)
                    elif ci + 1 < NT // T:
                        k.dma(xins[(ci + 1) % 2][:, :, :], tok_view(src_d, t0 + T), (), (xins[(ci + 1) % 2],), xins[(ci + 1) % 2])
                    if with_o:
                        oi = actT[:, 0:8, :].rearrange("p (j a) t -> p j (a t)", j=4)
                        k.dma(oi, tok_view(o_d, t0), (), (actT,), actT)
                        for j in range(4):
                            tb = tbanks[j % 2]
                            for kc in range(8):
                                k.tr(tb[:, kc * 128:(kc + 1) * 128], oi[:, j, kc * 128:(kc + 1) * 128], ident[:, :], (actT, ident), (tb,))
                            src = tb[:, :].rearrange("p (a b) -> p a b", a=8)
                            if j % 2 == 0:
                                k.act(hnT[:, :, j * 128:(j + 1) * 128], src, AF.Copy, (tb,), (hnT,))
                            else:
                                k.cp("dve", hnT[:, :, j * 128:(j + 1) * 128], src, (tb,), (hnT,))
                        for j in range(4):
                            for hf in range(2):
                                pb = banks[bi % 6]
                                bi += 1
                                for kc in range(8):
                                    k.mm(pb[:, :], hnT[:, kc, j * 128:(j + 1) * 128], wo[:, kc, hf * 512:(hf + 1) * 512], kc == 0, kc == 7,
                                         (hnT, wo), (pb,))
                                k.tt("dve", xin[:, j, hf * 512:(hf + 1) * 512], pb[:, :], xin[:, j, hf * 512:(hf + 1) * 512], ALU.add, (pb, xin), (xin,))
                    norm_transpose(k, xin, hnb, hnT, ss, rstd, mhalf, ident, tbanks)
                    bi = ffn_chunk(k, hnT, actT, wgu, wdn, banks, bi, tg, tA)
                    bi = ffn_down(k, actT, wdn, xin, banks, bi)
                    if final:
                        for j in range(4):
                            hb_ = hnb[j % 2]
                            k.act(hb_[:, :], xin[:, j, :], AF.Square, (xin,), (hb_, ss), accum=ss[:, j:j + 1])
                        k.ts("dve", rstd[:, 0:4], ss[:, 0:4], 1.0 / D, EPS, ALU.mult, ALU.add, (ss,), (rstd,))
                        k.tt("pool", rstd[:, 0:4], rstd[:, 0:4], mhalf[:, 0:4], ALU.pow, (rstd, mhalf), (rstd,))
                        for j in range(4):
                            k.stt(xin[:, j, :], xin[:, j, :], rstd[:, j:j + 1], gfin[:, :], ALU.mult, ALU.mult, (xin, rstd, gfin), (xin,))
                    k.dma(tok_view(dst_d, t0), xin[:, :, :], (xin,), (), xin)
                P.flush()

        if "B" in phases:
            phase_ffn(0, hmid_d, h1_d, False, False)

        if "C" in phases:
            with contextlib.ExitStack() as st:
                wkc = k.sb(st, [128, 8, NG * 96], BF16, "wkc")
                wvc = k.sb(st, [128, 8, NG * 64], BF16, "wvc")
                wks = k.sb(st, [128, 8, NG * 128], BF16, "wks")
                wkw = k.sb(st, [128, 8, NG * 128], BF16, "wkw")
                wv = k.sb(st, [128, 8, 512], BF16, "wv")
                wq = k.sb(st, [128, 8, NH * 128], BF16, "wq")
                wqg = k.sb(st, [128, 8, 48], BF16, "wqg")
                gbb = k.sb(st, [1, 48], BF16, "gbb")
                ones1 = k.sb(st, [1, 128], BF16, "ones1")
                pm = k.sb(st, [128, 64], BF16, "pm")
                rc = k.sb(st, [128, S], F32, "rc")
                rsn = k.sb(st, [128, S], F32, "rsn")
                with contextlib.ExitStack() as st2:
                    stages = [k.sb(st2, [128, 2048], F32, "stg") for _ in range(3)]
                    for kc in range(8):
                        for (wt, wd, nco, gi) in ((wkc, wkc_d, NG * 96, 2), (wvc, wvc_d, NG * 64, 2), (wks, wks_d, NG * 128, 2),
                                                  (wkw, wkw_d, NG * 128, 2), (wv, wv_d, 512, 2), (wq, wq_d, NH * 128, 3), (wqg, wqg_d, 48, 3)):
                            k.wload(stages, lambda a, b, wt=wt, kc=kc: wt[:, kc, a:b], lambda a, b, wd=wd, kc=kc, nco=nco: wd[:, kc * nco + a:kc * nco + b],
                                    nco, wt, piece=2048, gain=gall[:, gi, kc:kc + 1])
                    k.wload(stages, lambda a, b: pm[:, a:b], lambda a, b: pm_d[:, a:b], 64, pm)
                    k.wload(stages, lambda a, b: gbb[:, a:b], lambda a, b: gb_d[:, a:b], 48, gbb, npart=1)
                    k.memset("dve", ones1[:, :], 1.0, (ones1,))
                    k.dma(rc[:, :], ropec_d, (), (rc,), rc)
                    k.dma(rsn[:, :], ropes_d, (), (rsn,), rsn)
                    P.flush()
                xins = [k.sb(st, [128, 4, D], F32, "xin") for _ in range(2)]
                hnb = [k.sb(st, [128, D], BF16, "hnb") for _ in range(2)]
                hnT = k.sb(st, [128, 8, T], BF16, "hnT")
                ss = k.sb(st, [128, 4], F32)
                rstd = k.sb(st, [128, 4], F32)
                o_kc = [k.sb(st, [96, NG, T], BF16) for _ in range(1)] * 2
                o_vc = [k.sb(st, [64, NG, T], BF16) for _ in range(1)] * 2
                o_ks = [k.sb(st, [128, NG, T], BF16) for _ in range(1)] * 2
                o_kw = [k.sb(st, [128, NG, T], BF16) for _ in range(1)] * 2
                o_v = [k.sb(st, [128, 4, 520], BF16) for _ in range(2)]
                for ov_ in o_v:
                    k.memset("pool", ov_[:, :, :], 1.0, (ov_,))
                o_qn = [k.sb(st, [128, NH, T], BF16) for _ in range(1)] * 2
                o_qr = [k.sb(st, [64, NH, T], BF16) for _ in range(1)] * 2
                o_gt = [k.sb(st, [128, 4, 48], F32) for _ in range(2)]
                r1 = [k.sb(st, [64, T], F32) for _ in range(2)]
                r2 = [k.sb(st, [64, T], F32) for _ in range(2)]
                ks_r = [Res("ks%d" % i) for i in range(NG)]
                kw_r = [Res("kw%d" % i) for i in range(NG)]
                qn_r = [Res("qn%d" % i) for i in range(NH)]
                qr_r = [Res("qr%d" % i) for i in range(NH)]
                banks = [k.ps(st, [128, 512], F32, "bk") for _ in range(6)]
                tbanks = [k.ps(st, [128, 1024], BF16, "tb") for _ in range(2)]
                bi = 0
                ri = 0
                k.dma(xins[0][:, :, :], tok_view(h1_d, 0), (), (xins[0],), xins[0])
                for ci in range(NT // T):
                    t0 = ci * T
                    b = t0 // S
                    p0 = t0 % S
                    xin = xins[ci % 2]
                    s_ = ci % 2
                    if ci + 1 < NT // T:
                        k.dma(xins[(ci + 1) % 2][:, :, :], tok_view(h1_d, t0 + T), (), (xins[(ci + 1) % 2],), xins[(ci + 1) % 2])
                    norm_transpose(k, xin, hnb, hnT, ss, rstd, mhalf, ident, tbanks)

                    def proj(wt, c0, m):
                        nonlocal bi
                        pb = banks[bi % 6]
                        bi += 1
                        for kc in range(8):
                            k.mm(pb[0:m, :], wt[:, kc, c0:c0 + m], hnT[:, kc, :], kc == 0, kc == 7, (wt, hnT), (pb,))
                        return pb

                    for g in range(NG):
                        pb = proj(wkc, g * 96, 96)
                        k.act(o_kc[s_][:, g, :], pb[0:96, :], AF.Copy, (pb,), (o_kc[s_],))
                        pb = proj(wvc, g * 64, 64)
                        k.cp("dve", o_vc[s_][:, g, :], pb[0:64, :], (pb,), (o_vc[s_],))
                    units = []
                    for (wt, ot, rl) in ((wks, o_ks[s_], ks_r), (wkw, o_kw[s_], kw_r)):
                        for g in range(NG):
                            units.append((wt, g * 128, ot, g, rl[g], ot, rl[g], 1.0))
                    for h in range(NH):
                        units.append((wq, h * 128, o_qn[s_], h, qn_r[h], o_qr[s_], qr_r[h], float(DQK ** -0.5)))
                    prev = None
                    for u in units + [None]:
                        if u is not None:
                            (wt, c0, dst, slot, dres, rdst, rres, scale) = u
                            pb = proj(wt, c0, 128)
                            k.act(dst[:, slot, :], pb[:, :], AF.Copy, (pb,), (dres,), scale=scale)
                        if prev is not None:
                            (wt, c0, dst, slot, dres, rdst, rres, scale) = prev
                            pb = banks[bi % 6]
                            bi += 1
                            k.mm(pb[0:64, :], pm[:, :], dst[:, slot, :], True, True, (pm, dres), (pb,))
                            a = r1[ri % 2]
                            bb = r2[ri % 2]
                            ri += 1
                            k.tt("dve", a[32:64, :], pb[32:64, :], rsn[32:64, p0:p0 + T], ALU.mult, (pb, rsn), (a,))
                            k.tt("pool", bb[32:64, :], dst[32:64, slot, :], rc[32:64, p0:p0 + T], ALU.mult, (dres, rc), (bb,))
                            k.tt("dve", rdst[32:64, slot, :], a[32:64, :], bb[32:64, :], ALU.add, (a, bb), (rres,))
                        prev = u
                    for j in range(4):
                        pb = banks[bi % 6]
                        bi += 1
                        for kc in range(8):
                            k.mm(pb[:, :], hnT[:, kc, j * 128:(j + 1) * 128], wv[:, kc, :], kc == 0, kc == 7, (wv, hnT), (pb,))
                        k.cp("dve", o_v[s_][:, j, :].rearrange("p (a c) -> p a c", c=65)[:, :, 0:64], pb[:, :].rearrange("p (a c) -> p a c", c=64),
                             (pb,), (o_v[s_],))
                        pb = banks[bi % 6]
                        bi += 1
                        for kc in range(8):
                            k.mm(pb[:, 0:48], hnT[:, kc, j * 128:(j + 1) * 128], wqg[:, kc, :], kc == 0, False, (wqg, hnT), (pb,))
                        k.mm(pb[:, 0:48], ones1[:, :], gbb[:, :], False, True, (ones1, gbb), (pb,))
                        k.act(o_gt[s_][:, j, :], pb[:, 0:48], AF.Tanh, (pb,), (o_gt[s_],), scale=0.5)
                        k.ts("dve", o_gt[s_][:, j, :], o_gt[s_][:, j, :], 0.5, 0.5, ALU.mult, ALU.add, (o_gt[s_],), (o_gt[s_],))
                    k.dma(kcmp_d[b, :, :, p0:p0 + T], o_kc[s_][:, :, :], (o_kc[s_],), (), o_kc[s_])
                    k.dma(vcmp_d[b, :, :, p0:p0 + T], o_vc[s_][:, :, :], (o_vc[s_],), (), o_vc[s_])
                    k.dma(ksel_d[b, :, :, p0:p0 + T], o_ks[s_][32:128, :, :], tuple(ks_r), (), o_ks[s_])
                    k.dma(kwin_d[b, :, :, p0:p0 + T], o_kw[s_][32:128, :, :], tuple(kw_r), (), o_kw[s_])
                    k.dma(v_d[t0:t0 + T, :].rearrange("(j p) d -> p j d", p=128), o_v[s_][:, :, :], (o_v[s_],), (), o_v[s_])
                    k.dma(qn_d[b, :, :, p0:p0 + T], o_qn[s_][32:128, :, :], tuple(qn_r), (), o_qn[s_])
                    k.dma(qr_d[b, :, :, p0:p0 + T], o_qr[s_][32:64, :, :], tuple(qr_r) + tuple(qn_r), (), o_qr[s_])
                    k.dma(gt_d[t0:t0 + T, :].rearrange("(j p) d -> p j d", p=128), o_gt[s_][:, :, :], (o_gt[s_],), (), o_gt[s_])
                P.flush()

        if "D" in phases:
            with contextlib.ExitStack() as st:
                w1k = k.sb(st, [96, 32, 256], BF16, "w1k")
                w1v = k.sb(st, [64, 32, 256], BF16, "w1v")
                w2k = k.sb(st, [128, 2, 128], BF16, "w2k")
                w2v = k.sb(st, [128, 2, 64], BF16, "w2v")
                posk = k.sb(st, [96, 32], BF16, "posk")
                posv = k.sb(st, [64, 32], BF16, "posv")
                pbias = k.sb(st, [128, 4], F32, "pbias")
                tri = k.sb(st, [128, 256], BF16, "tri")
                maskc = k.sb(st, [128, S], BF16, "maskc")
                tkadd = k.sb(st, [128, 16, 32], F32, "tkadd")
                tkmul = k.sb(st, [128, 16, 32], F32, "tkmul")
                vcaug = k.sb(st, [128, NG, 97], BF16, "vcaug")
                kcT = k.sb(st, [128, NG, 128], BF16, "kcT")
                ksT = k.sb(st, [128, NG, S], BF16, "ksT")
                kwT = k.sb(st, [128, NG, S], BF16, "kwT")
                with contextlib.ExitStack() as st2:
                    stages = [k.sb(st2, [128, 2048], F32, "stg") for _ in range(3)]
                    for l0 in range(0, 32, 8):
                        k.wload(stages, lambda a, b, l0=l0: w1k[:, l0:l0 + 8, :].rearrange("p l h -> p (l h)")[:, a:b],
                                lambda a, b, l0=l0: w1k_d[:, l0 * 256 + a:l0 * 256 + b], 2048, w1k, npart=96)
                        k.wload(stages, lambda a, b, l0=l0: w1v[:, l0:l0 + 8, :].rearrange("p l h -> p (l h)")[:, a:b],
                                lambda a, b, l0=l0: w1v_d[:, l0 * 256 + a:l0 * 256 + b], 2048, w1v, npart=64)
                    k.wload(stages, lambda a, b: w2k[:, :, :].rearrange("p l h -> p (l h)")[:, a:b], lambda a, b: w2k_d[:, a:b], 256, w2k)
                    k.wload(stages, lambda a, b: w2v[:, :, :].rearrange("p l h -> p (l h)")[:, a:b], lambda a, b: w2v_d[:, a:b], 128, w2v)
                    k.wload(stages, lambda a, b: posk[:, a:b], lambda a, b: posk_d[:, a:b], 32, posk, npart=96)
                    k.wload(stages, lambda a, b: posv[:, a:b], lambda a, b: posv_d[:, a:b], 32, posv, npart=64)
                    k.wload(stages, lambda a, b: tri[:, a:b], lambda a, b: tri_d[:, a:b], 256, tri)
                    k.wload(stages, lambda a, b: maskc[:, a:b], lambda a, b: maskc_d[:, a:b], S, maskc)
                    k.dma(tkadd[:, :, :], tkadd_d.rearrange("(j p) n -> p j n", p=128), (), (tkadd,), tkadd)
                    k.dma(tkmul[:, :, :], tkmul_d.rearrange("(j p) n -> p j n", p=128), (), (tkmul,), tkmul)
                    k.memset("pool", vcaug[:, :, 0:64], 0.0, (vcaug,))
                    for g in range(NG):
                        k.wload(stages, lambda a, b, g=g: vcaug[:, g, 64 + a:64 + b], lambda a, b: ovl_d[:, a:b], 33, vcaug)
                        k.wload(stages, lambda a, b, g=g: ksT[0:32, g, a:b], lambda a, b: esel_d[:, a:b], S, ksT, npart=32)
                    k.memset("pool", kwT[0:32, :, :], 0.0, (kwT,))
                    k.memset("dve", kcT[:, :, :], 0.0, (kcT,))
                    pbk = k.ps(st2, [128, 512], F32)
                    for (w1, pos, col) in ((w1k, posk, 0), (w1v, posv, 2)):
                        for hh in range(2):
                            for l in range(32):
                                k.mm(pbk[:, col + hh:col + hh + 1], w1[:, l, hh * 128:(hh + 1) * 128], pos[:, l:l + 1], l == 0, l == 31,
                                     (w1, pos), (pbk,), skip=True)
                    k.cp("dve", pbias[:, :], pbk[:, 0:4], (pbk,), (pbias,))
                    P.flush()

                kcmp = k.sb(st, [96, NG, S], BF16, "kcmp")
                vcmp = k.sb(st, [64, NG, S], BF16, "vcmp")
                vaug = k.sb(st, [128, 16, 2, NG, 65], BF16, "vaug")
                gts = k.sb(st, [128, 16, 48], F32, "gts")
                qnt = k.sb(st, [128, 4, S], BF16, "qn")
                qat = k.sb(st, [128, 4, S], BF16, "qa")
                oacc = k.sb(st, [128, 16, 256], F32, "oacc")
                obf = k.sb(st, [128, 16, 256], BF16, "obf")
                hid = [k.sb(st, [128, 2, 512], BF16) for _ in range(2)]
                gx = k.sb(st, [128, 512], F32)
                gs = k.sb(st, [128, 512], F32)
                gt_ = k.sb(st, [128, 512], BF16)
                em = [k.sb(st, [128, 512], BF16) for _ in range(2)]
                pmk = [k.sb(st, [128, 512], BF16) for _ in range(5)]
                e2b = [k.sb(st, [128, 512], BF16) for _ in range(4)]
                qa_r = [Res("qa%d" % i) for i in range(4)]
                oa_r = [Res("oa%d" % i) for i in range(4)]
                rden = [k.sb(st, [128, 4], F32) for _ in range(3)]
                fac = [k.sb(st, [128, 4], F32) for _ in range(3)]
                tmp = [k.sb(st, [128, 4, 64], F32) for _ in range(3)]
                impn = k.sb(st, [128, 4, 32], F32)
                imp = k.sb(st, [128, 32], F32)
                top8 = k.sb(st, [128, 8], F32)
                selb = k.sb(st, [128, 128], BF16)
                sbanks = [k.ps(st, [128, 512], F32, "sb") for _ in range(4)]
                abanks = [k.ps(st, [128, 512], F32, "ab") for _ in range(3)]
                tbank = k.ps(st, [128, 1024], BF16, "tb")
                k.memset("dve", qnt[0:32, :, :], 0.0, (qnt,))
                k.memset("dve", selb[:, :], 0.0, (selb,))
                cnt = {"s": 0, "a": 0, "p": 0, "e": 0, "n": 0}

                def nxt(lst, key):
                    cnt[key] += 1
                    return lst[cnt[key] % len(lst)]

                def rcp(out, in_, rd, wr):
                    P.op("dve", lambda e: e.reciprocal(out, in_), _rs(rd), _rs(wr))

                for b in range(nseq):
                    k.dma(kcmp[:, :, :], kcmp_d[b], (), (kcmp,), kcmp)
                    k.dma(vcmp[:, :, :], vcmp_d[b], (), (vcmp,), vcmp)
                    k.dma(ksT[32:128, :, :], ksel_d[b], (), (ksT,), ksT)
                    k.dma(kwT[32:128, :, :], kwin_d[b], (), (kwT,), kwT)
                    k.dma(vaug[:, :, :, :, :].rearrange("p j b g c -> p j (b g c)"), v_d[b * S:(b + 1) * S, :].rearrange("(j p) c -> p j c", p=128),
                          (), (vaug,), vaug)
                    k.dma(gts[:, :, :], gt_d[b * S:(b + 1) * S, :].rearrange("(j p) n -> p j n", p=128), (), (gts,), gts)
                    for (which, raw, w1) in ((0, kcmp, w1k), (1, vcmp, w1v)):
                        for hh in range(2):
                            pb = nxt(sbanks, "s")
                            for l in range(32):
                                k.mm(pb[:, 0:NG * NCMP].rearrange("p (g c) -> p g c", g=NG), w1[:, l, hh * 128:(hh + 1) * 128],
                                     raw[:, :, l:l + 16 * (NCMP - 1) + 1:16], l == 0, l == 31, (w1, raw), (pb,))
                            bcol = pbias[:, which * 2 + hh:which * 2 + hh + 1]
                            k.act(gx[:, 0:508], pb[:, 0:508], AF.Identity, (pb, pbias), (gx,), bias=bcol)
                            k.act(gs[:, 0:508], pb[:, 0:508], AF.Square, (pb, pbias), (gs,), bias=bcol)
                            k.ts("dve", gs[:, 0:508], gs[:, 0:508], GC, 1.0, ALU.mult, ALU.add, (gs,), (gs,))
                            k.tt("dve", gs[:, 0:508], gs[:, 0:508], gx[:, 0:508], ALU.mult, (gs, gx), (gs,))
                            k.act(gt_[:, 0:508], gs[:, 0:508], AF.Tanh, (gs,), (gt_,), scale=GK)
                            k.stt(gs[:, 0:508], gt_[:, 0:508], 1.0, gx[:, 0:508], ALU.add, ALU.mult, (gt_, gx), (gs,))
                            k.ts("dve", hid[which][:, hh, 0:508], gs[:, 0:508], 0.5, 0.0, ALU.mult, ALU.add, (gs,), (hid[which],))
                    for g in range(NG):
                        pb = nxt(abanks, "a")
                        for hh in range(2):
                            k.mm(pb[:, 0:NCMP], w2k[:, hh, :], hid[0][:, hh, g * NCMP:(g + 1) * NCMP], hh == 0, hh == 1, (w2k, hid[0]), (pb,))
                        k.cp("dve", kcT[:, g, 0:NCMP], pb[:, 0:NCMP], (pb,), (kcT,))
                        pb = nxt(abanks, "a")
                        for hh in range(2):
                            k.mm(pb[0:NCMP, 0:64], hid[1][:, hh, g * NCMP:(g + 1) * NCMP], w2v[:, hh, :], hh == 0, hh == 1, (w2v, hid[1]), (pb,))
                        k.cp("dve", vcaug[0:NCMP, g, 0:64], pb[0:NCMP, 0:64], (pb,), (vcaug,))
                    for g in range(NG):
                        k.dma(qnt[32:128, :, :], qn_d[b, :, 4 * g:4 * g + 4, :], (), (qnt,), qnt)
                        k.dma(qat[64:128, :, :], qn_d[b, 32:96, 4 * g:4 * g + 4, :], (), tuple(qa_r), qat)
                        k.dma(qat[32:64, :, :], qr_d[b, :, 4 * g:4 * g + 4, :], (), tuple(qa_r), qat)
                        def d2_setup(qc, g=g):
                            for h in range(4):
                                pb = nxt(sbanks, "s")
                                k.mm(pb[0:NCMP, :], kcT[:, g, 0:NCMP], qnt[:, h, qc * T:(qc + 1) * T], True, True, (kcT, qnt), (pb,))
                                e1 = nxt(em, "e")
                                e2 = e2b[h]
                                k.act(e1[0:NCMP, :], pb[0:NCMP, :], AF.Exp, (pb,), (e1,))
                                k.tt("pool", e2[0:NCMP, :], e1[0:NCMP, :], maskc[0:NCMP, qc * T:(qc + 1) * T], ALU.mult, (e1, maskc), (e2,))

                        def d2_qtile(qc, j, g=g):
                            qt = qc * 4 + j
                            pa = nxt(abanks, "a")
                            for h in range(4):
                                k.mm(pa[:, h * 97:(h + 1) * 97], e2b[h][0:NCMP, j * 128:(j + 1) * 128], vcaug[0:NCMP, g, 0:97], h == 0, h == 3,
                                     (e2b[h], vcaug), (pa,), skip=True)
                            p3 = pa[:, 0:388].rearrange("p (h c) -> p h c", h=4)
                            rd_ = nxt(rden, "n")
                            fc_ = fac[cnt["n"] % 3]
                            k.ts("dve", rd_[:, :], p3[:, :, 96], 1e-30, 0.0, ALU.max, ALU.add, (pa,), (rd_,))
                            rcp(rd_[:, :], rd_[:, :], (rd_,), (rd_,))
                            k.tt("dve", impn[:, :, :], p3[:, :, 64:96], rd_[:, :].unsqueeze(2).to_broadcast([128, 4, 32]), ALU.mult, (pa, rd_), (impn,))
                            P.op("dve", lambda e, o=imp[:, :], i=impn[:, :, :].rearrange("p h n -> p n h"): e.tensor_reduce(o, i, axis=AX.X, op=ALU.add),
                                 _rs((impn,)), _rs((imp,)))
                            k.tt("dve", imp[:, :], imp[:, :], tkmul[:, qt, :], ALU.mult, (imp, tkmul), (imp,))
                            k.tt("dve", imp[:, :], imp[:, :], tkadd[:, qt, :], ALU.add, (imp, tkadd), (imp,))
                            P.op("dve", lambda e, o=top8[:, :], i=imp[:, :]: e.max(o, i), _rs((imp,)), _rs((top8,)))
                            k.ts("dve", imp[:, :], imp[:, :], top8[:, 7:8], 1.0, ALU.is_ge, ALU.subtract, (imp, top8), (imp,))
                            k.ts("dve", selb[:, 0:32], imp[:, :], -NEG, 0.0, ALU.mult, ALU.add, (imp,), (selb,))
                            k.tr(tbank[:, 0:128], selb[:, :], ident[:, :], (selb, ident), (tbank,))
                            k.cp("dve", qat[0:32, :, qt * 128:(qt + 1) * 128], tbank[0:32, 0:128].unsqueeze(1).to_broadcast([32, 4, 128]), (tbank,), (qa_r[qc],))
                            k.tt("dve", fc_[:, :], rd_[:, :], gts[:, qt, 12 * g:12 * g + 12:3], ALU.mult, (rd_, gts), (fc_,))
                            k.tt("dve", oacc[:, qt, :].rearrange("p (h v) -> p h v", h=4), p3[:, :, 0:64],
                                 fc_[:, :].unsqueeze(2).to_broadcast([128, 4, 64]), ALU.mult, (pa, fc_), (oa_r[qc],))

                        d2_setup(0)
                        for j in range(4):
                            d2_qtile(0, j)
                        tiles = []
                        for qc in range(4):
                            for h in range(4):
                                pre = []
                                if qc < 3:
                                    if h == 0:
                                        pre.append(lambda qc=qc: d2_setup(qc + 1))
                                    pre.append(lambda qc=qc, h=h: d2_qtile(qc + 1, h))
                                nkt = 4 * qc + 4
                                lst = []
                                for kt in range(nkt):
                                    q0 = max(qc * T, kt * 128)
                                    ncol = (qc + 1) * T - q0
                                    masks = [(0, 0)] if kt * 128 >= qc * T else []
                                    pv = [(qt - 4 * qc, qt * 128 - q0) for qt in range(q0 // 128, 4 * qc + 4)]
                                    lst.append(dict(kT=ksT, br=0, kt=kt, q0=q0, ncol=ncol, masks=masks, pv=pv))
                                tiles.append(dict(h=h, qc=qc, br=1, lst=lst, pre=pre))
                                lst = []
                                for kt in range(max(0, 4 * qc - 4), 4 * qc + 4):
                                    qlo = max(kt, 4 * qc)
                                    qhi = min(kt + 4, 4 * qc + 3)
                                    q0 = qlo * 128
                                    ncol = (qhi - qlo + 1) * 128
                                    masks = []
                                    if qlo == kt:
                                        masks.append((0, 0))
                                    if qhi == kt + 4:
                                        masks.append((ncol - 128, 128))
                                    pv = [(qt - 4 * qc, (qt - qlo) * 128) for qt in range(qlo, qhi + 1)]
                                    lst.append(dict(kT=kwT, br=1, kt=kt, q0=q0, ncol=ncol, masks=masks, pv=pv))
                                tiles.append(dict(h=h, qc=qc, br=2, lst=lst, pre=[]))
                        flat = []
                        for tg_ in tiles:
                            po = None
                            for i_, t_ in enumerate(tg_["lst"]):
                                flat.append((tg_, t_, i_ == 0, i_ == len(tg_["lst"]) - 1))
                        LOOK = 3
                        pend = []
                        for idx in range(len(flat) + LOOK):
                            if idx < len(flat):
                                tg_, t_, isf, isl = flat[idx]
                                h = tg_["h"]
                                if isf:
                                    for th_ in tg_["pre"]:
                                        th_()
                                    tg_["po"] = nxt(abanks, "a")
                                ps_ = nxt(sbanks, "s")
                                ncol = t_["ncol"]
                                k.mm(ps_[:, 0:ncol], t_["kT"][:, g, t_["kt"] * 128:(t_["kt"] + 1) * 128], qat[:, h, t_["q0"]:t_["q0"] + ncol], True, True,
                                     (t_["kT"], qa_r[tg_["qc"]]), (ps_,))
                                pb_ = nxt(pmk, "p")
                                k.act(pb_[:, 0:ncol], ps_[:, 0:ncol], AF.Exp, (ps_,), (pb_,))
                                for (c0, m0) in t_["masks"]:
                                    k.tt("pool", pb_[:, c0:c0 + 128], pb_[:, c0:c0 + 128], tri[:, m0:m0 + 128], ALU.mult, (pb_, tri), (pb_,))
                                t_["pb"] = pb_
                            if idx >= LOOK:
                                tg_, t_, isf, isl = flat[idx - LOOK]
                                h, qc, po = tg_["h"], tg_["qc"], tg_["po"]
                                pb_ = t_["pb"]
                                for pi_, (j, off) in enumerate(t_["pv"]):
                                    k.mm(po[:, j * 65:(j + 1) * 65], pb_[:, off:off + 128], vaug[:, t_["kt"], t_["br"], g, :], isf and pi_ == 0, isl,
                                         (pb_, vaug), (po,), skip=True)
                                if isl:
                                    br = tg_["br"]
                                    p3 = po[:, 0:260].rearrange("p (j c) -> p j c", j=4)
                                    rd_ = nxt(rden, "n")
                                    fc_ = fac[cnt["n"] % 3]
                                    tm_ = tmp[cnt["n"] % 3]
                                    rcp(rd_[:, :], p3[:, :, 64], (po,), (rd_,))
                                    gcol = (4 * g + h) * 3 + br
                                    k.tt("dve", fc_[:, :], rd_[:, :], gts[:, 4 * qc:4 * qc + 4, gcol], ALU.mult, (rd_, gts), (fc_,))
                                    k.tt("dve", tm_[:, :, :], p3[:, :, 0:64], fc_[:, :].unsqueeze(2).to_broadcast([128, 4, 64]), ALU.mult, (po, fc_), (tm_,))
                                    k.tt("pool", oacc[:, 4 * qc:4 * qc + 4, h * 64:(h + 1) * 64], oacc[:, 4 * qc:4 * qc + 4, h * 64:(h + 1) * 64],
                                         tm_[:, :, :], ALU.add, (oa_r[qc], tm_), (oa_r[qc],))
                        k.act(obf[:, :, :], oacc[:, :, :], AF.Copy, tuple(oa_r), (obf,))
                        k.dma(o_d[b * S:(b + 1) * S, g * 256:(g + 1) * 256].rearrange("(j p) v -> p j v", p=128), obf[:, :, :], (obf,), (), obf)
                P.flush()
        if "E" in phases:
            phase_ffn(1, h1_d, y_d, True, True)
    return nc


def _consts():
    c = {}
    c["c_ident"] = np.eye(128, dtype=np.float32)
    kk = np.arange(128)[:, None]
    qq = np.arange(128)[None, :]
    c["c_tri"] = np.concatenate([(kk <= qq), (kk > qq)], axis=1).astype(np.float32)
    cc = np.arange(128)[:, None]
    tt = np.arange(S)[None, :]
    c["c_maskc"] = ((16 * cc + 31 <= tt) & (cc < NCMP)).astype(np.float32)
    half = 12
    inv = (np.float32(500000.0) ** (-np.arange(half, dtype=np.float32) * np.float32(2.0) / np.float32(24))).astype(np.float32)
    ang = (np.arange(S, dtype=np.float32)[None, :] * inv[:, None]).astype(np.float32)
    cs = np.cos(ang.astype(np.float64)).astype(np.float32)
    sn = np.sin(ang.astype(np.float64)).astype(np.float32)
    rc = np.ones((128, S), np.float32)
    rs = np.zeros((128, S), np.float32)
    rc[32:44] = cs
    rc[44:56] = cs
    rs[32:44] = -sn
    rs[44:56] = sn
    c["c_ropec"] = rc
    c["c_ropes"] = rs
    pm = np.zeros((128, 64), np.float32)
    for d in range(12):
        pm[32 + d + 12, 32 + d] = 1.0
        pm[32 + d, 32 + d + 12] = 1.0
    c["c_pm"] = pm
    c["c_esel"] = (np.arange(S)[None, :] // 64 == np.arange(32)[:, None]).astype(np.float32)
    ncmp = NCMP
    cs_ = np.arange(ncmp) * 16
    ss_ = np.arange(32) * 64
    ovl = ((cs_[:, None] < ss_[None, :] + 64) & (cs_[:, None] + 32 > ss_[None, :])).astype(np.float32)
    o = np.zeros((128, 33), np.float32)
    o[:ncmp, :32] = ovl
    o[:ncmp, 32] = 1.0
    c["c_ovl"] = o
    t = np.arange(S)[:, None]
    n = np.arange(32)[None, :]
    cur = t // 64
    forced = (n == 0) | (n == cur) | (n == cur - 1)
    vis = (n * 64 <= t)
    c["c_tkadd"] = np.where(forced, 1e9, np.where(vis, 0.0, -1e9)).astype(np.float32)
    c["c_tkmul"] = (vis & ~forced).astype(np.float32)
    return c


def _kc(w):
    K_, M = w.shape
    return np.ascontiguousarray(w.reshape(K_ // 128, 128, M).transpose(1, 0, 2).reshape(128, -1))


def _prep_weights(inp):
    f = lambda a: np.ascontiguousarray(np.asarray(a, dtype=np.float32))
    w = {}
    ng = inp["norm_g"]
    gs = np.stack([ng[0, 0], ng[0, 1], inp["kv_norm_g"], ng[1, 0], ng[1, 1]], 0)
    w["g_all"] = f(gs.reshape(5, 8, 128).transpose(2, 0, 1))
    w["g_fin"] = f(inp["final_g"].reshape(1, D))
    w["w_in"] = _kc(f(inp["a_w_in"][0]))
    w["conv_w"] = f(inp["a_conv_w"][0].reshape(4, NBLK, RB).transpose(2, 1, 0).reshape(RB, NBLK * 4))
    sm = np.stack([inp["a_conv_b"][0].reshape(NBLK, RB).T, inp["a_b_ra"][0].T, inp["a_b_ix"][0].T,
                   inp["a_lambda"][0].reshape(NBLK, RB).T], 1)
    w["a_small"] = f(sm)
    w["w_ra"] = f(inp["a_w_ra"][0].transpose(1, 0, 2).reshape(RB, NBLK * RB))
    w["w_ix"] = f(inp["a_w_ix"][0].transpose(1, 0, 2).reshape(RB, NBLK * RB))
    w["w_out"] = f(inp["a_w_out"][0].reshape(NBLK, RB, D).transpose(1, 0, 2).reshape(RB, NBLK * D))
    for l in range(2):
        w["w_gu%d" % l] = _kc(f(inp["ffn_w_gu"][l]))
        w["w_dn%d" % l] = f(inp["ffn_w_down"][l].reshape(NFC, 128, D).transpose(1, 0, 2).reshape(128, NFC * D))
    kv = f(inp["kv_w"])
    kcmp_, vcmp_, ksel_, vsel_, kwin_, vwin_ = kv[:, 0:384], kv[:, 384:640], kv[:, 640:1024], kv[:, 1024:1280], kv[:, 1280:1664], kv[:, 1664:1920]
    w["w_kc"] = _kc(kcmp_)
    w["w_vc"] = _kc(vcmp_)

    def aug(m, n):
        z = np.zeros((m.shape[0], n, 128), np.float32)
        z[:, :, 32:] = m.reshape(m.shape[0], n, 96)
        return z.reshape(m.shape[0], n * 128)
    w["w_ks"] = _kc(aug(ksel_, NG))
    w["w_kw"] = _kc(aug(kwin_, NG))
    w["w_v"] = _kc(np.concatenate([vsel_, vwin_], 1))
    wq = f(inp["b_w_q"][0])
    w["w_q"] = _kc(aug(wq[:, :NH * DQK], NH))
    w["w_qg"] = _kc(wq[:, NH * DQK:])
    w["gate_b"] = f(inp["b_gate_bias"][0].reshape(1, 48))
    w["w_o"] = _kc(f(inp["b_w_o"][0]))
    w["w1k"] = f(inp["cmp_w1_k"].reshape(32, 96, 256).transpose(1, 0, 2).reshape(96, 32 * 256))
    w["w1v"] = f(inp["cmp_w1_v"].reshape(32, 64, 256).transpose(1, 0, 2).reshape(64, 32 * 256))
    w2k = np.zeros((256, 128), np.float32)
    w2k[:, 32:] = inp["cmp_w2_k"]
    w["w2k"] = f(w2k.reshape(2, 128, 128).transpose(1, 0, 2).reshape(128, 256))
    w["w2v"] = f(np.asarray(inp["cmp_w2_v"], np.float32).reshape(2, 128, 64).transpose(1, 0, 2).reshape(128, 128))
    w["posk"] = f(np.asarray(inp["cmp_pos_k"]).T)
    w["posv"] = f(np.asarray(inp["cmp_pos_v"]).T)
    return w


_CACHE = {}


def kernel(**inputs):
    inp = {k_: np.asarray(v) for k_, v in inputs.items()}
    ncores = 8
    nseq = inp["x"].shape[0] // ncores
    if "nc" not in _CACHE:
        _CACHE["nc"] = build_program(nseq)
    nc = _CACHE["nc"]
    shared = _consts()
    shared.update(_prep_weights(inp))
    x = np.ascontiguousarray(inp["x"], dtype=np.float32).reshape(ncores, nseq * S, D)
    in_maps = []
    for c in range(ncores):
        m = dict(shared)
        m["x"] = x[c]
        in_maps.append(m)
    res = run_bass_kernel_spmd(nc, in_maps, core_ids=list(range(ncores)))
    out = np.stack([np.asarray(r["y"]) for r in res.results], 0)
    return out.reshape(inp["x"].shape).astype(np.float32)
```

```python
import contextlib
import os
import numpy as np
import concourse.bass as bass
import concourse.mybir as mybir
from concourse.bass_utils import run_bass_kernel_spmd

F32 = mybir.dt.float32
BF16 = mybir.dt.bfloat16
AF = mybir.ActivationFunctionType
ALU = mybir.AluOpType
AX = mybir.AxisListType

D = 1024
S = 2048
T = 512
DRNN = 1536
NBLK = 16
RB = 96
DFF = 2816
NFC = 22
NH = 16
NG = 4
DQK = 96
DV = 64
NCMP = 127
EPS = 1e-6
NEG = -30000.0
GK = 0.7978845608028654
GC = 0.044715


class Res:
    __slots__ = ("name", "last_w", "readers")

    def __init__(self, name):
        self.name = name
        self.last_w = None
        self.readers = []


class Op:
    __slots__ = ("eng", "fn", "waits", "signal", "is_dma", "key", "pos", "sigidx", "is_nop")

    def __init__(self, eng, fn, is_dma=False, key=None):
        self.eng = eng
        self.fn = fn
        self.waits = []
        self.signal = False
        self.is_dma = is_dma
        self.key = key
        self.pos = -1
        self.sigidx = -1
        self.is_nop = False


ENGS = ("pe", "act", "dve", "pool", "sp")


class Prog:
    def __init__(self, nc, stack):
        self.nc = nc
        self.stack = stack
        self.streams = {e: [] for e in ENGS}
        self.emitted = {e: 0 for e in ENGS}
        self.nsig = {e: 0 for e in ENGS}
        self.npos = {e: 0 for e in ENGS}
        self.known = {e: {} for e in ENGS}
        self.dma_cnt = {}
        self.esem = {e: stack.enter_context(nc.semaphore("s_" + e)) for e in ENGS}
        self.ksem = {}

    def res(self, name="r"):
        return Res(name)

    def _dep(self, op, d, raw):
        if d is None or d is op:
            return
        E = op.eng
        k = self.known[E]
        if d.is_dma:
            src = ("dma", d.key)
            cnt = self.dma_cnt[d.key]
            if k.get(src, -1) >= cnt:
                return
            k[src] = cnt
            op.waits.append(("dma", d.key, cnt))
            return
        if d.eng == E and not op.is_dma:
            if E == "pe":
                return
        if k.get(d.eng, -1) >= d.pos:
            return
        k[d.eng] = d.pos
        d.signal = True
        op.waits.append(("eng", d))

    def op(self, eng, fn, reads=(), writes=(), is_dma=False, key=None):
        o = Op(eng, fn, is_dma, key)
        if is_dma:
            self.dma_cnt.setdefault(key, 0)
        else:
            o.pos = self.npos[eng]
            self.npos[eng] += 1
        for r in reads:
            self._dep(o, r.last_w, True)
        for w in writes:
            self._dep(o, w.last_w, False)
            for rd in w.readers:
                self._dep(o, rd, False)
        if is_dma:
            self.dma_cnt[key] += 1
            o.pos = self.dma_cnt[key]
        for r in reads:
            r.readers.append(o)
        for w in writes:
            w.last_w = o
            w.readers = []
        self.streams[eng].append(o)
        return o

    def dma(self, q, out, in_, reads=(), writes=(), key=None):
        key = id(key)
        return self.op(q, lambda e: e.dma_start(out=out, in_=in_), reads, writes, is_dma=True, key=key)

    def barrier(self):
        lasts = []
        for e in ENGS:
            comp = [o for o in self.streams[e] if not o.is_dma and not o.is_nop]
            if comp:
                lasts.append(comp[-1])
        for e in ENGS:
            o = Op(e, lambda eng: eng.nop())
            o.is_nop = True
            o.pos = self.npos[e]
            self.npos[e] += 1
            k = self.known[e]
            for d in lasts:
                if k.get(d.eng, -1) >= d.pos:
                    continue
                k[d.eng] = d.pos
                d.signal = True
                o.waits.append(("eng", d))
            for key, cnt in self.dma_cnt.items():
                src = ("dma", key)
                if k.get(src, -1) >= cnt:
                    continue
                k[src] = cnt
                o.waits.append(("dma", key, cnt))
            self.streams[e].append(o)

    def flush(self):
        self.barrier()
        nc = self.nc
        for e in ENGS:
            n = self.nsig[e]
            for o in self.streams[e]:
                if not o.is_dma and o.signal:
                    n += 1
                    o.sigidx = n
            self.nsig[e] = n
        for key in self.dma_cnt:
            if key not in self.ksem:
                self.ksem[key] = self.stack.enter_context(nc.semaphore("d_%d" % len(self.ksem)))
        esem, ksem = self.esem, self.ksem
        streams = self.streams

        def run(en, eng):
            for o in streams[en]:
                for w in o.waits:
                    if w[0] == "dma":
                        eng.wait_ge(ksem[w[1]], 16 * w[2])
                    else:
                        eng.wait_ge(esem[w[1].eng], w[1].sigidx)
                ins = o.fn(eng)
                if o.is_dma:
                    ins.then_inc(ksem[o.key], 16)
                elif o.signal:
                    ins.then_inc(esem[en], 1)

        with nc.Block() as block:
            @block.tensor
            def _(e):
                run("pe", e)

            @block.scalar
            def _(e):
                run("act", e)

            @block.vector
            def _(e):
                run("dve", e)

            @block.gpsimd
            def _(e):
                run("pool", e)

            @block.sync
            def _(e):
                run("sp", e)
        self.streams = {e: [] for e in ENGS}


class Tl:
    __slots__ = ("t", "r")

    def __init__(self, t, r):
        self.t = t
        self.r = r

    def __getitem__(self, k):
        return self.t[k]


def _rs(xs):
    out = []
    for x in xs:
        if x is None:
            continue
        out.append(x.r if isinstance(x, Tl) else x)
    return out


class K:
    def __init__(self, nc, P):
        self.nc = nc
        self.P = P
        self.n = 0
        self.rr = 0

    def sb(self, st, shape, dt=F32, name=None):
        self.n += 1
        nm = "%s_%d" % (name or "t", self.n)
        return Tl(st.enter_context(self.nc.sbuf_tensor(nm, list(shape), dt)), Res(nm))

    def ps(self, st, shape, dt=F32, name=None):
        self.n += 1
        nm = "%s_%d" % (name or "p", self.n)
        return Tl(st.enter_context(self.nc.psum_tensor(nm, list(shape), dt)), Res(nm))

    def mm(self, out, lhsT, rhs, start, stop, rd, wr, skip=False):
        if skip:
            f = lambda e: e.matmul(out, lhsT=lhsT, rhs=rhs, start=start, stop=stop, skip_group_check=True)
        else:
            f = lambda e: e.matmul(out, lhsT=lhsT, rhs=rhs, start=start, stop=stop)
        self.P.op("pe", f, _rs(rd), _rs(wr))

    def tr(self, out, in_, ident, rd, wr):
        self.P.op("pe", lambda e: e.transpose(out, in_, ident), _rs(rd), _rs(wr))

    def act(self, out, in_, func, rd, wr, scale=1.0, bias=None, accum=None):
        kw = {}
        if bias is not None:
            kw["bias"] = bias
        if accum is not None:
            kw["accum_out"] = accum
        self.P.op("act", lambda e: e.activation(out=out, in_=in_, func=func, scale=scale, **kw), _rs(rd), _rs(wr))

    def cp(self, eng, out, in_, rd, wr):
        self.P.op(eng, lambda e: e.tensor_copy(out, in_), _rs(rd), _rs(wr))

    def ts(self, eng, out, in0, s1, s2, op0, op1, rd, wr):
        self.P.op(eng, lambda e: e.tensor_scalar(out, in0, s1, s2, op0, op1), _rs(rd), _rs(wr))

    def tt(self, eng, out, in0, in1, op, rd, wr):
        self.P.op(eng, lambda e: e.tensor_tensor(out, in0, in1, op), _rs(rd), _rs(wr))

    def stt(self, out, in0, scalar, in1, op0, op1, rd, wr):
        self.P.op("dve", lambda e: e.scalar_tensor_tensor(out=out, in0=in0, scalar=scalar, in1=in1, op0=op0, op1=op1),
                  _rs(rd), _rs(wr))

    def memset(self, eng, out, val, wr):
        self.P.op(eng, lambda e: e.memset(out, val), (), _rs(wr))

    def dma(self, out, in_, rd, wr, key, q="sp"):
        self.P.dma(q, out, in_, _rs(rd), _rs(wr), key=(key.r if isinstance(key, Tl) else key))

    def ceng(self):
        self.rr += 1
        return ("dve", "pool")[self.rr % 2]

    def wload(self, stages, dst_fn, src_fn, ncol, dst, piece=2048, gain=None, npart=128):
        c0 = 0
        while c0 < ncol:
            c1 = min(ncol, c0 + piece)
            stg = stages[self.rr % len(stages)]
            self.dma(stg[0:npart, 0:c1 - c0], src_fn(c0, c1), (), (stg,), stg)
            eng = self.ceng()
            if gain is None:
                self.cp(eng, dst_fn(c0, c1), stg[0:npart, 0:c1 - c0], (stg,), (dst,))
            else:
                self.ts(eng, dst_fn(c0, c1), stg[0:npart, 0:c1 - c0], gain, 0.0, ALU.mult, ALU.add, (stg,), (dst,))
            c0 = c1


def norm_transpose(k, xin, hnbs, hnT, ss, rstd, mhalf, ident, tbanks):
    for j in range(4):
        hnb = hnbs[j % len(hnbs)]
        tb = tbanks[j % len(tbanks)]
        k.act(hnb[:, :], xin[:, j, :], AF.Square, (xin,), (hnb, ss), accum=ss[:, j:j + 1])
        k.ts("dve", rstd[:, j:j + 1], ss[:, j:j + 1], 1.0 / D, EPS, ALU.mult, ALU.add, (ss,), (rstd,))
        k.tt("pool", rstd[:, j:j + 1], rstd[:, j:j + 1], mhalf[:, 0:1], ALU.pow, (rstd, mhalf), (rstd,))
        if j % 2 == 0:
            k.ts("dve", hnb[:, :], xin[:, j, :], rstd[:, j:j + 1], 0.0, ALU.mult, ALU.add, (xin, rstd), (hnb,))
        else:
            k.act(hnb[:, :], xin[:, j, :], AF.Identity, (xin, rstd), (hnb,), scale=rstd[:, j:j + 1])
        for kc in range(8):
            k.tr(tb[:, kc * 128:(kc + 1) * 128], hnb[:, kc * 128:(kc + 1) * 128], ident[:, :], (hnb, ident), (tb,))
        src = tb[:, :].rearrange("p (a b) -> p a b", a=8)
        if j % 2 == 0:
            k.act(hnT[:, :, j * 128:(j + 1) * 128], src, AF.Copy, (tb,), (hnT,))
        else:
            k.cp("dve", hnT[:, :, j * 128:(j + 1) * 128], src, (tb,), (hnT,))


def ffn_chunk(k, hnT, actT, wgu, wdn, banks, bi, tg, tA):
    for fc in range(NFC):
        pg = banks[bi % len(banks)]
        pv = banks[(bi + 1) % len(banks)]
        bi += 2
        for kc in range(8):
            k.mm(pg[:, :], wgu[:, kc, fc * 128:(fc + 1) * 128], hnT[:, kc, :], kc == 0, kc == 7, (wgu, hnT), (pg,))
        for kc in range(8):
            k.mm(pv[:, :], wgu[:, kc, DFF + fc * 128:DFF + (fc + 1) * 128], hnT[:, kc, :], kc == 0, kc == 7, (wgu, hnT), (pv,))
        t1 = tg[fc % len(tg)]
        t2 = tA[fc % len(tA)]
        k.act(t1[:, :], pg[:, :], AF.Tanh, (pg,), (t1,), scale=0.5)
        k.stt(t2[:, :], t1[:, :], 1.0, pg[:, :], ALU.add, ALU.mult, (t1, pg), (t2,))
        k.stt(actT[:, fc, :], t2[:, :], 0.5, pv[:, :], ALU.mult, ALU.mult, (t2, pv), (actT,))
    return bi


def ffn_down(k, actT, wdn, xin, banks, bi):
    for j in range(4):
        for hf in range(2):
            pb = banks[bi % len(banks)]
            bi += 1
            for fc in range(NFC):
                k.mm(pb[:, :], actT[:, fc, j * 128:(j + 1) * 128], wdn[:, fc, hf * 512:(hf + 1) * 512], fc == 0, fc == NFC - 1,
                     (actT, wdn), (pb,))
            k.tt("dve", xin[:, j, hf * 512:(hf + 1) * 512], pb[:, :], xin[:, j, hf * 512:(hf + 1) * 512], ALU.add, (pb, xin), (xin,))
    return bi


def tok_view(ap2d, t0):
    return ap2d[t0:t0 + T, :].rearrange("(j p) d -> p j d", p=128)


def build_program(nseq, phases="ABCDE", debug=False):
    NT = nseq * S
    nc = bass.Bass("TRN2", target_bir_lowering=False)
    dt_in = {}

    def din(name, shape):
        dt_in[name] = shape
        return nc.dram_tensor(name, list(shape), F32, kind="ExternalInput").ap()

    def dscr(name, shape, dt):
        kind = "ExternalOutput" if debug else "Internal"
        return nc.dram_tensor(name, list(shape), dt, kind=kind).ap()

    x_d = din("x", (NT, D))
    identf_d = din("c_ident", (128, 128))
    tri_d = din("c_tri", (128, 256))
    maskc_d = din("c_maskc", (128, S))
    ropec_d = din("c_ropec", (128, S))
    ropes_d = din("c_ropes", (128, S))
    pm_d = din("c_pm", (128, 64))
    esel_d = din("c_esel", (32, S))
    ovl_d = din("c_ovl", (128, 33))
    tkadd_d = din("c_tkadd", (S, 32))
    tkmul_d = din("c_tkmul", (S, 32))
    g_d = din("g_all", (128, 5, 8))
    gfin_d = din("g_fin", (1, D))
    win_d = din("w_in", (128, 8 * 3072))
    cw_d = din("conv_w", (RB, NBLK * 4))
    sm_d = din("a_small", (RB, 4, NBLK))
    wra_d = din("w_ra", (RB, NBLK * RB))
    wix_d = din("w_ix", (RB, NBLK * RB))
    wout_d = din("w_out", (RB, NBLK * D))
    wgu_d = [din("w_gu%d" % l, (128, 8 * 2 * DFF)) for l in range(2)]
    wdn_d = [din("w_dn%d" % l, (128, NFC * D)) for l in range(2)]
    wkc_d = din("w_kc", (128, 8 * NG * 96))
    wvc_d = din("w_vc", (128, 8 * NG * 64))
    wks_d = din("w_ks", (128, 8 * NG * 128))
    wkw_d = din("w_kw", (128, 8 * NG * 128))
    wv_d = din("w_v", (128, 8 * 512))
    wq_d = din("w_q", (128, 8 * NH * 128))
    wqg_d = din("w_qg", (128, 8 * 48))
    gb_d = din("gate_b", (1, 48))
    wo_d = din("w_o", (128, 8 * D))
    w1k_d = din("w1k", (96, 32 * 256))
    w1v_d = din("w1v", (64, 32 * 256))
    w2k_d = din("w2k", (128, 2 * 128))
    w2v_d = din("w2v", (128, 2 * 64))
    posk_d = din("posk", (96, 32))
    posv_d = din("posv", (64, 32))

    y_d = nc.dram_tensor("y", [NT, D], F32, kind="ExternalOutput").ap()
    hmid_d = dscr("s_hmid", (NT, D), F32)
    h1_d = dscr("s_h1", (NT, D), F32)
    kcmp_d = dscr("s_kcmp", (nseq, 96, NG, S), BF16)
    vcmp_d = dscr("s_vcmp", (nseq, 64, NG, S), BF16)
    ksel_d = dscr("s_ksel", (nseq, 96, NG, S), BF16)
    kwin_d = dscr("s_kwin", (nseq, 96, NG, S), BF16)
    v_d = dscr("s_v", (NT, 520), BF16)
    qn_d = dscr("s_qn", (nseq, 96, NH, S), BF16)
    qr_d = dscr("s_qr", (nseq, 32, NH, S), BF16)
    gt_d = dscr("s_gt", (NT, 48), F32)
    o_d = dscr("s_o", (NT, D), BF16)

    with contextlib.ExitStack() as top:
        P = Prog(nc, top)
        k = K(nc, P)

        ident = k.sb(top, [128, 128], BF16, "ident")
        mhalf = k.sb(top, [128, 4], F32, "mhalf")
        gall = k.sb(top, [128, 5, 8], F32, "gall")
        with contextlib.ExitStack() as st:
            tmp = k.sb(st, [128, 128], F32)
            k.dma(tmp[:, :], identf_d, (), (tmp,), tmp)
            k.cp("dve", ident[:, :], tmp[:, :], (tmp,), (ident,))
            k.memset("dve", mhalf[:, :], -0.5, (mhalf,))
            k.dma(gall[:, :, :], g_d, (), (gall,), gall)
            P.flush()

        if "A" in phases:
            with contextlib.ExitStack() as st:
                win = k.sb(st, [128, 8, 3072], BF16, "win")
                wout = k.sb(st, [RB, NBLK, D], BF16, "wout")
                wra = k.sb(st, [RB, NBLK, RB], BF16, "wra")
                wix = k.sb(st, [RB, NBLK, RB], BF16, "wix")
                dg = k.sb(st, [RB, NBLK * 4, RB], BF16, "dg")
                cw = k.sb(st, [RB, NBLK * 4], F32, "cw")
                sm = k.sb(st, [RB, 4, NBLK], F32, "sm")
                hb = k.sb(st, [RB, 2, NBLK], F32, "hb")
                cc = k.sb(st, [RB, 2, NBLK], F32, "cc")
                cx = k.sb(st, [RB, NBLK, 4], BF16, "cx")
                hcar = k.sb(st, [RB, NBLK], F32, "hcar")
                one = k.sb(st, [RB, 1], F32, "one")
                k.memset("dve", one[:, :], 1.0, (one,))
                with contextlib.ExitStack() as st2:
                    stages = [k.sb(st2, [128, 3072], F32, "stg") for _ in range(3)]
                    for kc in range(8):
                        k.wload(stages, lambda a, b, kc=kc: win[:, kc, a:b], lambda a, b, kc=kc: win_d[:, kc * 3072 + a:kc * 3072 + b],
                                3072, win, piece=3072, gain=gall[:, 0, kc:kc + 1])
                    for n in range(NBLK):
                        k.wload(stages, lambda a, b, n=n: wout[:, n, a:b], lambda a, b, n=n: wout_d[:, n * D + a:n * D + b], D, wout,
                                piece=D, npart=RB)
                    k.wload(stages, lambda a, b: wra[:, :, :].rearrange("p n j -> p (n j)")[:, a:b], lambda a, b: wra_d[:, a:b],
                            NBLK * RB, wra, piece=NBLK * RB, npart=RB)
                    k.wload(stages, lambda a, b: wix[:, :, :].rearrange("p n j -> p (n j)")[:, a:b], lambda a, b: wix_d[:, a:b],
                            NBLK * RB, wix, piece=NBLK * RB, npart=RB)
                    k.dma(cw[:, :], cw_d, (), (cw,), cw)
                    k.dma(sm[:, :, :], sm_d, (), (sm,), sm)
                    for i in range(NBLK * 4):
                        k.ts(k.ceng(), dg[:, i, :], ident[0:RB, 0:RB], cw[:, i:i + 1], 0.0, ALU.mult, ALU.add, (ident, cw), (dg,))
                    k.ts("dve", hb[:, :, :], sm[:, 1:3, :], -1.0, 0.0, ALU.mult, ALU.add, (sm,), (hb,))
                    sg = k.sb(st2, [RB, NBLK], F32)
                    k.act(sg[:, :], sm[:, 3, :], AF.Exp, (sm,), (sg,), scale=-1.0)
                    k.ts("dve", sg[:, :], sg[:, :], 1.0, 1.0, ALU.mult, ALU.add, (sg,), (sg,))
                    k.act(sg[:, :], sg[:, :], AF.Ln, (sg,), (sg,))
                    k.ts("dve", cc[:, 0, :], sg[:, :], -8.0, 0.0, ALU.mult, ALU.add, (sg,), (cc,))
                    k.ts("dve", cc[:, 1, :], sg[:, :], -16.0, 0.0, ALU.mult, ALU.add, (sg,), (cc,))
                    P.flush()
                if os.environ.get("KSTOP") == "A0":
                    return nc

                xins = [k.sb(st, [128, 4, D], F32, "xin") for _ in range(2)]
                hnb = [k.sb(st, [128, D], BF16, "hnb") for _ in range(2)]
                hnTs = [k.sb(st, [128, 8, T], BF16, "hnT") for _ in range(1)] * 2
                recy = k.sb(st, [RB, NBLK, T], BF16, "recy")
                ss = k.sb(st, [128, 4], F32)
                rstd = k.sb(st, [128, 4], F32)
                NSET = 3
                xbT = [k.sb(st, [RB, T + 4], BF16) for _ in range(NSET)]
                xcT = [k.sb(st, [RB, T], BF16) for _ in range(NSET)]
                trr = [k.sb(st, [RB, T], F32) for _ in range(NSET)]
                tii = [k.sb(st, [RB, T], F32) for _ in range(NSET)]
                aa = [k.sb(st, [RB, T], F32) for _ in range(NSET)]
                a2 = [k.sb(st, [RB, T], F32) for _ in range(NSET)]
                hs = [k.sb(st, [RB, T], F32) for _ in range(NSET)]
                sq = [k.sb(st, [RB, T], F32) for _ in range(NSET)]
                th = [k.sb(st, [RB, T], F32) for _ in range(NSET)]
                banks = [k.ps(st, [128, 512], F32, "bk") for _ in range(4)]
                ybanks = [k.ps(st, [128, 512], F32, "yb") for _ in range(3)]
                tbanks = [k.ps(st, [128, 1024], BF16, "tb") for _ in range(1)]

                def rcp(out, in_, rd, wr):
                    P.op("dve", lambda e: e.reciprocal(out, in_), _rs(rd), _rs(wr))
                if os.environ.get("KDBG"):
                    print("phase A sbuf remaining", nc.sbuf_bytes_remaining)
                bi = 0
                ci = 0
                KNCH = int(os.environ.get("KNCH", "4"))
                KNBLK = int(os.environ.get("KNBLK", "16"))
                KSTOP = os.environ.get("KSTOP", "")
                nchA = min(KNCH, S // T)
                chunksA = [(b, c) for b in range(nseq) for c in range(nchA)]
                k.dma(xins[0][:, :, :], tok_view(x_d, chunksA[0][0] * S + chunksA[0][1] * T), (), (xins[0],), xins[0])
                for b in range(nseq):
                    k.memset("pool", cx[:, :, :], 0.0, (cx,))
                    k.memset("pool", hcar[:, :], 0.0, (hcar,))
                    for c in range(nchA):
                        t0 = b * S + c * T
                        xin = xins[ci % 2]
                        hnT = hnTs[ci % 2]
                        ci += 1
                        if ci < len(chunksA):
                            nb_, nc_ = chunksA[ci]
                            k.dma(xins[ci % 2][:, :, :], tok_view(x_d, nb_ * S + nc_ * T), (), (xins[ci % 2],), xins[ci % 2])
                        norm_transpose(k, xin, hnb, hnT, ss, rstd, mhalf, ident, tbanks)
                        def blk(n, xin=None, hnT=hnT):
                            s_ = n % NSET
                            bA = banks[(2 * n) % 4]
                            bB = banks[(2 * n + 1) % 4]
                            pxb, pr, pcv, pi = bA, bA, bB, bB
                            pyb = ybanks[n % 3]
                            er, ei, ey = trr[s_], tii[s_], th[s_]
                            for kc in range(8):
                                k.mm(pxb[0:RB, :], win[:, kc, n * RB:(n + 1) * RB], hnT[:, kc, :], kc == 0, kc == 7, (win, hnT), (pxb,))
                            for kc in range(8):
                                k.mm(pyb[0:RB, :], win[:, kc, DRNN + n * RB:DRNN + (n + 1) * RB], hnT[:, kc, :], kc == 0, kc == 7,
                                     (win, hnT), (pyb,))
                            yield
                            k.cp("pool", xbT[s_][:, 0:4], cx[:, n, :], (cx,), (xbT[s_],))
                            k.cp("dve", xbT[s_][:, 4:4 + T], pxb[0:RB, :], (pxb,), (xbT[s_],))
                            k.cp("pool", cx[:, n, :], xbT[s_][:, T:T + 4], (xbT[s_],), (cx,))
                            k.act(sq[s_][:, :], pyb[0:RB, :], AF.Square, (pyb,), (sq[s_],))
                            yield
                            for t in range(4):
                                k.mm(pcv[0:RB, :], dg[:, n * 4 + t, :], xbT[s_][:, 1 + t:1 + t + T], t == 0, t == 3, (dg, xbT[s_]), (pcv,))
                            k.ts("pool", sq[s_][:, :], sq[s_][:, :], GC, 1.0, ALU.mult, ALU.add, (sq[s_],), (sq[s_],))
                            yield
                            k.ts("dve", xcT[s_][:, :], pcv[0:RB, :], sm[:, 0, n:n + 1], 0.0, ALU.add, ALU.add, (pcv, sm), (xcT[s_],))
                            k.tt("dve", sq[s_][:, :], sq[s_][:, :], pyb[0:RB, :], ALU.mult, (sq[s_], pyb), (sq[s_],))
                            yield
                            k.mm(pr[0:RB, :], wra[:, n, :], xcT[s_][:, :], True, True, (wra, xcT[s_]), (pr,))
                            k.mm(pi[0:RB, :], wix[:, n, :], xcT[s_][:, :], True, True, (wix, xcT[s_]), (pi,))
                            k.act(ey[:, :], sq[s_][:, :], AF.Exp, (sq[s_],), (ey,), scale=-2.0 * GK)
                            yield
                            k.act(er[:, :], pr[0:RB, :], AF.Exp, (pr, hb), (er,), scale=-1.0, bias=hb[:, 0, n:n + 1])
                            k.act(ei[:, :], pi[0:RB, :], AF.Exp, (pi, hb), (ei,), scale=-1.0, bias=hb[:, 1, n:n + 1])
                            k.act(ey[:, :], ey[:, :], AF.Ln, (ey, one), (ey,), bias=one[:, 0:1])
                            yield
                            k.act(er[:, :], er[:, :], AF.Ln, (er, one), (er,), bias=one[:, 0:1])
                            k.act(ei[:, :], ei[:, :], AF.Ln, (ei, one), (ei,), bias=one[:, 0:1])
                            k.act(ey[:, :], ey[:, :], AF.Exp, (ey,), (ey,), scale=-1.0)
                            yield
                            k.act(er[:, :], er[:, :], AF.Exp, (er,), (er,), scale=-1.0)
                            k.tt("dve", ey[:, :], ey[:, :], pyb[0:RB, :], ALU.mult, (ey, pyb), (ey,))
                            yield
                            k.act(aa[s_][:, :], er[:, :], AF.Exp, (er, cc), (aa[s_],), scale=cc[:, 0, n:n + 1])
                            k.tt("pool", a2[s_][:, :], aa[s_][:, :], aa[s_][:, :], ALU.mult, (aa[s_],), (a2[s_],))
                            yield
                            k.ts("pool", a2[s_][:, :], a2[s_][:, :], -1.0, 1.0, ALU.mult, ALU.add, (a2[s_],), (a2[s_],))
                            k.ts("pool", a2[s_][:, :], a2[s_][:, :], 1.0, 1e-12, ALU.min, ALU.max, (a2[s_],), (a2[s_],))
                            yield
                            k.act(a2[s_][:, :], a2[s_][:, :], AF.Ln, (a2[s_],), (a2[s_],))
                            yield
                            k.stt(ei[:, :], a2[s_][:, :], 0.5, ei[:, :], ALU.mult, ALU.subtract, (a2[s_], ei), (ei,))
                            yield
                            k.act(ei[:, :], ei[:, :], AF.Exp, (ei,), (ei,))
                            yield
                            k.tt("dve", ei[:, :], ei[:, :], xcT[s_][:, :], ALU.mult, (ei, xcT[s_]), (ei,))
                            k.P.op("dve", (lambda e, o=hs[s_][:, :], d0=aa[s_][:, :], d1=ei[:, :], i0=hcar[:, n:n + 1]:
                                           e.tensor_tensor_scan(o, d0, d1, i0, ALU.mult, ALU.add)),
                                   _rs((aa[s_], ei, hcar)), _rs((hs[s_],)))
                            k.cp("pool", hcar[:, n:n + 1], hs[s_][:, T - 1:T], (hs[s_],), (hcar,))
                            k.tt("dve", recy[:, n, :], ey[:, :], hs[s_][:, :], ALU.mult, (ey, hs[s_]), (recy,))
                            yield

                        active = []
                        nxt_n = 0
                        nblk = min(KNBLK, NBLK)
                        while active or nxt_n < nblk:
                            if nxt_n < nblk and len(active) < NSET and (not active or active[-1][1] >= int(os.environ.get('KSKEW', '3'))):
                                active.append([blk(nxt_n), 0])
                                nxt_n += 1
                            for ent in list(active):
                                try:
                                    next(ent[0])
                                    ent[1] += 1
                                except StopIteration:
                                    active.remove(ent)
                        for j in range(4):
                            for hf in range(2):
                                pb = banks[bi % 4]
                                bi += 1
                                for n in range(NBLK):
                                    k.mm(pb[:, :], recy[:, n, j * 128:(j + 1) * 128], wout[:, n, hf * 512:(hf + 1) * 512], n == 0, n == NBLK - 1,
                                         (recy, wout), (pb,))
                                k.tt("dve", xin[:, j, hf * 512:(hf + 1) * 512], pb[:, :], xin[:, j, hf * 512:(hf + 1) * 512], ALU.add, (pb, xin), (xin,))
                        k.dma(tok_view(hmid_d, t0), xin[:, :, :], (xin,), (), xin)
                P.flush()

        def phase_ffn(layer, src_d, dst_d, with_o, final):
            with contextlib.ExitStack() as st:
                if os.environ.get("KDBG"):
                    print("ffn sbuf remaining at start", nc.sbuf_bytes_remaining)
                wgu = k.sb(st, [128, 8, 2 * DFF], BF16, "wgu")
                wdn = k.sb(st, [128, NFC, D], BF16, "wdn")
                wo = k.sb(st, [128, 8, D], BF16, "wo") if with_o else None
                gfin = k.sb(st, [128, D], F32, "gfin") if final else None
                with contextlib.ExitStack() as st2:
                    stages = [k.sb(st2, [128, 2816], F32, "stg") for _ in range(3)]
                    gi = 1 if layer == 0 else 4
                    for kc in range(8):
                        k.wload(stages, lambda a, b, kc=kc: wgu[:, kc, a:b], lambda a, b, kc=kc: wgu_d[layer][:, kc * 2 * DFF + a:kc * 2 * DFF + b],
                                2 * DFF, wgu, piece=2816, gain=gall[:, gi, kc:kc + 1])
                    for fc in range(0, NFC, 2):
                        k.wload(stages, lambda a, b, fc=fc: wdn[:, fc:fc + 2, :].rearrange("p f d -> p (f d)")[:, a:b],
                                lambda a, b, fc=fc: wdn_d[layer][:, fc * D + a:fc * D + b], 2 * D, wdn, piece=2 * D)
                    if with_o:
                        for kc in range(0, 8, 2):
                            k.wload(stages, lambda a, b, kc=kc: wo[:, kc:kc + 2, :].rearrange("p f d -> p (f d)")[:, a:b],
                                    lambda a, b, kc=kc: wo_d[:, kc * D + a:kc * D + b], 2 * D, wo, piece=2 * D)
                    if final:
                        g1 = k.sb(st2, [1, D], F32)
                        on = k.sb(st2, [1, 128], F32)
                        pbk = k.ps(st2, [128, 512], F32)
                        k.dma(g1[:, :], gfin_d, (), (g1,), g1)
                        k.memset("dve", on[:, :], 1.0, (on,))
                        for hf in range(2):
                            k.mm(pbk[:, :], on[:, :], g1[:, hf * 512:(hf + 1) * 512], True, True, (on, g1), (pbk,))
                            k.cp("dve", gfin[:, hf * 512:(hf + 1) * 512], pbk[:, :], (pbk,), (gfin,))
                    P.flush()
                if os.environ.get("KDBG"):
                    print("ffn sbuf remaining before acts", nc.sbuf_bytes_remaining)
                xins = [k.sb(st, [128, 4, D], F32, "xin") for _ in range(1 if with_o else 2)] * 2
                hnb = [k.sb(st, [128, D], BF16, "hnb") for _ in range(1 if with_o else 2)] * 2
                hnT = k.sb(st, [128, 8, T], BF16, "hnT")
                if os.environ.get("KDBG"):
                    print("ffn sbuf remaining before actT", nc.sbuf_bytes_remaining)
                actT = k.sb(st, [128, NFC, T], BF16, "actT")
                ss = k.sb(st, [128, 4], F32)
                rstd = k.sb(st, [128, 4], F32)
                tg = [k.sb(st, [128, T], BF16) for _ in range(1 if with_o else 2)]
                tA = [k.sb(st, [128, T], F32) for _ in range(1)]
                banks = [k.ps(st, [128, 512], F32, "bk") for _ in range(6)]
                tbanks = [k.ps(st, [128, 1024], BF16, "tb") for _ in range(2)]
                bi = 0
                if not with_o:
                    k.dma(xins[0][:, :, :], tok_view(src_d, 0), (), (xins[0],), xins[0])
                for ci in range(NT // T):
                    t0 = ci * T
                    xin = xins[ci % 2]
                    if with_o:
                        k.dma(xin[:, :, :], tok_view(src_d, t0), (), (xin,), xin)
                    elif ci + 1 < NT // T:
                        k.dma(xins[(ci + 1) % 2][:, :, :], tok_view(src_d, t0 + T), (), (xins[(ci + 1) % 2],), xins[(ci + 1) % 2])
                    if with_o:
                        oi = actT[:, 0:8, :].rearrange("p (j a) t -> p j (a t)", j=4)
                        k.dma(oi, tok_view(o_d, t0), (), (actT,), actT)
                        for j in range(4):
                            tb = tbanks[j % 2]
                            for kc in range(8):
                                k.tr(tb[:, kc * 128:(kc + 1) * 128], oi[:, j, kc * 128:(kc + 1) * 128], ident[:, :], (actT, ident), (tb,))
                            src = tb[:, :].rearrange("p (a b) -> p a b", a=8)
                            if j % 2 == 0:
                                k.act(hnT[:, :, j * 128:(j + 1) * 128], src, AF.Copy, (tb,), (hnT,))
                            else:
                                k.cp("dve", hnT[:, :, j * 128:(j + 1) * 128], src, (tb,), (hnT,))
                        for j in range(4):
                            for hf in range(2):
                                pb = banks[bi % 6]
                                bi += 1
                                for kc in range(8):
                                    k.mm(pb[:, :], hnT[:, kc, j * 128:(j + 1) * 128], wo[:, kc, hf * 512:(hf + 1) * 512], kc == 0, kc == 7,
                                         (hnT, wo), (pb,))
                                k.tt("dve", xin[:, j, hf * 512:(hf + 1) * 512], pb[:, :], xin[:, j, hf * 512:(hf + 1) * 512], ALU.add, (pb, xin), (xin,))
                    norm_transpose(k, xin, hnb, hnT, ss, rstd, mhalf, ident, tbanks)
                    bi = ffn_chunk(k, hnT, actT, wgu, wdn, banks, bi, tg, tA)
                    bi = ffn_down(k, actT, wdn, xin, banks, bi)
                    if final:
                        for j in range(4):
                            hb_ = hnb[j % 2]
                            k.act(hb_[:, :], xin[:, j, :], AF.Square, (xin,), (hb_, ss), accum=ss[:, j:j + 1])
                        k.ts("dve", rstd[:, 0:4], ss[:, 0:4], 1.0 / D, EPS, ALU.mult, ALU.add, (ss,), (rstd,))
                        k.tt("pool", rstd[:, 0:4], rstd[:, 0:4], mhalf[:, 0:4], ALU.pow, (rstd, mhalf), (rstd,))
                        for j in range(4):
                            k.stt(xin[:, j, :], xin[:, j, :], rstd[:, j:j + 1], gfin[:, :], ALU.mult, ALU.mult, (xin, rstd, gfin), (xin,))
                    k.dma(tok_view(dst_d, t0), xin[:, :, :], (xin,), (), xin)
                P.flush()

        if "B" in phases:
            phase_ffn(0, hmid_d, h1_d, False, False)

        if "C" in phases:
            with contextlib.ExitStack() as st:
                wkc = k.sb(st, [128, 8, NG * 96], BF16, "wkc")
                wvc = k.sb(st, [128, 8, NG * 64], BF16, "wvc")
                wks = k.sb(st, [128, 8, NG * 128], BF16, "wks")
                wkw = k.sb(st, [128, 8, NG * 128], BF16, "wkw")
                wv = k.sb(st, [128, 8, 512], BF16, "wv")
                wq = k.sb(st, [128, 8, NH * 128], BF16, "wq")
                wqg = k.sb(st, [128, 8, 48], BF16, "wqg")
                gbb = k.sb(st, [1, 48], BF16, "gbb")
                ones1 = k.sb(st, [1, 128], BF16, "ones1")
                pm = k.sb(st, [128, 64], BF16, "pm")
                rc = k.sb(st, [128, S], F32, "rc")
                rsn = k.sb(st, [128, S], F32, "rsn")
                with contextlib.ExitStack() as st2:
                    stages = [k.sb(st2, [128, 2048], F32, "stg") for _ in range(3)]
                    for kc in range(8):
                        for (wt, wd, nco, gi) in ((wkc, wkc_d, NG * 96, 2), (wvc, wvc_d, NG * 64, 2), (wks, wks_d, NG * 128, 2),
                                                  (wkw, wkw_d, NG * 128, 2), (wv, wv_d, 512, 2), (wq, wq_d, NH * 128, 3), (wqg, wqg_d, 48, 3)):
                            k.wload(stages, lambda a, b, wt=wt, kc=kc: wt[:, kc, a:b], lambda a, b, wd=wd, kc=kc, nco=nco: wd[:, kc * nco + a:kc * nco + b],
                                    nco, wt, piece=2048, gain=gall[:, gi, kc:kc + 1])
                    k.wload(stages, lambda a, b: pm[:, a:b], lambda a, b: pm_d[:, a:b], 64, pm)
                    k.wload(stages, lambda a, b: gbb[:, a:b], lambda a, b: gb_d[:, a:b], 48, gbb, npart=1)
                    k.memset("dve", ones1[:, :], 1.0, (ones1,))
                    k.dma(rc[:, :], ropec_d, (), (rc,), rc)
                    k.dma(rsn[:, :], ropes_d, (), (rsn,), rsn)
                    P.flush()
                xins = [k.sb(st, [128, 4, D], F32, "xin") for _ in range(2)]
                hnb = [k.sb(st, [128, D], BF16, "hnb") for _ in range(2)]
                hnT = k.sb(st, [128, 8, T], BF16, "hnT")
                ss = k.sb(st, [128, 4], F32)
                rstd = k.sb(st, [128, 4], F32)
                o_kc = [k.sb(st, [96, NG, T], BF16) for _ in range(1)] * 2
                o_vc = [k.sb(st, [64, NG, T], BF16) for _ in range(1)] * 2
                o_ks = [k.sb(st, [128, NG, T], BF16) for _ in range(1)] * 2
                o_kw = [k.sb(st, [128, NG, T], BF16) for _ in range(1)] * 2
                o_v = [k.sb(st, [128, 4, 520], BF16) for _ in range(2)]
                for ov_ in o_v:
                    k.memset("pool", ov_[:, :, :], 1.0, (ov_,))
                o_qn = [k.sb(st, [128, NH, T], BF16) for _ in range(1)] * 2
                o_qr = [k.sb(st, [64, NH, T], BF16) for _ in range(1)] * 2
                o_gt = [k.sb(st, [128, 4, 48], F32) for _ in range(2)]
                r1 = [k.sb(st, [64, T], F32) for _ in range(2)]
                r2 = [k.sb(st, [64, T], F32) for _ in range(2)]
                ks_r = [Res("ks%d" % i) for i in range(NG)]
                kw_r = [Res("kw%d" % i) for i in range(NG)]
                qn_r = [Res("qn%d" % i) for i in range(NH)]
                qr_r = [Res("qr%d" % i) for i in range(NH)]
                banks = [k.ps(st, [128, 512], F32, "bk") for _ in range(6)]
                tbanks = [k.ps(st, [128, 1024], BF16, "tb") for _ in range(2)]
                bi = 0
                ri = 0
                k.dma(xins[0][:, :, :], tok_view(h1_d, 0), (), (xins[0],), xins[0])
                for ci in range(NT // T):
                    t0 = ci * T
                    b = t0 // S
                    p0 = t0 % S
                    xin = xins[ci % 2]
                    s_ = ci % 2
                    if ci + 1 < NT // T:
                        k.dma(xins[(ci + 1) % 2][:, :, :], tok_view(h1_d, t0 + T), (), (xins[(ci + 1) % 2],), xins[(ci + 1) % 2])
                    norm_transpose(k, xin, hnb, hnT, ss, rstd, mhalf, ident, tbanks)

                    def proj(wt, c0, m):
                        nonlocal bi
                        pb = banks[bi % 6]
                        bi += 1
                        for kc in range(8):
                            k.mm(pb[0:m, :], wt[:, kc, c0:c0 + m], hnT[:, kc, :], kc == 0, kc == 7, (wt, hnT), (pb,))
                        return pb

                    for g in range(NG):
                        pb = proj(wkc, g * 96, 96)
                        k.act(o_kc[s_][:, g, :], pb[0:96, :], AF.Copy, (pb,), (o_kc[s_],))
                        pb = proj(wvc, g * 64, 64)
                        k.cp("dve", o_vc[s_][:, g, :], pb[0:64, :], (pb,), (o_vc[s_],))
                    units = []
                    for (wt, ot, rl) in ((wks, o_ks[s_], ks_r), (wkw, o_kw[s_], kw_r)):
                        for g in range(NG):
                            units.append((wt, g * 128, ot, g, rl[g], ot, rl[g], 1.0))
                    for h in range(NH):
                        units.append((wq, h * 128, o_qn[s_], h, qn_r[h], o_qr[s_], qr_r[h], float(DQK ** -0.5)))
                    prev = None
                    for u in units + [None]:
                        if u is not None:
                            (wt, c0, dst, slot, dres, rdst, rres, scale) = u
                            pb = proj(wt, c0, 128)
                            k.act(dst[:, slot, :], pb[:, :], AF.Copy, (pb,), (dres,), scale=scale)
                        if prev is not None:
                            (wt, c0, dst, slot, dres, rdst, rres, scale) = prev
                            pb = banks[bi % 6]
                            bi += 1
                            k.mm(pb[0:64, :], pm[:, :], dst[:, slot, :], True, True, (pm, dres), (pb,))
                            a = r1[ri % 2]
                            bb = r2[ri % 2]
                            ri += 1
                            k.tt("dve", a[32:64, :], pb[32:64, :], rsn[32:64, p0:p0 + T], ALU.mult, (pb, rsn), (a,))
                            k.tt("pool", bb[32:64, :], dst[32:64, slot, :], rc[32:64, p0:p0 + T], ALU.mult, (dres, rc), (bb,))
                            k.tt("dve", rdst[32:64, slot, :], a[32:64, :], bb[32:64, :], ALU.add, (a, bb), (rres,))
                        prev = u
                    for j in range(4):
                        pb = banks[bi % 6]
                        bi += 1
                        for kc in range(8):
                            k.mm(pb[:, :], hnT[:, kc, j * 128:(j + 1) * 128], wv[:, kc, :], kc == 0, kc == 7, (wv, hnT), (pb,))
                        k.cp("dve", o_v[s_][:, j, :].rearrange("p (a c) -> p a c", c=65)[:, :, 0:64], pb[:, :].rearrange("p (a c) -> p a c", c=64),
                             (pb,), (o_v[s_],))
                        pb = banks[bi % 6]
                        bi += 1
                        for kc in range(8):
                            k.mm(pb[:, 0:48], hnT[:, kc, j * 128:(j + 1) * 128], wqg[:, kc, :], kc == 0, False, (wqg, hnT), (pb,))
                        k.mm(pb[:, 0:48], ones1[:, :], gbb[:, :], False, True, (ones1, gbb), (pb,))
                        k.act(o_gt[s_][:, j, :], pb[:, 0:48], AF.Tanh, (pb,), (o_gt[s_],), scale=0.5)
                        k.ts("dve", o_gt[s_][:, j, :], o_gt[s_][:, j, :], 0.5, 0.5, ALU.mult, ALU.add, (o_gt[s_],), (o_gt[s_],))
                    k.dma(kcmp_d[b, :, :, p0:p0 + T], o_kc[s_][:, :, :], (o_kc[s_],), (), o_kc[s_])
                    k.dma(vcmp_d[b, :, :, p0:p0 + T], o_vc[s_][:, :, :], (o_vc[s_],), (), o_vc[s_])
                    k.dma(ksel_d[b, :, :, p0:p0 + T], o_ks[s_][32:128, :, :], tuple(ks_r), (), o_ks[s_])
                    k.dma(kwin_d[b, :, :, p0:p0 + T], o_kw[s_][32:128, :, :], tuple(kw_r), (), o_kw[s_])
                    k.dma(v_d[t0:t0 + T, :].rearrange("(j p) d -> p j d", p=128), o_v[s_][:, :, :], (o_v[s_],), (), o_v[s_])
                    k.dma(qn_d[b, :, :, p0:p0 + T], o_qn[s_][32:128, :, :], tuple(qn_r), (), o_qn[s_])
                    k.dma(qr_d[b, :, :, p0:p0 + T], o_qr[s_][32:64, :, :], tuple(qr_r) + tuple(qn_r), (), o_qr[s_])
                    k.dma(gt_d[t0:t0 + T, :].rearrange("(j p) d -> p j d", p=128), o_gt[s_][:, :, :], (o_gt[s_],), (), o_gt[s_])
                P.flush()

        if "D" in phases:
            with contextlib.ExitStack() as st:
                w1k = k.sb(st, [96, 32, 256], BF16, "w1k")
                w1v = k.sb(st, [64, 32, 256], BF16, "w1v")
                w2k = k.sb(st, [128, 2, 128], BF16, "w2k")
                w2v = k.sb(st, [128, 2, 64], BF16, "w2v")
                posk = k.sb(st, [96, 32], BF16, "posk")
                posv = k.sb(st, [64, 32], BF16, "posv")
                pbias = k.sb(st, [128, 4], F32, "pbias")
                tri = k.sb(st, [128, 256], BF16, "tri")
                maskc = k.sb(st, [128, S], BF16, "maskc")
                tkadd = k.sb(st, [128, 16, 32], F32, "tkadd")
                tkmul = k.sb(st, [128, 16, 32], F32, "tkmul")
                vcaug = k.sb(st, [128, NG, 97], BF16, "vcaug")
                kcT = k.sb(st, [128, NG, 128], BF16, "kcT")
                ksT = k.sb(st, [128, NG, S], BF16, "ksT")
                kwT = k.sb(st, [128, NG, S], BF16, "kwT")
                with contextlib.ExitStack() as st2:
                    stages = [k.sb(st2, [128, 2048], F32, "stg") for _ in range(3)]
                    for l0 in range(0, 32, 8):
                        k.wload(stages, lambda a, b, l0=l0: w1k[:, l0:l0 + 8, :].rearrange("p l h -> p (l h)")[:, a:b],
                                lambda a, b, l0=l0: w1k_d[:, l0 * 256 + a:l0 * 256 + b], 2048, w1k, npart=96)
                        k.wload(stages, lambda a, b, l0=l0: w1v[:, l0:l0 + 8, :].rearrange("p l h -> p (l h)")[:, a:b],
                                lambda a, b, l0=l0: w1v_d[:, l0 * 256 + a:l0 * 256 + b], 2048, w1v, npart=64)
                    k.wload(stages, lambda a, b: w2k[:, :, :].rearrange("p l h -> p (l h)")[:, a:b], lambda a, b: w2k_d[:, a:b], 256, w2k)
                    k.wload(stages, lambda a, b: w2v[:, :, :].rearrange("p l h -> p (l h)")[:, a:b], lambda a, b: w2v_d[:, a:b], 128, w2v)
                    k.wload(stages, lambda a, b: posk[:, a:b], lambda a, b: posk_d[:, a:b], 32, posk, npart=96)
                    k.wload(stages, lambda a, b: posv[:, a:b], lambda a, b: posv_d[:, a:b], 32, posv, npart=64)
                    k.wload(stages, lambda a, b: tri[:, a:b], lambda a, b: tri_d[:, a:b], 256, tri)
                    k.wload(stages, lambda a, b: maskc[:, a:b], lambda a, b: maskc_d[:, a:b], S, maskc)
                    k.dma(tkadd[:, :, :], tkadd_d.rearrange("(j p) n -> p j n", p=128), (), (tkadd,), tkadd)
                    k.dma(tkmul[:, :, :], tkmul_d.rearrange("(j p) n -> p j n", p=128), (), (tkmul,), tkmul)
                    k.memset("pool", vcaug[:, :, 0:64], 0.0, (vcaug,))
                    for g in range(NG):
                        k.wload(stages, lambda a, b, g=g: vcaug[:, g, 64 + a:64 + b], lambda a, b: ovl_d[:, a:b], 33, vcaug)
                        k.wload(stages, lambda a, b, g=g: ksT[0:32, g, a:b], lambda a, b: esel_d[:, a:b], S, ksT, npart=32)
                    k.memset("pool", kwT[0:32, :, :], 0.0, (kwT,))
                    k.memset("dve", kcT[:, :, :], 0.0, (kcT,))
                    pbk = k.ps(st2, [128, 512], F32)
                    for (w1, pos, col) in ((w1k, posk, 0), (w1v, posv, 2)):
                        for hh in range(2):
                            for l in range(32):
                                k.mm(pbk[:, col + hh:col + hh + 1], w1[:, l, hh * 128:(hh + 1) * 128], pos[:, l:l + 1], l == 0, l == 31,
                                     (w1, pos), (pbk,), skip=True)
                    k.cp("dve", pbias[:, :], pbk[:, 0:4], (pbk,), (pbias,))
                    P.flush()

                kcmp = k.sb(st, [96, NG, S], BF16, "kcmp")
                vcmp = k.sb(st, [64, NG, S], BF16, "vcmp")
                vaug = k.sb(st, [128, 16, 2, NG, 65], BF16, "vaug")
                gts = k.sb(st, [128, 16, 48], F32, "gts")
                qnt = k.sb(st, [128, 4, S], BF16, "qn")
                qat = k.sb(st, [128, 4, S], BF16, "qa")
                oacc = k.sb(st, [128, 16, 256], F32, "oacc")
                obf = k.sb(st, [128, 16, 256], BF16, "obf")
                hid = [k.sb(st, [128, 2, 512], BF16) for _ in range(2)]
                gx = k.sb(st, [128, 512], F32)
                gs = k.sb(st, [128, 512], F32)
                gt_ = k.sb(st, [128, 512], BF16)
                em = [k.sb(st, [128, 512], BF16) for _ in range(2)]
                pmk = [k.sb(st, [128, 512], BF16) for _ in range(5)]
                e2b = [k.sb(st, [128, 512], BF16) for _ in range(4)]
                qa_r = [Res("qa%d" % i) for i in range(4)]
                oa_r = [Res("oa%d" % i) for i in range(4)]
                rden = [k.sb(st, [128, 4], F32) for _ in range(3)]
                fac = [k.sb(st, [128, 4], F32) for _ in range(3)]
                tmp = [k.sb(st, [128, 4, 64], F32) for _ in range(3)]
                impn = k.sb(st, [128, 4, 32], F32)
                imp = k.sb(st, [128, 32], F32)
                top8 = k.sb(st, [128, 8], F32)
                selb = k.sb(st, [128, 128], BF16)
                sbanks = [k.ps(st, [128, 512], F32, "sb") for _ in range(4)]
                abanks = [k.ps(st, [128, 512], F32, "ab") for _ in range(3)]
                tbank = k.ps(st, [128, 1024], BF16, "tb")
                k.memset("dve", qnt[0:32, :, :], 0.0, (qnt,))
                k.memset("dve", selb[:, :], 0.0, (selb,))
                cnt = {"s": 0, "a": 0, "p": 0, "e": 0, "n": 0}

                def nxt(lst, key):
                    cnt[key] += 1
                    return lst[cnt[key] % len(lst)]

                def rcp(out, in_, rd, wr):
                    P.op("dve", lambda e: e.reciprocal(out, in_), _rs(rd), _rs(wr))

                def d1_hidden():
                    for (which, raw, w1) in ((0, kcmp, w1k), (1, vcmp, w1v)):
                        for hh in range(2):
                            pb = nxt(sbanks, "s")
                            for l in range(32):
                                k.mm(pb[:, 0:NG * NCMP].rearrange("p (g c) -> p g c", g=NG), w1[:, l, hh * 128:(hh + 1) * 128],
                                     raw[:, :, l:l + 16 * (NCMP - 1) + 1:16], l == 0, l == 31, (w1, raw), (pb,))
                            bcol = pbias[:, which * 2 + hh:which * 2 + hh + 1]
                            k.act(gx[:, 0:508], pb[:, 0:508], AF.Identity, (pb, pbias), (gx,), bias=bcol)
                            k.act(gs[:, 0:508], pb[:, 0:508], AF.Square, (pb, pbias), (gs,), bias=bcol)
                            k.ts("dve", gs[:, 0:508], gs[:, 0:508], GC, 1.0, ALU.mult, ALU.add, (gs,), (gs,))
                            k.tt("dve", gs[:, 0:508], gs[:, 0:508], gx[:, 0:508], ALU.mult, (gs, gx), (gs,))
                            k.act(gt_[:, 0:508], gs[:, 0:508], AF.Tanh, (gs,), (gt_,), scale=GK)
                            k.stt(gs[:, 0:508], gt_[:, 0:508], 1.0, gx[:, 0:508], ALU.add, ALU.mult, (gt_, gx), (gs,))
                            k.ts("dve", hid[which][:, hh, 0:508], gs[:, 0:508], 0.5, 0.0, ALU.mult, ALU.add, (gs,), (hid[which],))

                for b in range(nseq):
                    if b == 0:
                        k.dma(kcmp[:, :, :], kcmp_d[b], (), (kcmp,), kcmp)
                        k.dma(vcmp[:, :, :], vcmp_d[b], (), (vcmp,), vcmp)
                        d1_hidden()
                    k.dma(ksT[32:128, :, :], ksel_d[b], (), (ksT,), ksT)
                    k.dma(kwT[32:128, :, :], kwin_d[b], (), (kwT,), kwT)
                    k.dma(vaug[:, :, :, :, :].rearrange("p j b g c -> p j (b g c)"), v_d[b * S:(b + 1) * S, :].rearrange("(j p) c -> p j c", p=128),
                          (), (vaug,), vaug)
                    k.dma(gts[:, :, :], gt_d[b * S:(b + 1) * S, :].rearrange("(j p) n -> p j n", p=128), (), (gts,), gts)
                    for g in range(NG):
                        pb = nxt(abanks, "a")
                        for hh in range(2):
                            k.mm(pb[:, 0:NCMP], w2k[:, hh, :], hid[0][:, hh, g * NCMP:(g + 1) * NCMP], hh == 0, hh == 1, (w2k, hid[0]), (pb,))
                        k.cp("dve", kcT[:, g, 0:NCMP], pb[:, 0:NCMP], (pb,), (kcT,))
                        pb = nxt(abanks, "a")
                        for hh in range(2):
                            k.mm(pb[0:NCMP, 0:64], hid[1][:, hh, g * NCMP:(g + 1) * NCMP], w2v[:, hh, :], hh == 0, hh == 1, (w2v, hid[1]), (pb,))
                        k.cp("dve", vcaug[0:NCMP, g, 0:64], pb[0:NCMP, 0:64], (pb,), (vcaug,))
                    for g in range(NG):
                        k.dma(qnt[32:128, :, :], qn_d[b, :, 4 * g:4 * g + 4, :], (), (qnt,), qnt)
                        k.dma(qat[64:128, :, :], qn_d[b, 32:96, 4 * g:4 * g + 4, :], (), tuple(qa_r), qat)
                        k.dma(qat[32:64, :, :], qr_d[b, :, 4 * g:4 * g + 4, :], (), tuple(qa_r), qat)
                        if g == 0 and b + 1 < nseq:
                            k.dma(kcmp[:, :, :], kcmp_d[b + 1], (), (kcmp,), kcmp)
                            k.dma(vcmp[:, :, :], vcmp_d[b + 1], (), (vcmp,), vcmp)
                        if g == NG - 1 and b + 1 < nseq:
                            d1_hidden()
                        def d2_setup(qc, g=g):
                            for h in range(4):
                                pb = nxt(sbanks, "s")
                                k.mm(pb[0:NCMP, :], kcT[:, g, 0:NCMP], qnt[:, h, qc * T:(qc + 1) * T], True, True, (kcT, qnt), (pb,))
                                e1 = nxt(em, "e")
                                e2 = e2b[h]
                                k.act(e1[0:NCMP, :], pb[0:NCMP, :], AF.Exp, (pb,), (e1,))
                                k.tt("pool", e2[0:NCMP, :], e1[0:NCMP, :], maskc[0:NCMP, qc * T:(qc + 1) * T], ALU.mult, (e1, maskc), (e2,))

                        def d2_qtile(qc, j, g=g):
                            qt = qc * 4 + j
                            pa = nxt(abanks, "a")
                            for h in range(4):
                                k.mm(pa[:, h * 97:(h + 1) * 97], e2b[h][0:NCMP, j * 128:(j + 1) * 128], vcaug[0:NCMP, g, 0:97], h == 0, h == 3,
                                     (e2b[h], vcaug), (pa,), skip=True)
                            p3 = pa[:, 0:388].rearrange("p (h c) -> p h c", h=4)
                            rd_ = nxt(rden, "n")
                            fc_ = fac[cnt["n"] % 3]
                            k.ts("dve", rd_[:, :], p3[:, :, 96], 1e-30, 0.0, ALU.max, ALU.add, (pa,), (rd_,))
                            rcp(rd_[:, :], rd_[:, :], (rd_,), (rd_,))
                            k.tt("dve", impn[:, :, :], p3[:, :, 64:96], rd_[:, :].unsqueeze(2).to_broadcast([128, 4, 32]), ALU.mult, (pa, rd_), (impn,))
                            P.op("dve", lambda e, o=imp[:, :], i=impn[:, :, :].rearrange("p h n -> p n h"): e.tensor_reduce(o, i, axis=AX.X, op=ALU.add),
                                 _rs((impn,)), _rs((imp,)))
                            k.tt("dve", imp[:, :], imp[:, :], tkmul[:, qt, :], ALU.mult, (imp, tkmul), (imp,))
                            k.tt("dve", imp[:, :], imp[:, :], tkadd[:, qt, :], ALU.add, (imp, tkadd), (imp,))
                            P.op("dve", lambda e, o=top8[:, :], i=imp[:, :]: e.max(o, i), _rs((imp,)), _rs((top8,)))
                            k.ts("dve", imp[:, :], imp[:, :], top8[:, 7:8], 1.0, ALU.is_ge, ALU.subtract, (imp, top8), (imp,))
                            k.ts("dve", selb[:, 0:32], imp[:, :], -NEG, 0.0, ALU.mult, ALU.add, (imp,), (selb,))
                            k.tr(tbank[:, 0:128], selb[:, :], ident[:, :], (selb, ident), (tbank,))
                            k.cp("dve", qat[0:32, :, qt * 128:(qt + 1) * 128], tbank[0:32, 0:128].unsqueeze(1).to_broadcast([32, 4, 128]), (tbank,), (qa_r[qc],))
                            k.tt("dve", fc_[:, :], rd_[:, :], gts[:, qt, 12 * g:12 * g + 12:3], ALU.mult, (rd_, gts), (fc_,))
                            k.tt("dve", oacc[:, qt, :].rearrange("p (h v) -> p h v", h=4), p3[:, :, 0:64],
                                 fc_[:, :].unsqueeze(2).to_broadcast([128, 4, 64]), ALU.mult, (pa, fc_), (oa_r[qc],))

                        d2_setup(0)
                        for j in range(4):
                            d2_qtile(0, j)
                        tiles = []
                        for qc in range(4):
                            for h in range(4):
                                pre = []
                                if qc < 3:
                                    if h == 0:
                                        pre.append(lambda qc=qc: d2_setup(qc + 1))
                                    pre.append(lambda qc=qc, h=h: d2_qtile(qc + 1, h))
                                nkt = 4 * qc + 4
                                lst = []
                                for kt in range(nkt):
                                    q0 = max(qc * T, kt * 128)
                                    ncol = (qc + 1) * T - q0
                                    masks = [(0, 0)] if kt * 128 >= qc * T else []
                                    pv = [(qt - 4 * qc, qt * 128 - q0) for qt in range(q0 // 128, 4 * qc + 4)]
                                    lst.append(dict(kT=ksT, br=0, kt=kt, q0=q0, ncol=ncol, masks=masks, pv=pv))
                                tiles.append(dict(h=h, qc=qc, br=1, lst=lst, pre=pre))
                                lst = []
                                for kt in range(max(0, 4 * qc - 4), 4 * qc + 4):
                                    qlo = max(kt, 4 * qc)
                                    qhi = min(kt + 4, 4 * qc + 3)
                                    q0 = qlo * 128
                                    ncol = (qhi - qlo + 1) * 128
                                    masks = []
                                    if qlo == kt:
                                        masks.append((0, 0))
                                    if qhi == kt + 4:
                                        masks.append((ncol - 128, 128))
                                    pv = [(qt - 4 * qc, (qt - qlo) * 128) for qt in range(qlo, qhi + 1)]
                                    lst.append(dict(kT=kwT, br=1, kt=kt, q0=q0, ncol=ncol, masks=masks, pv=pv))
                                tiles.append(dict(h=h, qc=qc, br=2, lst=lst, pre=[]))
                        flat = []
                        for tg_ in tiles:
                            po = None
                            for i_, t_ in enumerate(tg_["lst"]):
                                flat.append((tg_, t_, i_ == 0, i_ == len(tg_["lst"]) - 1))
                        LOOK = 3
                        pend = []
                        for idx in range(len(flat) + LOOK):
                            if idx < len(flat):
                                tg_, t_, isf, isl = flat[idx]
                                h = tg_["h"]
                                if isf:
                                    for th_ in tg_["pre"]:
                                        th_()
                                    tg_["po"] = nxt(abanks, "a")
                                ps_ = nxt(sbanks, "s")
                                ncol = t_["ncol"]
                                k.mm(ps_[:, 0:ncol], t_["kT"][:, g, t_["kt"] * 128:(t_["kt"] + 1) * 128], qat[:, h, t_["q0"]:t_["q0"] + ncol], True, True,
                                     (t_["kT"], qa_r[tg_["qc"]]), (ps_,))
                                pb_ = nxt(pmk, "p")
                                k.act(pb_[:, 0:ncol], ps_[:, 0:ncol], AF.Exp, (ps_,), (pb_,))
                                for (c0, m0) in t_["masks"]:
                                    k.tt("pool", pb_[:, c0:c0 + 128], pb_[:, c0:c0 + 128], tri[:, m0:m0 + 128], ALU.mult, (pb_, tri), (pb_,))
                                t_["pb"] = pb_
                            if idx >= LOOK:
                                tg_, t_, isf, isl = flat[idx - LOOK]
                                h, qc, po = tg_["h"], tg_["qc"], tg_["po"]
                                pb_ = t_["pb"]
                                for pi_, (j, off) in enumerate(t_["pv"]):
                                    k.mm(po[:, j * 65:(j + 1) * 65], pb_[:, off:off + 128], vaug[:, t_["kt"], t_["br"], g, :], isf and pi_ == 0, isl,
                                         (pb_, vaug), (po,), skip=True)
                                if isl:
                                    br = tg_["br"]
                                    p3 = po[:, 0:260].rearrange("p (j c) -> p j c", j=4)
                                    rd_ = nxt(rden, "n")
                                    fc_ = fac[cnt["n"] % 3]
                                    tm_ = tmp[cnt["n"] % 3]
                                    rcp(rd_[:, :], p3[:, :, 64], (po,), (rd_,))
                                    gcol = (4 * g + h) * 3 + br
                                    k.tt("dve", fc_[:, :], rd_[:, :], gts[:, 4 * qc:4 * qc + 4, gcol], ALU.mult, (rd_, gts), (fc_,))
                                    k.tt("dve", tm_[:, :, :], p3[:, :, 0:64], fc_[:, :].unsqueeze(2).to_broadcast([128, 4, 64]), ALU.mult, (po, fc_), (tm_,))
                                    k.tt("pool", oacc[:, 4 * qc:4 * qc + 4, h * 64:(h + 1) * 64], oacc[:, 4 * qc:4 * qc + 4, h * 64:(h + 1) * 64],
                                         tm_[:, :, :], ALU.add, (oa_r[qc], tm_), (oa_r[qc],))
                        k.act(obf[:, :, :], oacc[:, :, :], AF.Copy, tuple(oa_r), (obf,))
                        k.dma(o_d[b * S:(b + 1) * S, g * 256:(g + 1) * 256].rearrange("(j p) v -> p j v", p=128), obf[:, :, :], (obf,), (), obf)
                P.flush()
        if "E" in phases:
            phase_ffn(1, h1_d, y_d, True, True)
    return nc


def _consts():
    c = {}
    c["c_ident"] = np.eye(128, dtype=np.float32)
    kk = np.arange(128)[:, None]
    qq = np.arange(128)[None, :]
    c["c_tri"] = np.concatenate([(kk <= qq), (kk > qq)], axis=1).astype(np.float32)
    cc = np.arange(128)[:, None]
    tt = np.arange(S)[None, :]
    c["c_maskc"] = ((16 * cc + 31 <= tt) & (cc < NCMP)).astype(np.float32)
    half = 12
    inv = (np.float32(500000.0) ** (-np.arange(half, dtype=np.float32) * np.float32(2.0) / np.float32(24))).astype(np.float32)
    ang = (np.arange(S, dtype=np.float32)[None, :] * inv[:, None]).astype(np.float32)
    cs = np.cos(ang.astype(np.float64)).astype(np.float32)
    sn = np.sin(ang.astype(np.float64)).astype(np.float32)
    rc = np.ones((128, S), np.float32)
    rs = np.zeros((128, S), np.float32)
    rc[32:44] = cs
    rc[44:56] = cs
    rs[32:44] = -sn
    rs[44:56] = sn
    c["c_ropec"] = rc
    c["c_ropes"] = rs
    pm = np.zeros((128, 64), np.float32)
    for d in range(12):
        pm[32 + d + 12, 32 + d] = 1.0
        pm[32 + d, 32 + d + 12] = 1.0
    c["c_pm"] = pm
    c["c_esel"] = (np.arange(S)[None, :] // 64 == np.arange(32)[:, None]).astype(np.float32)
    ncmp = NCMP
    cs_ = np.arange(ncmp) * 16
    ss_ = np.arange(32) * 64
    ovl = ((cs_[:, None] < ss_[None, :] + 64) & (cs_[:, None] + 32 > ss_[None, :])).astype(np.float32)
    o = np.zeros((128, 33), np.float32)
    o[:ncmp, :32] = ovl
    o[:ncmp, 32] = 1.0
    c["c_ovl"] = o
    t = np.arange(S)[:, None]
    n = np.arange(32)[None, :]
    cur = t // 64
    forced = (n == 0) | (n == cur) | (n == cur - 1)
    vis = (n * 64 <= t)
    c["c_tkadd"] = np.where(forced, 1e9, np.where(vis, 0.0, -1e9)).astype(np.float32)
    c["c_tkmul"] = (vis & ~forced).astype(np.float32)
    return c


def _kc(w):
    K_, M = w.shape
    return np.ascontiguousarray(w.reshape(K_ // 128, 128, M).transpose(1, 0, 2).reshape(128, -1))


def _prep_weights(inp):
    f = lambda a: np.ascontiguousarray(np.asarray(a, dtype=np.float32))
    w = {}
    ng = inp["norm_g"]
    gs = np.stack([ng[0, 0], ng[0, 1], inp["kv_norm_g"], ng[1, 0], ng[1, 1]], 0)
    w["g_all"] = f(gs.reshape(5, 8, 128).transpose(2, 0, 1))
    w["g_fin"] = f(inp["final_g"].reshape(1, D))
    w["w_in"] = _kc(f(inp["a_w_in"][0]))
    w["conv_w"] = f(inp["a_conv_w"][0].reshape(4, NBLK, RB).transpose(2, 1, 0).reshape(RB, NBLK * 4))
    sm = np.stack([inp["a_conv_b"][0].reshape(NBLK, RB).T, inp["a_b_ra"][0].T, inp["a_b_ix"][0].T,
                   inp["a_lambda"][0].reshape(NBLK, RB).T], 1)
    w["a_small"] = f(sm)
    w["w_ra"] = f(inp["a_w_ra"][0].transpose(1, 0, 2).reshape(RB, NBLK * RB))
    w["w_ix"] = f(inp["a_w_ix"][0].transpose(1, 0, 2).reshape(RB, NBLK * RB))
    w["w_out"] = f(inp["a_w_out"][0].reshape(NBLK, RB, D).transpose(1, 0, 2).reshape(RB, NBLK * D))
    for l in range(2):
        w["w_gu%d" % l] = _kc(f(inp["ffn_w_gu"][l]))
        w["w_dn%d" % l] = f(inp["ffn_w_down"][l].reshape(NFC, 128, D).transpose(1, 0, 2).reshape(128, NFC * D))
    kv = f(inp["kv_w"])
    kcmp_, vcmp_, ksel_, vsel_, kwin_, vwin_ = kv[:, 0:384], kv[:, 384:640], kv[:, 640:1024], kv[:, 1024:1280], kv[:, 1280:1664], kv[:, 1664:1920]
    w["w_kc"] = _kc(kcmp_)
    w["w_vc"] = _kc(vcmp_)

    def aug(m, n):
        z = np.zeros((m.shape[0], n, 128), np.float32)
        z[:, :, 32:] = m.reshape(m.shape[0], n, 96)
        return z.reshape(m.shape[0], n * 128)
    w["w_ks"] = _kc(aug(ksel_, NG))
    w["w_kw"] = _kc(aug(kwin_, NG))
    w["w_v"] = _kc(np.concatenate([vsel_, vwin_], 1))
    wq = f(inp["b_w_q"][0])
    w["w_q"] = _kc(aug(wq[:, :NH * DQK], NH))
    w["w_qg"] = _kc(wq[:, NH * DQK:])
    w["gate_b"] = f(inp["b_gate_bias"][0].reshape(1, 48))
    w["w_o"] = _kc(f(inp["b_w_o"][0]))
    w["w1k"] = f(inp["cmp_w1_k"].reshape(32, 96, 256).transpose(1, 0, 2).reshape(96, 32 * 256))
    w["w1v"] = f(inp["cmp_w1_v"].reshape(32, 64, 256).transpose(1, 0, 2).reshape(64, 32 * 256))
    w2k = np.zeros((256, 128), np.float32)
    w2k[:, 32:] = inp["cmp_w2_k"]
    w["w2k"] = f(w2k.reshape(2, 128, 128).transpose(1, 0, 2).reshape(128, 256))
    w["w2v"] = f(np.asarray(inp["cmp_w2_v"], np.float32).reshape(2, 128, 64).transpose(1, 0, 2).reshape(128, 128))
    w["posk"] = f(np.asarray(inp["cmp_pos_k"]).T)
    w["posv"] = f(np.asarray(inp["cmp_pos_v"]).T)
    return w


_CACHE = {}


def kernel(**inputs):
    inp = {k_: np.asarray(v) for k_, v in inputs.items()}
    ncores = 8
    nseq = inp["x"].shape[0] // ncores
    if "nc" not in _CACHE:
        _CACHE["nc"] = build_program(nseq)
    nc = _CACHE["nc"]
    shared = _consts()
    shared.update(_prep_weights(inp))
    x = np.ascontiguousarray(inp["x"], dtype=np.float32).reshape(ncores, nseq * S, D)
    in_maps = []
    for c in range(ncores):
        m = dict(shared)
        m["x"] = x[c]
        in_maps.append(m)
    res = run_bass_kernel_spmd(nc, in_maps, core_ids=list(range(ncores)))
    out = np.stack([np.asarray(r["y"]) for r in res.results], 0)
    return out.reshape(inp["x"].shape).astype(np.float32)
```
